# Optimizing a Trainium2 kernel written in Bass

```python
import math
import jax, jax.numpy as jnp
from jax import lax
import numpy as np

D_MODEL = 1024
BATCH = 8
SEQ = 4096
DEPTH = 4

CTX_LEN = 256
GRID_W = 64
HEAD_DIM = 64
RWKV_HEADS = D_MODEL // 128
RWKV_WIDTH = RWKV_HEADS * HEAD_DIM
RWKV_W_RANK = 64
RWKV_A_RANK = 64
RWKV_G_RANK = 128
ATTN_Q_HEADS = D_MODEL // HEAD_DIM
ATTN_KV_HEADS = ATTN_Q_HEADS // 4
GQA_GROUP = ATTN_Q_HEADS // ATTN_KV_HEADS
ATTN_Q_WIDTH = ATTN_Q_HEADS * HEAD_DIM
ATTN_KV_WIDTH = ATTN_KV_HEADS * HEAD_DIM
ATTN_SCALE = HEAD_DIM ** -0.5
Q_BLOCK = 128
ROPE_THETA = 10000.0
ROPE_PAIRS = HEAD_DIM // 4
SSM_WIDTH = D_MODEL // 2
SSM_GROUP = 16
SSM_GROUPS = SSM_WIDTH // SSM_GROUP
SSM_STATE = 64
D_FF = 4 * D_MODEL
N_BRANCHES = 3
RWKV_COLS = 3 * RWKV_WIDTH + RWKV_W_RANK + RWKV_A_RANK + RWKV_G_RANK
ATTN_COLS = ATTN_Q_WIDTH + 2 * ATTN_KV_WIDTH
SSM_COLS = SSM_WIDTH
GATE_COLS = N_BRANCHES * D_MODEL
IN_COLS = RWKV_COLS + ATTN_COLS + SSM_COLS + GATE_COLS
COL_SPLITS = [RWKV_COLS, RWKV_COLS + ATTN_COLS, RWKV_COLS + ATTN_COLS + SSM_COLS]
RWKV_SPLITS = [RWKV_WIDTH, 2 * RWKV_WIDTH, 3 * RWKV_WIDTH, 3 * RWKV_WIDTH + RWKV_W_RANK, 3 * RWKV_WIDTH + RWKV_W_RANK + RWKV_A_RANK]
ALPHA = (2 * DEPTH) ** 0.25
BETA = (8 * DEPTH) ** -0.25
LN_EPS = 1e-5
RMS_EPS = 1e-6
GN_EPS = 64e-5

kernel_name = 'hybrid_rwkv7_gqa_s5_dit_block'


def layer_norm(x, g, b):
    xf = x.astype(jnp.float32)
    mu = xf.mean(-1, keepdims=True)
    var = jnp.square(xf - mu).mean(-1, keepdims=True)
    return ((xf - mu) * lax.rsqrt(var + LN_EPS) * g + b).astype(x.dtype)


def rms_norm(x, g):
    xf = x.astype(jnp.float32)
    return (xf * lax.rsqrt(jnp.square(xf).mean(-1, keepdims=True) + RMS_EPS) * g).astype(x.dtype)


def centred_shift(p):
    zero = jnp.zeros_like(p[:, :1])
    prev = jnp.concatenate([zero, p[:, :-1]], axis=1)
    nxt = jnp.concatenate([p[:, 1:], zero], axis=1)
    return 0.5 * (prev + nxt)


def bidir_shared(t):
    return jnp.stack([t, jnp.flip(t, axis=1)])


def bidir_split(t):
    return jnp.stack([t[0], jnp.flip(t[1], axis=1)])


def axial_rope_tables(n_tokens):
    rows = n_tokens // GRID_W
    row = jnp.repeat(jnp.arange(rows, dtype=jnp.float32), GRID_W)
    col = jnp.tile(jnp.arange(GRID_W, dtype=jnp.float32), rows)
    inv = ROPE_THETA ** (-jnp.arange(ROPE_PAIRS, dtype=jnp.float32) / ROPE_PAIRS)
    ang = jnp.stack([row, col], axis=-1)[:, :, None] * inv
    ang = jnp.broadcast_to(ang[:, :, None, :], (n_tokens, 2, 2, ROPE_PAIRS)).reshape(n_tokens, HEAD_DIM)
    return jnp.cos(ang), jnp.sin(ang)


def apply_axial_rope(x, cos, sin):
    xf = x.astype(jnp.float32)
    xr = xf.reshape(*x.shape[:-1], 2, 2, ROPE_PAIRS)
    rot = jnp.stack([-xr[..., 1, :], xr[..., 0, :]], axis=-2).reshape(x.shape)
    return (xf * cos[None, :, None, :] + rot * sin[None, :, None, :]).astype(x.dtype)


def rwkv7_scan(state0, r, decay, k, v, kk, a):
    def step(state, inp):
        r_t, w_t, k_t, v_t, kk_t, a_t = inp
        sa = jnp.einsum('dbhvk,dbhk->dbhv', state, -kk_t)
        state = (state * w_t[..., None, :] + sa[..., :, None] * (kk_t * a_t)[..., None, :]
                 + v_t[..., :, None] * k_t[..., None, :])
        return state, jnp.einsum('dbhvk,dbhk->dbhv', state, r_t)
    xs = tuple(jnp.moveaxis(t.astype(jnp.float32), 2, 0) for t in (r, decay, k, v, kk, a))
    state, ys = lax.scan(step, state0, xs)
    return state, jnp.moveaxis(ys, 0, 2)


def rwkv7_branch(p_lat, p_ctx, mu, w0, w_up, a0, a_up, g_up, k_k, k_a, r_k, gn_w, gn_b, need_ctx_out):
    r_k_h = r_k.reshape(RWKV_HEADS, HEAD_DIM)

    def heads(t):
        return t.reshape(*t.shape[:-1], RWKV_HEADS, HEAD_DIM)

    def prepare(p):
        p = p + mu * (centred_shift(p) - p)
        r, k, v, wd, ad, gd = jnp.split(p, RWKV_SPLITS, axis=-1)
        g = jax.nn.sigmoid(gd) @ g_up
        w_pre = w0[:, None, None, :] + jnp.einsum('btr,drc->dbtc', jnp.tanh(wd), w_up)
        decay = jnp.exp(-jnp.exp(-jax.nn.softplus(-w_pre.astype(jnp.float32)) - 0.5))
        a = jax.nn.sigmoid(a0[:, None, None, :] + jnp.einsum('btr,drc->dbtc', ad, a_up))
        kk = heads(k * k_k).astype(jnp.float32)
        kk = kk * lax.rsqrt(jnp.maximum(jnp.sum(jnp.square(kk), axis=-1, keepdims=True), 1e-24))
        k_dir = k[None] * (1.0 + (a - 1.0) * k_a)
        return heads(r), heads(k_dir), heads(v), kk, heads(decay), heads(a), g

    def readout(ys, r, k_dir, v, g):
        n_b, n_t = g.shape[:2]
        y = ys[0] + jnp.flip(ys[1], axis=1)
        mean = y.mean(-1, keepdims=True)
        var = jnp.square(y - mean).mean(-1, keepdims=True)
        y = ((y - mean) * lax.rsqrt(var + GN_EPS)).reshape(n_b, n_t, RWKV_WIDTH) * gn_w + gn_b
        bonus = jnp.sum(r[None] * k_dir * r_k_h, axis=(0, -1))[..., None] * v
        return ((y + bonus.reshape(n_b, n_t, RWKV_WIDTH)) * g).astype(g.dtype)

    rc, kc, vc, kkc, dc, ac, gc = prepare(p_ctx)
    state0 = jnp.zeros((2, p_ctx.shape[0], RWKV_HEADS, HEAD_DIM, HEAD_DIM), jnp.float32)
    state_ctx, ys_c = rwkv7_scan(state0, bidir_shared(rc), bidir_split(dc), bidir_split(kc),
                                 bidir_shared(vc), bidir_shared(kkc), bidir_split(ac))
    rl, kl, vl, kkl, dl, al, gl = prepare(p_lat)
    _, ys_l = rwkv7_scan(state_ctx, bidir_shared(rl), bidir_split(dl), bidir_split(kl),
                         bidir_shared(vl), bidir_shared(kkl), bidir_split(al))
    y_lat = readout(ys_l, rl, kl, vl, gl)
    y_ctx = readout(ys_c, rc, kc, vc, gc) if need_ctx_out else None
    return y_lat, y_ctx


def grouped_attend(q, k, v):
    s = jnp.einsum('bqhgd,bkhd->bhgqk', q, k).astype(jnp.float32) * ATTN_SCALE
    p = jax.nn.softmax(s, axis=-1).astype(v.dtype)
    return jnp.einsum('bhgqk,bkhd->bqhgd', p, v)


def gqa_axial_branch(p_lat, p_ctx, q_gain, k_gain, cos, sin, need_ctx_out):
    def qkv(p):
        n_b, n_t = p.shape[:2]
        q, k, v = jnp.split(p, [ATTN_Q_WIDTH, ATTN_Q_WIDTH + ATTN_KV_WIDTH], axis=-1)
        q = rms_norm(q.reshape(n_b, n_t, ATTN_Q_HEADS, HEAD_DIM), q_gain)
        k = rms_norm(k.reshape(n_b, n_t, ATTN_KV_HEADS, HEAD_DIM), k_gain)
        return q, k, v.reshape(n_b, n_t, ATTN_KV_HEADS, HEAD_DIM)

    q_c, k_c, v_c = qkv(p_ctx)
    q_l, k_l, v_l = qkv(p_lat)
    q_l = apply_axial_rope(q_l, cos, sin)
    k_l = apply_axial_rope(k_l, cos, sin)
    k_all = jnp.concatenate([k_c, k_l], axis=1)
    v_all = jnp.concatenate([v_c, v_l], axis=1)
    n_b, n_t = q_l.shape[:2]
    q_blocks = q_l.reshape(n_b, n_t // Q_BLOCK, Q_BLOCK, ATTN_KV_HEADS, GQA_GROUP, HEAD_DIM).swapaxes(0, 1)
    o = lax.map(lambda qb: grouped_attend(qb, k_all, v_all), q_blocks)
    y_lat = o.swapaxes(0, 1).reshape(n_b, n_t, ATTN_Q_WIDTH)
    y_ctx = None
    if need_ctx_out:
        n_c = q_c.shape[1]
        y_ctx = grouped_attend(q_c.reshape(n_b, n_c, ATTN_KV_HEADS, GQA_GROUP, HEAD_DIM), k_c, v_c).reshape(n_b, n_c, ATTN_Q_WIDTH)
    return y_lat, y_ctx


def complex_affine_combine(earlier, later):
    a1r, a1i, b1r, b1i = earlier
    a2r, a2i, b2r, b2i = later
    return (a2r * a1r - a2i * a1i, a2r * a1i + a2i * a1r,
            a2r * b1r - a2i * b1i + b2r, a2r * b1i + a2i * b1r + b2i)


def zoh_discretise(a_re, a_im, log_dt):
    lam_re = jnp.minimum(a_re.astype(jnp.float32), -1e-4)
    lam_im = a_im.astype(jnp.float32)
    dt = jnp.exp(log_dt.astype(jnp.float32))[:, None]
    mag = jnp.exp(lam_re * dt)
    abar_re, abar_im = mag * jnp.cos(lam_im * dt), mag * jnp.sin(lam_im * dt)
    nr, ni = abar_re - 1.0, abar_im
    den = jnp.square(lam_re) + jnp.square(lam_im)
    coef_re = (nr * lam_re + ni * lam_im) / den
    coef_im = (ni * lam_re - nr * lam_im) / den
    return abar_re, abar_im, coef_re, coef_im


def s5_scan(bu_re, bu_im, disc, h0, reverse):
    abar_re, abar_im, coef_re, coef_im = disc
    b_re = coef_re * bu_re - coef_im * bu_im
    b_im = coef_re * bu_im + coef_im * bu_re
    if reverse:
        b_re, b_im = jnp.flip(b_re, axis=1), jnp.flip(b_im, axis=1)
    if h0 is not None:
        h_re, h_im = h0
        b_re = b_re.at[:, 0].add(abar_re * h_re - abar_im * h_im)
        b_im = b_im.at[:, 0].add(abar_re * h_im + abar_im * h_re)
    n_t = b_re.shape[1]
    a_r = jnp.broadcast_to(abar_re, (1, n_t, SSM_GROUPS, SSM_STATE))
    a_i = jnp.broadcast_to(abar_im, (1, n_t, SSM_GROUPS, SSM_STATE))
    _, _, x_re, x_im = lax.associative_scan(complex_affine_combine, (a_r, a_i, b_re, b_im), axis=1)
    final = (x_re[:, -1], x_im[:, -1])
    if reverse:
        x_re, x_im = jnp.flip(x_re, axis=1), jnp.flip(x_im, axis=1)
    return x_re, x_im, final


def s5_branch(u_lat, u_ctx, a_re, a_im, log_dt, b_re, b_im, c_re, c_im, d_skip, glu_w, glu_b, need_ctx_out):
    def drive(u):
        ug = u.reshape(u.shape[0], u.shape[1], SSM_GROUPS, SSM_GROUP).astype(jnp.float32)
        return (jnp.einsum('btgi,gni->btgn', ug, b_re.astype(jnp.float32)),
                jnp.einsum('btgi,gni->btgn', ug, b_im.astype(jnp.float32)))

    def readout(x_re, x_im, u):
        y = (jnp.einsum('btgn,gin->btgi', x_re, c_re.astype(jnp.float32))
             - jnp.einsum('btgn,gin->btgi', x_im, c_im.astype(jnp.float32)))
        y = y.reshape(u.shape).astype(u.dtype) + d_skip * u
        y = jax.nn.gelu(y)
        return y * jax.nn.sigmoid(y @ glu_w + glu_b)

    bu_c = drive(u_ctx)
    bu_l = drive(u_lat)
    lat_states, ctx_states = [], []
    for d in range(2):
        disc = zoh_discretise(a_re[d], a_im[d], log_dt[d])
        xc_re, xc_im, h_ctx = s5_scan(bu_c[0], bu_c[1], disc, None, d == 1)
        xl_re, xl_im, _ = s5_scan(bu_l[0], bu_l[1], disc, h_ctx, d == 1)
        lat_states.append((xl_re, xl_im))
        ctx_states.append((xc_re, xc_im))
    y_lat = readout(lat_states[0][0] + lat_states[1][0], lat_states[0][1] + lat_states[1][1], u_lat)
    y_ctx = None
    if need_ctx_out:
        y_ctx = readout(ctx_states[0][0] + ctx_states[1][0], ctx_states[0][1] + ctx_states[1][1], u_ctx)
    return y_lat, y_ctx


def gated_merge(ya, yb, yc, pg, proj_a, proj_b, proj_c, w_out):
    ga, gb, gc = jnp.split(jax.nn.sigmoid(pg), N_BRANCHES, axis=-1)
    merged = ga * (ya @ proj_a) + gb * (yb @ proj_b) + gc * (yc @ proj_c)
    return merged @ w_out


def squared_relu_mlp(h, w1, w2):
    return jnp.square(jax.nn.relu(h @ w1)) @ w2


def setup_inputs(seed: int = 0) -> dict:
    key = jax.random.key(seed)
    ks = iter(jax.random.split(key, 48))

    def nrm(shape, scale):
        return scale * jax.random.normal(next(ks), shape, jnp.float32)

    def unif(shape, lo, hi):
        return jax.random.uniform(next(ks), shape, jnp.float32, minval=lo, maxval=hi)

    L, D = DEPTH, D_MODEL
    n_idx = jnp.arange(SSM_STATE, dtype=jnp.float32)
    return {
        'x': nrm((BATCH, SEQ, D), 1.0),
        'c': nrm((BATCH, D), 1.0),
        'ctx': nrm((BATCH, CTX_LEN, D), 1.0),
        'c_ctx': nrm((D,), 1.0),
        'mod_w': nrm((L, D, 6 * D), 0.5 * D ** -0.5),
        'mod_b': nrm((L, 6 * D), 0.02),
        'w_in': nrm((L, D, IN_COLS), D ** -0.5),
        'rwkv_mu': unif((L, RWKV_COLS), 0.0, 1.0),
        'rwkv_w0': unif((L, 2, RWKV_WIDTH), -6.0, -1.0),
        'rwkv_w_up': nrm((L, 2, RWKV_W_RANK, RWKV_WIDTH), 0.5 * RWKV_W_RANK ** -0.5),
        'rwkv_a0': nrm((L, 2, RWKV_WIDTH), 0.1),
        'rwkv_a_up': nrm((L, 2, RWKV_A_RANK, RWKV_WIDTH), RWKV_A_RANK ** -0.5),
        'rwkv_g_up': nrm((L, RWKV_G_RANK, RWKV_WIDTH), RWKV_G_RANK ** -0.5),
        'rwkv_k_k': 0.85 + nrm((L, RWKV_WIDTH), 0.02),
        'rwkv_k_a': 1.0 + nrm((L, RWKV_WIDTH), 0.02),
        'rwkv_r_k': nrm((L, RWKV_WIDTH), 0.1),
        'rwkv_gn_w': 1.0 + nrm((L, RWKV_WIDTH), 0.02),
        'rwkv_gn_b': nrm((L, RWKV_WIDTH), 0.02),
        'attn_q_gain': 1.0 + nrm((L, HEAD_DIM), 0.02),
        'attn_k_gain': 1.0 + nrm((L, HEAD_DIM), 0.02),
        'ssm_a_re': -0.5 + nrm((L, 2, SSM_GROUPS, SSM_STATE), 0.01),
        'ssm_a_im': jnp.pi * n_idx + nrm((L, 2, SSM_GROUPS, SSM_STATE), 0.01),
        'ssm_log_dt': unif((L, 2, SSM_GROUPS), math.log(1e-3), math.log(1e-1)),
        'ssm_b_re': nrm((L, SSM_GROUPS, SSM_STATE, SSM_GROUP), (2 * SSM_GROUP) ** -0.5),
        'ssm_b_im': nrm((L, SSM_GROUPS, SSM_STATE, SSM_GROUP), (2 * SSM_GROUP) ** -0.5),
        'ssm_c_re': nrm((L, SSM_GROUPS, SSM_GROUP, SSM_STATE), 0.5),
        'ssm_c_im': nrm((L, SSM_GROUPS, SSM_GROUP, SSM_STATE), 0.5),
        'ssm_d': nrm((L, SSM_WIDTH), 0.5),
        'ssm_glu_w': nrm((L, SSM_WIDTH, SSM_WIDTH), SSM_WIDTH ** -0.5),
        'ssm_glu_b': nrm((L, SSM_WIDTH), 0.02),
        'proj_a': nrm((L, RWKV_WIDTH, D), RWKV_WIDTH ** -0.5),
        'proj_b': nrm((L, ATTN_Q_WIDTH, D), ATTN_Q_WIDTH ** -0.5),
        'proj_c': nrm((L, SSM_WIDTH, D), SSM_WIDTH ** -0.5),
        'w_out': nrm((L, D, D), BETA * D ** -0.5),
        'ln1_g': 1.0 + nrm((L, D), 0.02),
        'ln1_b': nrm((L, D), 0.02),
        'ln2_g': 1.0 + nrm((L, D), 0.02),
        'ln2_b': nrm((L, D), 0.02),
        'mlp_w1': nrm((L, D, D_FF), D ** -0.5),
        'mlp_w2': nrm((L, D_FF, D), BETA * D_FF ** -0.5),
    }


def reference(x, c, ctx, c_ctx, mod_w, mod_b, w_in, rwkv_mu, rwkv_w0, rwkv_w_up, rwkv_a0, rwkv_a_up, rwkv_g_up,
              rwkv_k_k, rwkv_k_a, rwkv_r_k, rwkv_gn_w, rwkv_gn_b, attn_q_gain, attn_k_gain, ssm_a_re, ssm_a_im,
              ssm_log_dt, ssm_b_re, ssm_b_im, ssm_c_re, ssm_c_im, ssm_d, ssm_glu_w, ssm_glu_b, proj_a, proj_b, proj_c,
              w_out, ln1_g, ln1_b, ln2_g, ln2_b, mlp_w1, mlp_w2):
    cos, sin = axial_rope_tables(x.shape[1])
    c_silu = jax.nn.silu(c)
    cc_silu = jax.nn.silu(c_ctx)
    xc = ctx
    for l in range(DEPTH):
        update_ctx = l < DEPTH - 1
        sh1, sc1, gt1, sh2, sc2, gt2 = jnp.split((c_silu @ mod_w[l] + mod_b[l])[:, None, :], 6, axis=-1)
        sh1c, sc1c, gt1c, sh2c, sc2c, gt2c = jnp.split(cc_silu @ mod_w[l] + mod_b[l], 6, axis=-1)
        proj = (x * (1.0 + sc1) + sh1) @ w_in[l]
        proj_ctx = (xc * (1.0 + sc1c) + sh1c) @ w_in[l]
        pa, pb, pc, pg = jnp.split(proj, COL_SPLITS, axis=-1)
        pa_c, pb_c, pc_c, pg_c = jnp.split(proj_ctx, COL_SPLITS, axis=-1)
        ya, ya_c = rwkv7_branch(pa, pa_c, rwkv_mu[l], rwkv_w0[l], rwkv_w_up[l], rwkv_a0[l], rwkv_a_up[l],
                                rwkv_g_up[l], rwkv_k_k[l], rwkv_k_a[l], rwkv_r_k[l], rwkv_gn_w[l], rwkv_gn_b[l],
                                update_ctx)
        yb, yb_c = gqa_axial_branch(pb, pb_c, attn_q_gain[l], attn_k_gain[l], cos, sin, update_ctx)
        yc, yc_c = s5_branch(pc, pc_c, ssm_a_re[l], ssm_a_im[l], ssm_log_dt[l], ssm_b_re[l], ssm_b_im[l],
                             ssm_c_re[l], ssm_c_im[l], ssm_d[l], ssm_glu_w[l], ssm_glu_b[l], update_ctx)
        mix = gated_merge(ya, yb, yc, pg, proj_a[l], proj_b[l], proj_c[l], w_out[l])
        x_mid = layer_norm(ALPHA * x + gt1 * mix, ln1_g[l], ln1_b[l])
        ff = squared_relu_mlp(x_mid * (1.0 + sc2) + sh2, mlp_w1[l], mlp_w2[l])
        x = layer_norm(ALPHA * x_mid + gt2 * ff, ln2_g[l], ln2_b[l])
        if update_ctx:
            mix_c = gated_merge(ya_c, yb_c, yc_c, pg_c, proj_a[l], proj_b[l], proj_c[l], w_out[l])
            xc_mid = layer_norm(ALPHA * xc + gt1c * mix_c, ln1_g[l], ln1_b[l])
            ff_c = squared_relu_mlp(xc_mid * (1.0 + sc2c) + sh2c, mlp_w1[l], mlp_w2[l])
            xc = layer_norm(ALPHA * xc_mid + gt2c * ff_c, ln2_g[l], ln2_b[l])
    return x
```

```python
import numpy as np
from contextlib import ExitStack
import concourse.bass as bass
import concourse.mybir as mybir
from concourse.bass_utils import run_bass_kernel_spmd

F32 = mybir.dt.float32
BF16 = mybir.dt.bfloat16
AF = mybir.ActivationFunctionType
ALU = mybir.AluOpType
AX = mybir.AxisListType

D = 1024
DEPTH = 4
NCTX = 256
NLAT = 4096
N = NCTX + NLAT
NT = N // 128
INC = 6912
C_RW, C_AT, C_SS, C_G = 0, 1792, 3328, 3840
ALPHA = (2 * DEPTH) ** 0.25
LN_EPS, RMS_EPS, GN_EPS = 1e-5, 1e-6, 64e-5
PI = float(np.pi)
import os
DSTAGE = int(os.environ.get("DSTAGE", "9"))
DSUB = int(os.environ.get("DSUB", "9"))


class _Res:
    __slots__ = ("lw", "rd")

    def __init__(self):
        self.lw = None
        self.rd = {}


class _Eng:
    def __init__(self, name, h, sem, selfsync):
        self.name, self.h, self.sem, self.selfsync = name, h, sem, selfsync
        self.cnt = 0
        self.seen = {}


class Sched:
    def __init__(self, nc, stack, ndma=(("sp", 8), ("pool", 6), ("act", 3))):
        self.nc = nc
        hs = {"pe": nc.tensor, "act": nc.scalar, "dve": nc.vector, "pool": nc.gpsimd, "sp": nc.sync}
        self.engs = {}
        for n, h in hs.items():
            sem = stack.enter_context(nc.semaphore("s_" + n))
            self.engs[n] = _Eng(n, h, sem, selfsync=(n in ("dve", "act", "pool")))
        self.dsems, self.dnext = {}, {}
        for q, k in ndma:
            self.dsems[q] = [[stack.enter_context(nc.semaphore(f"d_{q}{i}")), 0] for i in range(k)]
            self.dnext[q] = 0
        self.res = {}
        self.nins = 0

    def R(self, k):
        r = self.res.get(k)
        if r is None:
            r = self.res[k] = _Res()
        return r

    def _deps(self, r, w):
        deps = {}

        def need(ev):
            if ev is not None and deps.get(id(ev[0]), (None, 0))[1] < ev[1]:
                deps[id(ev[0])] = ev

        for k in r:
            need(self.R(k).lw)
        for k in w:
            R = self.R(k)
            need(R.lw)
            for ev in R.rd.values():
                need(ev)
        return deps

    def _wait(self, E, deps):
        for s, v in deps.values():
            if s is E.sem and not E.selfsync:
                continue
            if E.seen.get(id(s), 0) < v:
                E.seen[id(s)] = v
                E.h.wait_ge(s, v)

    def _mark(self, r, w, ev):
        for k in r:
            self.R(k).rd[id(ev[0])] = ev
        for k in w:
            R = self.R(k)
            R.lw = ev
            R.rd = {}

    def op(self, en, fn, r=(), w=()):
        E = self.engs[en]
        self._wait(E, self._deps(r, w))
        E.cnt += 1
        fn(E.h).then_inc(E.sem, 1)
        self.nins += 1
        self._mark(r, w, (E.sem, E.cnt))

    def dma(self, q, out, in_, r=(), w=(), **kw):
        E = self.engs[q]
        lst = self.dsems[q]
        ds = lst[self.dnext[q] % len(lst)]
        self.dnext[q] += 1
        deps = self._deps(r, w)
        if ds[1] > 0:
            deps[id(ds[0])] = (ds[0], ds[1])
        self._wait(E, deps)
        ds[1] += 16
        E.h.dma_start(out=out, in_=in_, **kw).then_inc(ds[0], 16)
        self.nins += 1
        self._mark(r, w, (ds[0], ds[1]))

    def barrier(self):
        evs = [(E.sem, E.cnt) for E in self.engs.values() if E.cnt]
        for lst in self.dsems.values():
            evs += [(s, v) for s, v in lst if v]
        for E in self.engs.values():
            for s, v in evs:
                if s is E.sem and not E.selfsync:
                    continue
                if E.seen.get(id(s), 0) < v:
                    E.seen[id(s)] = v
                    E.h.wait_ge(s, v)
        self.res = {}


def build(n_layers=DEPTH, stop_after=None, dbg=()):
    nc = bass.Bass("TRN2", target_bir_lowering=False)
    top = ExitStack()
    S = Sched(nc, top)
    dram_in = {}

    def din(name, shape, dt=F32):
        dram_in[name] = nc.dram_tensor(name, list(shape), dt, kind="ExternalInput").ap()
        return dram_in[name]

    def dscr(name, shape, dt=F32):
        kind = "ExternalOutput" if name in dbg else "Internal"
        return nc.dram_tensor(name, list(shape), dt, kind=kind).ap()

    x_in = din("x_in", [N, D])
    cT_in = din("cT", [128, 8, 2])
    ident_in = din("ident", [128, 128])
    mod_w = din("mod_w", [DEPTH, D, 6 * D])
    mod_bT = din("mod_bT", [DEPTH, 128, 48])
    w_in = din("w_in", [DEPTH, D, INC])
    out = nc.dram_tensor("out", [NLAT, D], F32, kind="ExternalOutput").ap()
    qkgain_in = din("qkgain", [DEPTH, 128, 1280])
    rope_cos = din("rope_cos", [NLAT, 64])
    rope_sin = din("rope_sin", [NLAT, 64])
    ones_in = din("ones", [128, 128])
    masks_in = din("masks", [4, 128, 128])
    mu_in = din("mu_bc", [DEPTH, 128, 1792])
    rwp_in = din("rwp_bc", [DEPTH, 5, 128, 512])
    lrb_in = din("lrb", [DEPTH, 4, 512])
    lrw_in = din("lrw", [DEPTH, 128, 2, 512])
    gup_in = din("g_up", [DEPTH, 128, 512])
    sstab_in = din("ss_tab", [DEPTH, 2, 3, 128, 2048])
    mcol_in = din("mcol", [128, 4])
    bblk_in = din("ss_bblk", [DEPTH, 2, 128, 4, 512])
    cblk_in = din("ss_cblk", [DEPTH, 128, 32, 128])
    ssvec_in = din("ss_vec", [DEPTH, 128, 2, 4])
    gluw_in = din("glu_w", [DEPTH, 512, 512])
    sel_in = din("sel", [4, 128, 128])
    proja_in = din("proj_a", [DEPTH, 512, D])
    projb_in = din("proj_b", [DEPTH, D, D])
    projc_in = din("proj_c", [DEPTH, 512, D])
    wout_in = din("w_out", [DEPTH, D, D])
    ln_in = din("ln_bc", [DEPTH, 4, 128, D])
    w1_in = din("mlp_w1", [DEPTH, D, 4 * D])
    w2_in = din("mlp_w2", [DEPTH, 4 * D, D])

    xs = dscr("xs", [N, D])
    p_tm = dscr("p_tm", [N, C_SS])
    p_ssT = dscr("p_ssT", [512, N])
    gT = dscr("gT", [3072, N], BF16)
    ybT = dscr("ybT", [16, 64, N], BF16)
    rw_s = {nm: dscr("rw_" + nm, [N, 512]) for nm in ("r", "v", "kk", "lw0", "lw1", "b0", "b1", "kd0", "kd1", "g", "y0", "y1")}
    yaT = dscr("yaT", [512, N], BF16)
    ysT = [dscr(f"ysT{d}", [512, N]) for d in range(2)]
    ycT = dscr("ycT", [512, N], BF16)
    xmid = dscr("xmid", [N, D])
    xm2T = dscr("xm2T", [D, N], BF16)
    h1T = dscr("h1T", [4 * D, N], BF16)

    uid = [0]

    def sb(st, name, shape, dt=F32):
        uid[0] += 1
        return st.enter_context(nc.sbuf_tensor(f"sb{uid[0]}_{name}", list(shape), dt))

    def ps(st, name, shape, dt=F32):
        uid[0] += 1
        ne = 512 if dt == F32 else 1024
        t = st.enter_context(nc.psum_tensor(f"ps{uid[0]}_{name}", [128, ne], dt))
        n = int(np.prod(shape[1:]))
        v = t[0:shape[0], 0:n]
        if len(shape) == 3:
            v = v.rearrange("p (a b) -> p a b", b=shape[2])
        return v

    def run_interleaved(gens):
        gens = list(gens)
        while gens:
            for g_ in list(gens):
                try:
                    next(g_)
                except StopIteration:
                    gens.remove(g_)

    def tt_(en, o, a, b, op, r, w):
        S.op(en, lambda e: e.tensor_tensor(out=o, in0=a, in1=b, op=op), r=r, w=w)

    def ts_(en, o, a, s1, s2, op0, op1, r, w):
        if s2 is None:
            S.op(en, lambda e: e.tensor_scalar(out=o, in0=a, scalar1=s1, scalar2=None, op0=op0), r=r, w=w)
        else:
            S.op(en, lambda e: e.tensor_scalar(out=o, in0=a, scalar1=s1, scalar2=s2, op0=op0, op1=op1), r=r, w=w)

    def act_(o, a, func, r, w, **kw):
        S.op("act", lambda e: e.activation(out=o, in_=a, func=func, **kw), r=r, w=w)

    def mm_(o, lhsT, rhs, start, stop, r, w):
        S.op("pe", lambda e: e.matmul(o, lhsT=lhsT, rhs=rhs, start=start, stop=stop), r=r, w=w)

    def rstd_(o, a, scale, eps, r, w):
        ts_("dve", o, a, scale, eps, ALU.mult, ALU.add, r=r, w=w)
        act_(o, o, AF.Sqrt, r=w, w=w)
        S.op("dve", lambda e: e.reciprocal(out=o, in_=o), r=w, w=w)

    def tr_(o, a, idn, r, w):
        S.op("pe", lambda e: e.transpose(o, a, idn), r=r, w=w)

    def cp_(en, o, a, r, w):
        S.op(en, lambda e: e.tensor_copy(out=o, in_=a), r=r, w=w)

    def red_(en, o, a, r, w):
        S.op(en, lambda e: e.tensor_reduce(out=o, in_=a, axis=AX.X, op=ALU.add), r=r, w=w)

    ones = sb(top, "ones", [128, 128])
    S.dma("sp", ones[:], ones_in, w=["ones"])
    masks = sb(top, "masks", [128, 4, 128])
    S.dma("sp", masks[:], masks_in.rearrange("m p t -> p m t"), w=["masks"])
    bsum = sb(top, "bsum", [128, NT, 8])
    ident = sb(top, "ident", [128, 128])
    identb = sb(top, "identb", [128, 128], BF16)
    modT = sb(top, "modT", [128, DEPTH, 48, 2])
    csil = sb(top, "csil", [128, 8, 2])
    S.dma("sp", ident[:], ident_in, w=["ident"])
    S.op("act", lambda e: e.activation(out=identb[:], in_=ident[:], func=AF.Copy), r=["ident"], w=["identb"])
    S.dma("sp", csil[:], cT_in, w=["csil"])
    S.op("act", lambda e: e.activation(out=csil[:], in_=csil[:], func=AF.Silu), r=["csil"], w=["csil"])

    for i in range(0, NT, 2):
        S.dma("sp", xs[i * 128:(i + 2) * 128, :], x_in[i * 128:(i + 2) * 128, :], w=[("xs", i), ("xs", i + 1)])

    with ExitStack() as st:
        wm = [sb(st, f"modw{i}", [128, 8, 512]) for i in range(2)]
        mbt = sb(st, "mbt", [128, DEPTH, 48])
        pm = ps(st, "pm", [128, 48, 2])
        S.dma("sp", mbt[:], mod_bT.rearrange("l p c -> p l c"), w=["mbt"])
        it = 0
        for l in range(n_layers):
            for cb in range(12):
                wt = wm[it % 2]
                key = ("modw", it % 2)
                it += 1
                S.dma("sp" if it % 2 else "pool", wt[:],
                      mod_w[l, :, cb * 512:(cb + 1) * 512].rearrange("(kc p) n -> p kc n", p=128), w=[key])
                for j in range(4):
                    cc = cb * 4 + j
                    for kc in range(8):
                        S.op("pe", lambda e, wt=wt, kc=kc, j=j, cc=cc: e.matmul(
                            pm[:, cc, :], lhsT=wt[:, kc, j * 128:(j + 1) * 128], rhs=csil[:, kc, :],
                            start=(kc == 0), stop=(kc == 7)), r=[key, "csil"], w=["pm"])
            for g in range(2):
                S.op("dve", lambda e, l=l, g=g: e.tensor_tensor(out=modT[:, l, :, g], in0=pm[:, :, g], in1=mbt[:, l, :],
                                                                op=ALU.add), r=["pm", "mbt"], w=["modT"])
            for c0 in (8, 32):
                S.op("dve", lambda e, l=l, c0=c0: e.tensor_scalar(out=modT[:, l, c0:c0 + 8, :], in0=modT[:, l, c0:c0 + 8, :],
                                                                  scalar1=1.0, scalar2=None, op0=ALU.add),
                     r=["modT"], w=["modT"])
    S.barrier()
    if stop_after == "A":
        dbg_t = nc.dram_tensor("dbg_modT", [128, DEPTH * 96], F32, kind="ExternalOutput").ap()
        S.dma("sp", dbg_t, modT[:].rearrange("p l c g -> p (l c g)"), r=["modT"])

    for l in range(n_layers):
        if stop_after == "A":
            break
        with ExitStack() as st:
            xmT = sb(st, "xmT", [128, 8, N], BF16)
            xt = [sb(st, f"xt{i}", [128, D]) for i in range(2)]
            pT = [ps(st, f"pT{i}", [128, 4, 128]) for i in range(2)]
            for tt in range(NT):
                g = 1 if tt < 2 else 0
                x_t = xt[tt % 2]
                S.dma("sp", x_t[:], xs[tt * 128:(tt + 1) * 128, :], r=[("xs", tt)], w=[("xt", tt % 2)])
                for hf in range(2):
                    for j in range(4):
                        kc = hf * 4 + j
                        S.op("pe", lambda e, x_t=x_t, kc=kc, hf=hf, j=j: e.transpose(
                            pT[hf][:, j, :], x_t[:, kc * 128:(kc + 1) * 128], ident[:]),
                            r=[("xt", tt % 2), "ident"], w=[("pT", hf)])
                    for j in range(4):
                        kc = hf * 4 + j
                        S.op("act", lambda e, kc=kc, hf=hf, j=j, tt=tt, g=g: e.activation(
                            out=xmT[:, kc, tt * 128:(tt + 1) * 128], in_=pT[hf][:, j, :], func=AF.Identity,
                            scale=modT[:, l, 8 + kc, g:g + 1], bias=modT[:, l, kc, g:g + 1]),
                            r=[("pT", hf), "modT"], w=[("xmT", tt)])
            wb = [sb(st, f"wb{i}", [128, 8, 512], BF16) for i in range(3)]
            pp = [ps(st, f"pp{i}", [128, 512]) for i in range(4)]
            stg = [sb(st, f"stg{i}", [128, 512]) for i in range(4)]
            stgT = [sb(st, f"stgT{i}", [128, N]) for i in range(2)]
            stgG = [sb(st, f"stgG{i}", [128, N], BF16) for i in range(2)]
            blks = [(c0, min(512, C_SS - c0)) for c0 in range(0, C_SS, 512)] + \
                   [(c0, min(512, INC - c0)) for c0 in range(C_SS, INC, 512)]
            k = 0
            kf = 0
            for bi, (c0, cw) in enumerate(blks):
                w_t = wb[bi % 3]
                wkey = ("wb", bi % 3)
                S.dma("pool", w_t[:, :, 0:cw], w_in[l, :, c0:c0 + cw].rearrange("(kc p) n -> p kc n", p=128), w=[wkey])
                if c0 < C_SS:
                    for tt in range(NT):
                        i = k % 4
                        k += 1
                        for kc in range(8):
                            S.op("pe", lambda e, i=i, kc=kc, tt=tt, w_t=w_t, cw=cw: e.matmul(
                                pp[i][:, 0:cw], lhsT=xmT[:, kc, tt * 128:(tt + 1) * 128], rhs=w_t[:, kc, 0:cw],
                                start=(kc == 0), stop=(kc == 7)), r=[("xmT", tt), wkey], w=[("pp", i)])
                        if i % 2 == 0:
                            S.op("act", lambda e, i=i, cw=cw: e.activation(out=stg[i][:, 0:cw], in_=pp[i][:, 0:cw], func=AF.Copy),
                                 r=[("pp", i)], w=[("stg", i)])
                        else:
                            S.op("dve", lambda e, i=i, cw=cw: e.tensor_copy(out=stg[i][:, 0:cw], in_=pp[i][:, 0:cw]),
                                 r=[("pp", i)], w=[("stg", i)])
                        S.dma("sp", p_tm[tt * 128:(tt + 1) * 128, c0:c0 + cw], stg[i][:, 0:cw], r=[("stg", i)],
                              w=[("p_tm", tt)])
                else:
                    for j in range(cw // 128):
                        cc = c0 + j * 128
                        is_g = cc >= C_G
                        f = kf % 2
                        kf += 1
                        dstt = stgG[f] if is_g else stgT[f]
                        skey = ("stgG", f) if is_g else ("stgT", f)
                        for t0 in range(0, N, 512):
                            tw = min(512, N - t0)
                            i = k % 4
                            k += 1
                            for kc in range(8):
                                S.op("pe", lambda e, i=i, kc=kc, t0=t0, tw=tw, w_t=w_t, j=j: e.matmul(
                                    pp[i][:, 0:tw], lhsT=w_t[:, kc, j * 128:(j + 1) * 128], rhs=xmT[:, kc, t0:t0 + tw],
                                    start=(kc == 0), stop=(kc == 7)),
                                    r=[("xmT", t) for t in range(t0 // 128, (t0 + tw) // 128)] + [wkey], w=[("pp", i)])
                            if is_g:
                                S.op("act", lambda e, i=i, t0=t0, tw=tw, dstt=dstt: e.activation(
                                    out=dstt[:, t0:t0 + tw], in_=pp[i][:, 0:tw], func=AF.Sigmoid), r=[("pp", i)], w=[skey])
                            else:
                                S.op("dve", lambda e, i=i, t0=t0, tw=tw, dstt=dstt: e.tensor_copy(
                                    out=dstt[:, t0:t0 + tw], in_=pp[i][:, 0:tw]), r=[("pp", i)], w=[skey])
                        if is_g:
                            S.dma("sp", gT[cc - C_G:cc - C_G + 128, :], dstt[:], r=[skey], w=[("gT", (cc - C_G) // 128)])
                        else:
                            S.dma("sp", p_ssT[cc - C_SS:cc - C_SS + 128, :], dstt[:], r=[skey], w=[("p_ssT", (cc - C_SS) // 128)])
        S.barrier()
        if stop_after == "B":
            break


        with ExitStack() as st:
            mu = sb(st, "mu", [128, 1792])
            rwp = sb(st, "rwp", [128, 5, 512])
            lrb = sb(st, "lrb", [1, 4, 512])
            lrw = sb(st, "lrw", [128, 2, 512])
            gup = sb(st, "gup", [128, 512])
            S.dma("sp", mu[:], mu_in[l], w=["mu"])
            S.dma("sp", rwp[:], rwp_in[l].rearrange("k p c -> p k c"), w=["rwp"])
            S.dma("sp", lrb[:], lrb_in[l:l + 1], w=["lrb"])
            S.dma("sp", lrw[:], lrw_in[l], w=["lrw"])
            S.dma("sp", gup[:], gup_in[l], w=["gup"])
            P0 = [sb(st, f"P0{i}", [128, 1792]) for i in range(2)]
            Pm = [sb(st, f"Pm{i}", [128, 1792]) for i in range(2)]
            Pp = [sb(st, f"Pp{i}", [128, 1792]) for i in range(2)]
            tl = sb(st, "tl", [128, 1792])
            pl = [sb(st, f"pl{i}", [128, 1792]) for i in range(2)]
            lrT = sb(st, "lrT", [128, 128])
            gsT = sb(st, "gsT", [128, 128])
            ptr = [ps(st, f"ptr{i}", [128, 128]) for i in range(2)]
            pq = [ps(st, f"pq{i}", [128, 512]) for i in range(5)]
            sg = [sb(st, f"sg{i}", [128, 512]) for i in range(4)]
            o5 = [sb(st, f"o5{i}", [128, 512]) for i in range(10)]
            kx = sb(st, "kx", [128, 512])
            sq5 = sb(st, "sq5", [128, 512])
            ss8 = sb(st, "ss8", [128, 8])
            for tt in range(NT):
                i = tt % 2
                lo, hi = tt * 128, (tt + 1) * 128
                first = tt in (0, 2)
                last = tt in (1, NT - 1)
                S.dma("sp", P0[i][:], p_tm[lo:hi, 0:1792], r=[("p_tm", tt)], w=[("P0", i)])
                if first:
                    S.op("pool", lambda e, i=i: e.memset(Pm[i][:], 0.0), w=[("Pm", i)])
                    S.dma("sp", Pm[i][1:128, :], p_tm[lo:hi - 1, 0:1792], r=[("p_tm", tt)], w=[("Pm", i)])
                else:
                    S.dma("sp", Pm[i][:], p_tm[lo - 1:hi - 1, 0:1792], r=[("p_tm", tt), ("p_tm", tt - 1)], w=[("Pm", i)])
                if last:
                    S.op("pool", lambda e, i=i: e.memset(Pp[i][:], 0.0), w=[("Pp", i)])
                    S.dma("sp", Pp[i][0:127, :], p_tm[lo + 1:hi, 0:1792], r=[("p_tm", tt)], w=[("Pp", i)])
                else:
                    S.dma("sp", Pp[i][:], p_tm[lo + 1:hi + 1, 0:1792], r=[("p_tm", tt), ("p_tm", tt + 1)], w=[("Pp", i)])
                p_ = pl[i]
                pk = ("pl", i)
                tt_("pool", tl[:], Pm[i][:], Pp[i][:], ALU.add, r=[("Pm", i), ("Pp", i)], w=["tl"])
                S.op("dve", lambda e, i=i: e.scalar_tensor_tensor(out=tl[:], in0=tl[:], scalar=0.5, in1=P0[i][:], op0=ALU.mult,
                                                                op1=ALU.subtract), r=["tl", ("P0", i)], w=["tl"])
                tt_("pool", tl[:], tl[:], mu[:], ALU.mult, r=["tl", "mu"], w=["tl"])
                tt_("pool", p_[:], tl[:], P0[i][:], ALU.add, r=["tl", ("P0", i)], w=[pk])
                S.dma("sp", rw_s["r"][lo:hi, :], p_[:, 0:512], r=[pk], w=[("rw_r", tt)])
                S.dma("sp", rw_s["v"][lo:hi, :], p_[:, 1024:1536], r=[pk], w=[("rw_v", tt)])
                tr_(ptr[0][:], p_[:, 1536:1664], ident[:], r=[pk, "ident"], w=[("ptr", 0)])
                tr_(ptr[1][:], p_[:, 1664:1792], ident[:], r=[pk, "ident"], w=[("ptr", 1)])
                act_(lrT[0:64, :], ptr[0][0:64, :], AF.Tanh, r=[("ptr", 0)], w=["lrT"])
                act_(lrT[64:128, :], ptr[0][64:128, :], AF.Copy, r=[("ptr", 0)], w=["lrT"])
                act_(gsT[:], ptr[1][:], AF.Sigmoid, r=[("ptr", 1)], w=["gsT"])
                for d in range(2):
                    mm_(pq[d][:], ones[0:1, :], lrb[0:1, d, :], True, False, r=["ones", "lrb"], w=[("pq", d)])
                    mm_(pq[d][:], lrT[0:64, :], lrw[0:64, d, :], False, True, r=["lrT", "lrw"], w=[("pq", d)])
                    mm_(pq[2 + d][:], ones[0:1, :], lrb[0:1, 2 + d, :], True, False, r=["ones", "lrb"], w=[("pq", 2 + d)])
                    mm_(pq[2 + d][:], lrT[64:128, :], lrw[64:128, d, :], False, True, r=["lrT", "lrw"], w=[("pq", 2 + d)])
                mm_(pq[4][:], gsT[:], gup[:], True, True, r=["gsT", "gup"], w=[("pq", 4)])
                for q in range(4):
                    act_(sg[q][:], pq[q][:], AF.Sigmoid, r=[("pq", q)], w=[("sg", q)])
                o = o5
                ok = lambda q: ("o5", q)
                cp_("dve", o[9][:], pq[4][:], r=[("pq", 4)], w=[ok(9)])
                S.dma("sp", rw_s["g"][lo:hi, :], o[9][:], r=[ok(9)], w=[("rw_g", tt)])
                k_ = p_[:, 512:1024]
                r_ = p_[:, 0:512]
                h3 = lambda a: a.rearrange("p (h d) -> p h d", d=64)
                tt_("pool", kx[:], k_, rwp[:, 0, :], ALU.mult, r=[pk, "rwp"], w=["kx"])
                tt_("pool", sq5[:], kx[:], kx[:], ALU.mult, r=["kx"], w=["sq5"])
                red_("dve", ss8[:], h3(sq5[:]), r=["sq5"], w=["ss8"])
                ts_("dve", ss8[:], ss8[:], 1e-24, None, ALU.max, None, r=["ss8"], w=["ss8"])
                act_(ss8[:], ss8[:], AF.Sqrt, r=["ss8"], w=["ss8"])
                S.op("dve", lambda e: e.reciprocal(out=ss8[:], in_=ss8[:]), r=["ss8"], w=["ss8"])
                tt_("pool", h3(o[0][:]), h3(kx[:]), ss8[:].unsqueeze(2).broadcast_to([128, 8, 64]), ALU.mult, r=["kx", "ss8"], w=[ok(0)])
                S.dma("sp", rw_s["kk"][lo:hi, :], o[0][:], r=[ok(0)], w=[("rw_kk", tt)])
                for d in range(2):
                    ts_("pool", o[1 + d][:], sg[d][:], -0.6065306597126334, None, ALU.mult, None, r=[("sg", d)], w=[ok(1 + d)])
                    S.dma("sp", rw_s[f"lw{d}"][lo:hi, :], o[1 + d][:], r=[ok(1 + d)], w=[(f"rw_lw{d}", tt)])
                    tt_("pool", o[3 + d][:], o[0][:], sg[2 + d][:], ALU.mult, r=[ok(0), ("sg", 2 + d)], w=[ok(3 + d)])
                    S.dma("sp", rw_s[f"b{d}"][lo:hi, :], o[3 + d][:], r=[ok(3 + d)], w=[(f"rw_b{d}", tt)])
                    S.op("dve", lambda e, d=d: e.scalar_tensor_tensor(out=o[5 + d][:], in0=sg[2 + d][:], scalar=-1.0, in1=rwp[:, 1, :],
                                                                     op0=ALU.add, op1=ALU.mult), r=[("sg", 2 + d), "rwp"], w=[ok(5 + d)])
                    S.op("dve", lambda e, d=d, k_=k_: e.scalar_tensor_tensor(out=o[5 + d][:], in0=o[5 + d][:], scalar=1.0, in1=k_,
                                                                            op0=ALU.add, op1=ALU.mult), r=[ok(5 + d), pk], w=[ok(5 + d)])
                    S.dma("sp", rw_s[f"kd{d}"][lo:hi, :], o[5 + d][:], r=[ok(5 + d)], w=[(f"rw_kd{d}", tt)])
                tt_("pool", o[7][:], o[5][:], o[6][:], ALU.add, r=[ok(5), ok(6)], w=[ok(7)])
                tt_("pool", o[8][:], r_, rwp[:, 2, :], ALU.mult, r=[pk, "rwp"], w=[ok(8)])
                tt_("pool", o[8][:], o[8][:], o[7][:], ALU.mult, r=[ok(8), ok(7)], w=[ok(8)])
                red_("dve", bsum[:, tt, :], h3(o[8][:]), r=[ok(8)], w=[("bsum", tt)])
        S.barrier()
        if stop_after == "C":
            break

        with ExitStack() as st:
            pf = [ps(st, f"pf{i}", [128, 512]) for i in range(8)]
            npf = [0]

            npfd = [0, 0]

            def PFd(d):
                i = 4 * d + npfd[d] % 4
                npfd[d] += 1
                return pf[i], ("pf", i)

            Bd = []
            for d in range(2):
                B = {}
                for nm in ("r", "v", "kk", "lw", "b", "kd", "cum", "e1", "e2", "e3", "e4", "abarf", "bbarf", "kbarf", "rbarf"):
                    B[nm] = sb(st, f"D{d}{nm}", [128, 512])
                for nm in ("abar", "Bt", "Kt", "Vb", "Z", "U"):
                    B[nm] = sb(st, f"D{d}{nm}", [128, 512], BF16)
                for nm in ("abarT", "bbarT", "kbarT"):
                    B[nm] = sb(st, f"D{d}{nm}", [128, 4, 128], BF16)
                for nm in ("AakT", "MrbT", "MrkT", "abarZ", "bbarZ", "rbarZ", "Rb"):
                    B[nm] = sb(st, f"D{d}{nm}", [128, 8, 128], BF16)
                for nm in ("AabT", "Aab", "P0", "P1", "Q0", "Q1", "R0", "R1"):
                    B[nm] = sb(st, f"D{d}{nm}", [128, 8, 128], F32)
                for nm in ("abarZ", "bbarZ", "rbarZ"):
                    S.op("pool", lambda e, B=B, nm=nm: e.memset(B[nm][:], 0.0), w=[(nm, d)])
                B["rbT1"] = sb(st, f"D{d}rbT1", [128, 8, 128], BF16)
                B["WT"] = sb(st, f"D{d}WT", [128, 8, 128], BF16)
                S.op("pool", lambda e, B=B: e.memset(B["rbT1"][:], 0.0), w=[("rbT1", d)])
                S.op("pool", lambda e, B=B: e.memset(B["WT"][:], 0.0), w=[("WT", d)])
                B["gl"] = sb(st, f"D{d}gl", [64, 8])
                B["Sf"] = sb(st, f"D{d}Sf", [64, 8, 64])
                B["Stmp"] = sb(st, f"D{d}Stmp", [64, 8, 64])
                B["Sbf"] = sb(st, f"D{d}Sbf", [128, 8, 64], BF16)
                S.op("pool", lambda e, B=B: e.memset(B["Sf"][:], 0.0), w=[("Sf", d)])
                S.op("pool", lambda e, B=B: e.memset(B["Sbf"][:], 0.0), w=[("Sbf", d)])
                Bd.append(B)

            def rw_chunk(c, d):
                B = Bd[d]
                K_ = lambda n: (n, d)
                lo, hi = c * 128, (c + 1) * 128
                hs = lambda h: slice(h * 64, (h + 1) * 64)
                for nm, src in (("r", "r"), ("v", "v"), ("kk", "kk"), ("lw", f"lw{d}"), ("b", f"b{d}"), ("kd", f"kd{d}")):
                    S.dma("sp", B[nm][:], rw_s[src][lo:hi, :], r=[("rw_" + src, c)], w=[K_(nm)])
                    yield
                pc, pck = PFd(d)
                mm_(pc[:], masks[:, 2 + d, :], B["lw"][:], True, True, r=["masks", K_("lw")], w=[pck])
                pt, ptk = PFd(d)
                mm_(pt[:], ones[:], B["lw"][:], True, True, r=["ones", K_("lw")], w=[ptk])
                act_(B["cum"][:], pc[:], AF.Copy, r=[pck], w=[K_("cum")])
                yield
                act_(B["e1"][:], B["cum"][:], AF.Exp, r=[K_("cum")], w=[K_("e1")])
                yield
                act_(B["e2"][:], B["cum"][:], AF.Exp, r=[K_("cum")], w=[K_("e2")], scale=-1.0)
                yield
                tt_("pool", B["e3"][:], B["cum"][:], B["lw"][:], ALU.subtract, r=[K_("cum"), K_("lw")], w=[K_("e3")])
                yield
                act_(B["e3"][:], B["e3"][:], AF.Exp, r=[K_("e3")], w=[K_("e3")])
                yield
                tt_("dve", B["e4"][:], pt[:], B["cum"][:], ALU.subtract, r=[ptk, K_("cum")], w=[K_("e4")])
                yield
                act_(B["e4"][:], B["e4"][:], AF.Exp, r=[K_("e4")], w=[K_("e4")])
                yield
                S.op("dve", lambda e: e.scalar_tensor_tensor(out=B["abarf"][:], in0=B["kk"][:], scalar=-1.0, in1=B["e3"][:],
                                                            op0=ALU.mult, op1=ALU.mult), r=[K_("kk"), K_("e3")], w=[K_("abarf")])
                yield
                tt_("pool", B["bbarf"][:], B["b"][:], B["e2"][:], ALU.mult, r=[K_("b"), K_("e2")], w=[K_("bbarf")])
                yield
                tt_("dve", B["kbarf"][:], B["kd"][:], B["e2"][:], ALU.mult, r=[K_("kd"), K_("e2")], w=[K_("kbarf")])
                yield
                tt_("pool", B["rbarf"][:], B["r"][:], B["e1"][:], ALU.mult, r=[K_("r"), K_("e1")], w=[K_("rbarf")])
                yield
                tt_("dve", B["Bt"][:], B["b"][:], B["e4"][:], ALU.mult, r=[K_("b"), K_("e4")], w=[K_("Bt")])
                yield
                tt_("pool", B["Kt"][:], B["kd"][:], B["e4"][:], ALU.mult, r=[K_("kd"), K_("e4")], w=[K_("Kt")])
                yield
                cp_("pool", B["Vb"][:], B["v"][:], r=[K_("v")], w=[K_("Vb")])
                yield
                pg, pgk = PFd(d)
                for h in range(8):
                    mm_(pg[0:64, 2 * h:2 * h + 2], B["lw"][:, hs(h)], ones[:, 0:2], True, True, r=[K_("lw"), "ones"], w=[pgk])
                act_(B["gl"][:], pg[0:64, 0:16].rearrange("p (h two) -> p h two", two=2)[:, :, 0], AF.Exp, r=[pgk], w=[K_("gl")])
                yield
                if DSTAGE < 1:
                    return
                for nm in ("abar", "bbar", "kbar", "rbar")[:DSUB]:
                    p2, p2k = PFd(d)
                    for j in range(4):
                        tr_(p2[:, j * 128:(j + 1) * 128], B[nm + "f"][:, j * 128:(j + 1) * 128], ident[:], r=[K_(nm + "f"), "ident"], w=[p2k])
                    p3 = p2[:].rearrange("p (j t) -> p j t", t=128)
                    if nm != "rbar":
                        act_(B[nm + "T"][:], p3, AF.Copy, r=[p2k], w=[K_(nm + "T")])
                        yield
                    if nm != "kbar":
                        z4 = lambda lo_: B[nm + "Z"][lo_:lo_ + 64].rearrange("p (j two) t -> p j two t", two=2)
                        act_(z4(0)[:, :, 0, :], p3[0:64], AF.Copy, r=[p2k], w=[K_(nm + "Z")])
                        yield
                        cp_("dve", z4(64)[:, :, 1, :], p3[64:128], r=[p2k], w=[K_(nm + "Z")])
                        yield
                for hh in range(2 if DSUB > 4 else 0):
                    p2, p2k = PFd(d)
                    for h4 in range(4):
                        h = hh * 4 + h4
                        tr_(p2[0:64, h4 * 128:(h4 + 1) * 128], B["rbarf"][:, hs(h)], ident[:], r=[K_("rbarf"), "ident"], w=[p2k])
                    act_(B["rbT1"][0:64, hh * 4:(hh + 1) * 4, :], p2[0:64, :].rearrange("p (j t) -> p j t", t=128), AF.Copy, r=[p2k], w=[K_("rbT1")])
                    yield
                cp_("dve", B["abar"][:], B["abarf"][:], r=[K_("abarf")], w=[K_("abar")])
                yield
                if DSTAGE < 2:
                    return
                for nm, lt, rt, mk in (("AabT", "bbarT", "abarZ", d), ("Aab", "abarT", "bbarZ", 1 - d), ("AakT", "kbarT", "abarZ", d),
                                       ("MrbT", "bbarT", "rbarZ", 2 + d), ("MrkT", "kbarT", "rbarZ", 2 + d)):
                    for hh in range(2):
                        p_, pk_ = PFd(d)
                        for h4 in range(4):
                            h = hh * 4 + h4
                            mm_(p_[:, h4 * 128:(h4 + 1) * 128], B[lt][:, h // 2, :], B[rt][:, h, :], True, True,
                                r=[K_(lt), K_(rt)], w=[pk_])
                        tt_("dve", B[nm][:, hh * 4:(hh + 1) * 4, :], p_[:].rearrange("p (h t) -> p h t", t=128),
                            masks[:, mk, :].unsqueeze(1).broadcast_to([128, 4, 128]), ALU.mult, r=[pk_, "masks"], w=[K_(nm)])
                        yield
                if DSTAGE < 3:
                    return
                P_, Q_ = "AabT", "Aab"
                tt_("pool", B["R0"][:], B["AabT"][:], ident[:].unsqueeze(1).broadcast_to([128, 8, 128]), ALU.add,
                    r=[K_("AabT"), "ident"], w=[K_("R0")])
                yield
                Rc = "R0"
                for lev in range(6):
                    Pn, Qn, Rn = f"P{lev % 2}", f"Q{lev % 2}", f"R{(lev + 1) % 2}"
                    for hh in range(2):
                        if lev < 5:
                            p_, pk_ = PFd(d)
                            for h4 in range(4):
                                h = hh * 4 + h4
                                mm_(p_[:, h4 * 128:(h4 + 1) * 128], B[Q_][:, h, :], B[P_][:, h, :], True, True, r=[K_(Q_), K_(P_)], w=[pk_])
                            act_(B[Pn][:, hh * 4:(hh + 1) * 4, :], p_[:].rearrange("p (h t) -> p h t", t=128), AF.Copy, r=[pk_], w=[K_(Pn)])
                            yield
                        p_, pk_ = PFd(d)
                        for h4 in range(4):
                            h = hh * 4 + h4
                            mm_(p_[:, h4 * 128:(h4 + 1) * 128], B[P_][:, h, :], B[Q_][:, h, :], True, True, r=[K_(Q_), K_(P_)], w=[pk_])
                        cp_("dve", B[Qn][:, hh * 4:(hh + 1) * 4, :], p_[:].rearrange("p (h t) -> p h t", t=128), r=[pk_], w=[K_(Qn)])
                        yield
                    for hh in range(2):
                        p_, pk_ = PFd(d)
                        for h4 in range(4):
                            h = hh * 4 + h4
                            mm_(p_[:, h4 * 128:(h4 + 1) * 128], B[Qn][:, h, :], B[Rc][:, h, :], True, True, r=[K_(Qn), K_(Rc)], w=[pk_])
                        tt_("dve", B[Rn][:, hh * 4:(hh + 1) * 4, :], p_[:].rearrange("p (h t) -> p h t", t=128),
                            B[Rc][:, hh * 4:(hh + 1) * 4, :], ALU.add, r=[pk_, K_(Rc)], w=[K_(Rn)])
                        yield
                    P_, Q_, Rc = Pn, Qn, Rn
                if DSTAGE < 4:
                    return
                cp_("pool", B["Rb"][:], B[Rc][:], r=[K_(Rc)], w=[K_("Rb")])
                yield
                Rc = "Rb"
                p_, pk_ = PFd(d)
                for h in range(8):
                    mm_(p_[:, hs(h)], B["AakT"][:, h, :], B["Vb"][:, hs(h)], True, True, r=[K_("AakT"), K_("Vb")], w=[pk_])
                act_(B["Z"][:], p_[:], AF.Copy, r=[pk_], w=[K_("Z")])
                yield
                p_, pk_ = PFd(d)
                for h in range(8):
                    mm_(p_[:, hs(h)], B[Rc][:, h, :], B["Z"][:, hs(h)], True, True, r=[K_(Rc), K_("Z")], w=[pk_])
                cp_("dve", B["e1"][:], p_[:], r=[pk_], w=[K_("e1")])
                yield
                for hh in range(2):
                    p_, pk_ = PFd(d)
                    for h4 in range(4):
                        h = hh * 4 + h4
                        mm_(p_[0:64, h4 * 128:(h4 + 1) * 128], B["abar"][:, hs(h)], B[Rc][:, h, :], True, True, r=[K_("abar"), K_(Rc)], w=[pk_])
                    act_(B["WT"][0:64, hh * 4:(hh + 1) * 4, :], p_[0:64, :].rearrange("p (h t) -> p h t", t=128), AF.Copy, r=[pk_], w=[K_("WT")])
                    yield
                if DSTAGE < 5:
                    return
                p_, pk_ = PFd(d)
                for h in range(8):
                    mm_(p_[:, hs(h)], B["WT"][:, h, :], B["Sbf"][:, h, :], True, True, r=[K_("WT"), K_("Sbf")], w=[pk_])
                tt_("dve", B["U"][:], p_[:], B["e1"][:], ALU.add, r=[pk_, K_("e1")], w=[K_("U")])
                yield
                py, pyk = PFd(d)
                for h in range(8):
                    mm_(py[:, hs(h)], B["rbT1"][:, h, :], B["Sbf"][:, h, :], True, False, r=[K_("rbT1"), K_("Sbf")], w=[pyk])
                    mm_(py[:, hs(h)], B["MrbT"][:, h, :], B["U"][:, hs(h)], False, False, r=[K_("MrbT"), K_("U")], w=[pyk])
                    mm_(py[:, hs(h)], B["MrkT"][:, h, :], B["Vb"][:, hs(h)], False, True, r=[K_("MrkT"), K_("Vb")], w=[pyk])
                act_(B["e2"][:], py[:], AF.Copy, r=[pyk], w=[K_("e2")])
                yield
                S.dma("sp", rw_s[f"y{d}"][lo:hi, :], B["e2"][:], r=[K_("e2")], w=[(f"rw_y{d}", c)])
                yield
                pS, pSk = PFd(d)
                for h in range(8):
                    mm_(pS[0:64, hs(h)], B["Kt"][:, hs(h)], B["Vb"][:, hs(h)], True, False, r=[K_("Kt"), K_("Vb")], w=[pSk])
                    mm_(pS[0:64, hs(h)], B["Bt"][:, hs(h)], B["U"][:, hs(h)], False, True, r=[K_("Bt"), K_("U")], w=[pSk])
                tt_("pool", B["Stmp"][:], B["Sf"][:], B["gl"][:].unsqueeze(2).broadcast_to([64, 8, 64]), ALU.mult,
                    r=[K_("Sf"), K_("gl")], w=[K_("Stmp")])
                yield
                tt_("dve", B["Sf"][:], pS[0:64, :].rearrange("p (h v) -> p h v", v=64), B["Stmp"][:], ALU.add, r=[pSk, K_("Stmp")], w=[K_("Sf")])
                yield
                act_(B["Sbf"][0:64], B["Sf"][:], AF.Copy, r=[K_("Sf")], w=[K_("Sbf")])
                yield

            ord0 = list(range(NT))
            ord1 = [1, 0] + list(range(NT - 1, 1, -1))
            for i in range(NT):
                run_interleaved([rw_chunk(ord0[i], 0), rw_chunk(ord1[i], 1)])
        S.barrier()
        if stop_after == "D":
            break

        with ExitStack() as st:
            rwp = sb(st, "rwpE", [128, 5, 512])
            S.dma("sp", rwp[:], rwp_in[l].rearrange("k p c -> p k c"), w=["rwpE"])
            Y0 = [sb(st, f"EY0{i}", [128, 512]) for i in range(2)]
            Y1 = [sb(st, f"EY1{i}", [128, 512]) for i in range(2)]
            Vv = [sb(st, f"EV{i}", [128, 512]) for i in range(2)]
            Gg = [sb(st, f"EG{i}", [128, 512]) for i in range(2)]
            y = sb(st, "Ey", [128, 512])
            ysq = sb(st, "Eysq", [128, 512])
            bon = sb(st, "Ebon", [128, 512])
            yab = sb(st, "Eyab", [128, 512], BF16)
            st8 = sb(st, "Est8", [128, 4, 8])
            yT = [sb(st, f"EyT{i}", [128, 4, 128], BF16) for i in range(2)]
            pbE = [ps(st, f"pbE{i}", [128, 4, 128], BF16) for i in range(2)]
            h3 = lambda a: a.rearrange("p (h d) -> p h d", d=64)
            b3 = lambda a: a.unsqueeze(2).broadcast_to([128, 8, 64])
            for tt in range(NT):
                i = tt % 2
                lo, hi = tt * 128, (tt + 1) * 128
                S.dma("sp", Y0[i][:], rw_s["y0"][lo:hi, :], r=[("rw_y0", tt)], w=[("EY0", i)])
                S.dma("sp", Y1[i][:], rw_s["y1"][lo:hi, :], r=[("rw_y1", tt)], w=[("EY1", i)])
                S.dma("sp", Vv[i][:], rw_s["v"][lo:hi, :], r=[("rw_v", tt)], w=[("EV", i)])
                S.dma("sp", Gg[i][:], rw_s["g"][lo:hi, :], r=[("rw_g", tt)], w=[("EG", i)])
                tt_("pool", y[:], Y0[i][:], Y1[i][:], ALU.add, r=[("EY0", i), ("EY1", i)], w=["Ey"])
                red_("dve", st8[:, 0, :], h3(y[:]), r=["Ey"], w=["Est8"])
                tt_("pool", ysq[:], y[:], y[:], ALU.mult, r=["Ey"], w=["Eysq"])
                red_("dve", st8[:, 1, :], h3(ysq[:]), r=["Eysq"], w=["Est8"])
                ts_("dve", st8[:, 0, :], st8[:, 0, :], 1.0 / 64, None, ALU.mult, None, r=["Est8"], w=["Est8"])
                tt_("dve", st8[:, 2, :], st8[:, 0, :], st8[:, 0, :], ALU.mult, r=["Est8"], w=["Est8"])
                S.op("dve", lambda e: e.scalar_tensor_tensor(out=st8[:, 3, :], in0=st8[:, 1, :], scalar=1.0 / 64, in1=st8[:, 2, :],
                                                           op0=ALU.mult, op1=ALU.subtract), r=["Est8"], w=["Est8"])
                rstd_(st8[:, 3, :], st8[:, 3, :], 1.0, GN_EPS, r=["Est8"], w=["Est8"])
                tt_("pool", h3(y[:]), h3(y[:]), b3(st8[:, 0, :]), ALU.subtract, r=["Ey", "Est8"], w=["Ey"])
                tt_("pool", h3(y[:]), h3(y[:]), b3(st8[:, 3, :]), ALU.mult, r=["Ey", "Est8"], w=["Ey"])
                tt_("pool", y[:], y[:], rwp[:, 3, :], ALU.mult, r=["Ey", "rwpE"], w=["Ey"])
                tt_("pool", y[:], y[:], rwp[:, 4, :], ALU.add, r=["Ey", "rwpE"], w=["Ey"])
                tt_("dve", h3(bon[:]), h3(Vv[i][:]), b3(bsum[:, tt, :]), ALU.mult, r=[("EV", i), ("bsum", tt)], w=["Ebon"])
                tt_("pool", y[:], y[:], bon[:], ALU.add, r=["Ey", "Ebon"], w=["Ey"])
                tt_("pool", yab[:], y[:], Gg[i][:], ALU.mult, r=["Ey", ("EG", i)], w=["Eyab"])
                for j in range(4):
                    tr_(pbE[i][:, j, :], yab[:, j * 128:(j + 1) * 128], identb[:], r=["Eyab", "identb"], w=[("pbE", i)])
                act_(yT[i][:], pbE[i][:], AF.Copy, r=[("pbE", i)], w=[("EyT", i)])
                S.dma("sp", yaT[:, lo:hi].rearrange("(j p) t -> p j t", p=128), yT[i][:], r=[("EyT", i)], w=[("yaT", tt)])
        S.barrier()
        if stop_after == "E":
            break

        with ExitStack() as st:
            mcol = sb(st, "mcol", [128, 4])
            S.dma("sp", mcol[:], mcol_in, w=["mcol"])
            Ep = [[sb(st, f"Ep{d}{k}", [128, 2048]) for k in range(2)] for d in range(2)]
            En = [[sb(st, f"En{d}{k}", [128, 2048]) for k in range(2)] for d in range(2)]
            Bp = [sb(st, f"Bp{d}", [128, 4, 1024], BF16) for d in range(2)]
            Cb = sb(st, "Cb", [128, 32, 128], BF16)
            S.dma("pool", Cb[:], cblk_in[l], w=["Cb"])
            trib = sb(st, "trib", [128, 2, 128], BF16)
            selb = sb(st, "selb", [128, 4, 128], BF16)
            S.dma("pool", trib[:], masks_in[2:4].rearrange("m p t -> p m t"), w=["trib"])
            S.dma("pool", selb[:], sel_in.rearrange("m p t -> p m t"), w=["selb"])
            MAGIC = 12582912.0
            C1 = 6.28125
            C2 = float(2 * np.pi - 6.28125)
            with ExitStack() as st2:
                T_ = {nm: sb(st2, "G" + nm, [128, 2048]) for nm in ("are", "aim", "dt", "x1", "x2", "arg", "k", "sn", "cs", "mg", "t1", "t2")}
                bb = [sb(st2, f"Gbb{k}", [128, 4, 512]) for k in range(2)]

                G = lambda nm: "G" + nm

                def g_tt(o, x, y, op, eng="pool"):
                    tt_(eng, T_[o][:], T_[x][:], T_[y][:], op, r=[G(x), G(y)], w=[G(o)])

                def sincos(src, shift, dst):
                    a_, k_ = T_["arg"], T_["k"]
                    ts_("dve", a_[:], T_[src][:], shift, None, ALU.add, None, r=[G(src)], w=[G("arg")])
                    ts_("dve", k_[:], a_[:], float(1 / (2 * np.pi)), MAGIC, ALU.mult, ALU.add, r=[G("arg")], w=[G("k")])
                    ts_("dve", k_[:], k_[:], -MAGIC, None, ALU.add, None, r=[G("k")], w=[G("k")])
                    S.op("dve", lambda e: e.scalar_tensor_tensor(out=a_[:], in0=k_[:], scalar=-C1, in1=a_[:], op0=ALU.mult, op1=ALU.add),
                         r=[G("k"), G("arg")], w=[G("arg")])
                    S.op("dve", lambda e: e.scalar_tensor_tensor(out=a_[:], in0=k_[:], scalar=-C2, in1=a_[:], op0=ALU.mult, op1=ALU.add),
                         r=[G("k"), G("arg")], w=[G("arg")])
                    act_(T_[dst][:], a_[:], AF.Sin, r=[G("arg")], w=[G(dst)])

                c4 = lambda t: t[:].rearrange("p (q c) -> p q c", c=512)
                for d in range(2):
                    S.dma("sp", T_["are"][:], sstab_in[l, d, 0], w=[G("are")])
                    S.dma("sp", T_["aim"][:], sstab_in[l, d, 1], w=[G("aim")])
                    S.dma("sp", T_["dt"][:], sstab_in[l, d, 2], w=[G("dt")])
                    if d == 0:
                        S.dma("sp", bb[0][:], bblk_in[l, 0], w=["Gbb0"])
                        S.dma("sp", bb[1][:], bblk_in[l, 1], w=["Gbb1"])
                    ts_("dve", T_["are"][:], T_["are"][:], -1e-4, None, ALU.min, None, r=[G("are")], w=[G("are")])
                    act_(T_["dt"][:], T_["dt"][:], AF.Exp, r=[G("dt")], w=[G("dt")])
                    g_tt("x1", "are", "dt", ALU.mult)
                    g_tt("x2", "aim", "dt", ALU.mult)
                    sincos("x2", 0.0, "sn")
                    sincos("x2", PI / 2, "cs")
                    act_(T_["mg"][:], T_["x1"][:], AF.Exp, r=[G("x1")], w=[G("mg")])
                    g_tt("t1", "sn", "mg", ALU.mult)
                    g_tt("t2", "cs", "mg", ALU.mult)
                    ts_("pool", T_["t2"][:], T_["t2"][:], -1.0, None, ALU.add, None, r=[G("t2")], w=[G("t2")])
                    g_tt("sn", "are", "are", ALU.mult)
                    g_tt("cs", "aim", "aim", ALU.mult)
                    g_tt("sn", "sn", "cs", ALU.add)
                    S.op("dve", lambda e: e.reciprocal(out=T_["sn"][:], in_=T_["sn"][:]), r=[G("sn")], w=[G("sn")])
                    g_tt("mg", "t2", "are", ALU.mult)
                    g_tt("cs", "t1", "aim", ALU.mult)
                    g_tt("mg", "mg", "cs", ALU.add)
                    g_tt("mg", "mg", "sn", ALU.mult)
                    g_tt("cs", "t1", "are", ALU.mult)
                    g_tt("k", "t2", "aim", ALU.mult)
                    g_tt("cs", "cs", "k", ALU.subtract)
                    g_tt("cs", "cs", "sn", ALU.mult)
                    tt_("pool", c4(T_["t1"]), bb[0][:], c4(T_["mg"]), ALU.mult, r=["Gbb0", G("mg")], w=[G("t1")])
                    tt_("pool", c4(T_["t2"]), bb[1][:], c4(T_["cs"]), ALU.mult, r=["Gbb1", G("cs")], w=[G("t2")])
                    tt_("pool", Bp[d][:, :, 0:512], c4(T_["t1"]), c4(T_["t2"]), ALU.subtract, r=[G("t1"), G("t2")], w=[("Bp", d)])
                    tt_("pool", c4(T_["t1"]), bb[0][:], c4(T_["cs"]), ALU.mult, r=["Gbb0", G("cs")], w=[G("t1")])
                    tt_("pool", c4(T_["t2"]), bb[1][:], c4(T_["mg"]), ALU.mult, r=["Gbb1", G("mg")], w=[G("t2")])
                    tt_("pool", Bp[d][:, :, 512:1024], c4(T_["t1"]), c4(T_["t2"]), ALU.add, r=[G("t1"), G("t2")], w=[("Bp", d)])
                    ts_("dve", T_["t1"][:], T_["x2"][:], mcol[:, d:d + 1], None, ALU.mult, None, r=[G("x2"), "mcol"], w=[G("t1")])
                    sincos("t1", 0.0, "sn")
                    sincos("t1", PI / 2, "cs")
                    act_(T_["mg"][:], T_["x1"][:], AF.Exp, r=[G("x1")], w=[G("mg")], scale=mcol[:, d:d + 1])
                    tt_("pool", Ep[d][0][:], T_["mg"][:], T_["cs"][:], ALU.mult, r=[G("mg"), G("cs")], w=[("Ep", d)])
                    tt_("pool", Ep[d][1][:], T_["mg"][:], T_["sn"][:], ALU.mult, r=[G("mg"), G("sn")], w=[("Ep", d)])
                    act_(T_["mg"][:], T_["x1"][:], AF.Exp, r=[G("x1")], w=[G("mg")], scale=mcol[:, 2 + d:3 + d])
                    tt_("pool", En[d][0][:], T_["mg"][:], T_["cs"][:], ALU.mult, r=[G("mg"), G("cs")], w=[("En", d)])
                    tt_("pool", En[d][1][:], T_["mg"][:], T_["sn"][:], ALU.mult, r=[G("mg"), G("sn")], w=[("En", d)])
            S.barrier()
            uf = [sb(st, f"Guf{i}", [128, 4, 128]) for i in range(2)]
            ub = [sb(st, f"Gub{i}", [128, 4, 128], BF16) for i in range(2)]
            hb = [[sb(st, f"Ghb{d}{i}", [128, 4, 2, 512], BF16) for i in range(2)] for d in range(2)]
            zb = [sb(st, f"Gzb{i}", [128, 2, 512], BF16) for i in range(2)]
            tmd = [[sb(st, f"Gtm{d_}{i}", [128, 512]) for i in range(4)] for d_ in range(2)]
            hT = [sb(st, f"GhT{i}", [128, 32, 128], BF16) for i in range(2)]
            ysb = [sb(st, f"Gys{i}", [128, 4, 128]) for i in range(2)]
            pg_ = [ps(st, f"pgf{i}", [128, 512]) for i in range(6)]
            pgb = [ps(st, f"pgb{i}", [128, 8, 128], BF16) for i in range(2)]
            cnt = {"f": 0, "b": 0, "z": 0, "u": 0}

            def PGF():
                i = cnt["f"] % 6
                cnt["f"] += 1
                return pg_[i], ("pgf", i)

            for d in range(2):
                for i in range(2):
                    S.op("pool", lambda e, d=d, i=i: e.memset(hb[d][i][:], 0.0), w=[("hb", d, i)])
            nstep = [0, 0]

            def s5_chunk(c, d):
                lo, hi = c * 128, (c + 1) * 128
                iu = d
                tm = tmd[d]
                S.dma("sp", uf[iu][:], p_ssT[:, lo:hi].rearrange("(q p) t -> p q t", p=128), r=[("p_ssT", q) for q in range(4)], w=[("uf", iu)])
                yield
                cp_("pool", ub[iu][:], uf[iu][:], r=[("uf", iu)], w=[("ub", iu)])
                yield
                hi_ = nstep[d] % 2
                nstep[d] += 1
                hcur, hprev = hb[d][hi_], hb[d][1 - hi_]
                kcur, kprev = ("hb", d, hi_), ("hb", d, 1 - hi_)
                for q in range(4):
                    cs_ = slice(q * 512, (q + 1) * 512)
                    pbr, pbrk = pg_[3 * d], ("pgf", 3 * d)
                    pbi, pbik = pg_[3 * d + 1], ("pgf", 3 * d + 1)
                    mm_(pbr[:], ub[iu][:, q, :], Bp[d][:, q, 0:512], True, True, r=[("ub", iu), ("Bp", d)], w=[pbrk])
                    mm_(pbi[:], ub[iu][:, q, :], Bp[d][:, q, 512:1024], True, True, r=[("ub", iu), ("Bp", d)], w=[pbik])
                    iz = d
                    z = zb[iz]
                    zk = ("zb", iz)
                    tt_("dve", tm[0][:], pbr[:], En[d][0][:, cs_], ALU.mult, r=[pbrk, ("En", d)], w=[("tm", d, 0)])
                    yield
                    tt_("dve", tm[1][:], pbi[:], En[d][1][:, cs_], ALU.mult, r=[pbik, ("En", d)], w=[("tm", d, 1)])
                    yield
                    tt_("pool", z[:, 0, :], tm[0][:], tm[1][:], ALU.add, r=[("tm", d, 0), ("tm", d, 1)], w=[zk])
                    yield
                    tt_("dve", tm[2][:], pbi[:], En[d][0][:, cs_], ALU.mult, r=[pbik, ("En", d)], w=[("tm", d, 2)])
                    yield
                    tt_("dve", tm[3][:], pbr[:], En[d][1][:, cs_], ALU.mult, r=[pbrk, ("En", d)], w=[("tm", d, 3)])
                    yield
                    tt_("pool", z[:, 1, :], tm[2][:], tm[3][:], ALU.subtract, r=[("tm", d, 2), ("tm", d, 3)], w=[zk])
                    yield
                    pcr, pcrk = pg_[3 * d + 2], ("pgf", 3 * d + 2)
                    pci, pcik = pg_[3 * d], ("pgf", 3 * d)
                    mm_(pcr[:], trib[:, d, :], z[:, 0, :], True, False, r=["trib", zk], w=[pcrk])
                    mm_(pcr[:], selb[:, d, :], hprev[:, q, 0, :], False, True, r=["selb", kprev], w=[pcrk])
                    mm_(pci[:], trib[:, d, :], z[:, 1, :], True, False, r=["trib", zk], w=[pcik])
                    mm_(pci[:], selb[:, 2 + d, :], hprev[:, q, 1, :], False, True, r=["selb", kprev], w=[pcik])
                    tt_("dve", tm[0][:], pcr[:], Ep[d][0][:, cs_], ALU.mult, r=[pcrk, ("Ep", d)], w=[("tm", d, 0)])
                    yield
                    tt_("dve", tm[1][:], pci[:], Ep[d][1][:, cs_], ALU.mult, r=[pcik, ("Ep", d)], w=[("tm", d, 1)])
                    yield
                    tt_("pool", hcur[:, q, 0, :], tm[0][:], tm[1][:], ALU.subtract, r=[("tm", d, 0), ("tm", d, 1)], w=[kcur])
                    yield
                    tt_("dve", tm[2][:], pcr[:], Ep[d][1][:, cs_], ALU.mult, r=[pcrk, ("Ep", d)], w=[("tm", d, 2)])
                    yield
                    tt_("dve", tm[3][:], pci[:], Ep[d][0][:, cs_], ALU.mult, r=[pcik, ("Ep", d)], w=[("tm", d, 3)])
                    yield
                    S.op("dve", lambda e, q=q: e.scalar_tensor_tensor(out=hcur[:, q, 1, :], in0=tm[2][:], scalar=-1.0, in1=tm[3][:],
                                                                   op0=ALU.mult, op1=ALU.subtract), r=[("tm", d, 2), ("tm", d, 3)], w=[kcur])
                    yield
                it = d
                for part in range(2):
                    for q2 in range(2):
                        ib = cnt["b"] % 2
                        cnt["b"] += 1
                        for q1 in range(2):
                            q = q2 * 2 + q1
                            for sb_ in range(4):
                                tr_(pgb[ib][:, q1 * 4 + sb_, :], hcur[:, q, part, sb_ * 128:(sb_ + 1) * 128], identb[:],
                                    r=[kcur, "identb"], w=[("pgb", ib)])
                        k0 = part * 16 + q2 * 8
                        act_(hT[it][:, k0:k0 + 8, :], pgb[ib][:], AF.Copy, r=[("pgb", ib)], w=[("hT", it)])
                        yield
                py, pyk = pg_[3 * d + 1], ("pgf", 3 * d + 1)
                for ct in range(4):
                    kqs = [part * 16 + 4 * ct + sb_ for part in range(2) for sb_ in range(4)]
                    for n_, kq in enumerate(kqs):
                        mm_(py[:, ct * 128:(ct + 1) * 128], Cb[:, kq, :], hT[it][:, kq, :], n_ == 0, n_ == 7, r=["Cb", ("hT", it)], w=[pyk])
                act_(ysb[it][:], py[:].rearrange("p (c t) -> p c t", t=128), AF.Copy, r=[pyk], w=[("ysb", it)])
                yield
                S.dma("sp", ysT[d][:, lo:hi].rearrange("(c p) t -> p c t", p=128), ysb[it][:], r=[("ysb", it)], w=[(f"ysT{d}", c)])
                yield

            ord0 = list(range(NT))
            ord1 = [1, 0] + list(range(NT - 1, 1, -1))
            for i in range(NT):
                run_interleaved([s5_chunk(ord0[i], 0), s5_chunk(ord1[i], 1)])
        S.barrier()
        with ExitStack() as st:
            vec = sb(st, "Gvec", [128, 2, 4])
            S.dma("sp", vec[:], ssvec_in[l], w=["Gvec"])
            gw = sb(st, "Ggw", [128, 4, 512], BF16)
            S.dma("pool", gw[:], gluw_in[l].rearrange("(kc p) n -> p kc n", p=128), w=["Ggw"])
            Y0 = [sb(st, f"GY0{i}", [128, 4, 512]) for i in range(2)]
            Y1 = [sb(st, f"GY1{i}", [128, 4, 512]) for i in range(2)]
            Uu = [sb(st, f"GU{i}", [128, 4, 512]) for i in range(2)]
            yy = sb(st, "Gyy", [128, 4, 512])
            y3 = sb(st, "Gy3", [128, 4, 512])
            y2b = sb(st, "Gy2b", [128, 4, 512], BF16)
            sg_ = sb(st, "Gsg", [128, 512])
            yo = [sb(st, f"Gyo{i}", [128, 4, 512], BF16) for i in range(2)]
            pz = [ps(st, f"pz{i}", [128, 512]) for i in range(2)]
            kz = 0
            for gi, t0 in enumerate(range(0, N, 512)):
                tw = min(512, N - t0)
                i = gi % 2
                tks = list(range(t0 // 128, (t0 + tw) // 128))
                S.dma("sp", Y0[i][:, :, 0:tw], ysT[0][:, t0:t0 + tw].rearrange("(c p) t -> p c t", p=128), r=[("ysT0", t) for t in tks], w=[("GY0", i)])
                S.dma("sp", Y1[i][:, :, 0:tw], ysT[1][:, t0:t0 + tw].rearrange("(c p) t -> p c t", p=128), r=[("ysT1", t) for t in tks], w=[("GY1", i)])
                S.dma("sp", Uu[i][:, :, 0:tw], p_ssT[:, t0:t0 + tw].rearrange("(c p) t -> p c t", p=128), r=[("p_ssT", q) for q in range(4)], w=[("GU", i)])
                tt_("pool", yy[:, :, 0:tw], Y0[i][:, :, 0:tw], Y1[i][:, :, 0:tw], ALU.add, r=[("GY0", i), ("GY1", i)], w=["Gyy"])
                for c in range(4):
                    S.op("dve", lambda e, c=c, i=i, tw=tw: e.scalar_tensor_tensor(out=yy[:, c, 0:tw], in0=Uu[i][:, c, 0:tw], scalar=vec[:, 0, c:c + 1],
                                                                            in1=yy[:, c, 0:tw], op0=ALU.mult, op1=ALU.add),
                         r=[("GU", i), "Gvec", "Gyy"], w=["Gyy"])
                tt_("pool", y3[:, :, 0:tw], yy[:, :, 0:tw], yy[:, :, 0:tw], ALU.mult, r=["Gyy"], w=["Gy3"])
                ts_("pool", y3[:, :, 0:tw], y3[:, :, 0:tw], 0.044715, 1.0, ALU.mult, ALU.add, r=["Gy3"], w=["Gy3"])
                tt_("pool", y3[:, :, 0:tw], y3[:, :, 0:tw], yy[:, :, 0:tw], ALU.mult, r=["Gy3", "Gyy"], w=["Gy3"])
                act_(y3[:, :, 0:tw], y3[:, :, 0:tw], AF.Tanh, r=["Gy3"], w=["Gy3"], scale=0.7978845608028654)
                ts_("pool", y3[:, :, 0:tw], y3[:, :, 0:tw], 1.0, 0.5, ALU.add, ALU.mult, r=["Gy3"], w=["Gy3"])
                tt_("pool", yy[:, :, 0:tw], yy[:, :, 0:tw], y3[:, :, 0:tw], ALU.mult, r=["Gyy", "Gy3"], w=["Gyy"])
                cp_("pool", y2b[:, :, 0:tw], yy[:, :, 0:tw], r=["Gyy"], w=["Gy2b"])
                for c in range(4):
                    p_ = pz[kz % 2]
                    pk_ = ("pz", kz % 2)
                    kz += 1
                    for kc in range(4):
                        mm_(p_[:, 0:tw], gw[:, kc, c * 128:(c + 1) * 128], y2b[:, kc, 0:tw], kc == 0, kc == 3, r=["Ggw", "Gy2b"], w=[pk_])
                    act_(sg_[:, 0:tw], p_[:, 0:tw], AF.Sigmoid, r=[pk_], w=["Gsg"], bias=vec[:, 1, c:c + 1])
                    tt_("dve", yo[i][:, c, 0:tw], yy[:, c, 0:tw], sg_[:, 0:tw], ALU.mult, r=["Gyy", "Gsg"], w=[("Gyo", i)])
                S.dma("sp", ycT[:, t0:t0 + tw].rearrange("(c p) t -> p c t", p=128), yo[i][:, :, 0:tw], r=[("Gyo", i)], w=[("ycT", t) for t in tks])
        S.barrier()
        if stop_after == "G":
            break

        with ExitStack() as st:
            KT = sb(st, "KT", [128, 4, N], BF16)
            S.op("pool", lambda e: e.memset(KT[64:128], 0.0), w=[("KT", t_) for t_ in range(NT)])
            Vt = sb(st, "Vt", [128, NT, 4, 65], BF16)
            gain = sb(st, "gain", [128, 1280])
            S.dma("sp", gain[:], qkgain_in[l], w=["gain"])
            S.op("pool", lambda e: e.memset(Vt[:], 1.0), w=["Vt"])
            xq = [sb(st, f"xq{i}", [128, 1280]) for i in range(2)]
            xv = [sb(st, f"xv{i}", [128, 256]) for i in range(2)]
            sq = sb(st, "sq", [128, 1280])
            ssq = sb(st, "ssq", [128, 20])
            xn = sb(st, "xn", [128, 1280])
            t1 = sb(st, "t1", [128, 1280])
            t2 = sb(st, "t2", [128, 1280])
            xb = sb(st, "xb", [128, 1280], BF16)
            cs = [sb(st, f"cs{i}", [128, 2, 64]) for i in range(2)]
            trp = [ps(st, f"trp{i}", [64, 8, 128], BF16) for i in range(1)]

            def qk_prep(tt, src, c0, nh, i):
                w_ = nh * 64
                v3 = lambda t: t[:, 0:w_].rearrange("p (h d) -> p h d", d=64)
                tt_("pool", sq[:, 0:w_], src[:, 0:w_], src[:, 0:w_], ALU.mult, r=[("xq", i)], w=["sq"])
                red_("dve", ssq[:, 0:nh], v3(sq), r=["sq"], w=["ssq"])
                rstd_(ssq[:, 0:nh], ssq[:, 0:nh], 1.0 / 64, RMS_EPS, r=["ssq"], w=["ssq"])
                tt_("pool", v3(xn), v3(src), ssq[:, 0:nh].unsqueeze(2).broadcast_to([128, nh, 64]), ALU.mult,
                    r=[("xq", i), "ssq"], w=["xn"])
                if tt < 2:
                    tt_("pool", xb[:, 0:w_], xn[:, 0:w_], gain[:, c0:c0 + w_], ALU.mult, r=["xn", "gain"], w=["xb"])
                    return
                tt_("pool", xn[:, 0:w_], xn[:, 0:w_], gain[:, c0:c0 + w_], ALU.mult, r=["xn", "gain"], w=["xn"])
                cst = cs[tt % 2]
                ck = ("cs", tt % 2)
                cosb = cst[:, 0, :].unsqueeze(1).broadcast_to([128, nh, 64])
                tt_("pool", v3(t1), v3(xn), cosb, ALU.mult, r=["xn", ck], w=["t1"])
                v4 = lambda t: t[:, 0:w_].rearrange("p (h a b i) -> p (h a) b i", a=2, b=2, i=16)
                sn4 = cst[:, 1, :].rearrange("p (a b i) -> p a b i", a=2, b=2)
                for hb in range(2):
                    snb = sn4[:, :, hb, :].unsqueeze(1).broadcast_to([128, nh, 2, 16])
                    o4 = t2[:, 0:w_].rearrange("p (h a b i) -> p h a b i", a=2, b=2, i=16)[:, :, :, hb, :]
                    i4 = xn[:, 0:w_].rearrange("p (h a b i) -> p h a b i", a=2, b=2, i=16)[:, :, :, 1 - hb, :]
                    tt_("dve", o4, i4, snb, ALU.mult, r=["xn", ck], w=["t2"])
                tt_("pool", xb[:, 0:w_], t1[:, 0:w_], t2[:, 0:w_], ALU.add, r=["t1", "t2"], w=["xb"])

            def load_cs(tt):
                if tt >= 2:
                    r0 = (tt - 2) * 128
                    S.dma("sp", cs[tt % 2][:, 0, :], rope_cos[r0:r0 + 128, :], w=[("cs", tt % 2)])
                    S.dma("sp", cs[tt % 2][:, 1, :], rope_sin[r0:r0 + 128, :], w=[("cs", tt % 2)])

            for tt in range(NT):
                i = tt % 2
                S.dma("sp", xq[i][:, 0:256], p_tm[tt * 128:(tt + 1) * 128, C_AT + 1024:C_AT + 1280], r=[("p_tm", tt)], w=[("xq", i)])
                S.dma("sp", xv[i][:], p_tm[tt * 128:(tt + 1) * 128, C_AT + 1280:C_AT + 1536], r=[("p_tm", tt)], w=[("xv", i)])
                load_cs(tt)
                qk_prep(tt, xq[i], 1024, 4, i)
                for g in range(4):
                    tr_(trp[0][:, g, :], xb[:, g * 64:(g + 1) * 64], identb[:], r=["xb", "identb"], w=[("trp", 0)])
                act_(KT[0:64, :, tt * 128:(tt + 1) * 128], trp[0][:, 0:4, :], AF.Copy, r=[("trp", 0)], w=[("KT", tt)])
                cp_("pool", Vt[:, tt, :, 0:64], xv[i][:].rearrange("p (g d) -> p g d", d=64), r=[("xv", i)], w=[("Vt", tt)])
            QT = [sb(st, f"QT{i}", [128, 16, 128], BF16) for i in range(2)]
            for i_ in range(2):
                S.op("pool", lambda e, i_=i_: e.memset(QT[i_][64:128], 0.0), w=[("QT", i_)])
            NB = 4
            Pt = [sb(st, f"Pt{i}", [128, 512], BF16) for i in range(NB)]
            sps = [ps(st, f"sps{i}", [128, 512]) for i in range(NB)]
            ops_ = [ps(st, f"ops{i}", [65, 512]) for i in range(2)]
            bcp = ps(st, "bcp", [64, 512])
            rd = sb(st, "rd", [65, 512])
            osb = sb(st, "osb", [64, 512])
            yb = [sb(st, f"yb{i}", [64, 512], BF16) for i in range(2)]
            kq = 0
            ko = 0
            pending = []
            for tt in range(NT):
                i = tt % 2
                S.dma("sp", xq[i][:, 0:1024], p_tm[tt * 128:(tt + 1) * 128, C_AT:C_AT + 1024], r=[("p_tm", tt)], w=[("xq", i)])
                load_cs(tt)
                qk_prep(tt, xq[i], 0, 16, i)
                for hh in range(2):
                    for h in range(8):
                        hd = hh * 8 + h
                        tr_(trp[0][:, h, :], xb[:, hd * 64:(hd + 1) * 64], identb[:], r=["xb", "identb"], w=[("trp", 0)])
                    act_(QT[i][0:64, hh * 8:(hh + 1) * 8, :], trp[0][:], AF.Copy, r=[("trp", 0)], w=[("QT", i)])
                kts = [0, 1] if tt < 2 else list(range(NT))
                for g in range(4):
                    o_ = ops_[ko % 2]
                    okey = ("ops", ko % 2)
                    ybt = yb[ko % 2]
                    ykey = ("yb", ko % 2)
                    ko += 1
                    PIPE = 3
                    nk = len(kts)
                    base = kq
                    kq += nk
                    for n_ in range(nk + PIPE):
                        if n_ == min(6, nk - 1) and pending:
                            pending.pop(0)()
                        if n_ < nk:
                            kt = kts[n_]
                            j = (base + n_) % NB
                            mm_(sps[j][:], KT[:, g, kt * 128:(kt + 1) * 128], QT[i][:, 4 * g:4 * g + 4, :].rearrange("p h t -> p (h t)"),
                                True, True, r=[("KT", kt), ("QT", i)], w=[("sps", j)])
                            act_(Pt[j][:], sps[j][:], AF.Exp, r=[("sps", j)], w=[("Pt", j)], scale=0.125)
                        m_ = n_ - PIPE
                        if m_ >= 0:
                            kt = kts[m_]
                            j = (base + m_) % NB
                            mm_(o_[:], Vt[:, kt, g, :], Pt[j][:], m_ == 0, m_ == nk - 1, r=[("Vt", kt), ("Pt", j)], w=[okey])
                    def epilogue(o_=o_, okey=okey, ybt=ybt, ykey=ykey, g=g, tt=tt):
                        S.op("dve", lambda e: e.reciprocal(out=rd[64:65, :], in_=o_[64:65, :]), r=[okey], w=["rd"])
                        mm_(bcp[:], ones[64:65, 0:64], rd[64:65, :], True, True, r=["ones", "rd"], w=["bcp"])
                        cp_("pool" if False else "dve", osb[:], o_[0:64, :], r=[okey], w=["osb"])
                        tt_("dve", ybt[:], osb[:], bcp[:], ALU.mult, r=["osb", "bcp"], w=[ykey])
                        S.dma("sp", ybT[4 * g:4 * g + 4, :, tt * 128:(tt + 1) * 128].rearrange("h d t -> d h t"),
                              ybt[:].rearrange("p (h t) -> p h t", t=128), r=[ykey], w=[("ybT", tt)])
                    pending.append(epilogue)
            while pending:
                pending.pop(0)()
        S.barrier()
        if stop_after == "F":
            break

        stHI = ExitStack()
        gtb = sb(stHI, "gtb", [128, 2, 2, D])
        with ExitStack() as st:
            dg = [sb(st, f"dg{i}", [128, 128]) for i in range(2)]
            pgt = [ps(st, f"pgt{i}", [128, 512]) for i in range(2)]
            kk_ = 0
            for wh, c0 in ((0, 16), (1, 40)):
                for g in range(2):
                    for hf in range(2):
                        p_ = pgt[kk_ % 2]
                        pk_ = ("pgt", kk_ % 2)
                        kk_ += 1
                        for j in range(4):
                            kc = hf * 4 + j
                            i = kc % 2
                            ts_("dve", dg[i][:], ident[:], modT[:, l, c0 + kc, g:g + 1], None, ALU.mult, None, r=["ident", "modT"], w=[("dg", i)])
                            mm_(p_[:, j * 128:(j + 1) * 128], ones[:], dg[i][:], True, True, r=["ones", ("dg", i)], w=[pk_])
                        act_(gtb[:, wh, g, hf * 512:(hf + 1) * 512], p_[:], AF.Copy, r=[pk_], w=["gtb"])
        S.barrier()

        def ln_tail(st_tiles, h, lng, lnb, outt, hk, ok):
            hsq, st4 = st_tiles
            red_("dve", st4[:, 0:1], h[:], r=[hk], w=["st4"])
            tt_("pool", hsq[:], h[:], h[:], ALU.mult, r=[hk], w=["hsq"])
            red_("dve", st4[:, 1:2], hsq[:], r=["hsq"], w=["st4"])
            ts_("dve", st4[:, 0:1], st4[:, 0:1], 1.0 / D, None, ALU.mult, None, r=["st4"], w=["st4"])
            tt_("dve", st4[:, 2:3], st4[:, 0:1], st4[:, 0:1], ALU.mult, r=["st4"], w=["st4"])
            S.op("dve", lambda e: e.scalar_tensor_tensor(out=st4[:, 3:4], in0=st4[:, 1:2], scalar=1.0 / D, in1=st4[:, 2:3],
                                                       op0=ALU.mult, op1=ALU.subtract), r=["st4"], w=["st4"])
            rstd_(st4[:, 3:4], st4[:, 3:4], 1.0, LN_EPS, r=["st4"], w=["st4"])
            ts_("dve", hsq[:], h[:], st4[:, 0:1], st4[:, 3:4], ALU.subtract, ALU.mult, r=[hk, "st4"], w=["hsq"])
            tt_("pool", hsq[:], hsq[:], lng, ALU.mult, r=["hsq", "lnw"], w=["hsq"])
            tt_("pool", outt[:], hsq[:], lnb, ALU.add, r=["hsq", "lnw"], w=[ok])

        with ExitStack() as st:
            wa = sb(st, "Hwa", [128, 4, D], BF16)
            wbb = sb(st, "Hwb", [128, 8, D], BF16)
            wc = sb(st, "Hwc", [128, 4, D], BF16)
            wo = sb(st, "Hwo", [128, 8, D], BF16)
            lnw = sb(st, "Hln", [128, 2, D])
            S.dma("pool", wa[:], proja_in[l].rearrange("(kc p) n -> p kc n", p=128), w=["Hwa"])
            S.dma("pool", wbb[:], projb_in[l].rearrange("(kc p) n -> p kc n", p=128), w=["Hwb"])
            S.dma("pool", wc[:], projc_in[l].rearrange("(kc p) n -> p kc n", p=128), w=["Hwc"])
            S.dma("pool", wo[:], wout_in[l].rearrange("(kc p) n -> p kc n", p=128), w=["Hwo"])
            S.dma("sp", lnw[:], ln_in[l, 0:2].rearrange("k p c -> p k c"), w=["lnw"])
            TW = 256
            ya_t = [sb(st, f"Hya{i}", [128, 4, TW], BF16) for i in range(2)]
            yb_t = [sb(st, f"Hyb{i}", [128, 8, TW], BF16) for i in range(2)]
            yc_t = [sb(st, f"Hyc{i}", [128, 4, TW], BF16) for i in range(2)]
            g_t = [sb(st, f"Hg{i}", [128, 24, TW], BF16) for i in range(2)]
            mg = sb(st, "Hmg", [128, TW])
            t_ = sb(st, "Ht", [128, TW])
            mT = sb(st, "HmT", [128, 8, TW], BF16)
            xt_ = [sb(st, f"Hx{i}", [128, D]) for i in range(2)]
            hh = sb(st, "Hh", [128, D])
            hsq = sb(st, "Hhsq", [128, D])
            st4 = sb(st, "Hst4", [128, 4])
            xo = [sb(st, f"Hxo{i}", [128, D]) for i in range(2)]
            x2T = [sb(st, f"Hx2T{i}", [128, 8, 128], BF16) for i in range(2)]
            pH = [ps(st, f"pH{i}", [128, 512]) for i in range(6)]
            pT2 = [ps(st, f"pT2{i}", [128, 4, 128]) for i in range(2)]
            kp = [0]

            def PH():
                i = kp[0] % 6
                kp[0] += 1
                return pH[i], ("pH", i)

            for gi, t0 in enumerate(range(0, N, TW)):
                i = gi % 2
                tks = [t0 // 128, t0 // 128 + 1]
                S.dma("sp", ya_t[i][:], yaT[:, t0:t0 + TW].rearrange("(kc p) t -> p kc t", p=128), r=[("yaT", t) for t in tks], w=[("Hya", i)])
                S.dma("sp", yb_t[i][:], ybT.rearrange("h d t -> (h d) t")[:, t0:t0 + TW].rearrange("(kc p) t -> p kc t", p=128),
                      r=[("ybT", t) for t in tks], w=[("Hyb", i)])
                S.dma("sp", yc_t[i][:], ycT[:, t0:t0 + TW].rearrange("(kc p) t -> p kc t", p=128), r=[("ycT", t) for t in tks], w=[("Hyc", i)])
                S.dma("sp", g_t[i][:], gT[:, t0:t0 + TW].rearrange("(kc p) t -> p kc t", p=128), r=[("gT", c) for c in range(24)], w=[("Hg", i)])
                for m_ in range(8):
                    ms = slice(m_ * 128, (m_ + 1) * 128)
                    pa, pak = PH()
                    for kc in range(4):
                        mm_(pa[:, 0:TW], wa[:, kc, ms], ya_t[i][:, kc, :], kc == 0, kc == 3, r=["Hwa", ("Hya", i)], w=[pak])
                    pb_, pbk = PH()
                    for kc in range(8):
                        mm_(pb_[:, 0:TW], wbb[:, kc, ms], yb_t[i][:, kc, :], kc == 0, kc == 7, r=["Hwb", ("Hyb", i)], w=[pbk])
                    pc_, pck = PH()
                    for kc in range(4):
                        mm_(pc_[:, 0:TW], wc[:, kc, ms], yc_t[i][:, kc, :], kc == 0, kc == 3, r=["Hwc", ("Hyc", i)], w=[pck])
                    tt_("dve", mg[:], pa[:, 0:TW], g_t[i][:, m_, :], ALU.mult, r=[pak, ("Hg", i)], w=["Hmg"])
                    tt_("dve", t_[:], pb_[:, 0:TW], g_t[i][:, 8 + m_, :], ALU.mult, r=[pbk, ("Hg", i)], w=["Ht"])
                    tt_("pool", mg[:], mg[:], t_[:], ALU.add, r=["Hmg", "Ht"], w=["Hmg"])
                    tt_("dve", t_[:], pc_[:, 0:TW], g_t[i][:, 16 + m_, :], ALU.mult, r=[pck, ("Hg", i)], w=["Ht"])
                    tt_("pool", mT[:, m_, :], mg[:], t_[:], ALU.add, r=["Hmg", "Ht"], w=["HmT"])
                for sub in range(2):
                    tt = t0 // 128 + sub
                    g = 1 if tt < 2 else 0
                    j = tt % 2
                    ts_l = slice(sub * 128, (sub + 1) * 128)
                    S.dma("sp", xt_[j][:], xs[tt * 128:(tt + 1) * 128, :], r=[("xs", tt)], w=[("Hx", j)])
                    for hf in range(2):
                        p_, pk_ = PH()
                        for kc in range(8):
                            mm_(p_[:], mT[:, kc, ts_l], wo[:, kc, hf * 512:(hf + 1) * 512], kc == 0, kc == 7, r=["HmT", "Hwo"], w=[pk_])
                        tt_("dve", hh[:, hf * 512:(hf + 1) * 512], p_[:], gtb[:, 0, g, hf * 512:(hf + 1) * 512], ALU.mult, r=[pk_, "gtb"], w=["Hh"])
                    S.op("dve", lambda e, j=j: e.scalar_tensor_tensor(out=hh[:], in0=xt_[j][:], scalar=ALPHA, in1=hh[:], op0=ALU.mult, op1=ALU.add),
                         r=[("Hx", j), "Hh"], w=["Hh"])
                    ln_tail((hsq, st4), hh, lnw[:, 0, :], lnw[:, 1, :], xo[j], "Hh", ("Hxo", j))
                    S.dma("sp", xmid[tt * 128:(tt + 1) * 128, :], xo[j][:], r=[("Hxo", j)], w=[("xmid", tt)])
                    for hf in range(2):
                        for q in range(4):
                            kc = hf * 4 + q
                            tr_(pT2[hf][:, q, :], xo[j][:, kc * 128:(kc + 1) * 128], ident[:], r=[("Hxo", j), "ident"], w=[("pT2", hf)])
                        for q in range(4):
                            kc = hf * 4 + q
                            act_(x2T[j][:, kc, :], pT2[hf][:, q, :], AF.Identity, r=[("pT2", hf), "modT"], w=[("Hx2T", j)],
                                 scale=modT[:, l, 32 + kc, g:g + 1], bias=modT[:, l, 24 + kc, g:g + 1])
                    S.dma("sp", xm2T[:, tt * 128:(tt + 1) * 128].rearrange("(kc p) t -> p kc t", p=128), x2T[j][:], r=[("Hx2T", j)], w=[("xm2T", tt)])
        S.barrier()
        if stop_after == "H":
            stHI.close()
            break

        with ExitStack() as st:
            w1 = sb(st, "Iw1", [128, 8, 4 * D], BF16)
            for q in range(4):
                S.dma("pool", w1[:, :, q * D:(q + 1) * D], w1_in[l, :, q * D:(q + 1) * D].rearrange("(kc p) n -> p kc n", p=128), w=["Iw1"])
            xi = [sb(st, f"Ixi{i}", [128, 8, 512], BF16) for i in range(2)]
            rl = [sb(st, f"Irl{i}", [128, 512]) for i in range(2)]
            ho = [sb(st, f"Iho{i}", [128, 8, 512], BF16) for i in range(2)]
            pI = [ps(st, f"pI{i}", [128, 512]) for i in range(4)]
            kp = 0
            ko = 0
            for gi, t0 in enumerate(range(0, N, 512)):
                tw = min(512, N - t0)
                i = gi % 2
                tks = list(range(t0 // 128, (t0 + tw) // 128))
                S.dma("sp", xi[i][:, :, 0:tw], xm2T[:, t0:t0 + tw].rearrange("(kc p) t -> p kc t", p=128), r=[("xm2T", t) for t in tks], w=[("Ixi", i)])
                for c8 in range(4):
                    o_ = ho[ko % 2]
                    okey = ("Iho", ko % 2)
                    ko += 1
                    for c in range(8):
                        cb = c8 * 8 + c
                        p_ = pI[kp % 4]
                        pk_ = ("pI", kp % 4)
                        r_ = rl[kp % 2]
                        rk_ = ("Irl", kp % 2)
                        kp += 1
                        for kc in range(8):
                            mm_(p_[:, 0:tw], w1[:, kc, cb * 128:(cb + 1) * 128], xi[i][:, kc, 0:tw], kc == 0, kc == 7, r=["Iw1", ("Ixi", i)], w=[pk_])
                        act_(r_[:, 0:tw], p_[:, 0:tw], AF.Relu, r=[pk_], w=[rk_])
                        tt_("pool" if c % 2 else "dve", o_[:, c, 0:tw], r_[:, 0:tw], r_[:, 0:tw], ALU.mult, r=[rk_], w=[okey])
                    S.dma("sp", h1T[c8 * 1024:(c8 + 1) * 1024, t0:t0 + tw].rearrange("(c p) t -> p c t", p=128), o_[:, :, 0:tw], r=[okey],
                          w=[("h1T", t, c8) for t in tks])
        S.barrier()
        with ExitStack() as st:
            w2 = sb(st, "Iw2", [128, 32, D], BF16)
            for q in range(4):
                S.dma("pool", w2[:, q * 8:(q + 1) * 8, :], w2_in[l, q * 1024:(q + 1) * 1024, :].rearrange("(kc p) n -> p kc n", p=128), w=["Iw2"])
            lnw = sb(st, "Iln", [128, 2, D])
            S.dma("sp", lnw[:], ln_in[l, 2:4].rearrange("k p c -> p k c"), w=["lnw"])
            h1 = [sb(st, f"Ih1{i}", [128, 32, 128], BF16) for i in range(2)]
            xm_ = [sb(st, f"Ixm{i}", [128, D]) for i in range(2)]
            hh = sb(st, "Ihh", [128, D])
            hsq = sb(st, "Ihsq", [128, D])
            st4 = sb(st, "Ist4", [128, 4])
            xo = [sb(st, f"Ixo{i}", [128, D]) for i in range(2)]
            pJ = [ps(st, f"pJ{i}", [128, 512]) for i in range(4)]
            kp = 0
            for tt in range(NT):
                i = tt % 2
                g = 1 if tt < 2 else 0
                lo, hi = tt * 128, (tt + 1) * 128
                S.dma("sp", h1[i][:], h1T[:, lo:hi].rearrange("(c p) t -> p c t", p=128), r=[("h1T", tt, c8) for c8 in range(4)], w=[("Ih1", i)])
                S.dma("sp", xm_[i][:], xmid[lo:hi, :], r=[("xmid", tt)], w=[("Ixm", i)])
                for hf in range(2):
                    p_ = pJ[kp % 4]
                    pk_ = ("pJ", kp % 4)
                    kp += 1
                    for kc in range(32):
                        mm_(p_[:], h1[i][:, kc, :], w2[:, kc, hf * 512:(hf + 1) * 512], kc == 0, kc == 31, r=[("Ih1", i), "Iw2"], w=[pk_])
                    tt_("dve", hh[:, hf * 512:(hf + 1) * 512], p_[:], gtb[:, 1, g, hf * 512:(hf + 1) * 512], ALU.mult, r=[pk_, "gtb"], w=["Ihh"])
                S.op("dve", lambda e, i=i: e.scalar_tensor_tensor(out=hh[:], in0=xm_[i][:], scalar=ALPHA, in1=hh[:], op0=ALU.mult, op1=ALU.add),
                     r=[("Ixm", i), "Ihh"], w=["Ihh"])
                ln_tail((hsq, st4), hh, lnw[:, 0, :], lnw[:, 1, :], xo[i], "Ihh", ("Ixo", i))
                S.dma("sp", xs[lo:hi, :], xo[i][:], r=[("Ixo", i)], w=[("xs", tt)])
                if l == n_layers - 1 and tt >= 2:
                    S.dma("sp", out[lo - NCTX:hi - NCTX, :], xo[i][:], r=[("Ixo", i)], w=[("out", tt)])
        S.barrier()
        stHI.close()
        if stop_after == "I":
            break

    S.barrier()
    top.close()
    return nc, S


def _prep_inputs(inp, b):
    f = lambda a: np.ascontiguousarray(a, dtype=np.float32)
    m = {}
    m["x_in"] = f(np.concatenate([inp["ctx"][b], inp["x"][b]], axis=0))
    cT = np.stack([inp["c"][b].reshape(8, 128).T, inp["c_ctx"].reshape(8, 128).T], axis=-1)
    m["cT"] = f(cT)
    m["ident"] = np.eye(128, dtype=np.float32)
    m["mod_w"] = f(inp["mod_w"])
    m["mod_bT"] = f(inp["mod_b"].reshape(DEPTH, 48, 128).transpose(0, 2, 1))
    m["w_in"] = f(inp["w_in"])
    m["qkgain"] = f(np.broadcast_to(np.concatenate([np.tile(inp["attn_q_gain"], (1, 16)), np.tile(inp["attn_k_gain"], (1, 4))],
                                                   axis=1)[:, None, :], (DEPTH, 128, 1280)))
    rows = NLAT // 64
    row = np.repeat(np.arange(rows, dtype=np.float32), 64)
    col = np.tile(np.arange(64, dtype=np.float32), rows)
    inv = (np.float32(10000.0) ** (-np.arange(16, dtype=np.float32) / np.float32(16))).astype(np.float32)
    ang = np.stack([row, col], axis=-1)[:, :, None] * inv
    ang = np.broadcast_to(ang[:, :, None, :], (NLAT, 2, 2, 16)).reshape(NLAT, 64)
    sgn = np.broadcast_to(np.array([-1.0, 1.0], np.float32)[None, None, :, None], (NLAT, 2, 2, 16)).reshape(NLAT, 64)
    m["rope_cos"] = f(np.cos(ang))
    m["rope_sin"] = f(np.sin(ang) * sgn)
    m["ones"] = np.ones((128, 128), np.float32)
    ii = np.arange(128)
    mS0 = (ii[:, None] < ii[None, :]).astype(np.float32)
    mS1 = (ii[:, None] > ii[None, :]).astype(np.float32)
    eye = np.eye(128, dtype=np.float32)
    m["masks"] = f(np.stack([mS0, mS1, mS0 + eye, mS1 + eye]))
    bc = lambda a: np.broadcast_to(a[:, None, :], (DEPTH, 128, a.shape[-1]))
    m["mu_bc"] = f(bc(inp["rwkv_mu"]))
    m["rwp_bc"] = f(np.stack([bc(inp[k]) for k in ("rwkv_k_k", "rwkv_k_a", "rwkv_r_k", "rwkv_gn_w", "rwkv_gn_b")], axis=1))
    m["lrb"] = f(np.concatenate([inp["rwkv_w0"], inp["rwkv_a0"]], axis=1))
    m["lrw"] = f(np.concatenate([inp["rwkv_w_up"], inp["rwkv_a_up"]], axis=2).transpose(0, 2, 1, 3))
    m["g_up"] = f(inp["rwkv_g_up"])
    rep = lambda a: np.broadcast_to(a[:, :, None, :], (DEPTH, 2, 128, 2048))
    m["ss_tab"] = f(np.stack([rep(inp["ssm_a_re"].reshape(DEPTH, 2, 2048)), rep(inp["ssm_a_im"].reshape(DEPTH, 2, 2048)),
                              rep(np.repeat(inp["ssm_log_dt"], 64, axis=-1))], axis=2))
    p_ = np.arange(128, dtype=np.float32)
    m["mcol"] = f(np.stack([p_ + 1, 128 - p_, -(p_ + 1), -(128 - p_)], axis=1))
    bblk = np.zeros((DEPTH, 2, 4, 8, 16, 8, 64), np.float32)
    for k_, nm in enumerate(("ssm_b_re", "ssm_b_im")):
        b5 = inp[nm].reshape(DEPTH, 4, 8, 64, 16)
        for g_ in range(8):
            bblk[:, k_, :, g_, :, g_, :] = b5[:, :, g_].transpose(0, 1, 3, 2)
    m["ss_bblk"] = f(bblk.reshape(DEPTH, 2, 4, 128, 512).transpose(0, 1, 3, 2, 4))
    cblk = np.zeros((DEPTH, 2, 16, 2, 64, 8, 16), np.float32)
    for k_, nm in enumerate(("ssm_c_re", "ssm_c_im")):
        c5 = inp[nm].reshape(DEPTH, 16, 2, 16, 64)
        for gp in range(16):
            for g2 in range(2):
                gl = (2 * gp + g2) % 8
                cblk[:, k_, gp, g2, :, gl, :] = c5[:, gp, g2].transpose(0, 2, 1)
    m["ss_cblk"] = f(cblk.reshape(DEPTH, 32, 128, 128).transpose(0, 2, 1, 3))
    m["ss_vec"] = f(np.stack([inp["ssm_d"].reshape(DEPTH, 4, 128).transpose(0, 2, 1),
                              inp["ssm_glu_b"].reshape(DEPTH, 4, 128).transpose(0, 2, 1)], axis=2))
    m["glu_w"] = f(inp["ssm_glu_w"])
    for k_ in ("proj_a", "proj_b", "proj_c", "w_out", "mlp_w1", "mlp_w2"):
        m[k_] = f(inp[k_])
    m["ln_bc"] = f(np.stack([bc(inp[k_]) for k_ in ("ln1_g", "ln1_b", "ln2_g", "ln2_b")], axis=1))
    selm = np.zeros((4, 128, 128), np.float32)
    selm[0, 127, :] = 1.0
    selm[1, 0, :] = 1.0
    selm[2, 127, :] = -1.0
    selm[3, 0, :] = -1.0
    m["sel"] = selm
    return m


def kernel(**inp):
    inp = {k: np.asarray(v) for k, v in inp.items()}
    nc, _ = build()
    in_maps = [_prep_inputs(inp, b) for b in range(8)]
    res = run_bass_kernel_spmd(nc, in_maps, core_ids=list(range(8)))
    return np.stack([r["out"] for r in res.results], axis=0).astype(np.float32)
```

```python
import numpy as np
from contextlib import ExitStack
import concourse.bass as bass
import concourse.mybir as mybir
from concourse.bass_utils import run_bass_kernel_spmd

F32 = mybir.dt.float32
BF16 = mybir.dt.bfloat16
AF = mybir.ActivationFunctionType
ALU = mybir.AluOpType
AX = mybir.AxisListType

D = 1024
DEPTH = 4
NCTX = 256
NLAT = 4096
N = NCTX + NLAT
NT = N // 128
INC = 6912
C_RW, C_AT, C_SS, C_G = 0, 1792, 3328, 3840
ALPHA = (2 * DEPTH) ** 0.25
LN_EPS, RMS_EPS, GN_EPS = 1e-5, 1e-6, 64e-5
PI = float(np.pi)
import os
DSTAGE = int(os.environ.get("DSTAGE", "9"))
DSUB = int(os.environ.get("DSUB", "9"))


class _Res:
    __slots__ = ("lw", "rd")

    def __init__(self):
        self.lw = None
        self.rd = {}


class _Eng:
    def __init__(self, name, h, sem, selfsync):
        self.name, self.h, self.sem, self.selfsync = name, h, sem, selfsync
        self.cnt = 0
        self.seen = {}


class Sched:
    def __init__(self, nc, stack, ndma=(("sp", 8), ("pool", 6), ("act", 8))):
        self.nc = nc
        hs = {"pe": nc.tensor, "act": nc.scalar, "dve": nc.vector, "pool": nc.gpsimd, "sp": nc.sync}
        self.engs = {}
        for n, h in hs.items():
            sem = stack.enter_context(nc.semaphore("s_" + n))
            self.engs[n] = _Eng(n, h, sem, selfsync=(n in ("dve", "act", "pool")))
        self.dsems, self.dnext = {}, {}
        for q, k in ndma:
            self.dsems[q] = [[stack.enter_context(nc.semaphore(f"d_{q}{i}")), 0] for i in range(k)]
            self.dnext[q] = 0
        self.res = {}
        self.nins = 0

    def R(self, k):
        r = self.res.get(k)
        if r is None:
            r = self.res[k] = _Res()
        return r

    def _deps(self, r, w):
        deps = {}

        def need(ev):
            if ev is not None and deps.get(id(ev[0]), (None, 0))[1] < ev[1]:
                deps[id(ev[0])] = ev

        for k in r:
            need(self.R(k).lw)
        for k in w:
            R = self.R(k)
            need(R.lw)
            for ev in R.rd.values():
                need(ev)
        return deps

    def _wait(self, E, deps):
        for s, v in deps.values():
            if s is E.sem and not E.selfsync:
                continue
            if E.seen.get(id(s), 0) < v:
                E.seen[id(s)] = v
                E.h.wait_ge(s, v)

    def _mark(self, r, w, ev):
        for k in r:
            self.R(k).rd[id(ev[0])] = ev
        for k in w:
            R = self.R(k)
            R.lw = ev
            R.rd = {}

    def op(self, en, fn, r=(), w=()):
        E = self.engs[en]
        self._wait(E, self._deps(r, w))
        E.cnt += 1
        fn(E.h).then_inc(E.sem, 1)
        self.nins += 1
        self._mark(r, w, (E.sem, E.cnt))

    def dma(self, q, out, in_, r=(), w=(), **kw):
        E = self.engs[q]
        lst = self.dsems[q]
        ds = lst[self.dnext[q] % len(lst)]
        self.dnext[q] += 1
        deps = self._deps(r, w)
        if ds[1] > 0:
            deps[id(ds[0])] = (ds[0], ds[1])
        self._wait(E, deps)
        ds[1] += 16
        E.h.dma_start(out=out, in_=in_, **kw).then_inc(ds[0], 16)
        self.nins += 1
        self._mark(r, w, (ds[0], ds[1]))

    def barrier(self):
        evs = [(E.sem, E.cnt) for E in self.engs.values() if E.cnt]
        for lst in self.dsems.values():
            evs += [(s, v) for s, v in lst if v]
        for E in self.engs.values():
            for s, v in evs:
                if s is E.sem and not E.selfsync:
                    continue
                if E.seen.get(id(s), 0) < v:
                    E.seen[id(s)] = v
                    E.h.wait_ge(s, v)
        self.res = {}


def build(n_layers=DEPTH, stop_after=None, dbg=()):
    nc = bass.Bass("TRN2", target_bir_lowering=False)
    top = ExitStack()
    S = Sched(nc, top)
    dram_in = {}

    def din(name, shape, dt=F32):
        dram_in[name] = nc.dram_tensor(name, list(shape), dt, kind="ExternalInput").ap()
        return dram_in[name]

    def dscr(name, shape, dt=F32):
        kind = "ExternalOutput" if name in dbg else "Internal"
        return nc.dram_tensor(name, list(shape), dt, kind=kind).ap()

    x_in = din("x_in", [N, D])
    cT_in = din("cT", [128, 8, 2])
    ident_in = din("ident", [128, 128])
    mod_w = din("mod_w", [DEPTH, D, 6 * D])
    mod_bT = din("mod_bT", [DEPTH, 128, 48])
    w_in = din("w_in", [DEPTH, D, INC])
    out = nc.dram_tensor("out", [NLAT, D], F32, kind="ExternalOutput").ap()
    qkgain_in = din("qkgain", [DEPTH, 128, 1280])
    rope_cos = din("rope_cos", [NLAT, 64])
    rope_sin = din("rope_sin", [NLAT, 64])
    ones_in = din("ones", [128, 128])
    masks_in = din("masks", [4, 128, 128])
    mu_in = din("mu_bc", [DEPTH, 128, 1792])
    rwp_in = din("rwp_bc", [DEPTH, 5, 128, 512])
    lrb_in = din("lrb", [DEPTH, 4, 512])
    lrw_in = din("lrw", [DEPTH, 128, 2, 512])
    gup_in = din("g_up", [DEPTH, 128, 512])
    sstab_in = din("ss_tab", [DEPTH, 2, 3, 128, 2048])
    mcol_in = din("mcol", [128, 4])
    bblk_in = din("ss_bblk", [DEPTH, 2, 128, 4, 512])
    cblk_in = din("ss_cblk", [DEPTH, 128, 32, 128])
    ssvec_in = din("ss_vec", [DEPTH, 128, 2, 4])
    gluw_in = din("glu_w", [DEPTH, 512, 512])
    sel_in = din("sel", [4, 128, 128])
    proja_in = din("proj_a", [DEPTH, 512, D])
    projb_in = din("proj_b", [DEPTH, D, D])
    projc_in = din("proj_c", [DEPTH, 512, D])
    wout_in = din("w_out", [DEPTH, D, D])
    ln_in = din("ln_bc", [DEPTH, 4, 128, D])
    w1_in = din("mlp_w1", [DEPTH, D, 4 * D])
    w2_in = din("mlp_w2", [DEPTH, 4 * D, D])

    xs = dscr("xs", [N, D])
    p_tm = dscr("p_tm", [N, C_SS])
    p_ssT = dscr("p_ssT", [512, N])
    gT = dscr("gT", [3072, N], BF16)
    ybT = dscr("ybT", [16, 64, N], BF16)
    rw_s = {nm: dscr("rw_" + nm, [N, 512]) for nm in ("r", "v", "kk", "lw0", "lw1", "b0", "b1", "kd0", "kd1", "g", "y0", "y1")}
    yaT = dscr("yaT", [512, N], BF16)
    ysT = [dscr(f"ysT{d}", [512, N]) for d in range(2)]
    ycT = dscr("ycT", [512, N], BF16)
    xmid = dscr("xmid", [N, D])
    xm2T = dscr("xm2T", [D, N], BF16)
    h1T = dscr("h1T", [4 * D, N], BF16)

    uid = [0]

    def sb(st, name, shape, dt=F32):
        uid[0] += 1
        return st.enter_context(nc.sbuf_tensor(f"sb{uid[0]}_{name}", list(shape), dt))

    def ps(st, name, shape, dt=F32):
        uid[0] += 1
        ne = 512 if dt == F32 else 1024
        t = st.enter_context(nc.psum_tensor(f"ps{uid[0]}_{name}", [128, ne], dt))
        n = int(np.prod(shape[1:]))
        v = t[0:shape[0], 0:n]
        if len(shape) == 3:
            v = v.rearrange("p (a b) -> p a b", b=shape[2])
        return v

    def run_interleaved(gens):
        gens = list(gens)
        while gens:
            for g_ in list(gens):
                try:
                    next(g_)
                except StopIteration:
                    gens.remove(g_)

    def tt_(en, o, a, b, op, r, w):
        S.op(en, lambda e: e.tensor_tensor(out=o, in0=a, in1=b, op=op), r=r, w=w)

    def ts_(en, o, a, s1, s2, op0, op1, r, w):
        if s2 is None:
            S.op(en, lambda e: e.tensor_scalar(out=o, in0=a, scalar1=s1, scalar2=None, op0=op0), r=r, w=w)
        else:
            S.op(en, lambda e: e.tensor_scalar(out=o, in0=a, scalar1=s1, scalar2=s2, op0=op0, op1=op1), r=r, w=w)

    def act_(o, a, func, r, w, **kw):
        S.op("act", lambda e: e.activation(out=o, in_=a, func=func, **kw), r=r, w=w)

    def mm_(o, lhsT, rhs, start, stop, r, w):
        S.op("pe", lambda e: e.matmul(o, lhsT=lhsT, rhs=rhs, start=start, stop=stop), r=r, w=w)

    def rstd_(o, a, scale, eps, r, w):
        ts_("dve", o, a, scale, eps, ALU.mult, ALU.add, r=r, w=w)
        act_(o, o, AF.Sqrt, r=w, w=w)
        S.op("dve", lambda e: e.reciprocal(out=o, in_=o), r=w, w=w)

    def tr_(o, a, idn, r, w):
        S.op("pe", lambda e: e.transpose(o, a, idn), r=r, w=w)

    def cp_(en, o, a, r, w):
        S.op(en, lambda e: e.tensor_copy(out=o, in_=a), r=r, w=w)

    def red_(en, o, a, r, w):
        S.op(en, lambda e: e.tensor_reduce(out=o, in_=a, axis=AX.X, op=ALU.add), r=r, w=w)

    ones = sb(top, "ones", [128, 128])
    S.dma("sp", ones[:], ones_in, w=["ones"])
    masks = sb(top, "masks", [128, 4, 128])
    S.dma("sp", masks[:], masks_in.rearrange("m p t -> p m t"), w=["masks"])
    bsum = sb(top, "bsum", [128, NT, 8])
    ident = sb(top, "ident", [128, 128])
    identb = sb(top, "identb", [128, 128], BF16)
    modT = sb(top, "modT", [128, DEPTH, 48, 2])
    csil = sb(top, "csil", [128, 8, 2])
    S.dma("sp", ident[:], ident_in, w=["ident"])
    S.op("act", lambda e: e.activation(out=identb[:], in_=ident[:], func=AF.Copy), r=["ident"], w=["identb"])
    S.dma("sp", csil[:], cT_in, w=["csil"])
    S.op("act", lambda e: e.activation(out=csil[:], in_=csil[:], func=AF.Silu), r=["csil"], w=["csil"])

    for i in range(0, NT, 2):
        S.dma("sp", xs[i * 128:(i + 2) * 128, :], x_in[i * 128:(i + 2) * 128, :], w=[("xs", i), ("xs", i + 1)])

    with ExitStack() as st:
        wm = [sb(st, f"modw{i}", [128, 8, 512]) for i in range(2)]
        mbt = sb(st, "mbt", [128, DEPTH, 48])
        pm = ps(st, "pm", [128, 48, 2])
        S.dma("sp", mbt[:], mod_bT.rearrange("l p c -> p l c"), w=["mbt"])
        it = 0
        for l in range(n_layers):
            for cb in range(12):
                wt = wm[it % 2]
                key = ("modw", it % 2)
                it += 1
                S.dma("sp" if it % 2 else "pool", wt[:],
                      mod_w[l, :, cb * 512:(cb + 1) * 512].rearrange("(kc p) n -> p kc n", p=128), w=[key])
                for j in range(4):
                    cc = cb * 4 + j
                    for kc in range(8):
                        S.op("pe", lambda e, wt=wt, kc=kc, j=j, cc=cc: e.matmul(
                            pm[:, cc, :], lhsT=wt[:, kc, j * 128:(j + 1) * 128], rhs=csil[:, kc, :],
                            start=(kc == 0), stop=(kc == 7)), r=[key, "csil"], w=["pm"])
            for g in range(2):
                S.op("dve", lambda e, l=l, g=g: e.tensor_tensor(out=modT[:, l, :, g], in0=pm[:, :, g], in1=mbt[:, l, :],
                                                                op=ALU.add), r=["pm", "mbt"], w=["modT"])
            for c0 in (8, 32):
                S.op("dve", lambda e, l=l, c0=c0: e.tensor_scalar(out=modT[:, l, c0:c0 + 8, :], in0=modT[:, l, c0:c0 + 8, :],
                                                                  scalar1=1.0, scalar2=None, op0=ALU.add),
                     r=["modT"], w=["modT"])
    S.barrier()
    if stop_after == "A":
        dbg_t = nc.dram_tensor("dbg_modT", [128, DEPTH * 96], F32, kind="ExternalOutput").ap()
        S.dma("sp", dbg_t, modT[:].rearrange("p l c g -> p (l c g)"), r=["modT"])

    for l in range(n_layers):
        if stop_after == "A":
            break
        with ExitStack() as st:
            xmT = sb(st, "xmT", [128, 8, N], BF16)
            xt = [sb(st, f"xt{i}", [128, D]) for i in range(2)]
            pT = [ps(st, f"pT{i}", [128, 4, 128]) for i in range(2)]
            for tt in range(NT):
                g = 1 if tt < 2 else 0
                x_t = xt[tt % 2]
                S.dma("sp", x_t[:], xs[tt * 128:(tt + 1) * 128, :], r=[("xs", tt)], w=[("xt", tt % 2)])
                for hf in range(2):
                    for j in range(4):
                        kc = hf * 4 + j
                        S.op("pe", lambda e, x_t=x_t, kc=kc, hf=hf, j=j: e.transpose(
                            pT[hf][:, j, :], x_t[:, kc * 128:(kc + 1) * 128], ident[:]),
                            r=[("xt", tt % 2), "ident"], w=[("pT", hf)])
                    for j in range(4):
                        kc = hf * 4 + j
                        S.op("act", lambda e, kc=kc, hf=hf, j=j, tt=tt, g=g: e.activation(
                            out=xmT[:, kc, tt * 128:(tt + 1) * 128], in_=pT[hf][:, j, :], func=AF.Identity,
                            scale=modT[:, l, 8 + kc, g:g + 1], bias=modT[:, l, kc, g:g + 1]),
                            r=[("pT", hf), "modT"], w=[("xmT", tt)])
            wb = [sb(st, f"wb{i}", [128, 8, 512], BF16) for i in range(3)]
            pp = [ps(st, f"pp{i}", [128, 512]) for i in range(4)]
            stg = [sb(st, f"stg{i}", [128, 512]) for i in range(4)]
            stgT = [sb(st, f"stgT{i}", [128, N]) for i in range(2)]
            stgG = [sb(st, f"stgG{i}", [128, N], BF16) for i in range(2)]
            blks = [(c0, min(512, C_SS - c0)) for c0 in range(0, C_SS, 512)] + \
                   [(c0, min(512, INC - c0)) for c0 in range(C_SS, INC, 512)]
            k = 0
            kf = 0
            for bi, (c0, cw) in enumerate(blks):
                w_t = wb[bi % 3]
                wkey = ("wb", bi % 3)
                S.dma("pool", w_t[:, :, 0:cw], w_in[l, :, c0:c0 + cw].rearrange("(kc p) n -> p kc n", p=128), w=[wkey])
                if c0 < C_SS:
                    for tt in range(NT):
                        i = k % 4
                        k += 1
                        for kc in range(8):
                            S.op("pe", lambda e, i=i, kc=kc, tt=tt, w_t=w_t, cw=cw: e.matmul(
                                pp[i][:, 0:cw], lhsT=xmT[:, kc, tt * 128:(tt + 1) * 128], rhs=w_t[:, kc, 0:cw],
                                start=(kc == 0), stop=(kc == 7)), r=[("xmT", tt), wkey], w=[("pp", i)])
                        if i % 2 == 0:
                            S.op("act", lambda e, i=i, cw=cw: e.activation(out=stg[i][:, 0:cw], in_=pp[i][:, 0:cw], func=AF.Copy),
                                 r=[("pp", i)], w=[("stg", i)])
                        else:
                            S.op("dve", lambda e, i=i, cw=cw: e.tensor_copy(out=stg[i][:, 0:cw], in_=pp[i][:, 0:cw]),
                                 r=[("pp", i)], w=[("stg", i)])
                        S.dma("sp", p_tm[tt * 128:(tt + 1) * 128, c0:c0 + cw], stg[i][:, 0:cw], r=[("stg", i)],
                              w=[("p_tm", tt)])
                else:
                    for j in range(cw // 128):
                        cc = c0 + j * 128
                        is_g = cc >= C_G
                        f = kf % 2
                        kf += 1
                        dstt = stgG[f] if is_g else stgT[f]
                        skey = ("stgG", f) if is_g else ("stgT", f)
                        for t0 in range(0, N, 512):
                            tw = min(512, N - t0)
                            i = k % 4
                            k += 1
                            for kc in range(8):
                                S.op("pe", lambda e, i=i, kc=kc, t0=t0, tw=tw, w_t=w_t, j=j: e.matmul(
                                    pp[i][:, 0:tw], lhsT=w_t[:, kc, j * 128:(j + 1) * 128], rhs=xmT[:, kc, t0:t0 + tw],
                                    start=(kc == 0), stop=(kc == 7)),
                                    r=[("xmT", t) for t in range(t0 // 128, (t0 + tw) // 128)] + [wkey], w=[("pp", i)])
                            if is_g:
                                S.op("act", lambda e, i=i, t0=t0, tw=tw, dstt=dstt: e.activation(
                                    out=dstt[:, t0:t0 + tw], in_=pp[i][:, 0:tw], func=AF.Sigmoid), r=[("pp", i)], w=[skey])
                            else:
                                S.op("dve", lambda e, i=i, t0=t0, tw=tw, dstt=dstt: e.tensor_copy(
                                    out=dstt[:, t0:t0 + tw], in_=pp[i][:, 0:tw]), r=[("pp", i)], w=[skey])
                        if is_g:
                            S.dma("sp", gT[cc - C_G:cc - C_G + 128, :], dstt[:], r=[skey], w=[("gT", (cc - C_G) // 128)])
                        else:
                            S.dma("sp", p_ssT[cc - C_SS:cc - C_SS + 128, :], dstt[:], r=[skey], w=[("p_ssT", (cc - C_SS) // 128)])
        S.barrier()
        if stop_after == "B":
            break


        with ExitStack() as st:
            mu = sb(st, "mu", [128, 1792])
            rwp = sb(st, "rwp", [128, 5, 512])
            lrb = sb(st, "lrb", [1, 4, 512])
            lrw = sb(st, "lrw", [128, 2, 512])
            gup = sb(st, "gup", [128, 512])
            S.dma("sp", mu[:], mu_in[l], w=["mu"])
            S.dma("sp", rwp[:], rwp_in[l].rearrange("k p c -> p k c"), w=["rwp"])
            S.dma("sp", lrb[:], lrb_in[l:l + 1], w=["lrb"])
            S.dma("sp", lrw[:], lrw_in[l], w=["lrw"])
            S.dma("sp", gup[:], gup_in[l], w=["gup"])
            P0 = [sb(st, f"P0{i}", [128, 1792]) for i in range(2)]
            Pm = [sb(st, f"Pm{i}", [128, 1792]) for i in range(2)]
            Pp = [sb(st, f"Pp{i}", [128, 1792]) for i in range(2)]
            tl2 = [sb(st, f"tl{i}", [128, 1792]) for i in range(2)]
            pl = [sb(st, f"pl{i}", [128, 1792]) for i in range(2)]
            lrT2 = [sb(st, f"lrT{i}", [128, 128]) for i in range(2)]
            gsT2 = [sb(st, f"gsT{i}", [128, 128]) for i in range(2)]
            ptr = [ps(st, f"ptr{i}", [128, 128]) for i in range(2)]
            pq = [ps(st, f"pq{i}", [128, 512]) for i in range(5)]
            sg2 = [[sb(st, f"sg{i}{q}", [128, 512]) for q in range(4)] for i in range(2)]
            o52 = [[sb(st, f"o5{i}{q}", [128, 512]) for q in range(10)] for i in range(2)]
            kx2 = [sb(st, f"kx{i}", [128, 512]) for i in range(2)]
            sq52 = [sb(st, f"sq5{i}", [128, 512]) for i in range(2)]
            ss82 = [sb(st, f"ss8{i}", [128, 8]) for i in range(2)]

            def c_tile(tt):
                i = tt % 2
                tl, lrT, gsT, sg, o5, kx, sq5, ss8 = tl2[i], lrT2[i], gsT2[i], sg2[i], o52[i], kx2[i], sq52[i], ss82[i]
                lo, hi = tt * 128, (tt + 1) * 128
                first = tt in (0, 2)
                last = tt in (1, NT - 1)
                S.dma("sp", P0[i][:], p_tm[lo:hi, 0:1792], r=[("p_tm", tt)], w=[("P0", i)])
                if first:
                    S.op("pool", lambda e, i=i: e.memset(Pm[i][:], 0.0), w=[("Pm", i)])
                    S.dma("sp", Pm[i][1:128, :], p_tm[lo:hi - 1, 0:1792], r=[("p_tm", tt)], w=[("Pm", i)])
                else:
                    S.dma("sp", Pm[i][:], p_tm[lo - 1:hi - 1, 0:1792], r=[("p_tm", tt), ("p_tm", tt - 1)], w=[("Pm", i)])
                if last:
                    S.op("pool", lambda e, i=i: e.memset(Pp[i][:], 0.0), w=[("Pp", i)])
                    S.dma("sp", Pp[i][0:127, :], p_tm[lo + 1:hi, 0:1792], r=[("p_tm", tt)], w=[("Pp", i)])
                else:
                    S.dma("sp", Pp[i][:], p_tm[lo + 1:hi + 1, 0:1792], r=[("p_tm", tt), ("p_tm", tt + 1)], w=[("Pp", i)])
                p_ = pl[i]
                pk = ("pl", i)
                tt_("dve", tl[:], Pm[i][:], Pp[i][:], ALU.add, r=[("Pm", i), ("Pp", i)], w=[("tl", i)])
                S.op("dve", lambda e, i=i: e.scalar_tensor_tensor(out=tl[:], in0=tl[:], scalar=0.5, in1=P0[i][:], op0=ALU.mult,
                                                                op1=ALU.subtract), r=[("tl", i), ("P0", i)], w=[("tl", i)])
                tt_("dve", tl[:], tl[:], mu[:], ALU.mult, r=[("tl", i), "mu"], w=[("tl", i)])
                tt_("dve", p_[:], tl[:], P0[i][:], ALU.add, r=[("tl", i), ("P0", i)], w=[pk])
                S.dma("act", rw_s["r"][lo:hi, :], p_[:, 0:512], r=[pk], w=[("rw_r", tt)])
                S.dma("act", rw_s["v"][lo:hi, :], p_[:, 1024:1536], r=[pk], w=[("rw_v", tt)])
                yield
                tr_(ptr[0][:], p_[:, 1536:1664], ident[:], r=[pk, "ident"], w=[("ptr", 0)])
                tr_(ptr[1][:], p_[:, 1664:1792], ident[:], r=[pk, "ident"], w=[("ptr", 1)])
                act_(lrT[0:64, :], ptr[0][0:64, :], AF.Tanh, r=[("ptr", 0)], w=[("lrT", i)])
                act_(lrT[64:128, :], ptr[0][64:128, :], AF.Copy, r=[("ptr", 0)], w=[("lrT", i)])
                act_(gsT[:], ptr[1][:], AF.Sigmoid, r=[("ptr", 1)], w=[("gsT", i)])
                yield
                for d in range(2):
                    mm_(pq[d][:], ones[0:1, :], lrb[0:1, d, :], True, False, r=["ones", "lrb"], w=[("pq", d)])
                    mm_(pq[d][:], lrT[0:64, :], lrw[0:64, d, :], False, True, r=[("lrT", i), "lrw"], w=[("pq", d)])
                    mm_(pq[2 + d][:], ones[0:1, :], lrb[0:1, 2 + d, :], True, False, r=["ones", "lrb"], w=[("pq", 2 + d)])
                    mm_(pq[2 + d][:], lrT[64:128, :], lrw[64:128, d, :], False, True, r=[("lrT", i), "lrw"], w=[("pq", 2 + d)])
                mm_(pq[4][:], gsT[:], gup[:], True, True, r=[("gsT", i), "gup"], w=[("pq", 4)])
                for q in range(4):
                    act_(sg[q][:], pq[q][:], AF.Sigmoid, r=[("pq", q)], w=[("sg", i, q)])
                o = o5
                ok = lambda q: ("o5", i, q)
                cp_("dve", o[9][:], pq[4][:], r=[("pq", 4)], w=[ok(9)])
                S.dma("act", rw_s["g"][lo:hi, :], o[9][:], r=[ok(9)], w=[("rw_g", tt)])
                yield
                k_ = p_[:, 512:1024]
                r_ = p_[:, 0:512]
                h3 = lambda a: a.rearrange("p (h d) -> p h d", d=64)
                tt_("pool", kx[:], k_, rwp[:, 0, :], ALU.mult, r=[pk, "rwp"], w=[("kx", i)])
                tt_("pool", sq5[:], kx[:], kx[:], ALU.mult, r=[("kx", i)], w=[("sq5", i)])
                red_("dve", ss8[:], h3(sq5[:]), r=[("sq5", i)], w=[("ss8", i)])
                ts_("dve", ss8[:], ss8[:], 1e-24, None, ALU.max, None, r=[("ss8", i)], w=[("ss8", i)])
                act_(ss8[:], ss8[:], AF.Sqrt, r=[("ss8", i)], w=[("ss8", i)])
                S.op("dve", lambda e: e.reciprocal(out=ss8[:], in_=ss8[:]), r=[("ss8", i)], w=[("ss8", i)])
                tt_("pool", h3(o[0][:]), h3(kx[:]), ss8[:].unsqueeze(2).broadcast_to([128, 8, 64]), ALU.mult, r=[("kx", i), ("ss8", i)], w=[ok(0)])
                S.dma("act", rw_s["kk"][lo:hi, :], o[0][:], r=[ok(0)], w=[("rw_kk", tt)])
                yield
                for d in range(2):
                    ts_("pool", o[1 + d][:], sg[d][:], -0.6065306597126334, None, ALU.mult, None, r=[("sg", i, d)], w=[ok(1 + d)])
                    S.dma("act", rw_s[f"lw{d}"][lo:hi, :], o[1 + d][:], r=[ok(1 + d)], w=[(f"rw_lw{d}", tt)])
                    tt_("pool", o[3 + d][:], o[0][:], sg[2 + d][:], ALU.mult, r=[ok(0), ("sg", i, 2 + d)], w=[ok(3 + d)])
                    S.dma("act", rw_s[f"b{d}"][lo:hi, :], o[3 + d][:], r=[ok(3 + d)], w=[(f"rw_b{d}", tt)])
                    S.op("dve", lambda e, d=d: e.scalar_tensor_tensor(out=o[5 + d][:], in0=sg[2 + d][:], scalar=-1.0, in1=rwp[:, 1, :],
                                                                     op0=ALU.add, op1=ALU.mult), r=[("sg", i, 2 + d), "rwp"], w=[ok(5 + d)])
                    S.op("dve", lambda e, d=d, k_=k_: e.scalar_tensor_tensor(out=o[5 + d][:], in0=o[5 + d][:], scalar=1.0, in1=k_,
                                                                            op0=ALU.add, op1=ALU.mult), r=[ok(5 + d), pk], w=[ok(5 + d)])
                    S.dma("act", rw_s[f"kd{d}"][lo:hi, :], o[5 + d][:], r=[ok(5 + d)], w=[(f"rw_kd{d}", tt)])
                tt_("pool", o[7][:], o[5][:], o[6][:], ALU.add, r=[ok(5), ok(6)], w=[ok(7)])
                tt_("pool", o[8][:], r_, rwp[:, 2, :], ALU.mult, r=[pk, "rwp"], w=[ok(8)])
                tt_("pool", o[8][:], o[8][:], o[7][:], ALU.mult, r=[ok(8), ok(7)], w=[ok(8)])
                red_("dve", bsum[:, tt, :], h3(o[8][:]), r=[ok(8)], w=[("bsum", tt)])

            for t2 in range(0, NT, 2):
                run_interleaved([c_tile(t2), c_tile(t2 + 1)])
        S.barrier()
        if stop_after == "C":
            break

        with ExitStack() as st:
            pf = [ps(st, f"pf{i}", [128, 512]) for i in range(8)]
            npf = [0]

            npfd = [0, 0]

            def PFd(d):
                i = 4 * d + npfd[d] % 4
                npfd[d] += 1
                return pf[i], ("pf", i)

            Bd = []
            for d in range(2):
                B = {}
                for nm in ("r", "v", "kk", "lw", "b", "kd", "cum", "e1", "e2", "e3", "e4", "abarf", "bbarf", "kbarf", "rbarf"):
                    B[nm] = sb(st, f"D{d}{nm}", [128, 512])
                for nm in ("abar", "Bt", "Kt", "Vb", "Z", "U"):
                    B[nm] = sb(st, f"D{d}{nm}", [128, 512], BF16)
                for nm in ("abarT", "bbarT", "kbarT"):
                    B[nm] = sb(st, f"D{d}{nm}", [128, 4, 128], BF16)
                for nm in ("AakT", "MrbT", "MrkT", "abarZ", "bbarZ", "rbarZ", "Rb"):
                    B[nm] = sb(st, f"D{d}{nm}", [128, 8, 128], BF16)
                for nm in ("AabT", "Aab", "P0", "P1", "Q0", "Q1", "R0", "R1"):
                    B[nm] = sb(st, f"D{d}{nm}", [128, 8, 128], F32)
                for nm in ("abarZ", "bbarZ", "rbarZ"):
                    S.op("pool", lambda e, B=B, nm=nm: e.memset(B[nm][:], 0.0), w=[(nm, d)])
                B["rbT1"] = sb(st, f"D{d}rbT1", [128, 8, 128], BF16)
                B["WT"] = sb(st, f"D{d}WT", [128, 8, 128], BF16)
                S.op("pool", lambda e, B=B: e.memset(B["rbT1"][:], 0.0), w=[("rbT1", d)])
                S.op("pool", lambda e, B=B: e.memset(B["WT"][:], 0.0), w=[("WT", d)])
                B["gl"] = sb(st, f"D{d}gl", [64, 8])
                B["Sf"] = sb(st, f"D{d}Sf", [64, 8, 64])
                B["Stmp"] = sb(st, f"D{d}Stmp", [64, 8, 64])
                B["Sbf"] = sb(st, f"D{d}Sbf", [128, 8, 64], BF16)
                S.op("pool", lambda e, B=B: e.memset(B["Sf"][:], 0.0), w=[("Sf", d)])
                S.op("pool", lambda e, B=B: e.memset(B["Sbf"][:], 0.0), w=[("Sbf", d)])
                Bd.append(B)

            def rw_chunk(c, d):
                B = Bd[d]
                K_ = lambda n: (n, d)
                lo, hi = c * 128, (c + 1) * 128
                hs = lambda h: slice(h * 64, (h + 1) * 64)
                for nm, src in (("r", "r"), ("v", "v"), ("kk", "kk"), ("lw", f"lw{d}"), ("b", f"b{d}"), ("kd", f"kd{d}")):
                    S.dma("sp", B[nm][:], rw_s[src][lo:hi, :], r=[("rw_" + src, c)], w=[K_(nm)])
                    yield
                pc, pck = PFd(d)
                mm_(pc[:], masks[:, 2 + d, :], B["lw"][:], True, True, r=["masks", K_("lw")], w=[pck])
                pt, ptk = PFd(d)
                mm_(pt[:], ones[:], B["lw"][:], True, True, r=["ones", K_("lw")], w=[ptk])
                act_(B["cum"][:], pc[:], AF.Copy, r=[pck], w=[K_("cum")])
                yield
                act_(B["e1"][:], B["cum"][:], AF.Exp, r=[K_("cum")], w=[K_("e1")])
                yield
                act_(B["e2"][:], B["cum"][:], AF.Exp, r=[K_("cum")], w=[K_("e2")], scale=-1.0)
                yield
                tt_("pool", B["e3"][:], B["cum"][:], B["lw"][:], ALU.subtract, r=[K_("cum"), K_("lw")], w=[K_("e3")])
                yield
                act_(B["e3"][:], B["e3"][:], AF.Exp, r=[K_("e3")], w=[K_("e3")])
                yield
                tt_("dve", B["e4"][:], pt[:], B["cum"][:], ALU.subtract, r=[ptk, K_("cum")], w=[K_("e4")])
                yield
                act_(B["e4"][:], B["e4"][:], AF.Exp, r=[K_("e4")], w=[K_("e4")])
                yield
                S.op("dve", lambda e: e.scalar_tensor_tensor(out=B["abarf"][:], in0=B["kk"][:], scalar=-1.0, in1=B["e3"][:],
                                                            op0=ALU.mult, op1=ALU.mult), r=[K_("kk"), K_("e3")], w=[K_("abarf")])
                yield
                tt_("pool", B["bbarf"][:], B["b"][:], B["e2"][:], ALU.mult, r=[K_("b"), K_("e2")], w=[K_("bbarf")])
                yield
                tt_("dve", B["kbarf"][:], B["kd"][:], B["e2"][:], ALU.mult, r=[K_("kd"), K_("e2")], w=[K_("kbarf")])
                yield
                tt_("pool", B["rbarf"][:], B["r"][:], B["e1"][:], ALU.mult, r=[K_("r"), K_("e1")], w=[K_("rbarf")])
                yield
                tt_("dve", B["Bt"][:], B["b"][:], B["e4"][:], ALU.mult, r=[K_("b"), K_("e4")], w=[K_("Bt")])
                yield
                tt_("pool", B["Kt"][:], B["kd"][:], B["e4"][:], ALU.mult, r=[K_("kd"), K_("e4")], w=[K_("Kt")])
                yield
                cp_("pool", B["Vb"][:], B["v"][:], r=[K_("v")], w=[K_("Vb")])
                yield
                pg, pgk = PFd(d)
                for h in range(8):
                    mm_(pg[0:64, 2 * h:2 * h + 2], B["lw"][:, hs(h)], ones[:, 0:2], True, True, r=[K_("lw"), "ones"], w=[pgk])
                act_(B["gl"][:], pg[0:64, 0:16].rearrange("p (h two) -> p h two", two=2)[:, :, 0], AF.Exp, r=[pgk], w=[K_("gl")])
                yield
                if DSTAGE < 1:
                    return
                for nm in ("abar", "bbar", "kbar", "rbar")[:DSUB]:
                    p2, p2k = PFd(d)
                    for j in range(4):
                        tr_(p2[:, j * 128:(j + 1) * 128], B[nm + "f"][:, j * 128:(j + 1) * 128], ident[:], r=[K_(nm + "f"), "ident"], w=[p2k])
                    p3 = p2[:].rearrange("p (j t) -> p j t", t=128)
                    if nm != "rbar":
                        act_(B[nm + "T"][:], p3, AF.Copy, r=[p2k], w=[K_(nm + "T")])
                        yield
                    if nm != "kbar":
                        z4 = lambda lo_: B[nm + "Z"][lo_:lo_ + 64].rearrange("p (j two) t -> p j two t", two=2)
                        act_(z4(0)[:, :, 0, :], p3[0:64], AF.Copy, r=[p2k], w=[K_(nm + "Z")])
                        yield
                        cp_("dve", z4(64)[:, :, 1, :], p3[64:128], r=[p2k], w=[K_(nm + "Z")])
                        yield
                for hh in range(2 if DSUB > 4 else 0):
                    p2, p2k = PFd(d)
                    for h4 in range(4):
                        h = hh * 4 + h4
                        tr_(p2[0:64, h4 * 128:(h4 + 1) * 128], B["rbarf"][:, hs(h)], ident[:], r=[K_("rbarf"), "ident"], w=[p2k])
                    act_(B["rbT1"][0:64, hh * 4:(hh + 1) * 4, :], p2[0:64, :].rearrange("p (j t) -> p j t", t=128), AF.Copy, r=[p2k], w=[K_("rbT1")])
                    yield
                cp_("dve", B["abar"][:], B["abarf"][:], r=[K_("abarf")], w=[K_("abar")])
                yield
                if DSTAGE < 2:
                    return
                for nm, lt, rt, mk in (("AabT", "bbarT", "abarZ", d), ("Aab", "abarT", "bbarZ", 1 - d), ("AakT", "kbarT", "abarZ", d),
                                       ("MrbT", "bbarT", "rbarZ", 2 + d), ("MrkT", "kbarT", "rbarZ", 2 + d)):
                    for hh in range(2):
                        p_, pk_ = PFd(d)
                        for h4 in range(4):
                            h = hh * 4 + h4
                            mm_(p_[:, h4 * 128:(h4 + 1) * 128], B[lt][:, h // 2, :], B[rt][:, h, :], True, True,
                                r=[K_(lt), K_(rt)], w=[pk_])
                        tt_("dve", B[nm][:, hh * 4:(hh + 1) * 4, :], p_[:].rearrange("p (h t) -> p h t", t=128),
                            masks[:, mk, :].unsqueeze(1).broadcast_to([128, 4, 128]), ALU.mult, r=[pk_, "masks"], w=[K_(nm)])
                        yield
                if DSTAGE < 3:
                    return
                P_, Q_ = "AabT", "Aab"
                tt_("pool", B["R0"][:], B["AabT"][:], ident[:].unsqueeze(1).broadcast_to([128, 8, 128]), ALU.add,
                    r=[K_("AabT"), "ident"], w=[K_("R0")])
                yield
                Rc = "R0"
                for lev in range(6):
                    Pn, Qn, Rn = f"P{lev % 2}", f"Q{lev % 2}", f"R{(lev + 1) % 2}"
                    for hh in range(2):
                        if lev < 5:
                            p_, pk_ = PFd(d)
                            for h4 in range(4):
                                h = hh * 4 + h4
                                mm_(p_[:, h4 * 128:(h4 + 1) * 128], B[Q_][:, h, :], B[P_][:, h, :], True, True, r=[K_(Q_), K_(P_)], w=[pk_])
                            act_(B[Pn][:, hh * 4:(hh + 1) * 4, :], p_[:].rearrange("p (h t) -> p h t", t=128), AF.Copy, r=[pk_], w=[K_(Pn)])
                            yield
                        p_, pk_ = PFd(d)
                        for h4 in range(4):
                            h = hh * 4 + h4
                            mm_(p_[:, h4 * 128:(h4 + 1) * 128], B[P_][:, h, :], B[Q_][:, h, :], True, True, r=[K_(Q_), K_(P_)], w=[pk_])
                        cp_("dve", B[Qn][:, hh * 4:(hh + 1) * 4, :], p_[:].rearrange("p (h t) -> p h t", t=128), r=[pk_], w=[K_(Qn)])
                        yield
                    for hh in range(2):
                        p_, pk_ = PFd(d)
                        for h4 in range(4):
                            h = hh * 4 + h4
                            mm_(p_[:, h4 * 128:(h4 + 1) * 128], B[Qn][:, h, :], B[Rc][:, h, :], True, True, r=[K_(Qn), K_(Rc)], w=[pk_])
                        tt_("dve", B[Rn][:, hh * 4:(hh + 1) * 4, :], p_[:].rearrange("p (h t) -> p h t", t=128),
                            B[Rc][:, hh * 4:(hh + 1) * 4, :], ALU.add, r=[pk_, K_(Rc)], w=[K_(Rn)])
                        yield
                    P_, Q_, Rc = Pn, Qn, Rn
                if DSTAGE < 4:
                    return
                cp_("pool", B["Rb"][:], B[Rc][:], r=[K_(Rc)], w=[K_("Rb")])
                yield
                Rc = "Rb"
                p_, pk_ = PFd(d)
                for h in range(8):
                    mm_(p_[:, hs(h)], B["AakT"][:, h, :], B["Vb"][:, hs(h)], True, True, r=[K_("AakT"), K_("Vb")], w=[pk_])
                act_(B["Z"][:], p_[:], AF.Copy, r=[pk_], w=[K_("Z")])
                yield
                p_, pk_ = PFd(d)
                for h in range(8):
                    mm_(p_[:, hs(h)], B[Rc][:, h, :], B["Z"][:, hs(h)], True, True, r=[K_(Rc), K_("Z")], w=[pk_])
                cp_("dve", B["e1"][:], p_[:], r=[pk_], w=[K_("e1")])
                yield
                for hh in range(2):
                    p_, pk_ = PFd(d)
                    for h4 in range(4):
                        h = hh * 4 + h4
                        mm_(p_[0:64, h4 * 128:(h4 + 1) * 128], B["abar"][:, hs(h)], B[Rc][:, h, :], True, True, r=[K_("abar"), K_(Rc)], w=[pk_])
                    act_(B["WT"][0:64, hh * 4:(hh + 1) * 4, :], p_[0:64, :].rearrange("p (h t) -> p h t", t=128), AF.Copy, r=[pk_], w=[K_("WT")])
                    yield
                if DSTAGE < 5:
                    return
                p_, pk_ = PFd(d)
                for h in range(8):
                    mm_(p_[:, hs(h)], B["WT"][:, h, :], B["Sbf"][:, h, :], True, True, r=[K_("WT"), K_("Sbf")], w=[pk_])
                tt_("dve", B["U"][:], p_[:], B["e1"][:], ALU.add, r=[pk_, K_("e1")], w=[K_("U")])
                yield
                py, pyk = PFd(d)
                for h in range(8):
                    mm_(py[:, hs(h)], B["rbT1"][:, h, :], B["Sbf"][:, h, :], True, False, r=[K_("rbT1"), K_("Sbf")], w=[pyk])
                    mm_(py[:, hs(h)], B["MrbT"][:, h, :], B["U"][:, hs(h)], False, False, r=[K_("MrbT"), K_("U")], w=[pyk])
                    mm_(py[:, hs(h)], B["MrkT"][:, h, :], B["Vb"][:, hs(h)], False, True, r=[K_("MrkT"), K_("Vb")], w=[pyk])
                act_(B["e2"][:], py[:], AF.Copy, r=[pyk], w=[K_("e2")])
                yield
                S.dma("act", rw_s[f"y{d}"][lo:hi, :], B["e2"][:], r=[K_("e2")], w=[(f"rw_y{d}", c)])
                yield
                pS, pSk = PFd(d)
                for h in range(8):
                    mm_(pS[0:64, hs(h)], B["Kt"][:, hs(h)], B["Vb"][:, hs(h)], True, False, r=[K_("Kt"), K_("Vb")], w=[pSk])
                    mm_(pS[0:64, hs(h)], B["Bt"][:, hs(h)], B["U"][:, hs(h)], False, True, r=[K_("Bt"), K_("U")], w=[pSk])
                tt_("pool", B["Stmp"][:], B["Sf"][:], B["gl"][:].unsqueeze(2).broadcast_to([64, 8, 64]), ALU.mult,
                    r=[K_("Sf"), K_("gl")], w=[K_("Stmp")])
                yield
                tt_("dve", B["Sf"][:], pS[0:64, :].rearrange("p (h v) -> p h v", v=64), B["Stmp"][:], ALU.add, r=[pSk, K_("Stmp")], w=[K_("Sf")])
                yield
                act_(B["Sbf"][0:64], B["Sf"][:], AF.Copy, r=[K_("Sf")], w=[K_("Sbf")])
                yield

            ord0 = list(range(NT))
            ord1 = [1, 0] + list(range(NT - 1, 1, -1))
            for i in range(NT):
                run_interleaved([rw_chunk(ord0[i], 0), rw_chunk(ord1[i], 1)])
        S.barrier()
        if stop_after == "D":
            break

        with ExitStack() as st:
            rwp = sb(st, "rwpE", [128, 5, 512])
            S.dma("sp", rwp[:], rwp_in[l].rearrange("k p c -> p k c"), w=["rwpE"])
            Y0 = [sb(st, f"EY0{i}", [128, 512]) for i in range(2)]
            Y1 = [sb(st, f"EY1{i}", [128, 512]) for i in range(2)]
            Vv = [sb(st, f"EV{i}", [128, 512]) for i in range(2)]
            Gg = [sb(st, f"EG{i}", [128, 512]) for i in range(2)]
            y = sb(st, "Ey", [128, 512])
            ysq = sb(st, "Eysq", [128, 512])
            bon = sb(st, "Ebon", [128, 512])
            yab = sb(st, "Eyab", [128, 512], BF16)
            st8 = sb(st, "Est8", [128, 4, 8])
            yT = [sb(st, f"EyT{i}", [128, 4, 128], BF16) for i in range(2)]
            pbE = [ps(st, f"pbE{i}", [128, 4, 128], BF16) for i in range(2)]
            h3 = lambda a: a.rearrange("p (h d) -> p h d", d=64)
            b3 = lambda a: a.unsqueeze(2).broadcast_to([128, 8, 64])
            for tt in range(NT):
                i = tt % 2
                lo, hi = tt * 128, (tt + 1) * 128
                S.dma("sp", Y0[i][:], rw_s["y0"][lo:hi, :], r=[("rw_y0", tt)], w=[("EY0", i)])
                S.dma("sp", Y1[i][:], rw_s["y1"][lo:hi, :], r=[("rw_y1", tt)], w=[("EY1", i)])
                S.dma("sp", Vv[i][:], rw_s["v"][lo:hi, :], r=[("rw_v", tt)], w=[("EV", i)])
                S.dma("sp", Gg[i][:], rw_s["g"][lo:hi, :], r=[("rw_g", tt)], w=[("EG", i)])
                tt_("pool", y[:], Y0[i][:], Y1[i][:], ALU.add, r=[("EY0", i), ("EY1", i)], w=["Ey"])
                red_("dve", st8[:, 0, :], h3(y[:]), r=["Ey"], w=["Est8"])
                tt_("pool", ysq[:], y[:], y[:], ALU.mult, r=["Ey"], w=["Eysq"])
                red_("dve", st8[:, 1, :], h3(ysq[:]), r=["Eysq"], w=["Est8"])
                ts_("dve", st8[:, 0, :], st8[:, 0, :], 1.0 / 64, None, ALU.mult, None, r=["Est8"], w=["Est8"])
                tt_("dve", st8[:, 2, :], st8[:, 0, :], st8[:, 0, :], ALU.mult, r=["Est8"], w=["Est8"])
                S.op("dve", lambda e: e.scalar_tensor_tensor(out=st8[:, 3, :], in0=st8[:, 1, :], scalar=1.0 / 64, in1=st8[:, 2, :],
                                                           op0=ALU.mult, op1=ALU.subtract), r=["Est8"], w=["Est8"])
                rstd_(st8[:, 3, :], st8[:, 3, :], 1.0, GN_EPS, r=["Est8"], w=["Est8"])
                tt_("pool", h3(y[:]), h3(y[:]), b3(st8[:, 0, :]), ALU.subtract, r=["Ey", "Est8"], w=["Ey"])
                tt_("pool", h3(y[:]), h3(y[:]), b3(st8[:, 3, :]), ALU.mult, r=["Ey", "Est8"], w=["Ey"])
                tt_("pool", y[:], y[:], rwp[:, 3, :], ALU.mult, r=["Ey", "rwpE"], w=["Ey"])
                tt_("pool", y[:], y[:], rwp[:, 4, :], ALU.add, r=["Ey", "rwpE"], w=["Ey"])
                tt_("dve", h3(bon[:]), h3(Vv[i][:]), b3(bsum[:, tt, :]), ALU.mult, r=[("EV", i), ("bsum", tt)], w=["Ebon"])
                tt_("pool", y[:], y[:], bon[:], ALU.add, r=["Ey", "Ebon"], w=["Ey"])
                tt_("pool", yab[:], y[:], Gg[i][:], ALU.mult, r=["Ey", ("EG", i)], w=["Eyab"])
                for j in range(4):
                    tr_(pbE[i][:, j, :], yab[:, j * 128:(j + 1) * 128], identb[:], r=["Eyab", "identb"], w=[("pbE", i)])
                act_(yT[i][:], pbE[i][:], AF.Copy, r=[("pbE", i)], w=[("EyT", i)])
                S.dma("act", yaT[:, lo:hi].rearrange("(j p) t -> p j t", p=128), yT[i][:], r=[("EyT", i)], w=[("yaT", tt)])
        S.barrier()
        if stop_after == "E":
            break

        with ExitStack() as st:
            mcol = sb(st, "mcol", [128, 4])
            S.dma("sp", mcol[:], mcol_in, w=["mcol"])
            Ep = [[sb(st, f"Ep{d}{k}", [128, 2048]) for k in range(2)] for d in range(2)]
            En = [[sb(st, f"En{d}{k}", [128, 2048]) for k in range(2)] for d in range(2)]
            Bp = [sb(st, f"Bp{d}", [128, 4, 1024], BF16) for d in range(2)]
            Cb = sb(st, "Cb", [128, 32, 128], BF16)
            S.dma("pool", Cb[:], cblk_in[l], w=["Cb"])
            trib = sb(st, "trib", [128, 2, 128], BF16)
            selb = sb(st, "selb", [128, 4, 128], BF16)
            S.dma("pool", trib[:], masks_in[2:4].rearrange("m p t -> p m t"), w=["trib"])
            S.dma("pool", selb[:], sel_in.rearrange("m p t -> p m t"), w=["selb"])
            MAGIC = 12582912.0
            C1 = 6.28125
            C2 = float(2 * np.pi - 6.28125)
            with ExitStack() as st2:
                T_ = {nm: sb(st2, "G" + nm, [128, 2048]) for nm in ("are", "aim", "dt", "x1", "x2", "arg", "k", "sn", "cs", "mg", "t1", "t2")}
                bb = [sb(st2, f"Gbb{k}", [128, 4, 512]) for k in range(2)]

                G = lambda nm: "G" + nm

                def g_tt(o, x, y, op, eng="pool"):
                    tt_(eng, T_[o][:], T_[x][:], T_[y][:], op, r=[G(x), G(y)], w=[G(o)])

                def sincos(src, shift, dst):
                    a_, k_ = T_["arg"], T_["k"]
                    ts_("dve", a_[:], T_[src][:], shift, None, ALU.add, None, r=[G(src)], w=[G("arg")])
                    ts_("dve", k_[:], a_[:], float(1 / (2 * np.pi)), MAGIC, ALU.mult, ALU.add, r=[G("arg")], w=[G("k")])
                    ts_("dve", k_[:], k_[:], -MAGIC, None, ALU.add, None, r=[G("k")], w=[G("k")])
                    S.op("dve", lambda e: e.scalar_tensor_tensor(out=a_[:], in0=k_[:], scalar=-C1, in1=a_[:], op0=ALU.mult, op1=ALU.add),
                         r=[G("k"), G("arg")], w=[G("arg")])
                    S.op("dve", lambda e: e.scalar_tensor_tensor(out=a_[:], in0=k_[:], scalar=-C2, in1=a_[:], op0=ALU.mult, op1=ALU.add),
                         r=[G("k"), G("arg")], w=[G("arg")])
                    act_(T_[dst][:], a_[:], AF.Sin, r=[G("arg")], w=[G(dst)])

                c4 = lambda t: t[:].rearrange("p (q c) -> p q c", c=512)
                for d in range(2):
                    S.dma("sp", T_["are"][:], sstab_in[l, d, 0], w=[G("are")])
                    S.dma("sp", T_["aim"][:], sstab_in[l, d, 1], w=[G("aim")])
                    S.dma("sp", T_["dt"][:], sstab_in[l, d, 2], w=[G("dt")])
                    if d == 0:
                        S.dma("sp", bb[0][:], bblk_in[l, 0], w=["Gbb0"])
                        S.dma("sp", bb[1][:], bblk_in[l, 1], w=["Gbb1"])
                    ts_("dve", T_["are"][:], T_["are"][:], -1e-4, None, ALU.min, None, r=[G("are")], w=[G("are")])
                    act_(T_["dt"][:], T_["dt"][:], AF.Exp, r=[G("dt")], w=[G("dt")])
                    g_tt("x1", "are", "dt", ALU.mult)
                    g_tt("x2", "aim", "dt", ALU.mult)
                    sincos("x2", 0.0, "sn")
                    sincos("x2", PI / 2, "cs")
                    act_(T_["mg"][:], T_["x1"][:], AF.Exp, r=[G("x1")], w=[G("mg")])
                    g_tt("t1", "sn", "mg", ALU.mult)
                    g_tt("t2", "cs", "mg", ALU.mult)
                    ts_("pool", T_["t2"][:], T_["t2"][:], -1.0, None, ALU.add, None, r=[G("t2")], w=[G("t2")])
                    g_tt("sn", "are", "are", ALU.mult)
                    g_tt("cs", "aim", "aim", ALU.mult)
                    g_tt("sn", "sn", "cs", ALU.add)
                    S.op("dve", lambda e: e.reciprocal(out=T_["sn"][:], in_=T_["sn"][:]), r=[G("sn")], w=[G("sn")])
                    g_tt("mg", "t2", "are", ALU.mult)
                    g_tt("cs", "t1", "aim", ALU.mult)
                    g_tt("mg", "mg", "cs", ALU.add)
                    g_tt("mg", "mg", "sn", ALU.mult)
                    g_tt("cs", "t1", "are", ALU.mult)
                    g_tt("k", "t2", "aim", ALU.mult)
                    g_tt("cs", "cs", "k", ALU.subtract)
                    g_tt("cs", "cs", "sn", ALU.mult)
                    tt_("pool", c4(T_["t1"]), bb[0][:], c4(T_["mg"]), ALU.mult, r=["Gbb0", G("mg")], w=[G("t1")])
                    tt_("pool", c4(T_["t2"]), bb[1][:], c4(T_["cs"]), ALU.mult, r=["Gbb1", G("cs")], w=[G("t2")])
                    tt_("pool", Bp[d][:, :, 0:512], c4(T_["t1"]), c4(T_["t2"]), ALU.subtract, r=[G("t1"), G("t2")], w=[("Bp", d)])
                    tt_("pool", c4(T_["t1"]), bb[0][:], c4(T_["cs"]), ALU.mult, r=["Gbb0", G("cs")], w=[G("t1")])
                    tt_("pool", c4(T_["t2"]), bb[1][:], c4(T_["mg"]), ALU.mult, r=["Gbb1", G("mg")], w=[G("t2")])
                    tt_("pool", Bp[d][:, :, 512:1024], c4(T_["t1"]), c4(T_["t2"]), ALU.add, r=[G("t1"), G("t2")], w=[("Bp", d)])
                    ts_("dve", T_["t1"][:], T_["x2"][:], mcol[:, d:d + 1], None, ALU.mult, None, r=[G("x2"), "mcol"], w=[G("t1")])
                    sincos("t1", 0.0, "sn")
                    sincos("t1", PI / 2, "cs")
                    act_(T_["mg"][:], T_["x1"][:], AF.Exp, r=[G("x1")], w=[G("mg")], scale=mcol[:, d:d + 1])
                    tt_("pool", Ep[d][0][:], T_["mg"][:], T_["cs"][:], ALU.mult, r=[G("mg"), G("cs")], w=[("Ep", d)])
                    tt_("pool", Ep[d][1][:], T_["mg"][:], T_["sn"][:], ALU.mult, r=[G("mg"), G("sn")], w=[("Ep", d)])
                    act_(T_["mg"][:], T_["x1"][:], AF.Exp, r=[G("x1")], w=[G("mg")], scale=mcol[:, 2 + d:3 + d])
                    tt_("pool", En[d][0][:], T_["mg"][:], T_["cs"][:], ALU.mult, r=[G("mg"), G("cs")], w=[("En", d)])
                    tt_("pool", En[d][1][:], T_["mg"][:], T_["sn"][:], ALU.mult, r=[G("mg"), G("sn")], w=[("En", d)])
            S.barrier()
            uf = [sb(st, f"Guf{i}", [128, 4, 128]) for i in range(2)]
            ub = [sb(st, f"Gub{i}", [128, 4, 128], BF16) for i in range(2)]
            hb = [[sb(st, f"Ghb{d}{i}", [128, 4, 2, 512], BF16) for i in range(2)] for d in range(2)]
            zb = [sb(st, f"Gzb{i}", [128, 2, 512], BF16) for i in range(2)]
            tmd = [[sb(st, f"Gtm{d_}{i}", [128, 512]) for i in range(4)] for d_ in range(2)]
            hT = [sb(st, f"GhT{i}", [128, 32, 128], BF16) for i in range(2)]
            ysb = [sb(st, f"Gys{i}", [128, 4, 128]) for i in range(2)]
            pg_ = [ps(st, f"pgf{i}", [128, 512]) for i in range(6)]
            pgb = [ps(st, f"pgb{i}", [128, 8, 128], BF16) for i in range(2)]
            cnt = {"f": 0, "b": 0, "z": 0, "u": 0}

            def PGF():
                i = cnt["f"] % 6
                cnt["f"] += 1
                return pg_[i], ("pgf", i)

            for d in range(2):
                for i in range(2):
                    S.op("pool", lambda e, d=d, i=i: e.memset(hb[d][i][:], 0.0), w=[("hb", d, i)])
            nstep = [0, 0]

            def s5_chunk(c, d):
                lo, hi = c * 128, (c + 1) * 128
                iu = d
                tm = tmd[d]
                S.dma("sp", uf[iu][:], p_ssT[:, lo:hi].rearrange("(q p) t -> p q t", p=128), r=[("p_ssT", q) for q in range(4)], w=[("uf", iu)])
                yield
                cp_("pool", ub[iu][:], uf[iu][:], r=[("uf", iu)], w=[("ub", iu)])
                yield
                hi_ = nstep[d] % 2
                nstep[d] += 1
                hcur, hprev = hb[d][hi_], hb[d][1 - hi_]
                kcur, kprev = ("hb", d, hi_), ("hb", d, 1 - hi_)
                for q in range(4):
                    cs_ = slice(q * 512, (q + 1) * 512)
                    pbr, pbrk = pg_[3 * d], ("pgf", 3 * d)
                    pbi, pbik = pg_[3 * d + 1], ("pgf", 3 * d + 1)
                    mm_(pbr[:], ub[iu][:, q, :], Bp[d][:, q, 0:512], True, True, r=[("ub", iu), ("Bp", d)], w=[pbrk])
                    mm_(pbi[:], ub[iu][:, q, :], Bp[d][:, q, 512:1024], True, True, r=[("ub", iu), ("Bp", d)], w=[pbik])
                    iz = d
                    z = zb[iz]
                    zk = ("zb", iz)
                    tt_("dve", tm[0][:], pbr[:], En[d][0][:, cs_], ALU.mult, r=[pbrk, ("En", d)], w=[("tm", d, 0)])
                    yield
                    tt_("dve", tm[1][:], pbi[:], En[d][1][:, cs_], ALU.mult, r=[pbik, ("En", d)], w=[("tm", d, 1)])
                    yield
                    tt_("dve", z[:, 0, :], tm[0][:], tm[1][:], ALU.add, r=[("tm", d, 0), ("tm", d, 1)], w=[zk])
                    yield
                    tt_("dve", tm[2][:], pbi[:], En[d][0][:, cs_], ALU.mult, r=[pbik, ("En", d)], w=[("tm", d, 2)])
                    yield
                    tt_("dve", tm[3][:], pbr[:], En[d][1][:, cs_], ALU.mult, r=[pbrk, ("En", d)], w=[("tm", d, 3)])
                    yield
                    tt_("dve", z[:, 1, :], tm[2][:], tm[3][:], ALU.subtract, r=[("tm", d, 2), ("tm", d, 3)], w=[zk])
                    yield
                    pcr, pcrk = pg_[3 * d + 2], ("pgf", 3 * d + 2)
                    pci, pcik = pg_[3 * d], ("pgf", 3 * d)
                    mm_(pcr[:], trib[:, d, :], z[:, 0, :], True, False, r=["trib", zk], w=[pcrk])
                    mm_(pcr[:], selb[:, d, :], hprev[:, q, 0, :], False, True, r=["selb", kprev], w=[pcrk])
                    mm_(pci[:], trib[:, d, :], z[:, 1, :], True, False, r=["trib", zk], w=[pcik])
                    mm_(pci[:], selb[:, 2 + d, :], hprev[:, q, 1, :], False, True, r=["selb", kprev], w=[pcik])
                    tt_("dve", tm[0][:], pcr[:], Ep[d][0][:, cs_], ALU.mult, r=[pcrk, ("Ep", d)], w=[("tm", d, 0)])
                    yield
                    tt_("dve", tm[1][:], pci[:], Ep[d][1][:, cs_], ALU.mult, r=[pcik, ("Ep", d)], w=[("tm", d, 1)])
                    yield
                    tt_("dve", hcur[:, q, 0, :], tm[0][:], tm[1][:], ALU.subtract, r=[("tm", d, 0), ("tm", d, 1)], w=[kcur])
                    yield
                    tt_("dve", tm[2][:], pcr[:], Ep[d][1][:, cs_], ALU.mult, r=[pcrk, ("Ep", d)], w=[("tm", d, 2)])
                    yield
                    tt_("dve", tm[3][:], pci[:], Ep[d][0][:, cs_], ALU.mult, r=[pcik, ("Ep", d)], w=[("tm", d, 3)])
                    yield
                    S.op("dve", lambda e, q=q: e.scalar_tensor_tensor(out=hcur[:, q, 1, :], in0=tm[2][:], scalar=-1.0, in1=tm[3][:],
                                                                   op0=ALU.mult, op1=ALU.subtract), r=[("tm", d, 2), ("tm", d, 3)], w=[kcur])
                    yield
                it = d
                for part in range(2):
                    for q2 in range(2):
                        ib = cnt["b"] % 2
                        cnt["b"] += 1
                        for q1 in range(2):
                            q = q2 * 2 + q1
                            for sb_ in range(4):
                                tr_(pgb[ib][:, q1 * 4 + sb_, :], hcur[:, q, part, sb_ * 128:(sb_ + 1) * 128], identb[:],
                                    r=[kcur, "identb"], w=[("pgb", ib)])
                        k0 = part * 16 + q2 * 8
                        act_(hT[it][:, k0:k0 + 8, :], pgb[ib][:], AF.Copy, r=[("pgb", ib)], w=[("hT", it)])
                        yield
                py, pyk = pg_[3 * d + 1], ("pgf", 3 * d + 1)
                for ct in range(4):
                    kqs = [part * 16 + 4 * ct + sb_ for part in range(2) for sb_ in range(4)]
                    for n_, kq in enumerate(kqs):
                        mm_(py[:, ct * 128:(ct + 1) * 128], Cb[:, kq, :], hT[it][:, kq, :], n_ == 0, n_ == 7, r=["Cb", ("hT", it)], w=[pyk])
                act_(ysb[it][:], py[:].rearrange("p (c t) -> p c t", t=128), AF.Copy, r=[pyk], w=[("ysb", it)])
                yield
                S.dma("act", ysT[d][:, lo:hi].rearrange("(c p) t -> p c t", p=128), ysb[it][:], r=[("ysb", it)], w=[(f"ysT{d}", c)])
                yield

            ord0 = list(range(NT))
            ord1 = [1, 0] + list(range(NT - 1, 1, -1))
            for i in range(NT):
                run_interleaved([s5_chunk(ord0[i], 0), s5_chunk(ord1[i], 1)])
        S.barrier()
        with ExitStack() as st:
            vec = sb(st, "Gvec", [128, 2, 4])
            S.dma("sp", vec[:], ssvec_in[l], w=["Gvec"])
            gw = sb(st, "Ggw", [128, 4, 512], BF16)
            S.dma("pool", gw[:], gluw_in[l].rearrange("(kc p) n -> p kc n", p=128), w=["Ggw"])
            Y0 = [sb(st, f"GY0{i}", [128, 4, 512]) for i in range(2)]
            Y1 = [sb(st, f"GY1{i}", [128, 4, 512]) for i in range(2)]
            Uu = [sb(st, f"GU{i}", [128, 4, 512]) for i in range(2)]
            yy = sb(st, "Gyy", [128, 4, 512])
            y3 = sb(st, "Gy3", [128, 4, 512])
            y2b = sb(st, "Gy2b", [128, 4, 512], BF16)
            sg_ = sb(st, "Gsg", [128, 512])
            yo = [sb(st, f"Gyo{i}", [128, 4, 512], BF16) for i in range(2)]
            pz = [ps(st, f"pz{i}", [128, 512]) for i in range(2)]
            kz = 0
            for gi, t0 in enumerate(range(0, N, 512)):
                tw = min(512, N - t0)
                i = gi % 2
                tks = list(range(t0 // 128, (t0 + tw) // 128))
                S.dma("sp", Y0[i][:, :, 0:tw], ysT[0][:, t0:t0 + tw].rearrange("(c p) t -> p c t", p=128), r=[("ysT0", t) for t in tks], w=[("GY0", i)])
                S.dma("sp", Y1[i][:, :, 0:tw], ysT[1][:, t0:t0 + tw].rearrange("(c p) t -> p c t", p=128), r=[("ysT1", t) for t in tks], w=[("GY1", i)])
                S.dma("sp", Uu[i][:, :, 0:tw], p_ssT[:, t0:t0 + tw].rearrange("(c p) t -> p c t", p=128), r=[("p_ssT", q) for q in range(4)], w=[("GU", i)])
                tt_("pool", yy[:, :, 0:tw], Y0[i][:, :, 0:tw], Y1[i][:, :, 0:tw], ALU.add, r=[("GY0", i), ("GY1", i)], w=["Gyy"])
                for c in range(4):
                    S.op("dve", lambda e, c=c, i=i, tw=tw: e.scalar_tensor_tensor(out=yy[:, c, 0:tw], in0=Uu[i][:, c, 0:tw], scalar=vec[:, 0, c:c + 1],
                                                                            in1=yy[:, c, 0:tw], op0=ALU.mult, op1=ALU.add),
                         r=[("GU", i), "Gvec", "Gyy"], w=["Gyy"])
                tt_("pool", y3[:, :, 0:tw], yy[:, :, 0:tw], yy[:, :, 0:tw], ALU.mult, r=["Gyy"], w=["Gy3"])
                ts_("pool", y3[:, :, 0:tw], y3[:, :, 0:tw], 0.044715, 1.0, ALU.mult, ALU.add, r=["Gy3"], w=["Gy3"])
                tt_("pool", y3[:, :, 0:tw], y3[:, :, 0:tw], yy[:, :, 0:tw], ALU.mult, r=["Gy3", "Gyy"], w=["Gy3"])
                act_(y3[:, :, 0:tw], y3[:, :, 0:tw], AF.Tanh, r=["Gy3"], w=["Gy3"], scale=0.7978845608028654)
                ts_("pool", y3[:, :, 0:tw], y3[:, :, 0:tw], 1.0, 0.5, ALU.add, ALU.mult, r=["Gy3"], w=["Gy3"])
                tt_("pool", yy[:, :, 0:tw], yy[:, :, 0:tw], y3[:, :, 0:tw], ALU.mult, r=["Gyy", "Gy3"], w=["Gyy"])
                cp_("pool", y2b[:, :, 0:tw], yy[:, :, 0:tw], r=["Gyy"], w=["Gy2b"])
                for c in range(4):
                    p_ = pz[kz % 2]
                    pk_ = ("pz", kz % 2)
                    kz += 1
                    for kc in range(4):
                        mm_(p_[:, 0:tw], gw[:, kc, c * 128:(c + 1) * 128], y2b[:, kc, 0:tw], kc == 0, kc == 3, r=["Ggw", "Gy2b"], w=[pk_])
                    act_(sg_[:, 0:tw], p_[:, 0:tw], AF.Sigmoid, r=[pk_], w=["Gsg"], bias=vec[:, 1, c:c + 1])
                    tt_("dve", yo[i][:, c, 0:tw], yy[:, c, 0:tw], sg_[:, 0:tw], ALU.mult, r=["Gyy", "Gsg"], w=[("Gyo", i)])
                S.dma("sp", ycT[:, t0:t0 + tw].rearrange("(c p) t -> p c t", p=128), yo[i][:, :, 0:tw], r=[("Gyo", i)], w=[("ycT", t) for t in tks])
        S.barrier()
        if stop_after == "G":
            break

        with ExitStack() as st:
            KT = sb(st, "KT", [128, 4, N], BF16)
            S.op("pool", lambda e: e.memset(KT[64:128], 0.0), w=[("KT", t_) for t_ in range(NT)])
            Vt = sb(st, "Vt", [128, NT, 4, 65], BF16)
            gain = sb(st, "gain", [128, 1280])
            S.dma("sp", gain[:], qkgain_in[l], w=["gain"])
            S.op("pool", lambda e: e.memset(Vt[:], 1.0), w=["Vt"])
            xq = [sb(st, f"xq{i}", [128, 1280]) for i in range(2)]
            xv = [sb(st, f"xv{i}", [128, 256]) for i in range(2)]
            sq = sb(st, "sq", [128, 1280])
            ssq = sb(st, "ssq", [128, 20])
            xn = sb(st, "xn", [128, 1280])
            t1 = sb(st, "t1", [128, 1280])
            t2 = sb(st, "t2", [128, 1280])
            xb = sb(st, "xb", [128, 1280], BF16)
            cs = [sb(st, f"cs{i}", [128, 2, 64]) for i in range(2)]
            trp = [ps(st, f"trp{i}", [64, 8, 128], BF16) for i in range(1)]

            def qk_prep(tt, src, c0, nh, i):
                w_ = nh * 64
                v3 = lambda t: t[:, 0:w_].rearrange("p (h d) -> p h d", d=64)
                tt_("pool", sq[:, 0:w_], src[:, 0:w_], src[:, 0:w_], ALU.mult, r=[("xq", i)], w=["sq"])
                red_("dve", ssq[:, 0:nh], v3(sq), r=["sq"], w=["ssq"])
                rstd_(ssq[:, 0:nh], ssq[:, 0:nh], 1.0 / 64, RMS_EPS, r=["ssq"], w=["ssq"])
                tt_("pool", v3(xn), v3(src), ssq[:, 0:nh].unsqueeze(2).broadcast_to([128, nh, 64]), ALU.mult,
                    r=[("xq", i), "ssq"], w=["xn"])
                if tt < 2:
                    tt_("pool", xb[:, 0:w_], xn[:, 0:w_], gain[:, c0:c0 + w_], ALU.mult, r=["xn", "gain"], w=["xb"])
                    return
                tt_("pool", xn[:, 0:w_], xn[:, 0:w_], gain[:, c0:c0 + w_], ALU.mult, r=["xn", "gain"], w=["xn"])
                cst = cs[tt % 2]
                ck = ("cs", tt % 2)
                cosb = cst[:, 0, :].unsqueeze(1).broadcast_to([128, nh, 64])
                tt_("pool", v3(t1), v3(xn), cosb, ALU.mult, r=["xn", ck], w=["t1"])
                v4 = lambda t: t[:, 0:w_].rearrange("p (h a b i) -> p (h a) b i", a=2, b=2, i=16)
                sn4 = cst[:, 1, :].rearrange("p (a b i) -> p a b i", a=2, b=2)
                for hb in range(2):
                    snb = sn4[:, :, hb, :].unsqueeze(1).broadcast_to([128, nh, 2, 16])
                    o4 = t2[:, 0:w_].rearrange("p (h a b i) -> p h a b i", a=2, b=2, i=16)[:, :, :, hb, :]
                    i4 = xn[:, 0:w_].rearrange("p (h a b i) -> p h a b i", a=2, b=2, i=16)[:, :, :, 1 - hb, :]
                    tt_("dve", o4, i4, snb, ALU.mult, r=["xn", ck], w=["t2"])
                tt_("pool", xb[:, 0:w_], t1[:, 0:w_], t2[:, 0:w_], ALU.add, r=["t1", "t2"], w=["xb"])

            def load_cs(tt):
                if tt >= 2:
                    r0 = (tt - 2) * 128
                    S.dma("sp", cs[tt % 2][:, 0, :], rope_cos[r0:r0 + 128, :], w=[("cs", tt % 2)])
                    S.dma("sp", cs[tt % 2][:, 1, :], rope_sin[r0:r0 + 128, :], w=[("cs", tt % 2)])

            for tt in range(NT):
                i = tt % 2
                S.dma("sp", xq[i][:, 0:256], p_tm[tt * 128:(tt + 1) * 128, C_AT + 1024:C_AT + 1280], r=[("p_tm", tt)], w=[("xq", i)])
                S.dma("sp", xv[i][:], p_tm[tt * 128:(tt + 1) * 128, C_AT + 1280:C_AT + 1536], r=[("p_tm", tt)], w=[("xv", i)])
                load_cs(tt)
                qk_prep(tt, xq[i], 1024, 4, i)
                for g in range(4):
                    tr_(trp[0][:, g, :], xb[:, g * 64:(g + 1) * 64], identb[:], r=["xb", "identb"], w=[("trp", 0)])
                act_(KT[0:64, :, tt * 128:(tt + 1) * 128], trp[0][:, 0:4, :], AF.Copy, r=[("trp", 0)], w=[("KT", tt)])
                cp_("pool", Vt[:, tt, :, 0:64], xv[i][:].rearrange("p (g d) -> p g d", d=64), r=[("xv", i)], w=[("Vt", tt)])
            QT = [sb(st, f"QT{i}", [128, 16, 128], BF16) for i in range(2)]
            for i_ in range(2):
                S.op("pool", lambda e, i_=i_: e.memset(QT[i_][64:128], 0.0), w=[("QT", i_)])
            NB = 4
            Pt = [sb(st, f"Pt{i}", [128, 512], BF16) for i in range(NB)]
            sps = [ps(st, f"sps{i}", [128, 512]) for i in range(NB)]
            ops_ = [ps(st, f"ops{i}", [65, 512]) for i in range(2)]
            bcp = ps(st, "bcp", [64, 512])
            rd = sb(st, "rd", [65, 512])
            osb = sb(st, "osb", [64, 512])
            yb = [sb(st, f"yb{i}", [64, 512], BF16) for i in range(2)]
            kq = 0
            ko = 0
            pending = []
            for tt in range(NT):
                i = tt % 2
                S.dma("sp", xq[i][:, 0:1024], p_tm[tt * 128:(tt + 1) * 128, C_AT:C_AT + 1024], r=[("p_tm", tt)], w=[("xq", i)])
                load_cs(tt)
                qk_prep(tt, xq[i], 0, 16, i)
                for hh in range(2):
                    for h in range(8):
                        hd = hh * 8 + h
                        tr_(trp[0][:, h, :], xb[:, hd * 64:(hd + 1) * 64], identb[:], r=["xb", "identb"], w=[("trp", 0)])
                    act_(QT[i][0:64, hh * 8:(hh + 1) * 8, :], trp[0][:], AF.Copy, r=[("trp", 0)], w=[("QT", i)])
                kts = [0, 1] if tt < 2 else list(range(NT))
                for g in range(4):
                    o_ = ops_[ko % 2]
                    okey = ("ops", ko % 2)
                    ybt = yb[ko % 2]
                    ykey = ("yb", ko % 2)
                    ko += 1
                    PIPE = 3
                    nk = len(kts)
                    base = kq
                    kq += nk
                    for n_ in range(nk + PIPE):
                        if n_ == min(6, nk - 1) and pending:
                            pending.pop(0)()
                        if n_ < nk:
                            kt = kts[n_]
                            j = (base + n_) % NB
                            mm_(sps[j][:], KT[:, g, kt * 128:(kt + 1) * 128], QT[i][:, 4 * g:4 * g + 4, :].rearrange("p h t -> p (h t)"),
                                True, True, r=[("KT", kt), ("QT", i)], w=[("sps", j)])
                            act_(Pt[j][:], sps[j][:], AF.Exp, r=[("sps", j)], w=[("Pt", j)], scale=0.125)
                        m_ = n_ - PIPE
                        if m_ >= 0:
                            kt = kts[m_]
                            j = (base + m_) % NB
                            mm_(o_[:], Vt[:, kt, g, :], Pt[j][:], m_ == 0, m_ == nk - 1, r=[("Vt", kt), ("Pt", j)], w=[okey])
                    def epilogue(o_=o_, okey=okey, ybt=ybt, ykey=ykey, g=g, tt=tt):
                        S.op("dve", lambda e: e.reciprocal(out=rd[64:65, :], in_=o_[64:65, :]), r=[okey], w=["rd"])
                        mm_(bcp[:], ones[64:65, 0:64], rd[64:65, :], True, True, r=["ones", "rd"], w=["bcp"])
                        cp_("pool" if False else "dve", osb[:], o_[0:64, :], r=[okey], w=["osb"])
                        tt_("dve", ybt[:], osb[:], bcp[:], ALU.mult, r=["osb", "bcp"], w=[ykey])
                        S.dma("sp", ybT[4 * g:4 * g + 4, :, tt * 128:(tt + 1) * 128].rearrange("h d t -> d h t"),
                              ybt[:].rearrange("p (h t) -> p h t", t=128), r=[ykey], w=[("ybT", tt)])
                    pending.append(epilogue)
            while pending:
                pending.pop(0)()
        S.barrier()
        if stop_after == "F":
            break

        stHI = ExitStack()
        gtb = sb(stHI, "gtb", [128, 2, 2, D])
        with ExitStack() as st:
            dg = [sb(st, f"dg{i}", [128, 128]) for i in range(2)]
            pgt = [ps(st, f"pgt{i}", [128, 512]) for i in range(2)]
            kk_ = 0
            for wh, c0 in ((0, 16), (1, 40)):
                for g in range(2):
                    for hf in range(2):
                        p_ = pgt[kk_ % 2]
                        pk_ = ("pgt", kk_ % 2)
                        kk_ += 1
                        for j in range(4):
                            kc = hf * 4 + j
                            i = kc % 2
                            ts_("dve", dg[i][:], ident[:], modT[:, l, c0 + kc, g:g + 1], None, ALU.mult, None, r=["ident", "modT"], w=[("dg", i)])
                            mm_(p_[:, j * 128:(j + 1) * 128], ones[:], dg[i][:], True, True, r=["ones", ("dg", i)], w=[pk_])
                        act_(gtb[:, wh, g, hf * 512:(hf + 1) * 512], p_[:], AF.Copy, r=[pk_], w=["gtb"])
        S.barrier()

        def ln_tail(st_tiles, h, lng, lnb, outt, hk, ok):
            hsq, st4 = st_tiles
            red_("dve", st4[:, 0:1], h[:], r=[hk], w=["st4"])
            tt_("pool", hsq[:], h[:], h[:], ALU.mult, r=[hk], w=["hsq"])
            red_("dve", st4[:, 1:2], hsq[:], r=["hsq"], w=["st4"])
            ts_("dve", st4[:, 0:1], st4[:, 0:1], 1.0 / D, None, ALU.mult, None, r=["st4"], w=["st4"])
            tt_("dve", st4[:, 2:3], st4[:, 0:1], st4[:, 0:1], ALU.mult, r=["st4"], w=["st4"])
            S.op("dve", lambda e: e.scalar_tensor_tensor(out=st4[:, 3:4], in0=st4[:, 1:2], scalar=1.0 / D, in1=st4[:, 2:3],
                                                       op0=ALU.mult, op1=ALU.subtract), r=["st4"], w=["st4"])
            rstd_(st4[:, 3:4], st4[:, 3:4], 1.0, LN_EPS, r=["st4"], w=["st4"])
            ts_("dve", hsq[:], h[:], st4[:, 0:1], st4[:, 3:4], ALU.subtract, ALU.mult, r=[hk, "st4"], w=["hsq"])
            tt_("pool", hsq[:], hsq[:], lng, ALU.mult, r=["hsq", "lnw"], w=["hsq"])
            tt_("pool", outt[:], hsq[:], lnb, ALU.add, r=["hsq", "lnw"], w=[ok])

        with ExitStack() as st:
            wa = sb(st, "Hwa", [128, 4, D], BF16)
            wbb = sb(st, "Hwb", [128, 8, D], BF16)
            wc = sb(st, "Hwc", [128, 4, D], BF16)
            wo = sb(st, "Hwo", [128, 8, D], BF16)
            lnw = sb(st, "Hln", [128, 2, D])
            S.dma("pool", wa[:], proja_in[l].rearrange("(kc p) n -> p kc n", p=128), w=["Hwa"])
            S.dma("pool", wbb[:], projb_in[l].rearrange("(kc p) n -> p kc n", p=128), w=["Hwb"])
            S.dma("pool", wc[:], projc_in[l].rearrange("(kc p) n -> p kc n", p=128), w=["Hwc"])
            S.dma("pool", wo[:], wout_in[l].rearrange("(kc p) n -> p kc n", p=128), w=["Hwo"])
            S.dma("sp", lnw[:], ln_in[l, 0:2].rearrange("k p c -> p k c"), w=["lnw"])
            TW = 256
            ya_t = [sb(st, f"Hya{i}", [128, 4, TW], BF16) for i in range(2)]
            yb_t = [sb(st, f"Hyb{i}", [128, 8, TW], BF16) for i in range(2)]
            yc_t = [sb(st, f"Hyc{i}", [128, 4, TW], BF16) for i in range(2)]
            g_t = [sb(st, f"Hg{i}", [128, 24, TW], BF16) for i in range(2)]
            mg = sb(st, "Hmg", [128, TW])
            t_ = sb(st, "Ht", [128, TW])
            mT = sb(st, "HmT", [128, 8, TW], BF16)
            xt_ = [sb(st, f"Hx{i}", [128, D]) for i in range(2)]
            hh = sb(st, "Hh", [128, D])
            hsq = sb(st, "Hhsq", [128, D])
            st4 = sb(st, "Hst4", [128, 4])
            xo = [sb(st, f"Hxo{i}", [128, D]) for i in range(2)]
            x2T = [sb(st, f"Hx2T{i}", [128, 8, 128], BF16) for i in range(2)]
            pH = [ps(st, f"pH{i}", [128, 512]) for i in range(6)]
            pT2 = [ps(st, f"pT2{i}", [128, 4, 128]) for i in range(2)]
            kp = [0]

            def PH():
                i = kp[0] % 6
                kp[0] += 1
                return pH[i], ("pH", i)

            for gi, t0 in enumerate(range(0, N, TW)):
                i = gi % 2
                tks = [t0 // 128, t0 // 128 + 1]
                S.dma("sp", ya_t[i][:], yaT[:, t0:t0 + TW].rearrange("(kc p) t -> p kc t", p=128), r=[("yaT", t) for t in tks], w=[("Hya", i)])
                S.dma("sp", yb_t[i][:], ybT.rearrange("h d t -> (h d) t")[:, t0:t0 + TW].rearrange("(kc p) t -> p kc t", p=128),
                      r=[("ybT", t) for t in tks], w=[("Hyb", i)])
                S.dma("sp", yc_t[i][:], ycT[:, t0:t0 + TW].rearrange("(kc p) t -> p kc t", p=128), r=[("ycT", t) for t in tks], w=[("Hyc", i)])
                S.dma("sp", g_t[i][:], gT[:, t0:t0 + TW].rearrange("(kc p) t -> p kc t", p=128), r=[("gT", c) for c in range(24)], w=[("Hg", i)])
                for m_ in range(8):
                    ms = slice(m_ * 128, (m_ + 1) * 128)
                    pa, pak = PH()
                    for kc in range(4):
                        mm_(pa[:, 0:TW], wa[:, kc, ms], ya_t[i][:, kc, :], kc == 0, kc == 3, r=["Hwa", ("Hya", i)], w=[pak])
                    pb_, pbk = PH()
                    for kc in range(8):
                        mm_(pb_[:, 0:TW], wbb[:, kc, ms], yb_t[i][:, kc, :], kc == 0, kc == 7, r=["Hwb", ("Hyb", i)], w=[pbk])
                    pc_, pck = PH()
                    for kc in range(4):
                        mm_(pc_[:, 0:TW], wc[:, kc, ms], yc_t[i][:, kc, :], kc == 0, kc == 3, r=["Hwc", ("Hyc", i)], w=[pck])
                    tt_("dve", mg[:], pa[:, 0:TW], g_t[i][:, m_, :], ALU.mult, r=[pak, ("Hg", i)], w=["Hmg"])
                    tt_("dve", t_[:], pb_[:, 0:TW], g_t[i][:, 8 + m_, :], ALU.mult, r=[pbk, ("Hg", i)], w=["Ht"])
                    tt_("pool", mg[:], mg[:], t_[:], ALU.add, r=["Hmg", "Ht"], w=["Hmg"])
                    tt_("dve", t_[:], pc_[:, 0:TW], g_t[i][:, 16 + m_, :], ALU.mult, r=[pck, ("Hg", i)], w=["Ht"])
                    tt_("pool", mT[:, m_, :], mg[:], t_[:], ALU.add, r=["Hmg", "Ht"], w=["HmT"])
                for sub in range(2):
                    tt = t0 // 128 + sub
                    g = 1 if tt < 2 else 0
                    j = tt % 2
                    ts_l = slice(sub * 128, (sub + 1) * 128)
                    S.dma("sp", xt_[j][:], xs[tt * 128:(tt + 1) * 128, :], r=[("xs", tt)], w=[("Hx", j)])
                    for hf in range(2):
                        p_, pk_ = PH()
                        for kc in range(8):
                            mm_(p_[:], mT[:, kc, ts_l], wo[:, kc, hf * 512:(hf + 1) * 512], kc == 0, kc == 7, r=["HmT", "Hwo"], w=[pk_])
                        tt_("dve", hh[:, hf * 512:(hf + 1) * 512], p_[:], gtb[:, 0, g, hf * 512:(hf + 1) * 512], ALU.mult, r=[pk_, "gtb"], w=["Hh"])
                    S.op("dve", lambda e, j=j: e.scalar_tensor_tensor(out=hh[:], in0=xt_[j][:], scalar=ALPHA, in1=hh[:], op0=ALU.mult, op1=ALU.add),
                         r=[("Hx", j), "Hh"], w=["Hh"])
                    ln_tail((hsq, st4), hh, lnw[:, 0, :], lnw[:, 1, :], xo[j], "Hh", ("Hxo", j))
                    S.dma("act", xmid[tt * 128:(tt + 1) * 128, :], xo[j][:], r=[("Hxo", j)], w=[("xmid", tt)])
                    for hf in range(2):
                        for q in range(4):
                            kc = hf * 4 + q
                            tr_(pT2[hf][:, q, :], xo[j][:, kc * 128:(kc + 1) * 128], ident[:], r=[("Hxo", j), "ident"], w=[("pT2", hf)])
                        for q in range(4):
                            kc = hf * 4 + q
                            act_(x2T[j][:, kc, :], pT2[hf][:, q, :], AF.Identity, r=[("pT2", hf), "modT"], w=[("Hx2T", j)],
                                 scale=modT[:, l, 32 + kc, g:g + 1], bias=modT[:, l, 24 + kc, g:g + 1])
                    S.dma("act", xm2T[:, tt * 128:(tt + 1) * 128].rearrange("(kc p) t -> p kc t", p=128), x2T[j][:], r=[("Hx2T", j)], w=[("xm2T", tt)])
        S.barrier()
        if stop_after == "H":
            stHI.close()
            break

        with ExitStack() as st:
            w1 = sb(st, "Iw1", [128, 8, 4 * D], BF16)
            for q in range(4):
                S.dma("pool", w1[:, :, q * D:(q + 1) * D], w1_in[l, :, q * D:(q + 1) * D].rearrange("(kc p) n -> p kc n", p=128), w=["Iw1"])
            xi = [sb(st, f"Ixi{i}", [128, 8, 512], BF16) for i in range(2)]
            rl = [sb(st, f"Irl{i}", [128, 512]) for i in range(2)]
            ho = [sb(st, f"Iho{i}", [128, 8, 512], BF16) for i in range(2)]
            pI = [ps(st, f"pI{i}", [128, 512]) for i in range(4)]
            kp = 0
            ko = 0
            for gi, t0 in enumerate(range(0, N, 512)):
                tw = min(512, N - t0)
                i = gi % 2
                tks = list(range(t0 // 128, (t0 + tw) // 128))
                S.dma("sp", xi[i][:, :, 0:tw], xm2T[:, t0:t0 + tw].rearrange("(kc p) t -> p kc t", p=128), r=[("xm2T", t) for t in tks], w=[("Ixi", i)])
                for c8 in range(4):
                    o_ = ho[ko % 2]
                    okey = ("Iho", ko % 2)
                    ko += 1
                    for c in range(8):
                        cb = c8 * 8 + c
                        p_ = pI[kp % 4]
                        pk_ = ("pI", kp % 4)
                        r_ = rl[kp % 2]
                        rk_ = ("Irl", kp % 2)
                        kp += 1
                        for kc in range(8):
                            mm_(p_[:, 0:tw], w1[:, kc, cb * 128:(cb + 1) * 128], xi[i][:, kc, 0:tw], kc == 0, kc == 7, r=["Iw1", ("Ixi", i)], w=[pk_])
                        act_(r_[:, 0:tw], p_[:, 0:tw], AF.Relu, r=[pk_], w=[rk_])
                        tt_("pool" if c % 2 else "dve", o_[:, c, 0:tw], r_[:, 0:tw], r_[:, 0:tw], ALU.mult, r=[rk_], w=[okey])
                    S.dma("sp", h1T[c8 * 1024:(c8 + 1) * 1024, t0:t0 + tw].rearrange("(c p) t -> p c t", p=128), o_[:, :, 0:tw], r=[okey],
                          w=[("h1T", t, c8) for t in tks])
        S.barrier()
        with ExitStack() as st:
            w2 = sb(st, "Iw2", [128, 32, D], BF16)
            for q in range(4):
                S.dma("pool", w2[:, q * 8:(q + 1) * 8, :], w2_in[l, q * 1024:(q + 1) * 1024, :].rearrange("(kc p) n -> p kc n", p=128), w=["Iw2"])
            lnw = sb(st, "Iln", [128, 2, D])
            S.dma("sp", lnw[:], ln_in[l, 2:4].rearrange("k p c -> p k c"), w=["lnw"])
            h1 = [sb(st, f"Ih1{i}", [128, 32, 128], BF16) for i in range(2)]
            xm_ = [sb(st, f"Ixm{i}", [128, D]) for i in range(2)]
            hh = sb(st, "Ihh", [128, D])
            hsq = sb(st, "Ihsq", [128, D])
            st4 = sb(st, "Ist4", [128, 4])
            xo = [sb(st, f"Ixo{i}", [128, D]) for i in range(2)]
            pJ = [ps(st, f"pJ{i}", [128, 512]) for i in range(4)]
            kp = 0
            for tt in range(NT):
                i = tt % 2
                g = 1 if tt < 2 else 0
                lo, hi = tt * 128, (tt + 1) * 128
                S.dma("sp", h1[i][:], h1T[:, lo:hi].rearrange("(c p) t -> p c t", p=128), r=[("h1T", tt, c8) for c8 in range(4)], w=[("Ih1", i)])
                S.dma("sp", xm_[i][:], xmid[lo:hi, :], r=[("xmid", tt)], w=[("Ixm", i)])
                for hf in range(2):
                    p_ = pJ[kp % 4]
                    pk_ = ("pJ", kp % 4)
                    kp += 1
                    for kc in range(32):
                        mm_(p_[:], h1[i][:, kc, :], w2[:, kc, hf * 512:(hf + 1) * 512], kc == 0, kc == 31, r=[("Ih1", i), "Iw2"], w=[pk_])
                    tt_("dve", hh[:, hf * 512:(hf + 1) * 512], p_[:], gtb[:, 1, g, hf * 512:(hf + 1) * 512], ALU.mult, r=[pk_, "gtb"], w=["Ihh"])
                S.op("dve", lambda e, i=i: e.scalar_tensor_tensor(out=hh[:], in0=xm_[i][:], scalar=ALPHA, in1=hh[:], op0=ALU.mult, op1=ALU.add),
                     r=[("Ixm", i), "Ihh"], w=["Ihh"])
                ln_tail((hsq, st4), hh, lnw[:, 0, :], lnw[:, 1, :], xo[i], "Ihh", ("Ixo", i))
                S.dma("act", xs[lo:hi, :], xo[i][:], r=[("Ixo", i)], w=[("xs", tt)])
                if l == n_layers - 1 and tt >= 2:
                    S.dma("act", out[lo - NCTX:hi - NCTX, :], xo[i][:], r=[("Ixo", i)], w=[("out", tt)])
        S.barrier()
        stHI.close()
        if stop_after == "I":
            break

    S.barrier()
    top.close()
    return nc, S


def _prep_inputs(inp, b):
    f = lambda a: np.ascontiguousarray(a, dtype=np.float32)
    m = {}
    m["x_in"] = f(np.concatenate([inp["ctx"][b], inp["x"][b]], axis=0))
    cT = np.stack([inp["c"][b].reshape(8, 128).T, inp["c_ctx"].reshape(8, 128).T], axis=-1)
    m["cT"] = f(cT)
    m["ident"] = np.eye(128, dtype=np.float32)
    m["mod_w"] = f(inp["mod_w"])
    m["mod_bT"] = f(inp["mod_b"].reshape(DEPTH, 48, 128).transpose(0, 2, 1))
    m["w_in"] = f(inp["w_in"])
    m["qkgain"] = f(np.broadcast_to(np.concatenate([np.tile(inp["attn_q_gain"], (1, 16)), np.tile(inp["attn_k_gain"], (1, 4))],
                                                   axis=1)[:, None, :], (DEPTH, 128, 1280)))
    rows = NLAT // 64
    row = np.repeat(np.arange(rows, dtype=np.float32), 64)
    col = np.tile(np.arange(64, dtype=np.float32), rows)
    inv = (np.float32(10000.0) ** (-np.arange(16, dtype=np.float32) / np.float32(16))).astype(np.float32)
    ang = np.stack([row, col], axis=-1)[:, :, None] * inv
    ang = np.broadcast_to(ang[:, :, None, :], (NLAT, 2, 2, 16)).reshape(NLAT, 64)
    sgn = np.broadcast_to(np.array([-1.0, 1.0], np.float32)[None, None, :, None], (NLAT, 2, 2, 16)).reshape(NLAT, 64)
    m["rope_cos"] = f(np.cos(ang))
    m["rope_sin"] = f(np.sin(ang) * sgn)
    m["ones"] = np.ones((128, 128), np.float32)
    ii = np.arange(128)
    mS0 = (ii[:, None] < ii[None, :]).astype(np.float32)
    mS1 = (ii[:, None] > ii[None, :]).astype(np.float32)
    eye = np.eye(128, dtype=np.float32)
    m["masks"] = f(np.stack([mS0, mS1, mS0 + eye, mS1 + eye]))
    bc = lambda a: np.broadcast_to(a[:, None, :], (DEPTH, 128, a.shape[-1]))
    m["mu_bc"] = f(bc(inp["rwkv_mu"]))
    m["rwp_bc"] = f(np.stack([bc(inp[k]) for k in ("rwkv_k_k", "rwkv_k_a", "rwkv_r_k", "rwkv_gn_w", "rwkv_gn_b")], axis=1))
    m["lrb"] = f(np.concatenate([inp["rwkv_w0"], inp["rwkv_a0"]], axis=1))
    m["lrw"] = f(np.concatenate([inp["rwkv_w_up"], inp["rwkv_a_up"]], axis=2).transpose(0, 2, 1, 3))
    m["g_up"] = f(inp["rwkv_g_up"])
    rep = lambda a: np.broadcast_to(a[:, :, None, :], (DEPTH, 2, 128, 2048))
    m["ss_tab"] = f(np.stack([rep(inp["ssm_a_re"].reshape(DEPTH, 2, 2048)), rep(inp["ssm_a_im"].reshape(DEPTH, 2, 2048)),
                              rep(np.repeat(inp["ssm_log_dt"], 64, axis=-1))], axis=2))
    p_ = np.arange(128, dtype=np.float32)
    m["mcol"] = f(np.stack([p_ + 1, 128 - p_, -(p_ + 1), -(128 - p_)], axis=1))
    bblk = np.zeros((DEPTH, 2, 4, 8, 16, 8, 64), np.float32)
    for k_, nm in enumerate(("ssm_b_re", "ssm_b_im")):
        b5 = inp[nm].reshape(DEPTH, 4, 8, 64, 16)
        for g_ in range(8):
            bblk[:, k_, :, g_, :, g_, :] = b5[:, :, g_].transpose(0, 1, 3, 2)
    m["ss_bblk"] = f(bblk.reshape(DEPTH, 2, 4, 128, 512).transpose(0, 1, 3, 2, 4))
    cblk = np.zeros((DEPTH, 2, 16, 2, 64, 8, 16), np.float32)
    for k_, nm in enumerate(("ssm_c_re", "ssm_c_im")):
        c5 = inp[nm].reshape(DEPTH, 16, 2, 16, 64)
        for gp in range(16):
            for g2 in range(2):
                gl = (2 * gp + g2) % 8
                cblk[:, k_, gp, g2, :, gl, :] = c5[:, gp, g2].transpose(0, 2, 1)
    m["ss_cblk"] = f(cblk.reshape(DEPTH, 32, 128, 128).transpose(0, 2, 1, 3))
    m["ss_vec"] = f(np.stack([inp["ssm_d"].reshape(DEPTH, 4, 128).transpose(0, 2, 1),
                              inp["ssm_glu_b"].reshape(DEPTH, 4, 128).transpose(0, 2, 1)], axis=2))
    m["glu_w"] = f(inp["ssm_glu_w"])
    for k_ in ("proj_a", "proj_b", "proj_c", "w_out", "mlp_w1", "mlp_w2"):
        m[k_] = f(inp[k_])
    m["ln_bc"] = f(np.stack([bc(inp[k_]) for k_ in ("ln1_g", "ln1_b", "ln2_g", "ln2_b")], axis=1))
    selm = np.zeros((4, 128, 128), np.float32)
    selm[0, 127, :] = 1.0
    selm[1, 0, :] = 1.0
    selm[2, 127, :] = -1.0
    selm[3, 0, :] = -1.0
    m["sel"] = selm
    return m


def kernel(**inp):
    inp = {k: np.asarray(v) for k, v in inp.items()}
    nc, _ = build()
    in_maps = [_prep_inputs(inp, b) for b in range(8)]
    res = run_bass_kernel_spmd(nc, in_maps, core_ids=list(range(8)))
    return np.stack([r["out"] for r in res.results], axis=0).astype(np.float32)
```

```python
import numpy as np
from contextlib import ExitStack
import concourse.bass as bass
import concourse.mybir as mybir
from concourse.bass_utils import run_bass_kernel_spmd

F32 = mybir.dt.float32
BF16 = mybir.dt.bfloat16
AF = mybir.ActivationFunctionType
ALU = mybir.AluOpType
AX = mybir.AxisListType

D = 1024
DEPTH = 4
NCTX = 256
NLAT = 4096
N = NCTX + NLAT
NT = N // 128
INC = 6912
C_RW, C_AT, C_SS, C_G = 0, 1792, 3328, 3840
ALPHA = (2 * DEPTH) ** 0.25
LN_EPS, RMS_EPS, GN_EPS = 1e-5, 1e-6, 64e-5
PI = float(np.pi)
DSTAGE = 9
DSUB = 9


class _Res:
    __slots__ = ("lw", "rd")

    def __init__(self):
        self.lw = None
        self.rd = {}


class _Eng:
    def __init__(self, name, h, sem, selfsync):
        self.name, self.h, self.sem, self.selfsync = name, h, sem, selfsync
        self.cnt = 0
        self.seen = {}


class Sched:
    def __init__(self, nc, stack, ndma=(("sp", 8), ("pool", 6), ("act", 8))):
        self.nc = nc
        hs = {"pe": nc.tensor, "act": nc.scalar, "dve": nc.vector, "pool": nc.gpsimd, "sp": nc.sync}
        self.engs = {}
        for n, h in hs.items():
            sem = stack.enter_context(nc.semaphore("s_" + n))
            self.engs[n] = _Eng(n, h, sem, selfsync=(n in ("dve", "act", "pool")))
        self.dsems, self.dnext = {}, {}
        for q, k in ndma:
            self.dsems[q] = [[stack.enter_context(nc.semaphore(f"d_{q}{i}")), 0] for i in range(k)]
            self.dnext[q] = 0
        self.res = {}
        self.nins = 0

    def R(self, k):
        r = self.res.get(k)
        if r is None:
            r = self.res[k] = _Res()
        return r

    def _deps(self, r, w):
        deps = {}

        def need(ev):
            if ev is not None and deps.get(id(ev[0]), (None, 0))[1] < ev[1]:
                deps[id(ev[0])] = ev

        for k in r:
            need(self.R(k).lw)
        for k in w:
            R = self.R(k)
            need(R.lw)
            for ev in R.rd.values():
                need(ev)
        return deps

    def _wait(self, E, deps):
        for s, v in deps.values():
            if s is E.sem and not E.selfsync:
                continue
            if E.seen.get(id(s), 0) < v:
                E.seen[id(s)] = v
                E.h.wait_ge(s, v)

    def _mark(self, r, w, ev):
        for k in r:
            self.R(k).rd[id(ev[0])] = ev
        for k in w:
            R = self.R(k)
            R.lw = ev
            R.rd = {}

    def op(self, en, fn, r=(), w=()):
        E = self.engs[en]
        self._wait(E, self._deps(r, w))
        E.cnt += 1
        fn(E.h).then_inc(E.sem, 1)
        self.nins += 1
        self._mark(r, w, (E.sem, E.cnt))

    def dma(self, q, out, in_, r=(), w=(), **kw):
        E = self.engs[q]
        lst = self.dsems[q]
        ds = lst[self.dnext[q] % len(lst)]
        self.dnext[q] += 1
        deps = self._deps(r, w)
        if ds[1] > 0:
            deps[id(ds[0])] = (ds[0], ds[1])
        self._wait(E, deps)
        ds[1] += 16
        E.h.dma_start(out=out, in_=in_, **kw).then_inc(ds[0], 16)
        self.nins += 1
        self._mark(r, w, (ds[0], ds[1]))

    def barrier(self):
        evs = [(E.sem, E.cnt) for E in self.engs.values() if E.cnt]
        for lst in self.dsems.values():
            evs += [(s, v) for s, v in lst if v]
        for E in self.engs.values():
            for s, v in evs:
                if s is E.sem and not E.selfsync:
                    continue
                if E.seen.get(id(s), 0) < v:
                    E.seen[id(s)] = v
                    E.h.wait_ge(s, v)
        self.res = {}


def build(n_layers=DEPTH, stop_after=None, dbg=()):
    nc = bass.Bass("TRN2", target_bir_lowering=False)
    top = ExitStack()
    S = Sched(nc, top)
    dram_in = {}

    def din(name, shape, dt=F32):
        dram_in[name] = nc.dram_tensor(name, list(shape), dt, kind="ExternalInput").ap()
        return dram_in[name]

    def dscr(name, shape, dt=F32):
        kind = "ExternalOutput" if name in dbg else "Internal"
        return nc.dram_tensor(name, list(shape), dt, kind=kind).ap()

    x_in = din("x_in", [N, D])
    cT_in = din("cT", [128, 8, 2])
    ident_in = din("ident", [128, 128])
    mod_w = din("mod_w", [DEPTH, D, 6 * D])
    mod_bT = din("mod_bT", [DEPTH, 128, 48])
    w_in = din("w_in", [DEPTH, D, INC])
    out = nc.dram_tensor("out", [NLAT, D], F32, kind="ExternalOutput").ap()
    qkgain_in = din("qkgain", [DEPTH, 128, 1280])
    rope_cos = din("rope_cos", [NLAT, 64])
    rope_sin = din("rope_sin", [NLAT, 64])
    ones_in = din("ones", [128, 128])
    masks_in = din("masks", [4, 128, 128])
    mu_in = din("mu_bc", [DEPTH, 128, 1792])
    rwp_in = din("rwp_bc", [DEPTH, 5, 128, 512])
    lrb_in = din("lrb", [DEPTH, 4, 512])
    lrw_in = din("lrw", [DEPTH, 128, 2, 512])
    gup_in = din("g_up", [DEPTH, 128, 512])
    sstab_in = din("ss_tab", [DEPTH, 2, 3, 128, 2048])
    mcol_in = din("mcol", [128, 4])
    bblk_in = din("ss_bblk", [DEPTH, 2, 128, 4, 512])
    cblk_in = din("ss_cblk", [DEPTH, 128, 32, 128])
    ssvec_in = din("ss_vec", [DEPTH, 128, 2, 4])
    gluw_in = din("glu_w", [DEPTH, 512, 512])
    sel_in = din("sel", [4, 128, 128])
    proja_in = din("proj_a", [DEPTH, 512, D])
    projb_in = din("proj_b", [DEPTH, D, D])
    projc_in = din("proj_c", [DEPTH, 512, D])
    wout_in = din("w_out", [DEPTH, D, D])
    ln_in = din("ln_bc", [DEPTH, 4, 128, D])
    w1_in = din("mlp_w1", [DEPTH, D, 4 * D])
    w2_in = din("mlp_w2", [DEPTH, 4 * D, D])

    xs = dscr("xs", [N, D])
    p_tm = dscr("p_tm", [N, C_SS])
    p_ssT = dscr("p_ssT", [512, N])
    gT = dscr("gT", [3072, N], BF16)
    ybT = dscr("ybT", [16, 64, N], BF16)
    rw_s = {nm: dscr("rw_" + nm, [N, 512]) for nm in ("r", "v", "kk", "lw0", "lw1", "b0", "b1", "kd0", "kd1", "g", "y0", "y1")}
    yaT = dscr("yaT", [512, N], BF16)
    ysT = [dscr(f"ysT{d}", [512, N]) for d in range(2)]
    ycT = dscr("ycT", [512, N], BF16)
    xmid = dscr("xmid", [N, D])
    xm2T = dscr("xm2T", [D, N], BF16)
    h1T = dscr("h1T", [4 * D, N], BF16)

    uid = [0]

    def sb(st, name, shape, dt=F32):
        uid[0] += 1
        return st.enter_context(nc.sbuf_tensor(f"sb{uid[0]}_{name}", list(shape), dt))

    def ps(st, name, shape, dt=F32):
        uid[0] += 1
        ne = 512 if dt == F32 else 1024
        t = st.enter_context(nc.psum_tensor(f"ps{uid[0]}_{name}", [128, ne], dt))
        n = int(np.prod(shape[1:]))
        v = t[0:shape[0], 0:n]
        if len(shape) == 3:
            v = v.rearrange("p (a b) -> p a b", b=shape[2])
        return v

    def run_interleaved(gens):
        gens = list(gens)
        while gens:
            for g_ in list(gens):
                try:
                    next(g_)
                except StopIteration:
                    gens.remove(g_)

    def tt_(en, o, a, b, op, r, w):
        S.op(en, lambda e: e.tensor_tensor(out=o, in0=a, in1=b, op=op), r=r, w=w)

    def ts_(en, o, a, s1, s2, op0, op1, r, w):
        if s2 is None:
            S.op(en, lambda e: e.tensor_scalar(out=o, in0=a, scalar1=s1, scalar2=None, op0=op0), r=r, w=w)
        else:
            S.op(en, lambda e: e.tensor_scalar(out=o, in0=a, scalar1=s1, scalar2=s2, op0=op0, op1=op1), r=r, w=w)

    def act_(o, a, func, r, w, **kw):
        S.op("act", lambda e: e.activation(out=o, in_=a, func=func, **kw), r=r, w=w)

    def mm_(o, lhsT, rhs, start, stop, r, w):
        S.op("pe", lambda e: e.matmul(o, lhsT=lhsT, rhs=rhs, start=start, stop=stop), r=r, w=w)

    def rstd_(o, a, scale, eps, r, w):
        ts_("dve", o, a, scale, eps, ALU.mult, ALU.add, r=r, w=w)
        act_(o, o, AF.Sqrt, r=w, w=w)
        S.op("dve", lambda e: e.reciprocal(out=o, in_=o), r=w, w=w)

    def tr_(o, a, idn, r, w):
        S.op("pe", lambda e: e.transpose(o, a, idn), r=r, w=w)

    def cp_(en, o, a, r, w):
        S.op(en, lambda e: e.tensor_copy(out=o, in_=a), r=r, w=w)

    def red_(en, o, a, r, w):
        S.op(en, lambda e: e.tensor_reduce(out=o, in_=a, axis=AX.X, op=ALU.add), r=r, w=w)

    ones = sb(top, "ones", [128, 128])
    S.dma("sp", ones[:], ones_in, w=["ones"])
    masks = sb(top, "masks", [128, 4, 128])
    S.dma("sp", masks[:], masks_in.rearrange("m p t -> p m t"), w=["masks"])
    bsum = sb(top, "bsum", [128, NT, 8])
    ident = sb(top, "ident", [128, 128])
    identb = sb(top, "identb", [128, 128], BF16)
    modT = sb(top, "modT", [128, DEPTH, 48, 2])
    csil = sb(top, "csil", [128, 8, 2])
    S.dma("sp", ident[:], ident_in, w=["ident"])
    S.op("act", lambda e: e.activation(out=identb[:], in_=ident[:], func=AF.Copy), r=["ident"], w=["identb"])
    S.dma("sp", csil[:], cT_in, w=["csil"])
    S.op("act", lambda e: e.activation(out=csil[:], in_=csil[:], func=AF.Silu), r=["csil"], w=["csil"])

    for i in range(0, NT, 2):
        S.dma("sp", xs[i * 128:(i + 2) * 128, :], x_in[i * 128:(i + 2) * 128, :], w=[("xs", i), ("xs", i + 1)])

    with ExitStack() as st:
        wm = [sb(st, f"modw{i}", [128, 8, 512]) for i in range(2)]
        mbt = sb(st, "mbt", [128, DEPTH, 48])
        pm = ps(st, "pm", [128, 48, 2])
        S.dma("sp", mbt[:], mod_bT.rearrange("l p c -> p l c"), w=["mbt"])
        it = 0
        for l in range(n_layers):
            for cb in range(12):
                wt = wm[it % 2]
                key = ("modw", it % 2)
                it += 1
                S.dma("sp" if it % 2 else "pool", wt[:],
                      mod_w[l, :, cb * 512:(cb + 1) * 512].rearrange("(kc p) n -> p kc n", p=128), w=[key])
                for j in range(4):
                    cc = cb * 4 + j
                    for kc in range(8):
                        S.op("pe", lambda e, wt=wt, kc=kc, j=j, cc=cc: e.matmul(
                            pm[:, cc, :], lhsT=wt[:, kc, j * 128:(j + 1) * 128], rhs=csil[:, kc, :],
                            start=(kc == 0), stop=(kc == 7)), r=[key, "csil"], w=["pm"])
            for g in range(2):
                S.op("dve", lambda e, l=l, g=g: e.tensor_tensor(out=modT[:, l, :, g], in0=pm[:, :, g], in1=mbt[:, l, :],
                                                                op=ALU.add), r=["pm", "mbt"], w=["modT"])
            for c0 in (8, 32):
                S.op("dve", lambda e, l=l, c0=c0: e.tensor_scalar(out=modT[:, l, c0:c0 + 8, :], in0=modT[:, l, c0:c0 + 8, :],
                                                                  scalar1=1.0, scalar2=None, op0=ALU.add),
                     r=["modT"], w=["modT"])
    S.barrier()
    if stop_after == "A":
        dbg_t = nc.dram_tensor("dbg_modT", [128, DEPTH * 96], F32, kind="ExternalOutput").ap()
        S.dma("sp", dbg_t, modT[:].rearrange("p l c g -> p (l c g)"), r=["modT"])

    for l in range(n_layers):
        if stop_after == "A":
            break
        with ExitStack() as st:
            xmT = sb(st, "xmT", [128, 8, N], BF16)
            xt = [sb(st, f"xt{i}", [128, D]) for i in range(2)]
            pT = [ps(st, f"pT{i}", [128, 4, 128]) for i in range(2)]
            for tt in range(NT):
                g = 1 if tt < 2 else 0
                x_t = xt[tt % 2]
                S.dma("sp", x_t[:], xs[tt * 128:(tt + 1) * 128, :], r=[("xs", tt)], w=[("xt", tt % 2)])
                for hf in range(2):
                    for j in range(4):
                        kc = hf * 4 + j
                        S.op("pe", lambda e, x_t=x_t, kc=kc, hf=hf, j=j: e.transpose(
                            pT[hf][:, j, :], x_t[:, kc * 128:(kc + 1) * 128], ident[:]),
                            r=[("xt", tt % 2), "ident"], w=[("pT", hf)])
                    for j in range(4):
                        kc = hf * 4 + j
                        S.op("act", lambda e, kc=kc, hf=hf, j=j, tt=tt, g=g: e.activation(
                            out=xmT[:, kc, tt * 128:(tt + 1) * 128], in_=pT[hf][:, j, :], func=AF.Identity,
                            scale=modT[:, l, 8 + kc, g:g + 1], bias=modT[:, l, kc, g:g + 1]),
                            r=[("pT", hf), "modT"], w=[("xmT", tt)])
            wb = [sb(st, f"wb{i}", [128, 8, 512], BF16) for i in range(3)]
            pp = [ps(st, f"pp{i}", [128, 512]) for i in range(4)]
            stg = [sb(st, f"stg{i}", [128, 512]) for i in range(4)]
            stgT = [sb(st, f"stgT{i}", [128, N]) for i in range(2)]
            stgG = [sb(st, f"stgG{i}", [128, N], BF16) for i in range(2)]
            blks = [(c0, min(512, C_SS - c0)) for c0 in range(0, C_SS, 512)] + \
                   [(c0, min(512, INC - c0)) for c0 in range(C_SS, INC, 512)]
            k = 0
            kf = 0
            for bi, (c0, cw) in enumerate(blks):
                w_t = wb[bi % 3]
                wkey = ("wb", bi % 3)
                S.dma("pool", w_t[:, :, 0:cw], w_in[l, :, c0:c0 + cw].rearrange("(kc p) n -> p kc n", p=128), w=[wkey])
                if c0 < C_SS:
                    for tt in range(NT):
                        i = k % 4
                        k += 1
                        for kc in range(8):
                            S.op("pe", lambda e, i=i, kc=kc, tt=tt, w_t=w_t, cw=cw: e.matmul(
                                pp[i][:, 0:cw], lhsT=xmT[:, kc, tt * 128:(tt + 1) * 128], rhs=w_t[:, kc, 0:cw],
                                start=(kc == 0), stop=(kc == 7)), r=[("xmT", tt), wkey], w=[("pp", i)])
                        if i % 2 == 0:
                            S.op("act", lambda e, i=i, cw=cw: e.activation(out=stg[i][:, 0:cw], in_=pp[i][:, 0:cw], func=AF.Copy),
                                 r=[("pp", i)], w=[("stg", i)])
                        else:
                            S.op("dve", lambda e, i=i, cw=cw: e.tensor_copy(out=stg[i][:, 0:cw], in_=pp[i][:, 0:cw]),
                                 r=[("pp", i)], w=[("stg", i)])
                        S.dma("sp", p_tm[tt * 128:(tt + 1) * 128, c0:c0 + cw], stg[i][:, 0:cw], r=[("stg", i)],
                              w=[("p_tm", tt)])
                else:
                    for j in range(cw // 128):
                        cc = c0 + j * 128
                        is_g = cc >= C_G
                        f = kf % 2
                        kf += 1
                        dstt = stgG[f] if is_g else stgT[f]
                        skey = ("stgG", f) if is_g else ("stgT", f)
                        for t0 in range(0, N, 512):
                            tw = min(512, N - t0)
                            i = k % 4
                            k += 1
                            for kc in range(8):
                                S.op("pe", lambda e, i=i, kc=kc, t0=t0, tw=tw, w_t=w_t, j=j: e.matmul(
                                    pp[i][:, 0:tw], lhsT=w_t[:, kc, j * 128:(j + 1) * 128], rhs=xmT[:, kc, t0:t0 + tw],
                                    start=(kc == 0), stop=(kc == 7)),
                                    r=[("xmT", t) for t in range(t0 // 128, (t0 + tw) // 128)] + [wkey], w=[("pp", i)])
                            if is_g:
                                S.op("act", lambda e, i=i, t0=t0, tw=tw, dstt=dstt: e.activation(
                                    out=dstt[:, t0:t0 + tw], in_=pp[i][:, 0:tw], func=AF.Sigmoid), r=[("pp", i)], w=[skey])
                            else:
                                S.op("dve", lambda e, i=i, t0=t0, tw=tw, dstt=dstt: e.tensor_copy(
                                    out=dstt[:, t0:t0 + tw], in_=pp[i][:, 0:tw]), r=[("pp", i)], w=[skey])
                        if is_g:
                            S.dma("sp", gT[cc - C_G:cc - C_G + 128, :], dstt[:], r=[skey], w=[("gT", (cc - C_G) // 128)])
                        else:
                            S.dma("sp", p_ssT[cc - C_SS:cc - C_SS + 128, :], dstt[:], r=[skey], w=[("p_ssT", (cc - C_SS) // 128)])
        S.barrier()
        if stop_after == "B":
            break


        with ExitStack() as st:
            mu = sb(st, "mu", [128, 1792])
            rwp = sb(st, "rwp", [128, 5, 512])
            lrb = sb(st, "lrb", [1, 4, 512])
            lrw = sb(st, "lrw", [128, 2, 512])
            gup = sb(st, "gup", [128, 512])
            S.dma("sp", mu[:], mu_in[l], w=["mu"])
            S.dma("sp", rwp[:], rwp_in[l].rearrange("k p c -> p k c"), w=["rwp"])
            S.dma("sp", lrb[:], lrb_in[l:l + 1], w=["lrb"])
            S.dma("sp", lrw[:], lrw_in[l], w=["lrw"])
            S.dma("sp", gup[:], gup_in[l], w=["gup"])
            P0 = [sb(st, f"P0{i}", [128, 1792]) for i in range(2)]
            Pm = [sb(st, f"Pm{i}", [128, 1792]) for i in range(2)]
            Pp = [sb(st, f"Pp{i}", [128, 1792]) for i in range(2)]
            tl2 = [sb(st, f"tl{i}", [128, 1792]) for i in range(2)]
            pl = [sb(st, f"pl{i}", [128, 1792]) for i in range(2)]
            lrT2 = [sb(st, f"lrT{i}", [128, 128]) for i in range(2)]
            gsT2 = [sb(st, f"gsT{i}", [128, 128]) for i in range(2)]
            ptr = [ps(st, f"ptr{i}", [128, 128]) for i in range(2)]
            pq = [ps(st, f"pq{i}", [128, 512]) for i in range(5)]
            sg2 = [[sb(st, f"sg{i}{q}", [128, 512]) for q in range(4)] for i in range(2)]
            o52 = [[sb(st, f"o5{i}{q}", [128, 512]) for q in range(10)] for i in range(2)]
            kx2 = [sb(st, f"kx{i}", [128, 512]) for i in range(2)]
            sq52 = [sb(st, f"sq5{i}", [128, 512]) for i in range(2)]
            ss82 = [sb(st, f"ss8{i}", [128, 8]) for i in range(2)]

            def c_tile(tt):
                i = tt % 2
                tl, lrT, gsT, sg, o5, kx, sq5, ss8 = tl2[i], lrT2[i], gsT2[i], sg2[i], o52[i], kx2[i], sq52[i], ss82[i]
                lo, hi = tt * 128, (tt + 1) * 128
                first = tt in (0, 2)
                last = tt in (1, NT - 1)
                S.dma("sp", P0[i][:], p_tm[lo:hi, 0:1792], r=[("p_tm", tt)], w=[("P0", i)])
                if first:
                    S.op("pool", lambda e, i=i: e.memset(Pm[i][:], 0.0), w=[("Pm", i)])
                    S.dma("sp", Pm[i][1:128, :], p_tm[lo:hi - 1, 0:1792], r=[("p_tm", tt)], w=[("Pm", i)])
                else:
                    S.dma("sp", Pm[i][:], p_tm[lo - 1:hi - 1, 0:1792], r=[("p_tm", tt), ("p_tm", tt - 1)], w=[("Pm", i)])
                if last:
                    S.op("pool", lambda e, i=i: e.memset(Pp[i][:], 0.0), w=[("Pp", i)])
                    S.dma("sp", Pp[i][0:127, :], p_tm[lo + 1:hi, 0:1792], r=[("p_tm", tt)], w=[("Pp", i)])
                else:
                    S.dma("sp", Pp[i][:], p_tm[lo + 1:hi + 1, 0:1792], r=[("p_tm", tt), ("p_tm", tt + 1)], w=[("Pp", i)])
                p_ = pl[i]
                pk = ("pl", i)
                tt_("dve", tl[:], Pm[i][:], Pp[i][:], ALU.add, r=[("Pm", i), ("Pp", i)], w=[("tl", i)])
                S.op("dve", lambda e, i=i: e.scalar_tensor_tensor(out=tl[:], in0=tl[:], scalar=0.5, in1=P0[i][:], op0=ALU.mult,
                                                                op1=ALU.subtract), r=[("tl", i), ("P0", i)], w=[("tl", i)])
                tt_("dve", tl[:], tl[:], mu[:], ALU.mult, r=[("tl", i), "mu"], w=[("tl", i)])
                tt_("dve", p_[:], tl[:], P0[i][:], ALU.add, r=[("tl", i), ("P0", i)], w=[pk])
                S.dma("act", rw_s["r"][lo:hi, :], p_[:, 0:512], r=[pk], w=[("rw_r", tt)])
                S.dma("act", rw_s["v"][lo:hi, :], p_[:, 1024:1536], r=[pk], w=[("rw_v", tt)])
                yield
                tr_(ptr[0][:], p_[:, 1536:1664], ident[:], r=[pk, "ident"], w=[("ptr", 0)])
                tr_(ptr[1][:], p_[:, 1664:1792], ident[:], r=[pk, "ident"], w=[("ptr", 1)])
                act_(lrT[0:64, :], ptr[0][0:64, :], AF.Tanh, r=[("ptr", 0)], w=[("lrT", i)])
                act_(lrT[64:128, :], ptr[0][64:128, :], AF.Copy, r=[("ptr", 0)], w=[("lrT", i)])
                act_(gsT[:], ptr[1][:], AF.Sigmoid, r=[("ptr", 1)], w=[("gsT", i)])
                yield
                for d in range(2):
                    mm_(pq[d][:], ones[0:1, :], lrb[0:1, d, :], True, False, r=["ones", "lrb"], w=[("pq", d)])
                    mm_(pq[d][:], lrT[0:64, :], lrw[0:64, d, :], False, True, r=[("lrT", i), "lrw"], w=[("pq", d)])
                    mm_(pq[2 + d][:], ones[0:1, :], lrb[0:1, 2 + d, :], True, False, r=["ones", "lrb"], w=[("pq", 2 + d)])
                    mm_(pq[2 + d][:], lrT[64:128, :], lrw[64:128, d, :], False, True, r=[("lrT", i), "lrw"], w=[("pq", 2 + d)])
                mm_(pq[4][:], gsT[:], gup[:], True, True, r=[("gsT", i), "gup"], w=[("pq", 4)])
                for q in range(4):
                    act_(sg[q][:], pq[q][:], AF.Sigmoid, r=[("pq", q)], w=[("sg", i, q)])
                o = o5
                ok = lambda q: ("o5", i, q)
                cp_("dve", o[9][:], pq[4][:], r=[("pq", 4)], w=[ok(9)])
                S.dma("act", rw_s["g"][lo:hi, :], o[9][:], r=[ok(9)], w=[("rw_g", tt)])
                yield
                k_ = p_[:, 512:1024]
                r_ = p_[:, 0:512]
                h3 = lambda a: a.rearrange("p (h d) -> p h d", d=64)
                tt_("dve", kx[:], k_, rwp[:, 0, :], ALU.mult, r=[pk, "rwp"], w=[("kx", i)])
                tt_("dve", sq5[:], kx[:], kx[:], ALU.mult, r=[("kx", i)], w=[("sq5", i)])
                red_("dve", ss8[:], h3(sq5[:]), r=[("sq5", i)], w=[("ss8", i)])
                ts_("dve", ss8[:], ss8[:], 1e-24, None, ALU.max, None, r=[("ss8", i)], w=[("ss8", i)])
                act_(ss8[:], ss8[:], AF.Sqrt, r=[("ss8", i)], w=[("ss8", i)])
                S.op("dve", lambda e: e.reciprocal(out=ss8[:], in_=ss8[:]), r=[("ss8", i)], w=[("ss8", i)])
                tt_("dve", h3(o[0][:]), h3(kx[:]), ss8[:].unsqueeze(2).broadcast_to([128, 8, 64]), ALU.mult, r=[("kx", i), ("ss8", i)], w=[ok(0)])
                S.dma("act", rw_s["kk"][lo:hi, :], o[0][:], r=[ok(0)], w=[("rw_kk", tt)])
                yield
                for d in range(2):
                    ts_("pool", o[1 + d][:], sg[d][:], -0.6065306597126334, None, ALU.mult, None, r=[("sg", i, d)], w=[ok(1 + d)])
                    S.dma("act", rw_s[f"lw{d}"][lo:hi, :], o[1 + d][:], r=[ok(1 + d)], w=[(f"rw_lw{d}", tt)])
                    tt_("dve", o[3 + d][:], o[0][:], sg[2 + d][:], ALU.mult, r=[ok(0), ("sg", i, 2 + d)], w=[ok(3 + d)])
                    S.dma("act", rw_s[f"b{d}"][lo:hi, :], o[3 + d][:], r=[ok(3 + d)], w=[(f"rw_b{d}", tt)])
                    S.op("dve", lambda e, d=d: e.scalar_tensor_tensor(out=o[5 + d][:], in0=sg[2 + d][:], scalar=-1.0, in1=rwp[:, 1, :],
                                                                     op0=ALU.add, op1=ALU.mult), r=[("sg", i, 2 + d), "rwp"], w=[ok(5 + d)])
                    S.op("dve", lambda e, d=d, k_=k_: e.scalar_tensor_tensor(out=o[5 + d][:], in0=o[5 + d][:], scalar=1.0, in1=k_,
                                                                            op0=ALU.add, op1=ALU.mult), r=[ok(5 + d), pk], w=[ok(5 + d)])
                    S.dma("act", rw_s[f"kd{d}"][lo:hi, :], o[5 + d][:], r=[ok(5 + d)], w=[(f"rw_kd{d}", tt)])
                tt_("pool", o[7][:], o[5][:], o[6][:], ALU.add, r=[ok(5), ok(6)], w=[ok(7)])
                tt_("dve", o[8][:], r_, rwp[:, 2, :], ALU.mult, r=[pk, "rwp"], w=[ok(8)])
                tt_("pool", o[8][:], o[8][:], o[7][:], ALU.mult, r=[ok(8), ok(7)], w=[ok(8)])
                red_("dve", bsum[:, tt, :], h3(o[8][:]), r=[ok(8)], w=[("bsum", tt)])

            for t2 in range(0, NT, 2):
                run_interleaved([c_tile(t2), c_tile(t2 + 1)])
        S.barrier()
        if stop_after == "C":
            break

        with ExitStack() as st:
            pf = [ps(st, f"pf{i}", [128, 512]) for i in range(8)]
            npf = [0]

            npfd = [0, 0]

            def PFd(d):
                i = 4 * d + npfd[d] % 4
                npfd[d] += 1
                return pf[i], ("pf", i)

            Bd = []
            for d in range(2):
                B = {}
                for nm in ("r", "v", "kk", "lw", "b", "kd", "cum", "e1", "e2", "e3", "e4", "abarf", "bbarf", "kbarf", "rbarf"):
                    B[nm] = sb(st, f"D{d}{nm}", [128, 512])
                for nm in ("abar", "Bt", "Kt", "Vb", "Z", "U"):
                    B[nm] = sb(st, f"D{d}{nm}", [128, 512], BF16)
                for nm in ("abarT", "bbarT", "kbarT"):
                    B[nm] = sb(st, f"D{d}{nm}", [128, 4, 128], BF16)
                for nm in ("AakT", "MrbT", "MrkT", "abarZ", "bbarZ", "rbarZ", "Rb"):
                    B[nm] = sb(st, f"D{d}{nm}", [128, 8, 128], BF16)
                for nm in ("AabT", "Aab", "P0", "P1", "Q0", "Q1", "R0", "R1"):
                    B[nm] = sb(st, f"D{d}{nm}", [128, 8, 128], F32)
                for nm in ("abarZ", "bbarZ", "rbarZ"):
                    S.op("pool", lambda e, B=B, nm=nm: e.memset(B[nm][:], 0.0), w=[(nm, d)])
                B["rbT1"] = sb(st, f"D{d}rbT1", [128, 8, 128], BF16)
                B["WT"] = sb(st, f"D{d}WT", [128, 8, 128], BF16)
                S.op("pool", lambda e, B=B: e.memset(B["rbT1"][:], 0.0), w=[("rbT1", d)])
                S.op("pool", lambda e, B=B: e.memset(B["WT"][:], 0.0), w=[("WT", d)])
                B["gl"] = sb(st, f"D{d}gl", [64, 8])
                B["Sf"] = sb(st, f"D{d}Sf", [64, 8, 64])
                B["Stmp"] = sb(st, f"D{d}Stmp", [64, 8, 64])
                B["Sbf"] = sb(st, f"D{d}Sbf", [128, 8, 64], BF16)
                S.op("pool", lambda e, B=B: e.memset(B["Sf"][:], 0.0), w=[("Sf", d)])
                S.op("pool", lambda e, B=B: e.memset(B["Sbf"][:], 0.0), w=[("Sbf", d)])
                Bd.append(B)

            def rw_chunk(c, d):
                B = Bd[d]
                K_ = lambda n: (n, d)
                lo, hi = c * 128, (c + 1) * 128
                hs = lambda h: slice(h * 64, (h + 1) * 64)
                for nm, src in (("r", "r"), ("v", "v"), ("kk", "kk"), ("lw", f"lw{d}"), ("b", f"b{d}"), ("kd", f"kd{d}")):
                    S.dma("sp", B[nm][:], rw_s[src][lo:hi, :], r=[("rw_" + src, c)], w=[K_(nm)])
                    yield
                pc, pck = PFd(d)
                mm_(pc[:], masks[:, 2 + d, :], B["lw"][:], True, True, r=["masks", K_("lw")], w=[pck])
                pt, ptk = PFd(d)
                mm_(pt[:], ones[:], B["lw"][:], True, True, r=["ones", K_("lw")], w=[ptk])
                act_(B["cum"][:], pc[:], AF.Copy, r=[pck], w=[K_("cum")])
                yield
                act_(B["e1"][:], B["cum"][:], AF.Exp, r=[K_("cum")], w=[K_("e1")])
                yield
                act_(B["e2"][:], B["cum"][:], AF.Exp, r=[K_("cum")], w=[K_("e2")], scale=-1.0)
                yield
                tt_("pool", B["e3"][:], B["cum"][:], B["lw"][:], ALU.subtract, r=[K_("cum"), K_("lw")], w=[K_("e3")])
                yield
                act_(B["e3"][:], B["e3"][:], AF.Exp, r=[K_("e3")], w=[K_("e3")])
                yield
                tt_("dve", B["e4"][:], pt[:], B["cum"][:], ALU.subtract, r=[ptk, K_("cum")], w=[K_("e4")])
                yield
                act_(B["e4"][:], B["e4"][:], AF.Exp, r=[K_("e4")], w=[K_("e4")])
                yield
                S.op("dve", lambda e: e.scalar_tensor_tensor(out=B["abarf"][:], in0=B["kk"][:], scalar=-1.0, in1=B["e3"][:],
                                                            op0=ALU.mult, op1=ALU.mult), r=[K_("kk"), K_("e3")], w=[K_("abarf")])
                yield
                tt_("pool", B["bbarf"][:], B["b"][:], B["e2"][:], ALU.mult, r=[K_("b"), K_("e2")], w=[K_("bbarf")])
                yield
                tt_("dve", B["kbarf"][:], B["kd"][:], B["e2"][:], ALU.mult, r=[K_("kd"), K_("e2")], w=[K_("kbarf")])
                yield
                tt_("pool", B["rbarf"][:], B["r"][:], B["e1"][:], ALU.mult, r=[K_("r"), K_("e1")], w=[K_("rbarf")])
                yield
                tt_("dve", B["Bt"][:], B["b"][:], B["e4"][:], ALU.mult, r=[K_("b"), K_("e4")], w=[K_("Bt")])
                yield
                tt_("pool", B["Kt"][:], B["kd"][:], B["e4"][:], ALU.mult, r=[K_("kd"), K_("e4")], w=[K_("Kt")])
                yield
                cp_("pool", B["Vb"][:], B["v"][:], r=[K_("v")], w=[K_("Vb")])
                yield
                pg, pgk = PFd(d)
                for h in range(8):
                    mm_(pg[0:64, 2 * h:2 * h + 2], B["lw"][:, hs(h)], ones[:, 0:2], True, True, r=[K_("lw"), "ones"], w=[pgk])
                act_(B["gl"][:], pg[0:64, 0:16].rearrange("p (h two) -> p h two", two=2)[:, :, 0], AF.Exp, r=[pgk], w=[K_("gl")])
                yield
                if DSTAGE < 1:
                    return
                for nm in ("abar", "bbar", "kbar", "rbar")[:DSUB]:
                    p2, p2k = PFd(d)
                    for j in range(4):
                        tr_(p2[:, j * 128:(j + 1) * 128], B[nm + "f"][:, j * 128:(j + 1) * 128], ident[:], r=[K_(nm + "f"), "ident"], w=[p2k])
                    p3 = p2[:].rearrange("p (j t) -> p j t", t=128)
                    if nm != "rbar":
                        act_(B[nm + "T"][:], p3, AF.Copy, r=[p2k], w=[K_(nm + "T")])
                        yield
                    if nm != "kbar":
                        z4 = lambda lo_: B[nm + "Z"][lo_:lo_ + 64].rearrange("p (j two) t -> p j two t", two=2)
                        act_(z4(0)[:, :, 0, :], p3[0:64], AF.Copy, r=[p2k], w=[K_(nm + "Z")])
                        yield
                        cp_("dve", z4(64)[:, :, 1, :], p3[64:128], r=[p2k], w=[K_(nm + "Z")])
                        yield
                for hh in range(2 if DSUB > 4 else 0):
                    p2, p2k = PFd(d)
                    for h4 in range(4):
                        h = hh * 4 + h4
                        tr_(p2[0:64, h4 * 128:(h4 + 1) * 128], B["rbarf"][:, hs(h)], ident[:], r=[K_("rbarf"), "ident"], w=[p2k])
                    act_(B["rbT1"][0:64, hh * 4:(hh + 1) * 4, :], p2[0:64, :].rearrange("p (j t) -> p j t", t=128), AF.Copy, r=[p2k], w=[K_("rbT1")])
                    yield
                cp_("dve", B["abar"][:], B["abarf"][:], r=[K_("abarf")], w=[K_("abar")])
                yield
                if DSTAGE < 2:
                    return
                for nm, lt, rt, mk in (("AabT", "bbarT", "abarZ", d), ("Aab", "abarT", "bbarZ", 1 - d), ("AakT", "kbarT", "abarZ", d),
                                       ("MrbT", "bbarT", "rbarZ", 2 + d), ("MrkT", "kbarT", "rbarZ", 2 + d)):
                    for hh in range(2):
                        p_, pk_ = PFd(d)
                        for h4 in range(4):
                            h = hh * 4 + h4
                            mm_(p_[:, h4 * 128:(h4 + 1) * 128], B[lt][:, h // 2, :], B[rt][:, h, :], True, True,
                                r=[K_(lt), K_(rt)], w=[pk_])
                        tt_("dve", B[nm][:, hh * 4:(hh + 1) * 4, :], p_[:].rearrange("p (h t) -> p h t", t=128),
                            masks[:, mk, :].unsqueeze(1).broadcast_to([128, 4, 128]), ALU.mult, r=[pk_, "masks"], w=[K_(nm)])
                        yield
                if DSTAGE < 3:
                    return
                P_, Q_ = "AabT", "Aab"
                tt_("pool", B["R0"][:], B["AabT"][:], ident[:].unsqueeze(1).broadcast_to([128, 8, 128]), ALU.add,
                    r=[K_("AabT"), "ident"], w=[K_("R0")])
                yield
                Rc = "R0"
                for lev in range(6):
                    Pn, Qn, Rn = f"P{lev % 2}", f"Q{lev % 2}", f"R{(lev + 1) % 2}"
                    for hh in range(2):
                        if lev < 5:
                            p_, pk_ = PFd(d)
                            for h4 in range(4):
                                h = hh * 4 + h4
                                mm_(p_[:, h4 * 128:(h4 + 1) * 128], B[Q_][:, h, :], B[P_][:, h, :], True, True, r=[K_(Q_), K_(P_)], w=[pk_])
                            act_(B[Pn][:, hh * 4:(hh + 1) * 4, :], p_[:].rearrange("p (h t) -> p h t", t=128), AF.Copy, r=[pk_], w=[K_(Pn)])
                            yield
                        p_, pk_ = PFd(d)
                        for h4 in range(4):
                            h = hh * 4 + h4
                            mm_(p_[:, h4 * 128:(h4 + 1) * 128], B[P_][:, h, :], B[Q_][:, h, :], True, True, r=[K_(Q_), K_(P_)], w=[pk_])
                        cp_("dve", B[Qn][:, hh * 4:(hh + 1) * 4, :], p_[:].rearrange("p (h t) -> p h t", t=128), r=[pk_], w=[K_(Qn)])
                        yield
                    for hh in range(2):
                        p_, pk_ = PFd(d)
                        for h4 in range(4):
                            h = hh * 4 + h4
                            mm_(p_[:, h4 * 128:(h4 + 1) * 128], B[Qn][:, h, :], B[Rc][:, h, :], True, True, r=[K_(Qn), K_(Rc)], w=[pk_])
                        tt_("dve", B[Rn][:, hh * 4:(hh + 1) * 4, :], p_[:].rearrange("p (h t) -> p h t", t=128),
                            B[Rc][:, hh * 4:(hh + 1) * 4, :], ALU.add, r=[pk_, K_(Rc)], w=[K_(Rn)])
                        yield
                    P_, Q_, Rc = Pn, Qn, Rn
                if DSTAGE < 4:
                    return
                cp_("pool", B["Rb"][:], B[Rc][:], r=[K_(Rc)], w=[K_("Rb")])
                yield
                Rc = "Rb"
                p_, pk_ = PFd(d)
                for h in range(8):
                    mm_(p_[:, hs(h)], B["AakT"][:, h, :], B["Vb"][:, hs(h)], True, True, r=[K_("AakT"), K_("Vb")], w=[pk_])
                act_(B["Z"][:], p_[:], AF.Copy, r=[pk_], w=[K_("Z")])
                yield
                p_, pk_ = PFd(d)
                for h in range(8):
                    mm_(p_[:, hs(h)], B[Rc][:, h, :], B["Z"][:, hs(h)], True, True, r=[K_(Rc), K_("Z")], w=[pk_])
                cp_("dve", B["e1"][:], p_[:], r=[pk_], w=[K_("e1")])
                yield
                for hh in range(2):
                    p_, pk_ = PFd(d)
                    for h4 in range(4):
                        h = hh * 4 + h4
                        mm_(p_[0:64, h4 * 128:(h4 + 1) * 128], B["abar"][:, hs(h)], B[Rc][:, h, :], True, True, r=[K_("abar"), K_(Rc)], w=[pk_])
                    act_(B["WT"][0:64, hh * 4:(hh + 1) * 4, :], p_[0:64, :].rearrange("p (h t) -> p h t", t=128), AF.Copy, r=[pk_], w=[K_("WT")])
                    yield
                if DSTAGE < 5:
                    return
                p_, pk_ = PFd(d)
                for h in range(8):
                    mm_(p_[:, hs(h)], B["WT"][:, h, :], B["Sbf"][:, h, :], True, True, r=[K_("WT"), K_("Sbf")], w=[pk_])
                tt_("dve", B["U"][:], p_[:], B["e1"][:], ALU.add, r=[pk_, K_("e1")], w=[K_("U")])
                yield
                py, pyk = PFd(d)
                for h in range(8):
                    mm_(py[:, hs(h)], B["rbT1"][:, h, :], B["Sbf"][:, h, :], True, False, r=[K_("rbT1"), K_("Sbf")], w=[pyk])
                    mm_(py[:, hs(h)], B["MrbT"][:, h, :], B["U"][:, hs(h)], False, False, r=[K_("MrbT"), K_("U")], w=[pyk])
                    mm_(py[:, hs(h)], B["MrkT"][:, h, :], B["Vb"][:, hs(h)], False, True, r=[K_("MrkT"), K_("Vb")], w=[pyk])
                act_(B["e2"][:], py[:], AF.Copy, r=[pyk], w=[K_("e2")])
                yield
                S.dma("act", rw_s[f"y{d}"][lo:hi, :], B["e2"][:], r=[K_("e2")], w=[(f"rw_y{d}", c)])
                yield
                pS, pSk = PFd(d)
                for h in range(8):
                    mm_(pS[0:64, hs(h)], B["Kt"][:, hs(h)], B["Vb"][:, hs(h)], True, False, r=[K_("Kt"), K_("Vb")], w=[pSk])
                    mm_(pS[0:64, hs(h)], B["Bt"][:, hs(h)], B["U"][:, hs(h)], False, True, r=[K_("Bt"), K_("U")], w=[pSk])
                tt_("pool", B["Stmp"][:], B["Sf"][:], B["gl"][:].unsqueeze(2).broadcast_to([64, 8, 64]), ALU.mult,
                    r=[K_("Sf"), K_("gl")], w=[K_("Stmp")])
                yield
                tt_("dve", B["Sf"][:], pS[0:64, :].rearrange("p (h v) -> p h v", v=64), B["Stmp"][:], ALU.add, r=[pSk, K_("Stmp")], w=[K_("Sf")])
                yield
                act_(B["Sbf"][0:64], B["Sf"][:], AF.Copy, r=[K_("Sf")], w=[K_("Sbf")])
                yield

            ord0 = list(range(NT))
            ord1 = [1, 0] + list(range(NT - 1, 1, -1))
            for i in range(NT):
                run_interleaved([rw_chunk(ord0[i], 0), rw_chunk(ord1[i], 1)])
        S.barrier()
        if stop_after == "D":
            break

        with ExitStack() as st:
            rwp = sb(st, "rwpE", [128, 5, 512])
            S.dma("sp", rwp[:], rwp_in[l].rearrange("k p c -> p k c"), w=["rwpE"])
            Y0 = [sb(st, f"EY0{i}", [128, 512]) for i in range(2)]
            Y1 = [sb(st, f"EY1{i}", [128, 512]) for i in range(2)]
            Vv = [sb(st, f"EV{i}", [128, 512]) for i in range(2)]
            Gg = [sb(st, f"EG{i}", [128, 512]) for i in range(2)]
            y = sb(st, "Ey", [128, 512])
            ysq = sb(st, "Eysq", [128, 512])
            bon = sb(st, "Ebon", [128, 512])
            yab = sb(st, "Eyab", [128, 512], BF16)
            st8 = sb(st, "Est8", [128, 4, 8])
            yT = [sb(st, f"EyT{i}", [128, 4, 128], BF16) for i in range(2)]
            pbE = [ps(st, f"pbE{i}", [128, 4, 128], BF16) for i in range(2)]
            h3 = lambda a: a.rearrange("p (h d) -> p h d", d=64)
            b3 = lambda a: a.unsqueeze(2).broadcast_to([128, 8, 64])
            for tt in range(NT):
                i = tt % 2
                lo, hi = tt * 128, (tt + 1) * 128
                S.dma("sp", Y0[i][:], rw_s["y0"][lo:hi, :], r=[("rw_y0", tt)], w=[("EY0", i)])
                S.dma("sp", Y1[i][:], rw_s["y1"][lo:hi, :], r=[("rw_y1", tt)], w=[("EY1", i)])
                S.dma("sp", Vv[i][:], rw_s["v"][lo:hi, :], r=[("rw_v", tt)], w=[("EV", i)])
                S.dma("sp", Gg[i][:], rw_s["g"][lo:hi, :], r=[("rw_g", tt)], w=[("EG", i)])
                tt_("pool", y[:], Y0[i][:], Y1[i][:], ALU.add, r=[("EY0", i), ("EY1", i)], w=["Ey"])
                red_("dve", st8[:, 0, :], h3(y[:]), r=["Ey"], w=["Est8"])
                tt_("dve", ysq[:], y[:], y[:], ALU.mult, r=["Ey"], w=["Eysq"])
                red_("dve", st8[:, 1, :], h3(ysq[:]), r=["Eysq"], w=["Est8"])
                ts_("dve", st8[:, 0, :], st8[:, 0, :], 1.0 / 64, None, ALU.mult, None, r=["Est8"], w=["Est8"])
                tt_("dve", st8[:, 2, :], st8[:, 0, :], st8[:, 0, :], ALU.mult, r=["Est8"], w=["Est8"])
                S.op("dve", lambda e: e.scalar_tensor_tensor(out=st8[:, 3, :], in0=st8[:, 1, :], scalar=1.0 / 64, in1=st8[:, 2, :],
                                                           op0=ALU.mult, op1=ALU.subtract), r=["Est8"], w=["Est8"])
                rstd_(st8[:, 3, :], st8[:, 3, :], 1.0, GN_EPS, r=["Est8"], w=["Est8"])
                tt_("dve", h3(y[:]), h3(y[:]), b3(st8[:, 0, :]), ALU.subtract, r=["Ey", "Est8"], w=["Ey"])
                tt_("dve", h3(y[:]), h3(y[:]), b3(st8[:, 3, :]), ALU.mult, r=["Ey", "Est8"], w=["Ey"])
                tt_("dve", y[:], y[:], rwp[:, 3, :], ALU.mult, r=["Ey", "rwpE"], w=["Ey"])
                tt_("pool", y[:], y[:], rwp[:, 4, :], ALU.add, r=["Ey", "rwpE"], w=["Ey"])
                tt_("pool", h3(bon[:]), h3(Vv[i][:]), b3(bsum[:, tt, :]), ALU.mult, r=[("EV", i), ("bsum", tt)], w=["Ebon"])
                tt_("dve", y[:], y[:], bon[:], ALU.add, r=["Ey", "Ebon"], w=["Ey"])
                tt_("pool", yab[:], y[:], Gg[i][:], ALU.mult, r=["Ey", ("EG", i)], w=["Eyab"])
                for j in range(4):
                    tr_(pbE[i][:, j, :], yab[:, j * 128:(j + 1) * 128], identb[:], r=["Eyab", "identb"], w=[("pbE", i)])
                act_(yT[i][:], pbE[i][:], AF.Copy, r=[("pbE", i)], w=[("EyT", i)])
                S.dma("act", yaT[:, lo:hi].rearrange("(j p) t -> p j t", p=128), yT[i][:], r=[("EyT", i)], w=[("yaT", tt)])
        S.barrier()
        if stop_after == "E":
            break

        with ExitStack() as st:
            mcol = sb(st, "mcol", [128, 4])
            S.dma("sp", mcol[:], mcol_in, w=["mcol"])
            Ep = [[sb(st, f"Ep{d}{k}", [128, 2048]) for k in range(2)] for d in range(2)]
            En = [[sb(st, f"En{d}{k}", [128, 2048]) for k in range(2)] for d in range(2)]
            Bp = [sb(st, f"Bp{d}", [128, 4, 1024], BF16) for d in range(2)]
            Cb = sb(st, "Cb", [128, 32, 128], BF16)
            S.dma("pool", Cb[:], cblk_in[l], w=["Cb"])
            trib = sb(st, "trib", [128, 2, 128], BF16)
            selb = sb(st, "selb", [128, 4, 128], BF16)
            S.dma("pool", trib[:], masks_in[2:4].rearrange("m p t -> p m t"), w=["trib"])
            S.dma("pool", selb[:], sel_in.rearrange("m p t -> p m t"), w=["selb"])
            MAGIC = 12582912.0
            C1 = 6.28125
            C2 = float(2 * np.pi - 6.28125)
            with ExitStack() as st2:
                T_ = {nm: sb(st2, "G" + nm, [128, 2048]) for nm in ("are", "aim", "dt", "x1", "x2", "arg", "k", "sn", "cs", "mg", "t1", "t2")}
                bb = [sb(st2, f"Gbb{k}", [128, 4, 512]) for k in range(2)]

                G = lambda nm: "G" + nm

                def g_tt(o, x, y, op, eng="pool"):
                    tt_(eng, T_[o][:], T_[x][:], T_[y][:], op, r=[G(x), G(y)], w=[G(o)])

                def sincos(src, shift, dst):
                    a_, k_ = T_["arg"], T_["k"]
                    ts_("dve", a_[:], T_[src][:], shift, None, ALU.add, None, r=[G(src)], w=[G("arg")])
                    ts_("dve", k_[:], a_[:], float(1 / (2 * np.pi)), MAGIC, ALU.mult, ALU.add, r=[G("arg")], w=[G("k")])
                    ts_("dve", k_[:], k_[:], -MAGIC, None, ALU.add, None, r=[G("k")], w=[G("k")])
                    S.op("dve", lambda e: e.scalar_tensor_tensor(out=a_[:], in0=k_[:], scalar=-C1, in1=a_[:], op0=ALU.mult, op1=ALU.add),
                         r=[G("k"), G("arg")], w=[G("arg")])
                    S.op("dve", lambda e: e.scalar_tensor_tensor(out=a_[:], in0=k_[:], scalar=-C2, in1=a_[:], op0=ALU.mult, op1=ALU.add),
                         r=[G("k"), G("arg")], w=[G("arg")])
                    act_(T_[dst][:], a_[:], AF.Sin, r=[G("arg")], w=[G(dst)])

                c4 = lambda t: t[:].rearrange("p (q c) -> p q c", c=512)
                for d in range(2):
                    S.dma("sp", T_["are"][:], sstab_in[l, d, 0], w=[G("are")])
                    S.dma("sp", T_["aim"][:], sstab_in[l, d, 1], w=[G("aim")])
                    S.dma("sp", T_["dt"][:], sstab_in[l, d, 2], w=[G("dt")])
                    if d == 0:
                        S.dma("sp", bb[0][:], bblk_in[l, 0], w=["Gbb0"])
                        S.dma("sp", bb[1][:], bblk_in[l, 1], w=["Gbb1"])
                    ts_("dve", T_["are"][:], T_["are"][:], -1e-4, None, ALU.min, None, r=[G("are")], w=[G("are")])
                    act_(T_["dt"][:], T_["dt"][:], AF.Exp, r=[G("dt")], w=[G("dt")])
                    g_tt("x1", "are", "dt", ALU.mult)
                    g_tt("x2", "aim", "dt", ALU.mult)
                    sincos("x2", 0.0, "sn")
                    sincos("x2", PI / 2, "cs")
                    act_(T_["mg"][:], T_["x1"][:], AF.Exp, r=[G("x1")], w=[G("mg")])
                    g_tt("t1", "sn", "mg", ALU.mult)
                    g_tt("t2", "cs", "mg", ALU.mult)
                    ts_("pool", T_["t2"][:], T_["t2"][:], -1.0, None, ALU.add, None, r=[G("t2")], w=[G("t2")])
                    g_tt("sn", "are", "are", ALU.mult)
                    g_tt("cs", "aim", "aim", ALU.mult)
                    g_tt("sn", "sn", "cs", ALU.add)
                    S.op("dve", lambda e: e.reciprocal(out=T_["sn"][:], in_=T_["sn"][:]), r=[G("sn")], w=[G("sn")])
                    g_tt("mg", "t2", "are", ALU.mult)
                    g_tt("cs", "t1", "aim", ALU.mult)
                    g_tt("mg", "mg", "cs", ALU.add)
                    g_tt("mg", "mg", "sn", ALU.mult)
                    g_tt("cs", "t1", "are", ALU.mult)
                    g_tt("k", "t2", "aim", ALU.mult)
                    g_tt("cs", "cs", "k", ALU.subtract)
                    g_tt("cs", "cs", "sn", ALU.mult)
                    tt_("pool", c4(T_["t1"]), bb[0][:], c4(T_["mg"]), ALU.mult, r=["Gbb0", G("mg")], w=[G("t1")])
                    tt_("pool", c4(T_["t2"]), bb[1][:], c4(T_["cs"]), ALU.mult, r=["Gbb1", G("cs")], w=[G("t2")])
                    tt_("pool", Bp[d][:, :, 0:512], c4(T_["t1"]), c4(T_["t2"]), ALU.subtract, r=[G("t1"), G("t2")], w=[("Bp", d)])
                    tt_("pool", c4(T_["t1"]), bb[0][:], c4(T_["cs"]), ALU.mult, r=["Gbb0", G("cs")], w=[G("t1")])
                    tt_("pool", c4(T_["t2"]), bb[1][:], c4(T_["mg"]), ALU.mult, r=["Gbb1", G("mg")], w=[G("t2")])
                    tt_("pool", Bp[d][:, :, 512:1024], c4(T_["t1"]), c4(T_["t2"]), ALU.add, r=[G("t1"), G("t2")], w=[("Bp", d)])
                    ts_("dve", T_["t1"][:], T_["x2"][:], mcol[:, d:d + 1], None, ALU.mult, None, r=[G("x2"), "mcol"], w=[G("t1")])
                    sincos("t1", 0.0, "sn")
                    sincos("t1", PI / 2, "cs")
                    act_(T_["mg"][:], T_["x1"][:], AF.Exp, r=[G("x1")], w=[G("mg")], scale=mcol[:, d:d + 1])
                    tt_("pool", Ep[d][0][:], T_["mg"][:], T_["cs"][:], ALU.mult, r=[G("mg"), G("cs")], w=[("Ep", d)])
                    tt_("pool", Ep[d][1][:], T_["mg"][:], T_["sn"][:], ALU.mult, r=[G("mg"), G("sn")], w=[("Ep", d)])
                    act_(T_["mg"][:], T_["x1"][:], AF.Exp, r=[G("x1")], w=[G("mg")], scale=mcol[:, 2 + d:3 + d])
                    tt_("pool", En[d][0][:], T_["mg"][:], T_["cs"][:], ALU.mult, r=[G("mg"), G("cs")], w=[("En", d)])
                    tt_("pool", En[d][1][:], T_["mg"][:], T_["sn"][:], ALU.mult, r=[G("mg"), G("sn")], w=[("En", d)])
            S.barrier()
            uf = [sb(st, f"Guf{i}", [128, 4, 128]) for i in range(2)]
            ub = [sb(st, f"Gub{i}", [128, 4, 128], BF16) for i in range(2)]
            hb = [[sb(st, f"Ghb{d}{i}", [128, 4, 2, 512], BF16) for i in range(2)] for d in range(2)]
            zb = [sb(st, f"Gzb{i}", [128, 2, 512], BF16) for i in range(2)]
            tmd = [[sb(st, f"Gtm{d_}{i}", [128, 512]) for i in range(4)] for d_ in range(2)]
            hT = [sb(st, f"GhT{i}", [128, 32, 128], BF16) for i in range(2)]
            ysb = [sb(st, f"Gys{i}", [128, 4, 128]) for i in range(2)]
            pg_ = [ps(st, f"pgf{i}", [128, 512]) for i in range(6)]
            pgb = [ps(st, f"pgb{i}", [128, 8, 128], BF16) for i in range(2)]
            cnt = {"f": 0, "b": 0, "z": 0, "u": 0}

            def PGF():
                i = cnt["f"] % 6
                cnt["f"] += 1
                return pg_[i], ("pgf", i)

            for d in range(2):
                for i in range(2):
                    S.op("pool", lambda e, d=d, i=i: e.memset(hb[d][i][:], 0.0), w=[("hb", d, i)])
            nstep = [0, 0]

            def s5_chunk(c, d):
                lo, hi = c * 128, (c + 1) * 128
                iu = d
                tm = tmd[d]
                S.dma("sp", uf[iu][:], p_ssT[:, lo:hi].rearrange("(q p) t -> p q t", p=128), r=[("p_ssT", q) for q in range(4)], w=[("uf", iu)])
                yield
                cp_("pool", ub[iu][:], uf[iu][:], r=[("uf", iu)], w=[("ub", iu)])
                yield
                hi_ = nstep[d] % 2
                nstep[d] += 1
                hcur, hprev = hb[d][hi_], hb[d][1 - hi_]
                kcur, kprev = ("hb", d, hi_), ("hb", d, 1 - hi_)
                for q in range(4):
                    cs_ = slice(q * 512, (q + 1) * 512)
                    pbr, pbrk = pg_[3 * d], ("pgf", 3 * d)
                    pbi, pbik = pg_[3 * d + 1], ("pgf", 3 * d + 1)
                    mm_(pbr[:], ub[iu][:, q, :], Bp[d][:, q, 0:512], True, True, r=[("ub", iu), ("Bp", d)], w=[pbrk])
                    mm_(pbi[:], ub[iu][:, q, :], Bp[d][:, q, 512:1024], True, True, r=[("ub", iu), ("Bp", d)], w=[pbik])
                    iz = d
                    z = zb[iz]
                    zk = ("zb", iz)
                    tt_("dve", tm[0][:], pbr[:], En[d][0][:, cs_], ALU.mult, r=[pbrk, ("En", d)], w=[("tm", d, 0)])
                    yield
                    tt_("dve", tm[1][:], pbi[:], En[d][1][:, cs_], ALU.mult, r=[pbik, ("En", d)], w=[("tm", d, 1)])
                    yield
                    tt_("dve", z[:, 0, :], tm[0][:], tm[1][:], ALU.add, r=[("tm", d, 0), ("tm", d, 1)], w=[zk])
                    yield
                    tt_("dve", tm[2][:], pbi[:], En[d][0][:, cs_], ALU.mult, r=[pbik, ("En", d)], w=[("tm", d, 2)])
                    yield
                    tt_("dve", tm[3][:], pbr[:], En[d][1][:, cs_], ALU.mult, r=[pbrk, ("En", d)], w=[("tm", d, 3)])
                    yield
                    tt_("dve", z[:, 1, :], tm[2][:], tm[3][:], ALU.subtract, r=[("tm", d, 2), ("tm", d, 3)], w=[zk])
                    yield
                    pcr, pcrk = pg_[3 * d + 2], ("pgf", 3 * d + 2)
                    pci, pcik = pg_[3 * d], ("pgf", 3 * d)
                    mm_(pcr[:], trib[:, d, :], z[:, 0, :], True, False, r=["trib", zk], w=[pcrk])
                    mm_(pcr[:], selb[:, d, :], hprev[:, q, 0, :], False, True, r=["selb", kprev], w=[pcrk])
                    mm_(pci[:], trib[:, d, :], z[:, 1, :], True, False, r=["trib", zk], w=[pcik])
                    mm_(pci[:], selb[:, 2 + d, :], hprev[:, q, 1, :], False, True, r=["selb", kprev], w=[pcik])
                    tt_("dve", tm[0][:], pcr[:], Ep[d][0][:, cs_], ALU.mult, r=[pcrk, ("Ep", d)], w=[("tm", d, 0)])
                    yield
                    tt_("dve", tm[1][:], pci[:], Ep[d][1][:, cs_], ALU.mult, r=[pcik, ("Ep", d)], w=[("tm", d, 1)])
                    yield
                    tt_("dve", hcur[:, q, 0, :], tm[0][:], tm[1][:], ALU.subtract, r=[("tm", d, 0), ("tm", d, 1)], w=[kcur])
                    yield
                    tt_("dve", tm[2][:], pcr[:], Ep[d][1][:, cs_], ALU.mult, r=[pcrk, ("Ep", d)], w=[("tm", d, 2)])
                    yield
                    tt_("dve", tm[3][:], pci[:], Ep[d][0][:, cs_], ALU.mult, r=[pcik, ("Ep", d)], w=[("tm", d, 3)])
                    yield
                    S.op("dve", lambda e, q=q: e.scalar_tensor_tensor(out=hcur[:, q, 1, :], in0=tm[2][:], scalar=-1.0, in1=tm[3][:],
                                                                   op0=ALU.mult, op1=ALU.subtract), r=[("tm", d, 2), ("tm", d, 3)], w=[kcur])
                    yield
                it = d
                for part in range(2):
                    for q2 in range(2):
                        ib = cnt["b"] % 2
                        cnt["b"] += 1
                        for q1 in range(2):
                            q = q2 * 2 + q1
                            for sb_ in range(4):
                                tr_(pgb[ib][:, q1 * 4 + sb_, :], hcur[:, q, part, sb_ * 128:(sb_ + 1) * 128], identb[:],
                                    r=[kcur, "identb"], w=[("pgb", ib)])
                        k0 = part * 16 + q2 * 8
                        act_(hT[it][:, k0:k0 + 8, :], pgb[ib][:], AF.Copy, r=[("pgb", ib)], w=[("hT", it)])
                        yield
                py, pyk = pg_[3 * d + 1], ("pgf", 3 * d + 1)
                for ct in range(4):
                    kqs = [part * 16 + 4 * ct + sb_ for part in range(2) for sb_ in range(4)]
                    for n_, kq in enumerate(kqs):
                        mm_(py[:, ct * 128:(ct + 1) * 128], Cb[:, kq, :], hT[it][:, kq, :], n_ == 0, n_ == 7, r=["Cb", ("hT", it)], w=[pyk])
                act_(ysb[it][:], py[:].rearrange("p (c t) -> p c t", t=128), AF.Copy, r=[pyk], w=[("ysb", it)])
                yield
                S.dma("act", ysT[d][:, lo:hi].rearrange("(c p) t -> p c t", p=128), ysb[it][:], r=[("ysb", it)], w=[(f"ysT{d}", c)])
                yield

            ord0 = list(range(NT))
            ord1 = [1, 0] + list(range(NT - 1, 1, -1))
            for i in range(NT):
                run_interleaved([s5_chunk(ord0[i], 0), s5_chunk(ord1[i], 1)])
        S.barrier()
        with ExitStack() as st:
            vec = sb(st, "Gvec", [128, 2, 4])
            S.dma("sp", vec[:], ssvec_in[l], w=["Gvec"])
            gw = sb(st, "Ggw", [128, 4, 512], BF16)
            S.dma("pool", gw[:], gluw_in[l].rearrange("(kc p) n -> p kc n", p=128), w=["Ggw"])
            Y0 = [sb(st, f"GY0{i}", [128, 4, 512]) for i in range(2)]
            Y1 = [sb(st, f"GY1{i}", [128, 4, 512]) for i in range(2)]
            Uu = [sb(st, f"GU{i}", [128, 4, 512]) for i in range(2)]
            yy = sb(st, "Gyy", [128, 4, 512])
            y3 = sb(st, "Gy3", [128, 4, 512])
            y2b = sb(st, "Gy2b", [128, 4, 512], BF16)
            sg_ = sb(st, "Gsg", [128, 512])
            yo = [sb(st, f"Gyo{i}", [128, 4, 512], BF16) for i in range(2)]
            pz = [ps(st, f"pz{i}", [128, 512]) for i in range(2)]
            kz = 0
            for gi, t0 in enumerate(range(0, N, 512)):
                tw = min(512, N - t0)
                i = gi % 2
                tks = list(range(t0 // 128, (t0 + tw) // 128))
                S.dma("sp", Y0[i][:, :, 0:tw], ysT[0][:, t0:t0 + tw].rearrange("(c p) t -> p c t", p=128), r=[("ysT0", t) for t in tks], w=[("GY0", i)])
                S.dma("sp", Y1[i][:, :, 0:tw], ysT[1][:, t0:t0 + tw].rearrange("(c p) t -> p c t", p=128), r=[("ysT1", t) for t in tks], w=[("GY1", i)])
                S.dma("sp", Uu[i][:, :, 0:tw], p_ssT[:, t0:t0 + tw].rearrange("(c p) t -> p c t", p=128), r=[("p_ssT", q) for q in range(4)], w=[("GU", i)])
                tt_("pool", yy[:, :, 0:tw], Y0[i][:, :, 0:tw], Y1[i][:, :, 0:tw], ALU.add, r=[("GY0", i), ("GY1", i)], w=["Gyy"])
                for c in range(4):
                    S.op("dve", lambda e, c=c, i=i, tw=tw: e.scalar_tensor_tensor(out=yy[:, c, 0:tw], in0=Uu[i][:, c, 0:tw], scalar=vec[:, 0, c:c + 1],
                                                                            in1=yy[:, c, 0:tw], op0=ALU.mult, op1=ALU.add),
                         r=[("GU", i), "Gvec", "Gyy"], w=["Gyy"])
                tt_("pool", y3[:, :, 0:tw], yy[:, :, 0:tw], yy[:, :, 0:tw], ALU.mult, r=["Gyy"], w=["Gy3"])
                ts_("pool", y3[:, :, 0:tw], y3[:, :, 0:tw], 0.044715, 1.0, ALU.mult, ALU.add, r=["Gy3"], w=["Gy3"])
                tt_("pool", y3[:, :, 0:tw], y3[:, :, 0:tw], yy[:, :, 0:tw], ALU.mult, r=["Gy3", "Gyy"], w=["Gy3"])
                act_(y3[:, :, 0:tw], y3[:, :, 0:tw], AF.Tanh, r=["Gy3"], w=["Gy3"], scale=0.7978845608028654)
                ts_("pool", y3[:, :, 0:tw], y3[:, :, 0:tw], 1.0, 0.5, ALU.add, ALU.mult, r=["Gy3"], w=["Gy3"])
                tt_("pool", yy[:, :, 0:tw], yy[:, :, 0:tw], y3[:, :, 0:tw], ALU.mult, r=["Gyy", "Gy3"], w=["Gyy"])
                cp_("pool", y2b[:, :, 0:tw], yy[:, :, 0:tw], r=["Gyy"], w=["Gy2b"])
                for c in range(4):
                    p_ = pz[kz % 2]
                    pk_ = ("pz", kz % 2)
                    kz += 1
                    for kc in range(4):
                        mm_(p_[:, 0:tw], gw[:, kc, c * 128:(c + 1) * 128], y2b[:, kc, 0:tw], kc == 0, kc == 3, r=["Ggw", "Gy2b"], w=[pk_])
                    act_(sg_[:, 0:tw], p_[:, 0:tw], AF.Sigmoid, r=[pk_], w=["Gsg"], bias=vec[:, 1, c:c + 1])
                    tt_("dve", yo[i][:, c, 0:tw], yy[:, c, 0:tw], sg_[:, 0:tw], ALU.mult, r=["Gyy", "Gsg"], w=[("Gyo", i)])
                S.dma("sp", ycT[:, t0:t0 + tw].rearrange("(c p) t -> p c t", p=128), yo[i][:, :, 0:tw], r=[("Gyo", i)], w=[("ycT", t) for t in tks])
        S.barrier()
        if stop_after == "G":
            break

        with ExitStack() as st:
            KT = sb(st, "KT", [128, 4, N], BF16)
            S.op("pool", lambda e: e.memset(KT[64:128], 0.0), w=[("KT", t_) for t_ in range(NT)])
            Vt = sb(st, "Vt", [128, NT, 4, 65], BF16)
            gain = sb(st, "gain", [128, 1280])
            S.dma("sp", gain[:], qkgain_in[l], w=["gain"])
            S.op("pool", lambda e: e.memset(Vt[:], 1.0), w=["Vt"])
            xq = [sb(st, f"xq{i}", [128, 1280]) for i in range(2)]
            xv = [sb(st, f"xv{i}", [128, 256]) for i in range(2)]
            sq = sb(st, "sq", [128, 1280])
            ssq = sb(st, "ssq", [128, 20])
            xn = sb(st, "xn", [128, 1280])
            t1 = sb(st, "t1", [128, 1280])
            t2 = sb(st, "t2", [128, 1280])
            xb = sb(st, "xb", [128, 1280], BF16)
            cs = [sb(st, f"cs{i}", [128, 2, 64]) for i in range(2)]
            trp = [ps(st, f"trp{i}", [64, 8, 128], BF16) for i in range(1)]

            def qk_prep(tt, src, c0, nh, i):
                w_ = nh * 64
                v3 = lambda t: t[:, 0:w_].rearrange("p (h d) -> p h d", d=64)
                tt_("pool", sq[:, 0:w_], src[:, 0:w_], src[:, 0:w_], ALU.mult, r=[("xq", i)], w=["sq"])
                red_("dve", ssq[:, 0:nh], v3(sq), r=["sq"], w=["ssq"])
                rstd_(ssq[:, 0:nh], ssq[:, 0:nh], 1.0 / 64, RMS_EPS, r=["ssq"], w=["ssq"])
                tt_("pool", v3(xn), v3(src), ssq[:, 0:nh].unsqueeze(2).broadcast_to([128, nh, 64]), ALU.mult,
                    r=[("xq", i), "ssq"], w=["xn"])
                if tt < 2:
                    tt_("pool", xb[:, 0:w_], xn[:, 0:w_], gain[:, c0:c0 + w_], ALU.mult, r=["xn", "gain"], w=["xb"])
                    return
                tt_("pool", xn[:, 0:w_], xn[:, 0:w_], gain[:, c0:c0 + w_], ALU.mult, r=["xn", "gain"], w=["xn"])
                cst = cs[tt % 2]
                ck = ("cs", tt % 2)
                cosb = cst[:, 0, :].unsqueeze(1).broadcast_to([128, nh, 64])
                tt_("pool", v3(t1), v3(xn), cosb, ALU.mult, r=["xn", ck], w=["t1"])
                v4 = lambda t: t[:, 0:w_].rearrange("p (h a b i) -> p (h a) b i", a=2, b=2, i=16)
                sn4 = cst[:, 1, :].rearrange("p (a b i) -> p a b i", a=2, b=2)
                for hb in range(2):
                    snb = sn4[:, :, hb, :].unsqueeze(1).broadcast_to([128, nh, 2, 16])
                    o4 = t2[:, 0:w_].rearrange("p (h a b i) -> p h a b i", a=2, b=2, i=16)[:, :, :, hb, :]
                    i4 = xn[:, 0:w_].rearrange("p (h a b i) -> p h a b i", a=2, b=2, i=16)[:, :, :, 1 - hb, :]
                    tt_("dve", o4, i4, snb, ALU.mult, r=["xn", ck], w=["t2"])
                tt_("pool", xb[:, 0:w_], t1[:, 0:w_], t2[:, 0:w_], ALU.add, r=["t1", "t2"], w=["xb"])

            def load_cs(tt):
                if tt >= 2:
                    r0 = (tt - 2) * 128
                    S.dma("sp", cs[tt % 2][:, 0, :], rope_cos[r0:r0 + 128, :], w=[("cs", tt % 2)])
                    S.dma("sp", cs[tt % 2][:, 1, :], rope_sin[r0:r0 + 128, :], w=[("cs", tt % 2)])

            for tt in range(NT):
                i = tt % 2
                S.dma("sp", xq[i][:, 0:256], p_tm[tt * 128:(tt + 1) * 128, C_AT + 1024:C_AT + 1280], r=[("p_tm", tt)], w=[("xq", i)])
                S.dma("sp", xv[i][:], p_tm[tt * 128:(tt + 1) * 128, C_AT + 1280:C_AT + 1536], r=[("p_tm", tt)], w=[("xv", i)])
                load_cs(tt)
                qk_prep(tt, xq[i], 1024, 4, i)
                for g in range(4):
                    tr_(trp[0][:, g, :], xb[:, g * 64:(g + 1) * 64], identb[:], r=["xb", "identb"], w=[("trp", 0)])
                act_(KT[0:64, :, tt * 128:(tt + 1) * 128], trp[0][:, 0:4, :], AF.Copy, r=[("trp", 0)], w=[("KT", tt)])
                cp_("pool", Vt[:, tt, :, 0:64], xv[i][:].rearrange("p (g d) -> p g d", d=64), r=[("xv", i)], w=[("Vt", tt)])
            QT = [sb(st, f"QT{i}", [128, 16, 128], BF16) for i in range(2)]
            for i_ in range(2):
                S.op("pool", lambda e, i_=i_: e.memset(QT[i_][64:128], 0.0), w=[("QT", i_)])
            NB = 4
            Pt = [sb(st, f"Pt{i}", [128, 512], BF16) for i in range(NB)]
            sps = [ps(st, f"sps{i}", [128, 512]) for i in range(NB)]
            ops_ = [ps(st, f"ops{i}", [65, 512]) for i in range(2)]
            bcp = ps(st, "bcp", [64, 512])
            rd = sb(st, "rd", [65, 512])
            osb = sb(st, "osb", [64, 512])
            yb = [sb(st, f"yb{i}", [64, 512], BF16) for i in range(2)]
            kq = 0
            ko = 0
            pending = []
            for tt in range(NT):
                i = tt % 2
                S.dma("sp", xq[i][:, 0:1024], p_tm[tt * 128:(tt + 1) * 128, C_AT:C_AT + 1024], r=[("p_tm", tt)], w=[("xq", i)])
                load_cs(tt)
                qk_prep(tt, xq[i], 0, 16, i)
                for hh in range(2):
                    for h in range(8):
                        hd = hh * 8 + h
                        tr_(trp[0][:, h, :], xb[:, hd * 64:(hd + 1) * 64], identb[:], r=["xb", "identb"], w=[("trp", 0)])
                    act_(QT[i][0:64, hh * 8:(hh + 1) * 8, :], trp[0][:], AF.Copy, r=[("trp", 0)], w=[("QT", i)])
                kts = [0, 1] if tt < 2 else list(range(NT))
                for g in range(4):
                    o_ = ops_[ko % 2]
                    okey = ("ops", ko % 2)
                    ybt = yb[ko % 2]
                    ykey = ("yb", ko % 2)
                    ko += 1
                    PIPE = 3
                    nk = len(kts)
                    base = kq
                    kq += nk
                    for n_ in range(nk + PIPE):
                        if n_ == min(6, nk - 1) and pending:
                            pending.pop(0)()
                        if n_ < nk:
                            kt = kts[n_]
                            j = (base + n_) % NB
                            mm_(sps[j][:], KT[:, g, kt * 128:(kt + 1) * 128], QT[i][:, 4 * g:4 * g + 4, :].rearrange("p h t -> p (h t)"),
                                True, True, r=[("KT", kt), ("QT", i)], w=[("sps", j)])
                            act_(Pt[j][:], sps[j][:], AF.Exp, r=[("sps", j)], w=[("Pt", j)], scale=0.125)
                        m_ = n_ - PIPE
                        if m_ >= 0:
                            kt = kts[m_]
                            j = (base + m_) % NB
                            mm_(o_[:], Vt[:, kt, g, :], Pt[j][:], m_ == 0, m_ == nk - 1, r=[("Vt", kt), ("Pt", j)], w=[okey])
                    def epilogue(o_=o_, okey=okey, ybt=ybt, ykey=ykey, g=g, tt=tt):
                        S.op("dve", lambda e: e.reciprocal(out=rd[64:65, :], in_=o_[64:65, :]), r=[okey], w=["rd"])
                        mm_(bcp[:], ones[64:65, 0:64], rd[64:65, :], True, True, r=["ones", "rd"], w=["bcp"])
                        cp_("pool" if False else "dve", osb[:], o_[0:64, :], r=[okey], w=["osb"])
                        tt_("dve", ybt[:], osb[:], bcp[:], ALU.mult, r=["osb", "bcp"], w=[ykey])
                        S.dma("sp", ybT[4 * g:4 * g + 4, :, tt * 128:(tt + 1) * 128].rearrange("h d t -> d h t"),
                              ybt[:].rearrange("p (h t) -> p h t", t=128), r=[ykey], w=[("ybT", tt)])
                    pending.append(epilogue)
            while pending:
                pending.pop(0)()
        S.barrier()
        if stop_after == "F":
            break

        stHI = ExitStack()
        gtb = sb(stHI, "gtb", [128, 2, 2, D])
        with ExitStack() as st:
            dg = [sb(st, f"dg{i}", [128, 128]) for i in range(2)]
            pgt = [ps(st, f"pgt{i}", [128, 512]) for i in range(2)]
            kk_ = 0
            for wh, c0 in ((0, 16), (1, 40)):
                for g in range(2):
                    for hf in range(2):
                        p_ = pgt[kk_ % 2]
                        pk_ = ("pgt", kk_ % 2)
                        kk_ += 1
                        for j in range(4):
                            kc = hf * 4 + j
                            i = kc % 2
                            ts_("dve", dg[i][:], ident[:], modT[:, l, c0 + kc, g:g + 1], None, ALU.mult, None, r=["ident", "modT"], w=[("dg", i)])
                            mm_(p_[:, j * 128:(j + 1) * 128], ones[:], dg[i][:], True, True, r=["ones", ("dg", i)], w=[pk_])
                        act_(gtb[:, wh, g, hf * 512:(hf + 1) * 512], p_[:], AF.Copy, r=[pk_], w=["gtb"])
        S.barrier()

        def ln_tail(st_tiles, h, lng, lnb, outt, hk, ok):
            hsq, st4 = st_tiles
            red_("dve", st4[:, 0:1], h[:], r=[hk], w=["st4"])
            tt_("dve", hsq[:], h[:], h[:], ALU.mult, r=[hk], w=["hsq"])
            red_("dve", st4[:, 1:2], hsq[:], r=["hsq"], w=["st4"])
            ts_("dve", st4[:, 0:1], st4[:, 0:1], 1.0 / D, None, ALU.mult, None, r=["st4"], w=["st4"])
            tt_("dve", st4[:, 2:3], st4[:, 0:1], st4[:, 0:1], ALU.mult, r=["st4"], w=["st4"])
            S.op("dve", lambda e: e.scalar_tensor_tensor(out=st4[:, 3:4], in0=st4[:, 1:2], scalar=1.0 / D, in1=st4[:, 2:3],
                                                       op0=ALU.mult, op1=ALU.subtract), r=["st4"], w=["st4"])
            rstd_(st4[:, 3:4], st4[:, 3:4], 1.0, LN_EPS, r=["st4"], w=["st4"])
            ts_("dve", hsq[:], h[:], st4[:, 0:1], st4[:, 3:4], ALU.subtract, ALU.mult, r=[hk, "st4"], w=["hsq"])
            tt_("dve", hsq[:], hsq[:], lng, ALU.mult, r=["hsq", "lnw"], w=["hsq"])
            tt_("dve", outt[:], hsq[:], lnb, ALU.add, r=["hsq", "lnw"], w=[ok])

        with ExitStack() as st:
            wa = sb(st, "Hwa", [128, 4, D], BF16)
            wbb = sb(st, "Hwb", [128, 8, D], BF16)
            wc = sb(st, "Hwc", [128, 4, D], BF16)
            wo = sb(st, "Hwo", [128, 8, D], BF16)
            lnw = sb(st, "Hln", [128, 2, D])
            S.dma("pool", wa[:], proja_in[l].rearrange("(kc p) n -> p kc n", p=128), w=["Hwa"])
            S.dma("pool", wbb[:], projb_in[l].rearrange("(kc p) n -> p kc n", p=128), w=["Hwb"])
            S.dma("pool", wc[:], projc_in[l].rearrange("(kc p) n -> p kc n", p=128), w=["Hwc"])
            S.dma("pool", wo[:], wout_in[l].rearrange("(kc p) n -> p kc n", p=128), w=["Hwo"])
            S.dma("sp", lnw[:], ln_in[l, 0:2].rearrange("k p c -> p k c"), w=["lnw"])
            TW = 256
            ya_t = [sb(st, f"Hya{i}", [128, 4, TW], BF16) for i in range(2)]
            yb_t = [sb(st, f"Hyb{i}", [128, 8, TW], BF16) for i in range(2)]
            yc_t = [sb(st, f"Hyc{i}", [128, 4, TW], BF16) for i in range(2)]
            g_t = [sb(st, f"Hg{i}", [128, 24, TW], BF16) for i in range(2)]
            mg = sb(st, "Hmg", [128, TW])
            t_ = sb(st, "Ht", [128, TW])
            mT = sb(st, "HmT", [128, 8, TW], BF16)
            xt_ = [sb(st, f"Hx{i}", [128, D]) for i in range(2)]
            hh = sb(st, "Hh", [128, D])
            hsq = sb(st, "Hhsq", [128, D])
            st4 = sb(st, "Hst4", [128, 4])
            xo = [sb(st, f"Hxo{i}", [128, D]) for i in range(2)]
            x2T = [sb(st, f"Hx2T{i}", [128, 8, 128], BF16) for i in range(2)]
            pH = [ps(st, f"pH{i}", [128, 512]) for i in range(6)]
            pT2 = [ps(st, f"pT2{i}", [128, 4, 128]) for i in range(2)]
            kp = [0]

            def PH():
                i = kp[0] % 6
                kp[0] += 1
                return pH[i], ("pH", i)

            for gi, t0 in enumerate(range(0, N, TW)):
                i = gi % 2
                tks = [t0 // 128, t0 // 128 + 1]
                S.dma("sp", ya_t[i][:], yaT[:, t0:t0 + TW].rearrange("(kc p) t -> p kc t", p=128), r=[("yaT", t) for t in tks], w=[("Hya", i)])
                S.dma("sp", yb_t[i][:], ybT.rearrange("h d t -> (h d) t")[:, t0:t0 + TW].rearrange("(kc p) t -> p kc t", p=128),
                      r=[("ybT", t) for t in tks], w=[("Hyb", i)])
                S.dma("sp", yc_t[i][:], ycT[:, t0:t0 + TW].rearrange("(kc p) t -> p kc t", p=128), r=[("ycT", t) for t in tks], w=[("Hyc", i)])
                S.dma("sp", g_t[i][:], gT[:, t0:t0 + TW].rearrange("(kc p) t -> p kc t", p=128), r=[("gT", c) for c in range(24)], w=[("Hg", i)])
                for m_ in range(8):
                    ms = slice(m_ * 128, (m_ + 1) * 128)
                    pa, pak = PH()
                    for kc in range(4):
                        mm_(pa[:, 0:TW], wa[:, kc, ms], ya_t[i][:, kc, :], kc == 0, kc == 3, r=["Hwa", ("Hya", i)], w=[pak])
                    pb_, pbk = PH()
                    for kc in range(8):
                        mm_(pb_[:, 0:TW], wbb[:, kc, ms], yb_t[i][:, kc, :], kc == 0, kc == 7, r=["Hwb", ("Hyb", i)], w=[pbk])
                    pc_, pck = PH()
                    for kc in range(4):
                        mm_(pc_[:, 0:TW], wc[:, kc, ms], yc_t[i][:, kc, :], kc == 0, kc == 3, r=["Hwc", ("Hyc", i)], w=[pck])
                    tt_("dve", mg[:], pa[:, 0:TW], g_t[i][:, m_, :], ALU.mult, r=[pak, ("Hg", i)], w=["Hmg"])
                    tt_("dve", t_[:], pb_[:, 0:TW], g_t[i][:, 8 + m_, :], ALU.mult, r=[pbk, ("Hg", i)], w=["Ht"])
                    tt_("dve", mg[:], mg[:], t_[:], ALU.add, r=["Hmg", "Ht"], w=["Hmg"])
                    tt_("dve", t_[:], pc_[:, 0:TW], g_t[i][:, 16 + m_, :], ALU.mult, r=[pck, ("Hg", i)], w=["Ht"])
                    tt_("dve", mT[:, m_, :], mg[:], t_[:], ALU.add, r=["Hmg", "Ht"], w=["HmT"])
                for sub in range(2):
                    tt = t0 // 128 + sub
                    g = 1 if tt < 2 else 0
                    j = tt % 2
                    ts_l = slice(sub * 128, (sub + 1) * 128)
                    S.dma("sp", xt_[j][:], xs[tt * 128:(tt + 1) * 128, :], r=[("xs", tt)], w=[("Hx", j)])
                    for hf in range(2):
                        p_, pk_ = PH()
                        for kc in range(8):
                            mm_(p_[:], mT[:, kc, ts_l], wo[:, kc, hf * 512:(hf + 1) * 512], kc == 0, kc == 7, r=["HmT", "Hwo"], w=[pk_])
                        tt_("dve", hh[:, hf * 512:(hf + 1) * 512], p_[:], gtb[:, 0, g, hf * 512:(hf + 1) * 512], ALU.mult, r=[pk_, "gtb"], w=["Hh"])
                    S.op("dve", lambda e, j=j: e.scalar_tensor_tensor(out=hh[:], in0=xt_[j][:], scalar=ALPHA, in1=hh[:], op0=ALU.mult, op1=ALU.add),
                         r=[("Hx", j), "Hh"], w=["Hh"])
                    ln_tail((hsq, st4), hh, lnw[:, 0, :], lnw[:, 1, :], xo[j], "Hh", ("Hxo", j))
                    S.dma("act", xmid[tt * 128:(tt + 1) * 128, :], xo[j][:], r=[("Hxo", j)], w=[("xmid", tt)])
                    for hf in range(2):
                        for q in range(4):
                            kc = hf * 4 + q
                            tr_(pT2[hf][:, q, :], xo[j][:, kc * 128:(kc + 1) * 128], ident[:], r=[("Hxo", j), "ident"], w=[("pT2", hf)])
                        for q in range(4):
                            kc = hf * 4 + q
                            act_(x2T[j][:, kc, :], pT2[hf][:, q, :], AF.Identity, r=[("pT2", hf), "modT"], w=[("Hx2T", j)],
                                 scale=modT[:, l, 32 + kc, g:g + 1], bias=modT[:, l, 24 + kc, g:g + 1])
                    S.dma("act", xm2T[:, tt * 128:(tt + 1) * 128].rearrange("(kc p) t -> p kc t", p=128), x2T[j][:], r=[("Hx2T", j)], w=[("xm2T", tt)])
        S.barrier()
        if stop_after == "H":
            stHI.close()
            break

        with ExitStack() as st:
            w1 = sb(st, "Iw1", [128, 8, 4 * D], BF16)
            for q in range(4):
                S.dma("pool", w1[:, :, q * D:(q + 1) * D], w1_in[l, :, q * D:(q + 1) * D].rearrange("(kc p) n -> p kc n", p=128), w=["Iw1"])
            xi = [sb(st, f"Ixi{i}", [128, 8, 512], BF16) for i in range(2)]
            rl = [sb(st, f"Irl{i}", [128, 512]) for i in range(2)]
            ho = [sb(st, f"Iho{i}", [128, 8, 512], BF16) for i in range(2)]
            pI = [ps(st, f"pI{i}", [128, 512]) for i in range(4)]
            kp = 0
            ko = 0
            for gi, t0 in enumerate(range(0, N, 512)):
                tw = min(512, N - t0)
                i = gi % 2
                tks = list(range(t0 // 128, (t0 + tw) // 128))
                S.dma("sp", xi[i][:, :, 0:tw], xm2T[:, t0:t0 + tw].rearrange("(kc p) t -> p kc t", p=128), r=[("xm2T", t) for t in tks], w=[("Ixi", i)])
                for c8 in range(4):
                    o_ = ho[ko % 2]
                    okey = ("Iho", ko % 2)
                    ko += 1
                    for c in range(8):
                        cb = c8 * 8 + c
                        p_ = pI[kp % 4]
                        pk_ = ("pI", kp % 4)
                        r_ = rl[kp % 2]
                        rk_ = ("Irl", kp % 2)
                        kp += 1
                        for kc in range(8):
                            mm_(p_[:, 0:tw], w1[:, kc, cb * 128:(cb + 1) * 128], xi[i][:, kc, 0:tw], kc == 0, kc == 7, r=["Iw1", ("Ixi", i)], w=[pk_])
                        act_(r_[:, 0:tw], p_[:, 0:tw], AF.Relu, r=[pk_], w=[rk_])
                        tt_("dve", o_[:, c, 0:tw], r_[:, 0:tw], r_[:, 0:tw], ALU.mult, r=[rk_], w=[okey])
                    S.dma("sp", h1T[c8 * 1024:(c8 + 1) * 1024, t0:t0 + tw].rearrange("(c p) t -> p c t", p=128), o_[:, :, 0:tw], r=[okey],
                          w=[("h1T", t, c8) for t in tks])
        S.barrier()
        with ExitStack() as st:
            w2 = sb(st, "Iw2", [128, 32, D], BF16)
            for q in range(4):
                S.dma("pool", w2[:, q * 8:(q + 1) * 8, :], w2_in[l, q * 1024:(q + 1) * 1024, :].rearrange("(kc p) n -> p kc n", p=128), w=["Iw2"])
            lnw = sb(st, "Iln", [128, 2, D])
            S.dma("sp", lnw[:], ln_in[l, 2:4].rearrange("k p c -> p k c"), w=["lnw"])
            h1 = [sb(st, f"Ih1{i}", [128, 32, 128], BF16) for i in range(2)]
            xm_ = [sb(st, f"Ixm{i}", [128, D]) for i in range(2)]
            hh = sb(st, "Ihh", [128, D])
            hsq = sb(st, "Ihsq", [128, D])
            st4 = sb(st, "Ist4", [128, 4])
            xo = [sb(st, f"Ixo{i}", [128, D]) for i in range(2)]
            pJ = [ps(st, f"pJ{i}", [128, 512]) for i in range(4)]
            kp = 0
            for tt in range(NT):
                i = tt % 2
                g = 1 if tt < 2 else 0
                lo, hi = tt * 128, (tt + 1) * 128
                S.dma("sp", h1[i][:], h1T[:, lo:hi].rearrange("(c p) t -> p c t", p=128), r=[("h1T", tt, c8) for c8 in range(4)], w=[("Ih1", i)])
                S.dma("sp", xm_[i][:], xmid[lo:hi, :], r=[("xmid", tt)], w=[("Ixm", i)])
                for hf in range(2):
                    p_ = pJ[kp % 4]
                    pk_ = ("pJ", kp % 4)
                    kp += 1
                    for kc in range(32):
                        mm_(p_[:], h1[i][:, kc, :], w2[:, kc, hf * 512:(hf + 1) * 512], kc == 0, kc == 31, r=[("Ih1", i), "Iw2"], w=[pk_])
                    tt_("dve", hh[:, hf * 512:(hf + 1) * 512], p_[:], gtb[:, 1, g, hf * 512:(hf + 1) * 512], ALU.mult, r=[pk_, "gtb"], w=["Ihh"])
                S.op("dve", lambda e, i=i: e.scalar_tensor_tensor(out=hh[:], in0=xm_[i][:], scalar=ALPHA, in1=hh[:], op0=ALU.mult, op1=ALU.add),
                     r=[("Ixm", i), "Ihh"], w=["Ihh"])
                ln_tail((hsq, st4), hh, lnw[:, 0, :], lnw[:, 1, :], xo[i], "Ihh", ("Ixo", i))
                S.dma("act", xs[lo:hi, :], xo[i][:], r=[("Ixo", i)], w=[("xs", tt)])
                if l == n_layers - 1 and tt >= 2:
                    S.dma("act", out[lo - NCTX:hi - NCTX, :], xo[i][:], r=[("Ixo", i)], w=[("out", tt)])
        S.barrier()
        stHI.close()
        if stop_after == "I":
            break

    S.barrier()
    top.close()
    return nc, S


def _prep_inputs(inp, b):
    f = lambda a: np.ascontiguousarray(a, dtype=np.float32)
    m = {}
    m["x_in"] = f(np.concatenate([inp["ctx"][b], inp["x"][b]], axis=0))
    cT = np.stack([inp["c"][b].reshape(8, 128).T, inp["c_ctx"].reshape(8, 128).T], axis=-1)
    m["cT"] = f(cT)
    m["ident"] = np.eye(128, dtype=np.float32)
    m["mod_w"] = f(inp["mod_w"])
    m["mod_bT"] = f(inp["mod_b"].reshape(DEPTH, 48, 128).transpose(0, 2, 1))
    m["w_in"] = f(inp["w_in"])
    m["qkgain"] = f(np.broadcast_to(np.concatenate([np.tile(inp["attn_q_gain"], (1, 16)), np.tile(inp["attn_k_gain"], (1, 4))],
                                                   axis=1)[:, None, :], (DEPTH, 128, 1280)))
    rows = NLAT // 64
    row = np.repeat(np.arange(rows, dtype=np.float32), 64)
    col = np.tile(np.arange(64, dtype=np.float32), rows)
    inv = (np.float32(10000.0) ** (-np.arange(16, dtype=np.float32) / np.float32(16))).astype(np.float32)
    ang = np.stack([row, col], axis=-1)[:, :, None] * inv
    ang = np.broadcast_to(ang[:, :, None, :], (NLAT, 2, 2, 16)).reshape(NLAT, 64)
    sgn = np.broadcast_to(np.array([-1.0, 1.0], np.float32)[None, None, :, None], (NLAT, 2, 2, 16)).reshape(NLAT, 64)
    m["rope_cos"] = f(np.cos(ang))
    m["rope_sin"] = f(np.sin(ang) * sgn)
    m["ones"] = np.ones((128, 128), np.float32)
    ii = np.arange(128)
    mS0 = (ii[:, None] < ii[None, :]).astype(np.float32)
    mS1 = (ii[:, None] > ii[None, :]).astype(np.float32)
    eye = np.eye(128, dtype=np.float32)
    m["masks"] = f(np.stack([mS0, mS1, mS0 + eye, mS1 + eye]))
    bc = lambda a: np.broadcast_to(a[:, None, :], (DEPTH, 128, a.shape[-1]))
    m["mu_bc"] = f(bc(inp["rwkv_mu"]))
    m["rwp_bc"] = f(np.stack([bc(inp[k]) for k in ("rwkv_k_k", "rwkv_k_a", "rwkv_r_k", "rwkv_gn_w", "rwkv_gn_b")], axis=1))
    m["lrb"] = f(np.concatenate([inp["rwkv_w0"], inp["rwkv_a0"]], axis=1))
    m["lrw"] = f(np.concatenate([inp["rwkv_w_up"], inp["rwkv_a_up"]], axis=2).transpose(0, 2, 1, 3))
    m["g_up"] = f(inp["rwkv_g_up"])
    rep = lambda a: np.broadcast_to(a[:, :, None, :], (DEPTH, 2, 128, 2048))
    m["ss_tab"] = f(np.stack([rep(inp["ssm_a_re"].reshape(DEPTH, 2, 2048)), rep(inp["ssm_a_im"].reshape(DEPTH, 2, 2048)),
                              rep(np.repeat(inp["ssm_log_dt"], 64, axis=-1))], axis=2))
    p_ = np.arange(128, dtype=np.float32)
    m["mcol"] = f(np.stack([p_ + 1, 128 - p_, -(p_ + 1), -(128 - p_)], axis=1))
    bblk = np.zeros((DEPTH, 2, 4, 8, 16, 8, 64), np.float32)
    for k_, nm in enumerate(("ssm_b_re", "ssm_b_im")):
        b5 = inp[nm].reshape(DEPTH, 4, 8, 64, 16)
        for g_ in range(8):
            bblk[:, k_, :, g_, :, g_, :] = b5[:, :, g_].transpose(0, 1, 3, 2)
    m["ss_bblk"] = f(bblk.reshape(DEPTH, 2, 4, 128, 512).transpose(0, 1, 3, 2, 4))
    cblk = np.zeros((DEPTH, 2, 16, 2, 64, 8, 16), np.float32)
    for k_, nm in enumerate(("ssm_c_re", "ssm_c_im")):
        c5 = inp[nm].reshape(DEPTH, 16, 2, 16, 64)
        for gp in range(16):
            for g2 in range(2):
                gl = (2 * gp + g2) % 8
                cblk[:, k_, gp, g2, :, gl, :] = c5[:, gp, g2].transpose(0, 2, 1)
    m["ss_cblk"] = f(cblk.reshape(DEPTH, 32, 128, 128).transpose(0, 2, 1, 3))
    m["ss_vec"] = f(np.stack([inp["ssm_d"].reshape(DEPTH, 4, 128).transpose(0, 2, 1),
                              inp["ssm_glu_b"].reshape(DEPTH, 4, 128).transpose(0, 2, 1)], axis=2))
    m["glu_w"] = f(inp["ssm_glu_w"])
    for k_ in ("proj_a", "proj_b", "proj_c", "w_out", "mlp_w1", "mlp_w2"):
        m[k_] = f(inp[k_])
    m["ln_bc"] = f(np.stack([bc(inp[k_]) for k_ in ("ln1_g", "ln1_b", "ln2_g", "ln2_b")], axis=1))
    selm = np.zeros((4, 128, 128), np.float32)
    selm[0, 127, :] = 1.0
    selm[1, 0, :] = 1.0
    selm[2, 127, :] = -1.0
    selm[3, 0, :] = -1.0
    m["sel"] = selm
    return m


def kernel(**inp):
    inp = {k: np.asarray(v) for k, v in inp.items()}
    nc, _ = build()
    in_maps = [_prep_inputs(inp, b) for b in range(8)]
    res = run_bass_kernel_spmd(nc, in_maps, core_ids=list(range(8)))
    return np.stack([r["out"] for r in res.results], axis=0).astype(np.float32)
```

```python
import numpy as np
from contextlib import ExitStack
import concourse.bass as bass
import concourse.mybir as mybir
from concourse.bass_utils import run_bass_kernel_spmd

F32 = mybir.dt.float32
BF16 = mybir.dt.bfloat16
AF = mybir.ActivationFunctionType
ALU = mybir.AluOpType
AX = mybir.AxisListType

D = 1024
DEPTH = 4
NCTX = 256
NLAT = 4096
N = NCTX + NLAT
NT = N // 128
INC = 6912
C_RW, C_AT, C_SS, C_G = 0, 1792, 3328, 3840
ALPHA = (2 * DEPTH) ** 0.25
LN_EPS, RMS_EPS, GN_EPS = 1e-5, 1e-6, 64e-5
PI = float(np.pi)
DSTAGE = 9
DSUB = 9


class _Res:
    __slots__ = ("lw", "rd")

    def __init__(self):
        self.lw = None
        self.rd = {}


class _Eng:
    def __init__(self, name, h, sem, selfsync):
        self.name, self.h, self.sem, self.selfsync = name, h, sem, selfsync
        self.cnt = 0
        self.seen = {}


class Sched:
    def __init__(self, nc, stack, ndma=(("sp", 8), ("pool", 6), ("act", 8))):
        self.nc = nc
        hs = {"pe": nc.tensor, "act": nc.scalar, "dve": nc.vector, "pool": nc.gpsimd, "sp": nc.sync}
        self.engs = {}
        for n, h in hs.items():
            sem = stack.enter_context(nc.semaphore("s_" + n))
            self.engs[n] = _Eng(n, h, sem, selfsync=(n in ("dve", "act", "pool")))
        self.dsems, self.dnext = {}, {}
        for q, k in ndma:
            self.dsems[q] = [[stack.enter_context(nc.semaphore(f"d_{q}{i}")), 0] for i in range(k)]
            self.dnext[q] = 0
        self.res = {}
        self.nins = 0

    def R(self, k):
        r = self.res.get(k)
        if r is None:
            r = self.res[k] = _Res()
        return r

    def _deps(self, r, w):
        deps = {}

        def need(ev):
            if ev is not None and deps.get(id(ev[0]), (None, 0))[1] < ev[1]:
                deps[id(ev[0])] = ev

        for k in r:
            need(self.R(k).lw)
        for k in w:
            R = self.R(k)
            need(R.lw)
            for ev in R.rd.values():
                need(ev)
        return deps

    def _wait(self, E, deps):
        for s, v in deps.values():
            if s is E.sem and not E.selfsync:
                continue
            if E.seen.get(id(s), 0) < v:
                E.seen[id(s)] = v
                E.h.wait_ge(s, v)

    def _mark(self, r, w, ev):
        for k in r:
            self.R(k).rd[id(ev[0])] = ev
        for k in w:
            R = self.R(k)
            R.lw = ev
            R.rd = {}

    def op(self, en, fn, r=(), w=()):
        E = self.engs[en]
        self._wait(E, self._deps(r, w))
        E.cnt += 1
        fn(E.h).then_inc(E.sem, 1)
        self.nins += 1
        self._mark(r, w, (E.sem, E.cnt))

    def dma(self, q, out, in_, r=(), w=(), **kw):
        E = self.engs[q]
        lst = self.dsems[q]
        ds = lst[self.dnext[q] % len(lst)]
        self.dnext[q] += 1
        deps = self._deps(r, w)
        if ds[1] > 0:
            deps[id(ds[0])] = (ds[0], ds[1])
        self._wait(E, deps)
        ds[1] += 16
        E.h.dma_start(out=out, in_=in_, **kw).then_inc(ds[0], 16)
        self.nins += 1
        self._mark(r, w, (ds[0], ds[1]))

    def barrier(self):
        evs = [(E.sem, E.cnt) for E in self.engs.values() if E.cnt]
        for lst in self.dsems.values():
            evs += [(s, v) for s, v in lst if v]
        for E in self.engs.values():
            for s, v in evs:
                if s is E.sem and not E.selfsync:
                    continue
                if E.seen.get(id(s), 0) < v:
                    E.seen[id(s)] = v
                    E.h.wait_ge(s, v)
        self.res = {}


def build(n_layers=DEPTH, stop_after=None, dbg=()):
    nc = bass.Bass("TRN2", target_bir_lowering=False)
    top = ExitStack()
    S = Sched(nc, top)
    dram_in = {}

    def din(name, shape, dt=F32):
        dram_in[name] = nc.dram_tensor(name, list(shape), dt, kind="ExternalInput").ap()
        return dram_in[name]

    def dscr(name, shape, dt=F32):
        kind = "ExternalOutput" if name in dbg else "Internal"
        return nc.dram_tensor(name, list(shape), dt, kind=kind).ap()

    x_in = din("x_in", [N, D])
    cT_in = din("cT", [128, 8, 2])
    ident_in = din("ident", [128, 128])
    mod_w = din("mod_w", [DEPTH, D, 6 * D])
    mod_bT = din("mod_bT", [DEPTH, 128, 48])
    w_in = din("w_in", [DEPTH, D, INC])
    out = nc.dram_tensor("out", [NLAT, D], F32, kind="ExternalOutput").ap()
    qkgain_in = din("qkgain", [DEPTH, 128, 1280])
    rope_cos = din("rope_cos", [NLAT, 64])
    rope_sin = din("rope_sin", [NLAT, 64])
    ones_in = din("ones", [128, 128])
    masks_in = din("masks", [4, 128, 128])
    mu_in = din("mu_bc", [DEPTH, 128, 1792])
    rwp_in = din("rwp_bc", [DEPTH, 5, 128, 512])
    lrb_in = din("lrb", [DEPTH, 4, 512])
    lrw_in = din("lrw", [DEPTH, 128, 2, 512])
    gup_in = din("g_up", [DEPTH, 128, 512])
    sstab_in = din("ss_tab", [DEPTH, 2, 3, 128, 2048])
    mcol_in = din("mcol", [128, 4])
    bblk_in = din("ss_bblk", [DEPTH, 2, 128, 4, 512])
    cblk_in = din("ss_cblk", [DEPTH, 128, 32, 128])
    ssvec_in = din("ss_vec", [DEPTH, 128, 2, 4])
    gluw_in = din("glu_w", [DEPTH, 512, 512])
    sel_in = din("sel", [4, 128, 128])
    proja_in = din("proj_a", [DEPTH, 512, D])
    projb_in = din("proj_b", [DEPTH, D, D])
    projc_in = din("proj_c", [DEPTH, 512, D])
    wout_in = din("w_out", [DEPTH, D, D])
    ln_in = din("ln_bc", [DEPTH, 4, 128, D])
    w1_in = din("mlp_w1", [DEPTH, D, 4 * D])
    w2_in = din("mlp_w2", [DEPTH, 4 * D, D])

    xs = dscr("xs", [N, D])
    p_tm = dscr("p_tm", [N, C_SS])
    p_ssT = dscr("p_ssT", [512, N])
    gT = dscr("gT", [3072, N], BF16)
    ybT = dscr("ybT", [16, 64, N], BF16)
    rw_s = {nm: dscr("rw_" + nm, [N, 512]) for nm in ("r", "v", "kk", "lw0", "lw1", "b0", "b1", "kd0", "kd1", "g", "y0", "y1")}
    yaT = dscr("yaT", [512, N], BF16)
    ysT = [dscr(f"ysT{d}", [512, N]) for d in range(2)]
    ycT = dscr("ycT", [512, N], BF16)
    xmid = dscr("xmid", [N, D])
    xm2T = dscr("xm2T", [D, N], BF16)
    h1T = dscr("h1T", [4 * D, N], BF16)

    uid = [0]

    def sb(st, name, shape, dt=F32):
        uid[0] += 1
        return st.enter_context(nc.sbuf_tensor(f"sb{uid[0]}_{name}", list(shape), dt))

    def ps(st, name, shape, dt=F32):
        uid[0] += 1
        ne = 512 if dt == F32 else 1024
        t = st.enter_context(nc.psum_tensor(f"ps{uid[0]}_{name}", [128, ne], dt))
        n = int(np.prod(shape[1:]))
        v = t[0:shape[0], 0:n]
        if len(shape) == 3:
            v = v.rearrange("p (a b) -> p a b", b=shape[2])
        return v

    def run_interleaved(gens):
        gens = list(gens)
        while gens:
            for g_ in list(gens):
                try:
                    next(g_)
                except StopIteration:
                    gens.remove(g_)

    def tt_(en, o, a, b, op, r, w):
        S.op(en, lambda e: e.tensor_tensor(out=o, in0=a, in1=b, op=op), r=r, w=w)

    def ts_(en, o, a, s1, s2, op0, op1, r, w):
        if s2 is None:
            S.op(en, lambda e: e.tensor_scalar(out=o, in0=a, scalar1=s1, scalar2=None, op0=op0), r=r, w=w)
        else:
            S.op(en, lambda e: e.tensor_scalar(out=o, in0=a, scalar1=s1, scalar2=s2, op0=op0, op1=op1), r=r, w=w)

    def act_(o, a, func, r, w, **kw):
        S.op("act", lambda e: e.activation(out=o, in_=a, func=func, **kw), r=r, w=w)

    def mm_(o, lhsT, rhs, start, stop, r, w):
        S.op("pe", lambda e: e.matmul(o, lhsT=lhsT, rhs=rhs, start=start, stop=stop), r=r, w=w)

    def rstd_(o, a, scale, eps, r, w):
        ts_("dve", o, a, scale, eps, ALU.mult, ALU.add, r=r, w=w)
        act_(o, o, AF.Sqrt, r=w, w=w)
        S.op("dve", lambda e: e.reciprocal(out=o, in_=o), r=w, w=w)

    def tr_(o, a, idn, r, w):
        S.op("pe", lambda e: e.transpose(o, a, idn), r=r, w=w)

    def cp_(en, o, a, r, w):
        S.op(en, lambda e: e.tensor_copy(out=o, in_=a), r=r, w=w)

    def red_(en, o, a, r, w):
        S.op(en, lambda e: e.tensor_reduce(out=o, in_=a, axis=AX.X, op=ALU.add), r=r, w=w)

    ones = sb(top, "ones", [128, 128])
    S.dma("sp", ones[:], ones_in, w=["ones"])
    masks = sb(top, "masks", [128, 4, 128])
    S.dma("sp", masks[:], masks_in.rearrange("m p t -> p m t"), w=["masks"])
    bsum = sb(top, "bsum", [128, NT, 8])
    ident = sb(top, "ident", [128, 128])
    identb = sb(top, "identb", [128, 128], BF16)
    modT = sb(top, "modT", [128, DEPTH, 48, 2])
    csil = sb(top, "csil", [128, 8, 2])
    S.dma("sp", ident[:], ident_in, w=["ident"])
    S.op("act", lambda e: e.activation(out=identb[:], in_=ident[:], func=AF.Copy), r=["ident"], w=["identb"])
    S.dma("sp", csil[:], cT_in, w=["csil"])
    S.op("act", lambda e: e.activation(out=csil[:], in_=csil[:], func=AF.Silu), r=["csil"], w=["csil"])

    for i in range(0, NT, 2):
        S.dma("sp", xs[i * 128:(i + 2) * 128, :], x_in[i * 128:(i + 2) * 128, :], w=[("xs", i), ("xs", i + 1)])

    with ExitStack() as st:
        wm = [sb(st, f"modw{i}", [128, 8, 512]) for i in range(2)]
        mbt = sb(st, "mbt", [128, DEPTH, 48])
        pm = ps(st, "pm", [128, 48, 2])
        S.dma("sp", mbt[:], mod_bT.rearrange("l p c -> p l c"), w=["mbt"])
        it = 0
        for l in range(n_layers):
            for cb in range(12):
                wt = wm[it % 2]
                key = ("modw", it % 2)
                it += 1
                S.dma("sp" if it % 2 else "pool", wt[:],
                      mod_w[l, :, cb * 512:(cb + 1) * 512].rearrange("(kc p) n -> p kc n", p=128), w=[key])
                for j in range(4):
                    cc = cb * 4 + j
                    for kc in range(8):
                        S.op("pe", lambda e, wt=wt, kc=kc, j=j, cc=cc: e.matmul(
                            pm[:, cc, :], lhsT=wt[:, kc, j * 128:(j + 1) * 128], rhs=csil[:, kc, :],
                            start=(kc == 0), stop=(kc == 7)), r=[key, "csil"], w=["pm"])
            for g in range(2):
                S.op("dve", lambda e, l=l, g=g: e.tensor_tensor(out=modT[:, l, :, g], in0=pm[:, :, g], in1=mbt[:, l, :],
                                                                op=ALU.add), r=["pm", "mbt"], w=["modT"])
            for c0 in (8, 32):
                S.op("dve", lambda e, l=l, c0=c0: e.tensor_scalar(out=modT[:, l, c0:c0 + 8, :], in0=modT[:, l, c0:c0 + 8, :],
                                                                  scalar1=1.0, scalar2=None, op0=ALU.add),
                     r=["modT"], w=["modT"])
    S.barrier()
    if stop_after == "A":
        dbg_t = nc.dram_tensor("dbg_modT", [128, DEPTH * 96], F32, kind="ExternalOutput").ap()
        S.dma("sp", dbg_t, modT[:].rearrange("p l c g -> p (l c g)"), r=["modT"])

    for l in range(n_layers):
        if stop_after == "A":
            break
        with ExitStack() as st:
            xmT = sb(st, "xmT", [128, 8, N], BF16)
            xt = [sb(st, f"xt{i}", [128, D]) for i in range(2)]
            pT = [ps(st, f"pT{i}", [128, 4, 128]) for i in range(2)]
            for tt in range(NT):
                g = 1 if tt < 2 else 0
                x_t = xt[tt % 2]
                S.dma("sp", x_t[:], xs[tt * 128:(tt + 1) * 128, :], r=[("xs", tt)], w=[("xt", tt % 2)])
                for hf in range(2):
                    for j in range(4):
                        kc = hf * 4 + j
                        S.op("pe", lambda e, x_t=x_t, kc=kc, hf=hf, j=j: e.transpose(
                            pT[hf][:, j, :], x_t[:, kc * 128:(kc + 1) * 128], ident[:]),
                            r=[("xt", tt % 2), "ident"], w=[("pT", hf)])
                    for j in range(4):
                        kc = hf * 4 + j
                        S.op("act", lambda e, kc=kc, hf=hf, j=j, tt=tt, g=g: e.activation(
                            out=xmT[:, kc, tt * 128:(tt + 1) * 128], in_=pT[hf][:, j, :], func=AF.Identity,
                            scale=modT[:, l, 8 + kc, g:g + 1], bias=modT[:, l, kc, g:g + 1]),
                            r=[("pT", hf), "modT"], w=[("xmT", tt)])
            wb = [sb(st, f"wb{i}", [128, 8, 512], BF16) for i in range(3)]
            pp = [ps(st, f"pp{i}", [128, 512]) for i in range(4)]
            stg = [sb(st, f"stg{i}", [128, 512]) for i in range(4)]
            stgT = [sb(st, f"stgT{i}", [128, N]) for i in range(2)]
            stgG = [sb(st, f"stgG{i}", [128, N], BF16) for i in range(2)]
            blks = [(c0, min(512, C_SS - c0)) for c0 in range(0, C_SS, 512)] + \
                   [(c0, min(512, INC - c0)) for c0 in range(C_SS, INC, 512)]
            k = 0
            kf = 0
            for bi, (c0, cw) in enumerate(blks):
                w_t = wb[bi % 3]
                wkey = ("wb", bi % 3)
                S.dma("pool", w_t[:, :, 0:cw], w_in[l, :, c0:c0 + cw].rearrange("(kc p) n -> p kc n", p=128), w=[wkey])
                if c0 < C_SS:
                    for tt in range(NT):
                        i = k % 4
                        k += 1
                        for kc in range(8):
                            S.op("pe", lambda e, i=i, kc=kc, tt=tt, w_t=w_t, cw=cw: e.matmul(
                                pp[i][:, 0:cw], lhsT=xmT[:, kc, tt * 128:(tt + 1) * 128], rhs=w_t[:, kc, 0:cw],
                                start=(kc == 0), stop=(kc == 7)), r=[("xmT", tt), wkey], w=[("pp", i)])
                        if i % 2 == 0:
                            S.op("act", lambda e, i=i, cw=cw: e.activation(out=stg[i][:, 0:cw], in_=pp[i][:, 0:cw], func=AF.Copy),
                                 r=[("pp", i)], w=[("stg", i)])
                        else:
                            S.op("dve", lambda e, i=i, cw=cw: e.tensor_copy(out=stg[i][:, 0:cw], in_=pp[i][:, 0:cw]),
                                 r=[("pp", i)], w=[("stg", i)])
                        S.dma("sp", p_tm[tt * 128:(tt + 1) * 128, c0:c0 + cw], stg[i][:, 0:cw], r=[("stg", i)],
                              w=[("p_tm", tt)])
                else:
                    for j in range(cw // 128):
                        cc = c0 + j * 128
                        is_g = cc >= C_G
                        f = kf % 2
                        kf += 1
                        dstt = stgG[f] if is_g else stgT[f]
                        skey = ("stgG", f) if is_g else ("stgT", f)
                        for t0 in range(0, N, 512):
                            tw = min(512, N - t0)
                            i = k % 4
                            k += 1
                            for kc in range(8):
                                S.op("pe", lambda e, i=i, kc=kc, t0=t0, tw=tw, w_t=w_t, j=j: e.matmul(
                                    pp[i][:, 0:tw], lhsT=w_t[:, kc, j * 128:(j + 1) * 128], rhs=xmT[:, kc, t0:t0 + tw],
                                    start=(kc == 0), stop=(kc == 7)),
                                    r=[("xmT", t) for t in range(t0 // 128, (t0 + tw) // 128)] + [wkey], w=[("pp", i)])
                            if is_g:
                                S.op("act", lambda e, i=i, t0=t0, tw=tw, dstt=dstt: e.activation(
                                    out=dstt[:, t0:t0 + tw], in_=pp[i][:, 0:tw], func=AF.Sigmoid), r=[("pp", i)], w=[skey])
                            else:
                                S.op("dve", lambda e, i=i, t0=t0, tw=tw, dstt=dstt: e.tensor_copy(
                                    out=dstt[:, t0:t0 + tw], in_=pp[i][:, 0:tw]), r=[("pp", i)], w=[skey])
                        if is_g:
                            S.dma("sp", gT[cc - C_G:cc - C_G + 128, :], dstt[:], r=[skey], w=[("gT", (cc - C_G) // 128)])
                        else:
                            S.dma("sp", p_ssT[cc - C_SS:cc - C_SS + 128, :], dstt[:], r=[skey], w=[("p_ssT", (cc - C_SS) // 128)])
        S.barrier()
        if stop_after == "B":
            break


        with ExitStack() as st:
            mu = sb(st, "mu", [128, 1792])
            rwp = sb(st, "rwp", [128, 5, 512])
            lrb = sb(st, "lrb", [1, 4, 512])
            lrw = sb(st, "lrw", [128, 2, 512])
            gup = sb(st, "gup", [128, 512])
            S.dma("sp", mu[:], mu_in[l], w=["mu"])
            S.dma("sp", rwp[:], rwp_in[l].rearrange("k p c -> p k c"), w=["rwp"])
            S.dma("sp", lrb[:], lrb_in[l:l + 1], w=["lrb"])
            S.dma("sp", lrw[:], lrw_in[l], w=["lrw"])
            S.dma("sp", gup[:], gup_in[l], w=["gup"])
            P0 = [sb(st, f"P0{i}", [128, 1792]) for i in range(2)]
            Pm = [sb(st, f"Pm{i}", [128, 1792]) for i in range(2)]
            Pp = [sb(st, f"Pp{i}", [128, 1792]) for i in range(2)]
            tl2 = [sb(st, f"tl{i}", [128, 1792]) for i in range(2)]
            pl = [sb(st, f"pl{i}", [128, 1792]) for i in range(2)]
            lrT2 = [sb(st, f"lrT{i}", [128, 128]) for i in range(2)]
            gsT2 = [sb(st, f"gsT{i}", [128, 128]) for i in range(2)]
            ptr = [ps(st, f"ptr{i}", [128, 128]) for i in range(2)]
            pq = [ps(st, f"pq{i}", [128, 512]) for i in range(5)]
            sg2 = [[sb(st, f"sg{i}{q}", [128, 512]) for q in range(4)] for i in range(2)]
            o52 = [[sb(st, f"o5{i}{q}", [128, 512]) for q in range(10)] for i in range(2)]
            kx2 = [sb(st, f"kx{i}", [128, 512]) for i in range(2)]
            sq52 = [sb(st, f"sq5{i}", [128, 512]) for i in range(2)]
            ss82 = [sb(st, f"ss8{i}", [128, 8]) for i in range(2)]

            def c_tile(tt):
                i = tt % 2
                tl, lrT, gsT, sg, o5, kx, sq5, ss8 = tl2[i], lrT2[i], gsT2[i], sg2[i], o52[i], kx2[i], sq52[i], ss82[i]
                lo, hi = tt * 128, (tt + 1) * 128
                first = tt in (0, 2)
                last = tt in (1, NT - 1)
                S.dma("sp", P0[i][:], p_tm[lo:hi, 0:1792], r=[("p_tm", tt)], w=[("P0", i)])
                if first:
                    S.op("pool", lambda e, i=i: e.memset(Pm[i][:], 0.0), w=[("Pm", i)])
                    S.dma("sp", Pm[i][1:128, :], p_tm[lo:hi - 1, 0:1792], r=[("p_tm", tt)], w=[("Pm", i)])
                else:
                    S.dma("sp", Pm[i][:], p_tm[lo - 1:hi - 1, 0:1792], r=[("p_tm", tt), ("p_tm", tt - 1)], w=[("Pm", i)])
                if last:
                    S.op("pool", lambda e, i=i: e.memset(Pp[i][:], 0.0), w=[("Pp", i)])
                    S.dma("sp", Pp[i][0:127, :], p_tm[lo + 1:hi, 0:1792], r=[("p_tm", tt)], w=[("Pp", i)])
                else:
                    S.dma("sp", Pp[i][:], p_tm[lo + 1:hi + 1, 0:1792], r=[("p_tm", tt), ("p_tm", tt + 1)], w=[("Pp", i)])
                p_ = pl[i]
                pk = ("pl", i)
                tt_("dve", tl[:], Pm[i][:], Pp[i][:], ALU.add, r=[("Pm", i), ("Pp", i)], w=[("tl", i)])
                S.op("dve", lambda e, i=i: e.scalar_tensor_tensor(out=tl[:], in0=tl[:], scalar=0.5, in1=P0[i][:], op0=ALU.mult,
                                                                op1=ALU.subtract), r=[("tl", i), ("P0", i)], w=[("tl", i)])
                tt_("dve", tl[:], tl[:], mu[:], ALU.mult, r=[("tl", i), "mu"], w=[("tl", i)])
                tt_("dve", p_[:], tl[:], P0[i][:], ALU.add, r=[("tl", i), ("P0", i)], w=[pk])
                S.dma("act", rw_s["r"][lo:hi, :], p_[:, 0:512], r=[pk], w=[("rw_r", tt)])
                S.dma("act", rw_s["v"][lo:hi, :], p_[:, 1024:1536], r=[pk], w=[("rw_v", tt)])
                yield
                tr_(ptr[0][:], p_[:, 1536:1664], ident[:], r=[pk, "ident"], w=[("ptr", 0)])
                tr_(ptr[1][:], p_[:, 1664:1792], ident[:], r=[pk, "ident"], w=[("ptr", 1)])
                act_(lrT[0:64, :], ptr[0][0:64, :], AF.Tanh, r=[("ptr", 0)], w=[("lrT", i)])
                act_(lrT[64:128, :], ptr[0][64:128, :], AF.Copy, r=[("ptr", 0)], w=[("lrT", i)])
                act_(gsT[:], ptr[1][:], AF.Sigmoid, r=[("ptr", 1)], w=[("gsT", i)])
                yield
                for d in range(2):
                    mm_(pq[d][:], ones[0:1, :], lrb[0:1, d, :], True, False, r=["ones", "lrb"], w=[("pq", d)])
                    mm_(pq[d][:], lrT[0:64, :], lrw[0:64, d, :], False, True, r=[("lrT", i), "lrw"], w=[("pq", d)])
                    mm_(pq[2 + d][:], ones[0:1, :], lrb[0:1, 2 + d, :], True, False, r=["ones", "lrb"], w=[("pq", 2 + d)])
                    mm_(pq[2 + d][:], lrT[64:128, :], lrw[64:128, d, :], False, True, r=[("lrT", i), "lrw"], w=[("pq", 2 + d)])
                mm_(pq[4][:], gsT[:], gup[:], True, True, r=[("gsT", i), "gup"], w=[("pq", 4)])
                for q in range(4):
                    act_(sg[q][:], pq[q][:], AF.Sigmoid, r=[("pq", q)], w=[("sg", i, q)])
                o = o5
                ok = lambda q: ("o5", i, q)
                cp_("dve", o[9][:], pq[4][:], r=[("pq", 4)], w=[ok(9)])
                S.dma("act", rw_s["g"][lo:hi, :], o[9][:], r=[ok(9)], w=[("rw_g", tt)])
                yield
                k_ = p_[:, 512:1024]
                r_ = p_[:, 0:512]
                h3 = lambda a: a.rearrange("p (h d) -> p h d", d=64)
                tt_("dve", kx[:], k_, rwp[:, 0, :], ALU.mult, r=[pk, "rwp"], w=[("kx", i)])
                tt_("dve", sq5[:], kx[:], kx[:], ALU.mult, r=[("kx", i)], w=[("sq5", i)])
                red_("dve", ss8[:], h3(sq5[:]), r=[("sq5", i)], w=[("ss8", i)])
                ts_("dve", ss8[:], ss8[:], 1e-24, None, ALU.max, None, r=[("ss8", i)], w=[("ss8", i)])
                act_(ss8[:], ss8[:], AF.Sqrt, r=[("ss8", i)], w=[("ss8", i)])
                S.op("dve", lambda e: e.reciprocal(out=ss8[:], in_=ss8[:]), r=[("ss8", i)], w=[("ss8", i)])
                tt_("dve", h3(o[0][:]), h3(kx[:]), ss8[:].unsqueeze(2).broadcast_to([128, 8, 64]), ALU.mult, r=[("kx", i), ("ss8", i)], w=[ok(0)])
                S.dma("act", rw_s["kk"][lo:hi, :], o[0][:], r=[ok(0)], w=[("rw_kk", tt)])
                yield
                for d in range(2):
                    ts_("pool", o[1 + d][:], sg[d][:], -0.6065306597126334, None, ALU.mult, None, r=[("sg", i, d)], w=[ok(1 + d)])
                    S.dma("act", rw_s[f"lw{d}"][lo:hi, :], o[1 + d][:], r=[ok(1 + d)], w=[(f"rw_lw{d}", tt)])
                    tt_("dve", o[3 + d][:], o[0][:], sg[2 + d][:], ALU.mult, r=[ok(0), ("sg", i, 2 + d)], w=[ok(3 + d)])
                    S.dma("act", rw_s[f"b{d}"][lo:hi, :], o[3 + d][:], r=[ok(3 + d)], w=[(f"rw_b{d}", tt)])
                    S.op("dve", lambda e, d=d: e.scalar_tensor_tensor(out=o[5 + d][:], in0=sg[2 + d][:], scalar=-1.0, in1=rwp[:, 1, :],
                                                                     op0=ALU.add, op1=ALU.mult), r=[("sg", i, 2 + d), "rwp"], w=[ok(5 + d)])
                    S.op("dve", lambda e, d=d, k_=k_: e.scalar_tensor_tensor(out=o[5 + d][:], in0=o[5 + d][:], scalar=1.0, in1=k_,
                                                                            op0=ALU.add, op1=ALU.mult), r=[ok(5 + d), pk], w=[ok(5 + d)])
                    S.dma("act", rw_s[f"kd{d}"][lo:hi, :], o[5 + d][:], r=[ok(5 + d)], w=[(f"rw_kd{d}", tt)])
                tt_("pool", o[7][:], o[5][:], o[6][:], ALU.add, r=[ok(5), ok(6)], w=[ok(7)])
                tt_("dve", o[8][:], r_, rwp[:, 2, :], ALU.mult, r=[pk, "rwp"], w=[ok(8)])
                tt_("pool", o[8][:], o[8][:], o[7][:], ALU.mult, r=[ok(8), ok(7)], w=[ok(8)])
                red_("dve", bsum[:, tt, :], h3(o[8][:]), r=[ok(8)], w=[("bsum", tt)])

            for t2 in range(0, NT, 2):
                run_interleaved([c_tile(t2), c_tile(t2 + 1)])
        S.barrier()
        if stop_after == "C":
            break

        with ExitStack() as st:
            pf = [ps(st, f"pf{i}", [128, 512]) for i in range(8)]
            npf = [0]

            npfd = [0, 0]

            def PFd(d):
                i = 4 * d + npfd[d] % 4
                npfd[d] += 1
                return pf[i], ("pf", i)

            Bd = []
            for d in range(2):
                B = {}
                for nm in ("r", "v", "kk", "lw", "b", "kd", "cum", "e1", "e2", "e3", "e4", "abarf", "bbarf", "kbarf", "rbarf"):
                    B[nm] = sb(st, f"D{d}{nm}", [128, 512])
                for nm in ("abar", "Bt", "Kt", "Vb", "Z", "U"):
                    B[nm] = sb(st, f"D{d}{nm}", [128, 512], BF16)
                for nm in ("abarT", "bbarT", "kbarT"):
                    B[nm] = sb(st, f"D{d}{nm}", [128, 4, 128], BF16)
                for nm in ("AakT", "MrbT", "MrkT", "abarZ", "bbarZ", "rbarZ", "Rb"):
                    B[nm] = sb(st, f"D{d}{nm}", [128, 8, 128], BF16)
                for nm in ("AabT", "Aab", "P0", "P1", "Q0", "Q1", "R0", "R1"):
                    B[nm] = sb(st, f"D{d}{nm}", [128, 8, 128], F32)
                for nm in ("abarZ", "bbarZ", "rbarZ"):
                    S.op("pool", lambda e, B=B, nm=nm: e.memset(B[nm][:], 0.0), w=[(nm, d)])
                B["rbT1"] = sb(st, f"D{d}rbT1", [128, 8, 128], BF16)
                B["WT"] = sb(st, f"D{d}WT", [128, 8, 128], BF16)
                S.op("pool", lambda e, B=B: e.memset(B["rbT1"][:], 0.0), w=[("rbT1", d)])
                S.op("pool", lambda e, B=B: e.memset(B["WT"][:], 0.0), w=[("WT", d)])
                B["gl"] = sb(st, f"D{d}gl", [64, 8])
                B["Sf"] = sb(st, f"D{d}Sf", [64, 8, 64])
                B["Stmp"] = sb(st, f"D{d}Stmp", [64, 8, 64])
                B["Sbf"] = sb(st, f"D{d}Sbf", [128, 8, 64], BF16)
                S.op("pool", lambda e, B=B: e.memset(B["Sf"][:], 0.0), w=[("Sf", d)])
                S.op("pool", lambda e, B=B: e.memset(B["Sbf"][:], 0.0), w=[("Sbf", d)])
                Bd.append(B)

            def rw_chunk(c, d):
                B = Bd[d]
                K_ = lambda n: (n, d)
                lo, hi = c * 128, (c + 1) * 128
                hs = lambda h: slice(h * 64, (h + 1) * 64)
                for nm, src in (("r", "r"), ("v", "v"), ("kk", "kk"), ("lw", f"lw{d}"), ("b", f"b{d}"), ("kd", f"kd{d}")):
                    S.dma("sp", B[nm][:], rw_s[src][lo:hi, :], r=[("rw_" + src, c)], w=[K_(nm)])
                    yield
                pc, pck = PFd(d)
                mm_(pc[:], masks[:, 2 + d, :], B["lw"][:], True, True, r=["masks", K_("lw")], w=[pck])
                pt, ptk = PFd(d)
                mm_(pt[:], ones[:], B["lw"][:], True, True, r=["ones", K_("lw")], w=[ptk])
                act_(B["cum"][:], pc[:], AF.Copy, r=[pck], w=[K_("cum")])
                yield
                act_(B["e1"][:], B["cum"][:], AF.Exp, r=[K_("cum")], w=[K_("e1")])
                yield
                act_(B["e2"][:], B["cum"][:], AF.Exp, r=[K_("cum")], w=[K_("e2")], scale=-1.0)
                yield
                tt_("pool", B["e3"][:], B["cum"][:], B["lw"][:], ALU.subtract, r=[K_("cum"), K_("lw")], w=[K_("e3")])
                yield
                act_(B["e3"][:], B["e3"][:], AF.Exp, r=[K_("e3")], w=[K_("e3")])
                yield
                tt_("dve", B["e4"][:], pt[:], B["cum"][:], ALU.subtract, r=[ptk, K_("cum")], w=[K_("e4")])
                yield
                act_(B["e4"][:], B["e4"][:], AF.Exp, r=[K_("e4")], w=[K_("e4")])
                yield
                S.op("dve", lambda e: e.scalar_tensor_tensor(out=B["abarf"][:], in0=B["kk"][:], scalar=-1.0, in1=B["e3"][:],
                                                            op0=ALU.mult, op1=ALU.mult), r=[K_("kk"), K_("e3")], w=[K_("abarf")])
                yield
                tt_("pool", B["bbarf"][:], B["b"][:], B["e2"][:], ALU.mult, r=[K_("b"), K_("e2")], w=[K_("bbarf")])
                yield
                tt_("dve", B["kbarf"][:], B["kd"][:], B["e2"][:], ALU.mult, r=[K_("kd"), K_("e2")], w=[K_("kbarf")])
                yield
                tt_("pool", B["rbarf"][:], B["r"][:], B["e1"][:], ALU.mult, r=[K_("r"), K_("e1")], w=[K_("rbarf")])
                yield
                tt_("dve", B["Bt"][:], B["b"][:], B["e4"][:], ALU.mult, r=[K_("b"), K_("e4")], w=[K_("Bt")])
                yield
                tt_("pool", B["Kt"][:], B["kd"][:], B["e4"][:], ALU.mult, r=[K_("kd"), K_("e4")], w=[K_("Kt")])
                yield
                cp_("pool", B["Vb"][:], B["v"][:], r=[K_("v")], w=[K_("Vb")])
                yield
                pg, pgk = PFd(d)
                for h in range(8):
                    mm_(pg[0:64, 2 * h:2 * h + 2], B["lw"][:, hs(h)], ones[:, 0:2], True, True, r=[K_("lw"), "ones"], w=[pgk])
                act_(B["gl"][:], pg[0:64, 0:16].rearrange("p (h two) -> p h two", two=2)[:, :, 0], AF.Exp, r=[pgk], w=[K_("gl")])
                yield
                if DSTAGE < 1:
                    return
                for nm in ("abar", "bbar", "kbar", "rbar")[:DSUB]:
                    p2, p2k = PFd(d)
                    for j in range(4):
                        tr_(p2[:, j * 128:(j + 1) * 128], B[nm + "f"][:, j * 128:(j + 1) * 128], ident[:], r=[K_(nm + "f"), "ident"], w=[p2k])
                    p3 = p2[:].rearrange("p (j t) -> p j t", t=128)
                    if nm != "rbar":
                        act_(B[nm + "T"][:], p3, AF.Copy, r=[p2k], w=[K_(nm + "T")])
                        yield
                    if nm != "kbar":
                        z4 = lambda lo_: B[nm + "Z"][lo_:lo_ + 64].rearrange("p (j two) t -> p j two t", two=2)
                        act_(z4(0)[:, :, 0, :], p3[0:64], AF.Copy, r=[p2k], w=[K_(nm + "Z")])
                        yield
                        cp_("dve", z4(64)[:, :, 1, :], p3[64:128], r=[p2k], w=[K_(nm + "Z")])
                        yield
                for hh in range(2 if DSUB > 4 else 0):
                    p2, p2k = PFd(d)
                    for h4 in range(4):
                        h = hh * 4 + h4
                        tr_(p2[0:64, h4 * 128:(h4 + 1) * 128], B["rbarf"][:, hs(h)], ident[:], r=[K_("rbarf"), "ident"], w=[p2k])
                    act_(B["rbT1"][0:64, hh * 4:(hh + 1) * 4, :], p2[0:64, :].rearrange("p (j t) -> p j t", t=128), AF.Copy, r=[p2k], w=[K_("rbT1")])
                    yield
                cp_("dve", B["abar"][:], B["abarf"][:], r=[K_("abarf")], w=[K_("abar")])
                yield
                if DSTAGE < 2:
                    return
                for nm, lt, rt, mk in (("AabT", "bbarT", "abarZ", d), ("Aab", "abarT", "bbarZ", 1 - d), ("AakT", "kbarT", "abarZ", d),
                                       ("MrbT", "bbarT", "rbarZ", 2 + d), ("MrkT", "kbarT", "rbarZ", 2 + d)):
                    for hh in range(2):
                        p_, pk_ = PFd(d)
                        for h4 in range(4):
                            h = hh * 4 + h4
                            mm_(p_[:, h4 * 128:(h4 + 1) * 128], B[lt][:, h // 2, :], B[rt][:, h, :], True, True,
                                r=[K_(lt), K_(rt)], w=[pk_])
                        tt_("dve", B[nm][:, hh * 4:(hh + 1) * 4, :], p_[:].rearrange("p (h t) -> p h t", t=128),
                            masks[:, mk, :].unsqueeze(1).broadcast_to([128, 4, 128]), ALU.mult, r=[pk_, "masks"], w=[K_(nm)])
                        yield
                if DSTAGE < 3:
                    return
                P_, Q_ = "AabT", "Aab"
                tt_("pool", B["R0"][:], B["AabT"][:], ident[:].unsqueeze(1).broadcast_to([128, 8, 128]), ALU.add,
                    r=[K_("AabT"), "ident"], w=[K_("R0")])
                yield
                Rc = "R0"
                for lev in range(6):
                    Pn, Qn, Rn = f"P{lev % 2}", f"Q{lev % 2}", f"R{(lev + 1) % 2}"
                    for hh in range(2):
                        if lev < 5:
                            p_, pk_ = PFd(d)
                            for h4 in range(4):
                                h = hh * 4 + h4
                                mm_(p_[:, h4 * 128:(h4 + 1) * 128], B[Q_][:, h, :], B[P_][:, h, :], True, True, r=[K_(Q_), K_(P_)], w=[pk_])
                            act_(B[Pn][:, hh * 4:(hh + 1) * 4, :], p_[:].rearrange("p (h t) -> p h t", t=128), AF.Copy, r=[pk_], w=[K_(Pn)])
                            yield
                        p_, pk_ = PFd(d)
                        for h4 in range(4):
                            h = hh * 4 + h4
                            mm_(p_[:, h4 * 128:(h4 + 1) * 128], B[P_][:, h, :], B[Q_][:, h, :], True, True, r=[K_(Q_), K_(P_)], w=[pk_])
                        cp_("dve", B[Qn][:, hh * 4:(hh + 1) * 4, :], p_[:].rearrange("p (h t) -> p h t", t=128), r=[pk_], w=[K_(Qn)])
                        yield
                    for hh in range(2):
                        p_, pk_ = PFd(d)
                        for h4 in range(4):
                            h = hh * 4 + h4
                            mm_(p_[:, h4 * 128:(h4 + 1) * 128], B[Qn][:, h, :], B[Rc][:, h, :], True, True, r=[K_(Qn), K_(Rc)], w=[pk_])
                        tt_("dve", B[Rn][:, hh * 4:(hh + 1) * 4, :], p_[:].rearrange("p (h t) -> p h t", t=128),
                            B[Rc][:, hh * 4:(hh + 1) * 4, :], ALU.add, r=[pk_, K_(Rc)], w=[K_(Rn)])
                        yield
                    P_, Q_, Rc = Pn, Qn, Rn
                if DSTAGE < 4:
                    return
                cp_("pool", B["Rb"][:], B[Rc][:], r=[K_(Rc)], w=[K_("Rb")])
                yield
                Rc = "Rb"
                p_, pk_ = PFd(d)
                for h in range(8):
                    mm_(p_[:, hs(h)], B["AakT"][:, h, :], B["Vb"][:, hs(h)], True, True, r=[K_("AakT"), K_("Vb")], w=[pk_])
                act_(B["Z"][:], p_[:], AF.Copy, r=[pk_], w=[K_("Z")])
                yield
                p_, pk_ = PFd(d)
                for h in range(8):
                    mm_(p_[:, hs(h)], B[Rc][:, h, :], B["Z"][:, hs(h)], True, True, r=[K_(Rc), K_("Z")], w=[pk_])
                cp_("dve", B["e1"][:], p_[:], r=[pk_], w=[K_("e1")])
                yield
                for hh in range(2):
                    p_, pk_ = PFd(d)
                    for h4 in range(4):
                        h = hh * 4 + h4
                        mm_(p_[0:64, h4 * 128:(h4 + 1) * 128], B["abar"][:, hs(h)], B[Rc][:, h, :], True, True, r=[K_("abar"), K_(Rc)], w=[pk_])
                    act_(B["WT"][0:64, hh * 4:(hh + 1) * 4, :], p_[0:64, :].rearrange("p (h t) -> p h t", t=128), AF.Copy, r=[pk_], w=[K_("WT")])
                    yield
                if DSTAGE < 5:
                    return
                p_, pk_ = PFd(d)
                for h in range(8):
                    mm_(p_[:, hs(h)], B["WT"][:, h, :], B["Sbf"][:, h, :], True, True, r=[K_("WT"), K_("Sbf")], w=[pk_])
                tt_("dve", B["U"][:], p_[:], B["e1"][:], ALU.add, r=[pk_, K_("e1")], w=[K_("U")])
                yield
                py, pyk = PFd(d)
                for h in range(8):
                    mm_(py[:, hs(h)], B["rbT1"][:, h, :], B["Sbf"][:, h, :], True, False, r=[K_("rbT1"), K_("Sbf")], w=[pyk])
                    mm_(py[:, hs(h)], B["MrbT"][:, h, :], B["U"][:, hs(h)], False, False, r=[K_("MrbT"), K_("U")], w=[pyk])
                    mm_(py[:, hs(h)], B["MrkT"][:, h, :], B["Vb"][:, hs(h)], False, True, r=[K_("MrkT"), K_("Vb")], w=[pyk])
                act_(B["e2"][:], py[:], AF.Copy, r=[pyk], w=[K_("e2")])
                yield
                S.dma("act", rw_s[f"y{d}"][lo:hi, :], B["e2"][:], r=[K_("e2")], w=[(f"rw_y{d}", c)])
                yield
                pS, pSk = PFd(d)
                for h in range(8):
                    mm_(pS[0:64, hs(h)], B["Kt"][:, hs(h)], B["Vb"][:, hs(h)], True, False, r=[K_("Kt"), K_("Vb")], w=[pSk])
                    mm_(pS[0:64, hs(h)], B["Bt"][:, hs(h)], B["U"][:, hs(h)], False, True, r=[K_("Bt"), K_("U")], w=[pSk])
                tt_("pool", B["Stmp"][:], B["Sf"][:], B["gl"][:].unsqueeze(2).broadcast_to([64, 8, 64]), ALU.mult,
                    r=[K_("Sf"), K_("gl")], w=[K_("Stmp")])
                yield
                tt_("dve", B["Sf"][:], pS[0:64, :].rearrange("p (h v) -> p h v", v=64), B["Stmp"][:], ALU.add, r=[pSk, K_("Stmp")], w=[K_("Sf")])
                yield
                act_(B["Sbf"][0:64], B["Sf"][:], AF.Copy, r=[K_("Sf")], w=[K_("Sbf")])
                yield

            ord0 = list(range(NT))
            ord1 = [1, 0] + list(range(NT - 1, 1, -1))
            for i in range(NT):
                run_interleaved([rw_chunk(ord0[i], 0), rw_chunk(ord1[i], 1)])
        S.barrier()
        if stop_after == "D":
            break

        with ExitStack() as st:
            rwp = sb(st, "rwpE", [128, 5, 512])
            S.dma("sp", rwp[:], rwp_in[l].rearrange("k p c -> p k c"), w=["rwpE"])
            Y0 = [sb(st, f"EY0{i}", [128, 512]) for i in range(2)]
            Y1 = [sb(st, f"EY1{i}", [128, 512]) for i in range(2)]
            Vv = [sb(st, f"EV{i}", [128, 512]) for i in range(2)]
            Gg = [sb(st, f"EG{i}", [128, 512]) for i in range(2)]
            y = sb(st, "Ey", [128, 512])
            ysq = sb(st, "Eysq", [128, 512])
            bon = sb(st, "Ebon", [128, 512])
            yab = sb(st, "Eyab", [128, 512], BF16)
            st8 = sb(st, "Est8", [128, 4, 8])
            yT = [sb(st, f"EyT{i}", [128, 4, 128], BF16) for i in range(2)]
            pbE = [ps(st, f"pbE{i}", [128, 4, 128], BF16) for i in range(2)]
            h3 = lambda a: a.rearrange("p (h d) -> p h d", d=64)
            b3 = lambda a: a.unsqueeze(2).broadcast_to([128, 8, 64])
            for tt in range(NT):
                i = tt % 2
                lo, hi = tt * 128, (tt + 1) * 128
                S.dma("sp", Y0[i][:], rw_s["y0"][lo:hi, :], r=[("rw_y0", tt)], w=[("EY0", i)])
                S.dma("sp", Y1[i][:], rw_s["y1"][lo:hi, :], r=[("rw_y1", tt)], w=[("EY1", i)])
                S.dma("sp", Vv[i][:], rw_s["v"][lo:hi, :], r=[("rw_v", tt)], w=[("EV", i)])
                S.dma("sp", Gg[i][:], rw_s["g"][lo:hi, :], r=[("rw_g", tt)], w=[("EG", i)])
                tt_("pool", y[:], Y0[i][:], Y1[i][:], ALU.add, r=[("EY0", i), ("EY1", i)], w=["Ey"])
                red_("dve", st8[:, 0, :], h3(y[:]), r=["Ey"], w=["Est8"])
                tt_("dve", ysq[:], y[:], y[:], ALU.mult, r=["Ey"], w=["Eysq"])
                red_("dve", st8[:, 1, :], h3(ysq[:]), r=["Eysq"], w=["Est8"])
                ts_("dve", st8[:, 0, :], st8[:, 0, :], 1.0 / 64, None, ALU.mult, None, r=["Est8"], w=["Est8"])
                tt_("dve", st8[:, 2, :], st8[:, 0, :], st8[:, 0, :], ALU.mult, r=["Est8"], w=["Est8"])
                S.op("dve", lambda e: e.scalar_tensor_tensor(out=st8[:, 3, :], in0=st8[:, 1, :], scalar=1.0 / 64, in1=st8[:, 2, :],
                                                           op0=ALU.mult, op1=ALU.subtract), r=["Est8"], w=["Est8"])
                rstd_(st8[:, 3, :], st8[:, 3, :], 1.0, GN_EPS, r=["Est8"], w=["Est8"])
                tt_("dve", h3(y[:]), h3(y[:]), b3(st8[:, 0, :]), ALU.subtract, r=["Ey", "Est8"], w=["Ey"])
                tt_("dve", h3(y[:]), h3(y[:]), b3(st8[:, 3, :]), ALU.mult, r=["Ey", "Est8"], w=["Ey"])
                tt_("dve", y[:], y[:], rwp[:, 3, :], ALU.mult, r=["Ey", "rwpE"], w=["Ey"])
                tt_("pool", y[:], y[:], rwp[:, 4, :], ALU.add, r=["Ey", "rwpE"], w=["Ey"])
                tt_("pool", h3(bon[:]), h3(Vv[i][:]), b3(bsum[:, tt, :]), ALU.mult, r=[("EV", i), ("bsum", tt)], w=["Ebon"])
                tt_("dve", y[:], y[:], bon[:], ALU.add, r=["Ey", "Ebon"], w=["Ey"])
                tt_("pool", yab[:], y[:], Gg[i][:], ALU.mult, r=["Ey", ("EG", i)], w=["Eyab"])
                for j in range(4):
                    tr_(pbE[i][:, j, :], yab[:, j * 128:(j + 1) * 128], identb[:], r=["Eyab", "identb"], w=[("pbE", i)])
                act_(yT[i][:], pbE[i][:], AF.Copy, r=[("pbE", i)], w=[("EyT", i)])
                S.dma("act", yaT[:, lo:hi].rearrange("(j p) t -> p j t", p=128), yT[i][:], r=[("EyT", i)], w=[("yaT", tt)])
        S.barrier()
        if stop_after == "E":
            break

        with ExitStack() as st:
            mcol = sb(st, "mcol", [128, 4])
            S.dma("sp", mcol[:], mcol_in, w=["mcol"])
            Ep = [[sb(st, f"Ep{d}{k}", [128, 2048]) for k in range(2)] for d in range(2)]
            En = [[sb(st, f"En{d}{k}", [128, 2048]) for k in range(2)] for d in range(2)]
            Bp = [sb(st, f"Bp{d}", [128, 4, 1024], BF16) for d in range(2)]
            Cb = sb(st, "Cb", [128, 32, 128], BF16)
            S.dma("pool", Cb[:], cblk_in[l], w=["Cb"])
            trib = sb(st, "trib", [128, 2, 128], BF16)
            selb = sb(st, "selb", [128, 4, 128], BF16)
            S.dma("pool", trib[:], masks_in[2:4].rearrange("m p t -> p m t"), w=["trib"])
            S.dma("pool", selb[:], sel_in.rearrange("m p t -> p m t"), w=["selb"])
            MAGIC = 12582912.0
            C1 = 6.28125
            C2 = float(2 * np.pi - 6.28125)
            with ExitStack() as st2:
                T_ = {nm: sb(st2, "G" + nm, [128, 2048]) for nm in ("are", "aim", "dt", "x1", "x2", "arg", "k", "sn", "cs", "mg", "t1", "t2")}
                bb = [sb(st2, f"Gbb{k}", [128, 4, 512]) for k in range(2)]

                G = lambda nm: "G" + nm

                def g_tt(o, x, y, op, eng="dve"):
                    tt_(eng, T_[o][:], T_[x][:], T_[y][:], op, r=[G(x), G(y)], w=[G(o)])

                def sincos(src, shift, dst):
                    a_, k_ = T_["arg"], T_["k"]
                    ts_("dve", a_[:], T_[src][:], shift, None, ALU.add, None, r=[G(src)], w=[G("arg")])
                    ts_("dve", k_[:], a_[:], float(1 / (2 * np.pi)), MAGIC, ALU.mult, ALU.add, r=[G("arg")], w=[G("k")])
                    ts_("dve", k_[:], k_[:], -MAGIC, None, ALU.add, None, r=[G("k")], w=[G("k")])
                    S.op("dve", lambda e: e.scalar_tensor_tensor(out=a_[:], in0=k_[:], scalar=-C1, in1=a_[:], op0=ALU.mult, op1=ALU.add),
                         r=[G("k"), G("arg")], w=[G("arg")])
                    S.op("dve", lambda e: e.scalar_tensor_tensor(out=a_[:], in0=k_[:], scalar=-C2, in1=a_[:], op0=ALU.mult, op1=ALU.add),
                         r=[G("k"), G("arg")], w=[G("arg")])
                    act_(T_[dst][:], a_[:], AF.Sin, r=[G("arg")], w=[G(dst)])

                c4 = lambda t: t[:].rearrange("p (q c) -> p q c", c=512)
                for d in range(2):
                    S.dma("sp", T_["are"][:], sstab_in[l, d, 0], w=[G("are")])
                    S.dma("sp", T_["aim"][:], sstab_in[l, d, 1], w=[G("aim")])
                    S.dma("sp", T_["dt"][:], sstab_in[l, d, 2], w=[G("dt")])
                    if d == 0:
                        S.dma("sp", bb[0][:], bblk_in[l, 0], w=["Gbb0"])
                        S.dma("sp", bb[1][:], bblk_in[l, 1], w=["Gbb1"])
                    ts_("dve", T_["are"][:], T_["are"][:], -1e-4, None, ALU.min, None, r=[G("are")], w=[G("are")])
                    act_(T_["dt"][:], T_["dt"][:], AF.Exp, r=[G("dt")], w=[G("dt")])
                    g_tt("x1", "are", "dt", ALU.mult)
                    g_tt("x2", "aim", "dt", ALU.mult)
                    sincos("x2", 0.0, "sn")
                    sincos("x2", PI / 2, "cs")
                    act_(T_["mg"][:], T_["x1"][:], AF.Exp, r=[G("x1")], w=[G("mg")])
                    g_tt("t1", "sn", "mg", ALU.mult)
                    g_tt("t2", "cs", "mg", ALU.mult)
                    ts_("dve", T_["t2"][:], T_["t2"][:], -1.0, None, ALU.add, None, r=[G("t2")], w=[G("t2")])
                    g_tt("sn", "are", "are", ALU.mult)
                    g_tt("cs", "aim", "aim", ALU.mult)
                    g_tt("sn", "sn", "cs", ALU.add)
                    S.op("dve", lambda e: e.reciprocal(out=T_["sn"][:], in_=T_["sn"][:]), r=[G("sn")], w=[G("sn")])
                    g_tt("mg", "t2", "are", ALU.mult)
                    g_tt("cs", "t1", "aim", ALU.mult)
                    g_tt("mg", "mg", "cs", ALU.add)
                    g_tt("mg", "mg", "sn", ALU.mult)
                    g_tt("cs", "t1", "are", ALU.mult)
                    g_tt("k", "t2", "aim", ALU.mult)
                    g_tt("cs", "cs", "k", ALU.subtract)
                    g_tt("cs", "cs", "sn", ALU.mult)
                    tt_("dve", c4(T_["t1"]), bb[0][:], c4(T_["mg"]), ALU.mult, r=["Gbb0", G("mg")], w=[G("t1")])
                    tt_("dve", c4(T_["t2"]), bb[1][:], c4(T_["cs"]), ALU.mult, r=["Gbb1", G("cs")], w=[G("t2")])
                    tt_("pool", Bp[d][:, :, 0:512], c4(T_["t1"]), c4(T_["t2"]), ALU.subtract, r=[G("t1"), G("t2")], w=[("Bp", d)])
                    tt_("dve", c4(T_["t1"]), bb[0][:], c4(T_["cs"]), ALU.mult, r=["Gbb0", G("cs")], w=[G("t1")])
                    tt_("dve", c4(T_["t2"]), bb[1][:], c4(T_["mg"]), ALU.mult, r=["Gbb1", G("mg")], w=[G("t2")])
                    tt_("pool", Bp[d][:, :, 512:1024], c4(T_["t1"]), c4(T_["t2"]), ALU.add, r=[G("t1"), G("t2")], w=[("Bp", d)])
                    ts_("dve", T_["t1"][:], T_["x2"][:], mcol[:, d:d + 1], None, ALU.mult, None, r=[G("x2"), "mcol"], w=[G("t1")])
                    sincos("t1", 0.0, "sn")
                    sincos("t1", PI / 2, "cs")
                    act_(T_["mg"][:], T_["x1"][:], AF.Exp, r=[G("x1")], w=[G("mg")], scale=mcol[:, d:d + 1])
                    tt_("dve", Ep[d][0][:], T_["mg"][:], T_["cs"][:], ALU.mult, r=[G("mg"), G("cs")], w=[("Ep", d)])
                    tt_("pool", Ep[d][1][:], T_["mg"][:], T_["sn"][:], ALU.mult, r=[G("mg"), G("sn")], w=[("Ep", d)])
                    act_(T_["mg"][:], T_["x1"][:], AF.Exp, r=[G("x1")], w=[G("mg")], scale=mcol[:, 2 + d:3 + d])
                    tt_("dve", En[d][0][:], T_["mg"][:], T_["cs"][:], ALU.mult, r=[G("mg"), G("cs")], w=[("En", d)])
                    tt_("pool", En[d][1][:], T_["mg"][:], T_["sn"][:], ALU.mult, r=[G("mg"), G("sn")], w=[("En", d)])
            S.barrier()
            uf = [sb(st, f"Guf{i}", [128, 4, 128]) for i in range(2)]
            ub = [sb(st, f"Gub{i}", [128, 4, 128], BF16) for i in range(2)]
            hb = [[sb(st, f"Ghb{d}{i}", [128, 4, 2, 512], BF16) for i in range(2)] for d in range(2)]
            zb = [sb(st, f"Gzb{i}", [128, 2, 512], BF16) for i in range(2)]
            tmd = [[sb(st, f"Gtm{d_}{i}", [128, 512]) for i in range(4)] for d_ in range(2)]
            hT = [sb(st, f"GhT{i}", [128, 32, 128], BF16) for i in range(2)]
            ysb = [sb(st, f"Gys{i}", [128, 4, 128]) for i in range(2)]
            pg_ = [ps(st, f"pgf{i}", [128, 512]) for i in range(6)]
            pgb = [ps(st, f"pgb{i}", [128, 8, 128], BF16) for i in range(2)]
            cnt = {"f": 0, "b": 0, "z": 0, "u": 0}

            def PGF():
                i = cnt["f"] % 6
                cnt["f"] += 1
                return pg_[i], ("pgf", i)

            for d in range(2):
                for i in range(2):
                    S.op("pool", lambda e, d=d, i=i: e.memset(hb[d][i][:], 0.0), w=[("hb", d, i)])
            nstep = [0, 0]

            def s5_chunk(c, d):
                lo, hi = c * 128, (c + 1) * 128
                iu = d
                tm = tmd[d]
                S.dma("sp", uf[iu][:], p_ssT[:, lo:hi].rearrange("(q p) t -> p q t", p=128), r=[("p_ssT", q) for q in range(4)], w=[("uf", iu)])
                yield
                cp_("pool", ub[iu][:], uf[iu][:], r=[("uf", iu)], w=[("ub", iu)])
                yield
                hi_ = nstep[d] % 2
                nstep[d] += 1
                hcur, hprev = hb[d][hi_], hb[d][1 - hi_]
                kcur, kprev = ("hb", d, hi_), ("hb", d, 1 - hi_)
                for q in range(4):
                    cs_ = slice(q * 512, (q + 1) * 512)
                    pbr, pbrk = pg_[3 * d], ("pgf", 3 * d)
                    pbi, pbik = pg_[3 * d + 1], ("pgf", 3 * d + 1)
                    mm_(pbr[:], ub[iu][:, q, :], Bp[d][:, q, 0:512], True, True, r=[("ub", iu), ("Bp", d)], w=[pbrk])
                    mm_(pbi[:], ub[iu][:, q, :], Bp[d][:, q, 512:1024], True, True, r=[("ub", iu), ("Bp", d)], w=[pbik])
                    iz = d
                    z = zb[iz]
                    zk = ("zb", iz)
                    tt_("dve", tm[0][:], pbr[:], En[d][0][:, cs_], ALU.mult, r=[pbrk, ("En", d)], w=[("tm", d, 0)])
                    yield
                    tt_("dve", tm[1][:], pbi[:], En[d][1][:, cs_], ALU.mult, r=[pbik, ("En", d)], w=[("tm", d, 1)])
                    yield
                    tt_("dve", z[:, 0, :], tm[0][:], tm[1][:], ALU.add, r=[("tm", d, 0), ("tm", d, 1)], w=[zk])
                    yield
                    tt_("dve", tm[2][:], pbi[:], En[d][0][:, cs_], ALU.mult, r=[pbik, ("En", d)], w=[("tm", d, 2)])
                    yield
                    tt_("dve", tm[3][:], pbr[:], En[d][1][:, cs_], ALU.mult, r=[pbrk, ("En", d)], w=[("tm", d, 3)])
                    yield
                    tt_("dve", z[:, 1, :], tm[2][:], tm[3][:], ALU.subtract, r=[("tm", d, 2), ("tm", d, 3)], w=[zk])
                    yield
                    pcr, pcrk = pg_[3 * d + 2], ("pgf", 3 * d + 2)
                    pci, pcik = pg_[3 * d], ("pgf", 3 * d)
                    mm_(pcr[:], trib[:, d, :], z[:, 0, :], True, False, r=["trib", zk], w=[pcrk])
                    mm_(pcr[:], selb[:, d, :], hprev[:, q, 0, :], False, True, r=["selb", kprev], w=[pcrk])
                    mm_(pci[:], trib[:, d, :], z[:, 1, :], True, False, r=["trib", zk], w=[pcik])
                    mm_(pci[:], selb[:, 2 + d, :], hprev[:, q, 1, :], False, True, r=["selb", kprev], w=[pcik])
                    tt_("dve", tm[0][:], pcr[:], Ep[d][0][:, cs_], ALU.mult, r=[pcrk, ("Ep", d)], w=[("tm", d, 0)])
                    yield
                    tt_("dve", tm[1][:], pci[:], Ep[d][1][:, cs_], ALU.mult, r=[pcik, ("Ep", d)], w=[("tm", d, 1)])
                    yield
                    tt_("dve", hcur[:, q, 0, :], tm[0][:], tm[1][:], ALU.subtract, r=[("tm", d, 0), ("tm", d, 1)], w=[kcur])
                    yield
                    tt_("dve", tm[2][:], pcr[:], Ep[d][1][:, cs_], ALU.mult, r=[pcrk, ("Ep", d)], w=[("tm", d, 2)])
                    yield
                    tt_("dve", tm[3][:], pci[:], Ep[d][0][:, cs_], ALU.mult, r=[pcik, ("Ep", d)], w=[("tm", d, 3)])
                    yield
                    S.op("dve", lambda e, q=q: e.scalar_tensor_tensor(out=hcur[:, q, 1, :], in0=tm[2][:], scalar=-1.0, in1=tm[3][:],
                                                                   op0=ALU.mult, op1=ALU.subtract), r=[("tm", d, 2), ("tm", d, 3)], w=[kcur])
                    yield
                it = d
                for part in range(2):
                    for q2 in range(2):
                        ib = cnt["b"] % 2
                        cnt["b"] += 1
                        for q1 in range(2):
                            q = q2 * 2 + q1
                            for sb_ in range(4):
                                tr_(pgb[ib][:, q1 * 4 + sb_, :], hcur[:, q, part, sb_ * 128:(sb_ + 1) * 128], identb[:],
                                    r=[kcur, "identb"], w=[("pgb", ib)])
                        k0 = part * 16 + q2 * 8
                        act_(hT[it][:, k0:k0 + 8, :], pgb[ib][:], AF.Copy, r=[("pgb", ib)], w=[("hT", it)])
                        yield
                py, pyk = pg_[3 * d + 1], ("pgf", 3 * d + 1)
                for ct in range(4):
                    kqs = [part * 16 + 4 * ct + sb_ for part in range(2) for sb_ in range(4)]
                    for n_, kq in enumerate(kqs):
                        mm_(py[:, ct * 128:(ct + 1) * 128], Cb[:, kq, :], hT[it][:, kq, :], n_ == 0, n_ == 7, r=["Cb", ("hT", it)], w=[pyk])
                act_(ysb[it][:], py[:].rearrange("p (c t) -> p c t", t=128), AF.Copy, r=[pyk], w=[("ysb", it)])
                yield
                S.dma("act", ysT[d][:, lo:hi].rearrange("(c p) t -> p c t", p=128), ysb[it][:], r=[("ysb", it)], w=[(f"ysT{d}", c)])
                yield

            ord0 = list(range(NT))
            ord1 = [1, 0] + list(range(NT - 1, 1, -1))
            for i in range(NT):
                run_interleaved([s5_chunk(ord0[i], 0), s5_chunk(ord1[i], 1)])
        S.barrier()
        with ExitStack() as st:
            vec = sb(st, "Gvec", [128, 2, 4])
            S.dma("sp", vec[:], ssvec_in[l], w=["Gvec"])
            gw = sb(st, "Ggw", [128, 4, 512], BF16)
            S.dma("pool", gw[:], gluw_in[l].rearrange("(kc p) n -> p kc n", p=128), w=["Ggw"])
            Y0 = [sb(st, f"GY0{i}", [128, 4, 512]) for i in range(2)]
            Y1 = [sb(st, f"GY1{i}", [128, 4, 512]) for i in range(2)]
            Uu = [sb(st, f"GU{i}", [128, 4, 512]) for i in range(2)]
            yy = sb(st, "Gyy", [128, 4, 512])
            y3 = sb(st, "Gy3", [128, 4, 512])
            y2b = sb(st, "Gy2b", [128, 4, 512], BF16)
            sg_ = sb(st, "Gsg", [128, 512])
            yo = [sb(st, f"Gyo{i}", [128, 4, 512], BF16) for i in range(2)]
            pz = [ps(st, f"pz{i}", [128, 512]) for i in range(2)]
            kz = 0
            for gi, t0 in enumerate(range(0, N, 512)):
                tw = min(512, N - t0)
                i = gi % 2
                tks = list(range(t0 // 128, (t0 + tw) // 128))
                S.dma("sp", Y0[i][:, :, 0:tw], ysT[0][:, t0:t0 + tw].rearrange("(c p) t -> p c t", p=128), r=[("ysT0", t) for t in tks], w=[("GY0", i)])
                S.dma("sp", Y1[i][:, :, 0:tw], ysT[1][:, t0:t0 + tw].rearrange("(c p) t -> p c t", p=128), r=[("ysT1", t) for t in tks], w=[("GY1", i)])
                S.dma("sp", Uu[i][:, :, 0:tw], p_ssT[:, t0:t0 + tw].rearrange("(c p) t -> p c t", p=128), r=[("p_ssT", q) for q in range(4)], w=[("GU", i)])
                tt_("pool", yy[:, :, 0:tw], Y0[i][:, :, 0:tw], Y1[i][:, :, 0:tw], ALU.add, r=[("GY0", i), ("GY1", i)], w=["Gyy"])
                for c in range(4):
                    S.op("dve", lambda e, c=c, i=i, tw=tw: e.scalar_tensor_tensor(out=yy[:, c, 0:tw], in0=Uu[i][:, c, 0:tw], scalar=vec[:, 0, c:c + 1],
                                                                            in1=yy[:, c, 0:tw], op0=ALU.mult, op1=ALU.add),
                         r=[("GU", i), "Gvec", "Gyy"], w=["Gyy"])
                tt_("pool", y3[:, :, 0:tw], yy[:, :, 0:tw], yy[:, :, 0:tw], ALU.mult, r=["Gyy"], w=["Gy3"])
                ts_("pool", y3[:, :, 0:tw], y3[:, :, 0:tw], 0.044715, 1.0, ALU.mult, ALU.add, r=["Gy3"], w=["Gy3"])
                tt_("pool", y3[:, :, 0:tw], y3[:, :, 0:tw], yy[:, :, 0:tw], ALU.mult, r=["Gy3", "Gyy"], w=["Gy3"])
                act_(y3[:, :, 0:tw], y3[:, :, 0:tw], AF.Tanh, r=["Gy3"], w=["Gy3"], scale=0.7978845608028654)
                ts_("pool", y3[:, :, 0:tw], y3[:, :, 0:tw], 1.0, 0.5, ALU.add, ALU.mult, r=["Gy3"], w=["Gy3"])
                tt_("pool", yy[:, :, 0:tw], yy[:, :, 0:tw], y3[:, :, 0:tw], ALU.mult, r=["Gyy", "Gy3"], w=["Gyy"])
                cp_("pool", y2b[:, :, 0:tw], yy[:, :, 0:tw], r=["Gyy"], w=["Gy2b"])
                for c in range(4):
                    p_ = pz[kz % 2]
                    pk_ = ("pz", kz % 2)
                    kz += 1
                    for kc in range(4):
                        mm_(p_[:, 0:tw], gw[:, kc, c * 128:(c + 1) * 128], y2b[:, kc, 0:tw], kc == 0, kc == 3, r=["Ggw", "Gy2b"], w=[pk_])
                    act_(sg_[:, 0:tw], p_[:, 0:tw], AF.Sigmoid, r=[pk_], w=["Gsg"], bias=vec[:, 1, c:c + 1])
                    tt_("dve", yo[i][:, c, 0:tw], yy[:, c, 0:tw], sg_[:, 0:tw], ALU.mult, r=["Gyy", "Gsg"], w=[("Gyo", i)])
                S.dma("sp", ycT[:, t0:t0 + tw].rearrange("(c p) t -> p c t", p=128), yo[i][:, :, 0:tw], r=[("Gyo", i)], w=[("ycT", t) for t in tks])
        S.barrier()
        if stop_after == "G":
            break

        with ExitStack() as st:
            KT = sb(st, "KT", [128, 4, N], BF16)
            S.op("pool", lambda e: e.memset(KT[64:128], 0.0), w=[("KT", t_) for t_ in range(NT)])
            Vt = sb(st, "Vt", [128, NT, 4, 65], BF16)
            gain = sb(st, "gain", [128, 1280])
            S.dma("sp", gain[:], qkgain_in[l], w=["gain"])
            S.op("pool", lambda e: e.memset(Vt[:], 1.0), w=["Vt"])
            xq = [sb(st, f"xq{i}", [128, 1280]) for i in range(2)]
            xv = [sb(st, f"xv{i}", [128, 256]) for i in range(2)]
            sq = sb(st, "sq", [128, 1280])
            ssq = sb(st, "ssq", [128, 20])
            xn = sb(st, "xn", [128, 1280])
            t1 = sb(st, "t1", [128, 1280])
            t2 = sb(st, "t2", [128, 1280])
            xb = sb(st, "xb", [128, 1280], BF16)
            cs = [sb(st, f"cs{i}", [128, 2, 64]) for i in range(2)]
            trp = [ps(st, f"trp{i}", [64, 8, 128], BF16) for i in range(1)]

            def qk_prep(tt, src, c0, nh, i):
                w_ = nh * 64
                v3 = lambda t: t[:, 0:w_].rearrange("p (h d) -> p h d", d=64)
                tt_("dve", sq[:, 0:w_], src[:, 0:w_], src[:, 0:w_], ALU.mult, r=[("xq", i)], w=["sq"])
                red_("dve", ssq[:, 0:nh], v3(sq), r=["sq"], w=["ssq"])
                rstd_(ssq[:, 0:nh], ssq[:, 0:nh], 1.0 / 64, RMS_EPS, r=["ssq"], w=["ssq"])
                tt_("dve", v3(xn), v3(src), ssq[:, 0:nh].unsqueeze(2).broadcast_to([128, nh, 64]), ALU.mult,
                    r=[("xq", i), "ssq"], w=["xn"])
                if tt < 2:
                    tt_("dve", xb[:, 0:w_], xn[:, 0:w_], gain[:, c0:c0 + w_], ALU.mult, r=["xn", "gain"], w=["xb"])
                    return
                tt_("dve", xn[:, 0:w_], xn[:, 0:w_], gain[:, c0:c0 + w_], ALU.mult, r=["xn", "gain"], w=["xn"])
                cst = cs[tt % 2]
                ck = ("cs", tt % 2)
                cosb = cst[:, 0, :].unsqueeze(1).broadcast_to([128, nh, 64])
                tt_("dve", v3(t1), v3(xn), cosb, ALU.mult, r=["xn", ck], w=["t1"])
                v4 = lambda t: t[:, 0:w_].rearrange("p (h a b i) -> p (h a) b i", a=2, b=2, i=16)
                sn4 = cst[:, 1, :].rearrange("p (a b i) -> p a b i", a=2, b=2)
                for hb in range(2):
                    snb = sn4[:, :, hb, :].unsqueeze(1).broadcast_to([128, nh, 2, 16])
                    o4 = t2[:, 0:w_].rearrange("p (h a b i) -> p h a b i", a=2, b=2, i=16)[:, :, :, hb, :]
                    i4 = xn[:, 0:w_].rearrange("p (h a b i) -> p h a b i", a=2, b=2, i=16)[:, :, :, 1 - hb, :]
                    tt_("dve", o4, i4, snb, ALU.mult, r=["xn", ck], w=["t2"])
                tt_("dve", xb[:, 0:w_], t1[:, 0:w_], t2[:, 0:w_], ALU.add, r=["t1", "t2"], w=["xb"])

            def load_cs(tt):
                if tt >= 2:
                    r0 = (tt - 2) * 128
                    S.dma("sp", cs[tt % 2][:, 0, :], rope_cos[r0:r0 + 128, :], w=[("cs", tt % 2)])
                    S.dma("sp", cs[tt % 2][:, 1, :], rope_sin[r0:r0 + 128, :], w=[("cs", tt % 2)])

            for tt in range(NT):
                i = tt % 2
                S.dma("sp", xq[i][:, 0:256], p_tm[tt * 128:(tt + 1) * 128, C_AT + 1024:C_AT + 1280], r=[("p_tm", tt)], w=[("xq", i)])
                S.dma("sp", xv[i][:], p_tm[tt * 128:(tt + 1) * 128, C_AT + 1280:C_AT + 1536], r=[("p_tm", tt)], w=[("xv", i)])
                load_cs(tt)
                qk_prep(tt, xq[i], 1024, 4, i)
                for g in range(4):
                    tr_(trp[0][:, g, :], xb[:, g * 64:(g + 1) * 64], identb[:], r=["xb", "identb"], w=[("trp", 0)])
                act_(KT[0:64, :, tt * 128:(tt + 1) * 128], trp[0][:, 0:4, :], AF.Copy, r=[("trp", 0)], w=[("KT", tt)])
                cp_("pool", Vt[:, tt, :, 0:64], xv[i][:].rearrange("p (g d) -> p g d", d=64), r=[("xv", i)], w=[("Vt", tt)])
            QT = [sb(st, f"QT{i}", [128, 16, 128], BF16) for i in range(2)]
            for i_ in range(2):
                S.op("pool", lambda e, i_=i_: e.memset(QT[i_][64:128], 0.0), w=[("QT", i_)])
            NB = 4
            Pt = [sb(st, f"Pt{i}", [128, 512], BF16) for i in range(NB)]
            sps = [ps(st, f"sps{i}", [128, 512]) for i in range(NB)]
            ops_ = [ps(st, f"ops{i}", [65, 512]) for i in range(2)]
            bcp = ps(st, "bcp", [64, 512])
            rd = sb(st, "rd", [65, 512])
            osb = sb(st, "osb", [64, 512])
            yb = [sb(st, f"yb{i}", [64, 512], BF16) for i in range(2)]
            kq = 0
            ko = 0
            pending = []
            for tt in range(NT):
                i = tt % 2
                S.dma("sp", xq[i][:, 0:1024], p_tm[tt * 128:(tt + 1) * 128, C_AT:C_AT + 1024], r=[("p_tm", tt)], w=[("xq", i)])
                load_cs(tt)
                qk_prep(tt, xq[i], 0, 16, i)
                for hh in range(2):
                    for h in range(8):
                        hd = hh * 8 + h
                        tr_(trp[0][:, h, :], xb[:, hd * 64:(hd + 1) * 64], identb[:], r=["xb", "identb"], w=[("trp", 0)])
                    act_(QT[i][0:64, hh * 8:(hh + 1) * 8, :], trp[0][:], AF.Copy, r=[("trp", 0)], w=[("QT", i)])
                kts = [0, 1] if tt < 2 else list(range(NT))
                for g in range(4):
                    o_ = ops_[ko % 2]
                    okey = ("ops", ko % 2)
                    ybt = yb[ko % 2]
                    ykey = ("yb", ko % 2)
                    ko += 1
                    PIPE = 3
                    nk = len(kts)
                    base = kq
                    kq += nk
                    for n_ in range(nk + PIPE):
                        if n_ == min(6, nk - 1) and pending:
                            pending.pop(0)()
                        if n_ < nk:
                            kt = kts[n_]
                            j = (base + n_) % NB
                            mm_(sps[j][:], KT[:, g, kt * 128:(kt + 1) * 128], QT[i][:, 4 * g:4 * g + 4, :].rearrange("p h t -> p (h t)"),
                                True, True, r=[("KT", kt), ("QT", i)], w=[("sps", j)])
                            act_(Pt[j][:], sps[j][:], AF.Exp, r=[("sps", j)], w=[("Pt", j)], scale=0.125)
                        m_ = n_ - PIPE
                        if m_ >= 0:
                            kt = kts[m_]
                            j = (base + m_) % NB
                            mm_(o_[:], Vt[:, kt, g, :], Pt[j][:], m_ == 0, m_ == nk - 1, r=[("Vt", kt), ("Pt", j)], w=[okey])
                    def epilogue(o_=o_, okey=okey, ybt=ybt, ykey=ykey, g=g, tt=tt):
                        S.op("dve", lambda e: e.reciprocal(out=rd[64:65, :], in_=o_[64:65, :]), r=[okey], w=["rd"])
                        mm_(bcp[:], ones[64:65, 0:64], rd[64:65, :], True, True, r=["ones", "rd"], w=["bcp"])
                        cp_("pool" if False else "dve", osb[:], o_[0:64, :], r=[okey], w=["osb"])
                        tt_("dve", ybt[:], osb[:], bcp[:], ALU.mult, r=["osb", "bcp"], w=[ykey])
                        S.dma("sp", ybT[4 * g:4 * g + 4, :, tt * 128:(tt + 1) * 128].rearrange("h d t -> d h t"),
                              ybt[:].rearrange("p (h t) -> p h t", t=128), r=[ykey], w=[("ybT", tt)])
                    pending.append(epilogue)
            while pending:
                pending.pop(0)()
        S.barrier()
        if stop_after == "F":
            break

        stHI = ExitStack()
        gtb = sb(stHI, "gtb", [128, 2, 2, D])
        with ExitStack() as st:
            dg = [sb(st, f"dg{i}", [128, 128]) for i in range(2)]
            pgt = [ps(st, f"pgt{i}", [128, 512]) for i in range(2)]
            kk_ = 0
            for wh, c0 in ((0, 16), (1, 40)):
                for g in range(2):
                    for hf in range(2):
                        p_ = pgt[kk_ % 2]
                        pk_ = ("pgt", kk_ % 2)
                        kk_ += 1
                        for j in range(4):
                            kc = hf * 4 + j
                            i = kc % 2
                            ts_("dve", dg[i][:], ident[:], modT[:, l, c0 + kc, g:g + 1], None, ALU.mult, None, r=["ident", "modT"], w=[("dg", i)])
                            mm_(p_[:, j * 128:(j + 1) * 128], ones[:], dg[i][:], True, True, r=["ones", ("dg", i)], w=[pk_])
                        act_(gtb[:, wh, g, hf * 512:(hf + 1) * 512], p_[:], AF.Copy, r=[pk_], w=["gtb"])
        S.barrier()

        def ln_tail(st_tiles, h, lng, lnb, outt, hk, ok):
            hsq, st4 = st_tiles
            red_("dve", st4[:, 0:1], h[:], r=[hk], w=["st4"])
            tt_("dve", hsq[:], h[:], h[:], ALU.mult, r=[hk], w=["hsq"])
            red_("dve", st4[:, 1:2], hsq[:], r=["hsq"], w=["st4"])
            ts_("dve", st4[:, 0:1], st4[:, 0:1], 1.0 / D, None, ALU.mult, None, r=["st4"], w=["st4"])
            tt_("dve", st4[:, 2:3], st4[:, 0:1], st4[:, 0:1], ALU.mult, r=["st4"], w=["st4"])
            S.op("dve", lambda e: e.scalar_tensor_tensor(out=st4[:, 3:4], in0=st4[:, 1:2], scalar=1.0 / D, in1=st4[:, 2:3],
                                                       op0=ALU.mult, op1=ALU.subtract), r=["st4"], w=["st4"])
            rstd_(st4[:, 3:4], st4[:, 3:4], 1.0, LN_EPS, r=["st4"], w=["st4"])
            ts_("dve", hsq[:], h[:], st4[:, 0:1], st4[:, 3:4], ALU.subtract, ALU.mult, r=[hk, "st4"], w=["hsq"])
            tt_("dve", hsq[:], hsq[:], lng, ALU.mult, r=["hsq", "lnw"], w=["hsq"])
            tt_("dve", outt[:], hsq[:], lnb, ALU.add, r=["hsq", "lnw"], w=[ok])

        with ExitStack() as st:
            wa = sb(st, "Hwa", [128, 4, D], BF16)
            wbb = sb(st, "Hwb", [128, 8, D], BF16)
            wc = sb(st, "Hwc", [128, 4, D], BF16)
            wo = sb(st, "Hwo", [128, 8, D], BF16)
            lnw = sb(st, "Hln", [128, 2, D])
            S.dma("pool", wa[:], proja_in[l].rearrange("(kc p) n -> p kc n", p=128), w=["Hwa"])
            S.dma("pool", wbb[:], projb_in[l].rearrange("(kc p) n -> p kc n", p=128), w=["Hwb"])
            S.dma("pool", wc[:], projc_in[l].rearrange("(kc p) n -> p kc n", p=128), w=["Hwc"])
            S.dma("pool", wo[:], wout_in[l].rearrange("(kc p) n -> p kc n", p=128), w=["Hwo"])
            S.dma("sp", lnw[:], ln_in[l, 0:2].rearrange("k p c -> p k c"), w=["lnw"])
            TW = 256
            ya_t = [sb(st, f"Hya{i}", [128, 4, TW], BF16) for i in range(2)]
            yb_t = [sb(st, f"Hyb{i}", [128, 8, TW], BF16) for i in range(2)]
            yc_t = [sb(st, f"Hyc{i}", [128, 4, TW], BF16) for i in range(2)]
            g_t = [sb(st, f"Hg{i}", [128, 24, TW], BF16) for i in range(2)]
            mg = sb(st, "Hmg", [128, TW])
            t_ = sb(st, "Ht", [128, TW])
            mT = sb(st, "HmT", [128, 8, TW], BF16)
            xt_ = [sb(st, f"Hx{i}", [128, D]) for i in range(2)]
            hh = sb(st, "Hh", [128, D])
            hsq = sb(st, "Hhsq", [128, D])
            st4 = sb(st, "Hst4", [128, 4])
            xo = [sb(st, f"Hxo{i}", [128, D]) for i in range(2)]
            x2T = [sb(st, f"Hx2T{i}", [128, 8, 128], BF16) for i in range(2)]
            pH = [ps(st, f"pH{i}", [128, 512]) for i in range(6)]
            pT2 = [ps(st, f"pT2{i}", [128, 4, 128]) for i in range(2)]
            kp = [0]

            def PH():
                i = kp[0] % 6
                kp[0] += 1
                return pH[i], ("pH", i)

            for gi, t0 in enumerate(range(0, N, TW)):
                i = gi % 2
                tks = [t0 // 128, t0 // 128 + 1]
                S.dma("sp", ya_t[i][:], yaT[:, t0:t0 + TW].rearrange("(kc p) t -> p kc t", p=128), r=[("yaT", t) for t in tks], w=[("Hya", i)])
                S.dma("sp", yb_t[i][:], ybT.rearrange("h d t -> (h d) t")[:, t0:t0 + TW].rearrange("(kc p) t -> p kc t", p=128),
                      r=[("ybT", t) for t in tks], w=[("Hyb", i)])
                S.dma("sp", yc_t[i][:], ycT[:, t0:t0 + TW].rearrange("(kc p) t -> p kc t", p=128), r=[("ycT", t) for t in tks], w=[("Hyc", i)])
                S.dma("sp", g_t[i][:], gT[:, t0:t0 + TW].rearrange("(kc p) t -> p kc t", p=128), r=[("gT", c) for c in range(24)], w=[("Hg", i)])
                for m_ in range(8):
                    ms = slice(m_ * 128, (m_ + 1) * 128)
                    pa, pak = PH()
                    for kc in range(4):
                        mm_(pa[:, 0:TW], wa[:, kc, ms], ya_t[i][:, kc, :], kc == 0, kc == 3, r=["Hwa", ("Hya", i)], w=[pak])
                    pb_, pbk = PH()
                    for kc in range(8):
                        mm_(pb_[:, 0:TW], wbb[:, kc, ms], yb_t[i][:, kc, :], kc == 0, kc == 7, r=["Hwb", ("Hyb", i)], w=[pbk])
                    pc_, pck = PH()
                    for kc in range(4):
                        mm_(pc_[:, 0:TW], wc[:, kc, ms], yc_t[i][:, kc, :], kc == 0, kc == 3, r=["Hwc", ("Hyc", i)], w=[pck])
                    tt_("dve", mg[:], pa[:, 0:TW], g_t[i][:, m_, :], ALU.mult, r=[pak, ("Hg", i)], w=["Hmg"])
                    tt_("dve", t_[:], pb_[:, 0:TW], g_t[i][:, 8 + m_, :], ALU.mult, r=[pbk, ("Hg", i)], w=["Ht"])
                    tt_("dve", mg[:], mg[:], t_[:], ALU.add, r=["Hmg", "Ht"], w=["Hmg"])
                    tt_("dve", t_[:], pc_[:, 0:TW], g_t[i][:, 16 + m_, :], ALU.mult, r=[pck, ("Hg", i)], w=["Ht"])
                    tt_("dve", mT[:, m_, :], mg[:], t_[:], ALU.add, r=["Hmg", "Ht"], w=["HmT"])
                for sub in range(2):
                    tt = t0 // 128 + sub
                    g = 1 if tt < 2 else 0
                    j = tt % 2
                    ts_l = slice(sub * 128, (sub + 1) * 128)
                    S.dma("sp", xt_[j][:], xs[tt * 128:(tt + 1) * 128, :], r=[("xs", tt)], w=[("Hx", j)])
                    for hf in range(2):
                        p_, pk_ = PH()
                        for kc in range(8):
                            mm_(p_[:], mT[:, kc, ts_l], wo[:, kc, hf * 512:(hf + 1) * 512], kc == 0, kc == 7, r=["HmT", "Hwo"], w=[pk_])
                        tt_("dve", hh[:, hf * 512:(hf + 1) * 512], p_[:], gtb[:, 0, g, hf * 512:(hf + 1) * 512], ALU.mult, r=[pk_, "gtb"], w=["Hh"])
                    S.op("dve", lambda e, j=j: e.scalar_tensor_tensor(out=hh[:], in0=xt_[j][:], scalar=ALPHA, in1=hh[:], op0=ALU.mult, op1=ALU.add),
                         r=[("Hx", j), "Hh"], w=["Hh"])
                    ln_tail((hsq, st4), hh, lnw[:, 0, :], lnw[:, 1, :], xo[j], "Hh", ("Hxo", j))
                    S.dma("act", xmid[tt * 128:(tt + 1) * 128, :], xo[j][:], r=[("Hxo", j)], w=[("xmid", tt)])
                    for hf in range(2):
                        for q in range(4):
                            kc = hf * 4 + q
                            tr_(pT2[hf][:, q, :], xo[j][:, kc * 128:(kc + 1) * 128], ident[:], r=[("Hxo", j), "ident"], w=[("pT2", hf)])
                        for q in range(4):
                            kc = hf * 4 + q
                            act_(x2T[j][:, kc, :], pT2[hf][:, q, :], AF.Identity, r=[("pT2", hf), "modT"], w=[("Hx2T", j)],
                                 scale=modT[:, l, 32 + kc, g:g + 1], bias=modT[:, l, 24 + kc, g:g + 1])
                    S.dma("act", xm2T[:, tt * 128:(tt + 1) * 128].rearrange("(kc p) t -> p kc t", p=128), x2T[j][:], r=[("Hx2T", j)], w=[("xm2T", tt)])
        S.barrier()
        if stop_after == "H":
            stHI.close()
            break

        with ExitStack() as st:
            w1 = sb(st, "Iw1", [128, 8, 4 * D], BF16)
            for q in range(4):
                S.dma("pool", w1[:, :, q * D:(q + 1) * D], w1_in[l, :, q * D:(q + 1) * D].rearrange("(kc p) n -> p kc n", p=128), w=["Iw1"])
            xi = [sb(st, f"Ixi{i}", [128, 8, 512], BF16) for i in range(2)]
            rl = [sb(st, f"Irl{i}", [128, 512]) for i in range(2)]
            ho = [sb(st, f"Iho{i}", [128, 8, 512], BF16) for i in range(2)]
            pI = [ps(st, f"pI{i}", [128, 512]) for i in range(4)]
            kp = 0
            ko = 0
            for gi, t0 in enumerate(range(0, N, 512)):
                tw = min(512, N - t0)
                i = gi % 2
                tks = list(range(t0 // 128, (t0 + tw) // 128))
                S.dma("sp", xi[i][:, :, 0:tw], xm2T[:, t0:t0 + tw].rearrange("(kc p) t -> p kc t", p=128), r=[("xm2T", t) for t in tks], w=[("Ixi", i)])
                for c8 in range(4):
                    o_ = ho[ko % 2]
                    okey = ("Iho", ko % 2)
                    ko += 1
                    for c in range(8):
                        cb = c8 * 8 + c
                        p_ = pI[kp % 4]
                        pk_ = ("pI", kp % 4)
                        r_ = rl[kp % 2]
                        rk_ = ("Irl", kp % 2)
                        kp += 1
                        for kc in range(8):
                            mm_(p_[:, 0:tw], w1[:, kc, cb * 128:(cb + 1) * 128], xi[i][:, kc, 0:tw], kc == 0, kc == 7, r=["Iw1", ("Ixi", i)], w=[pk_])
                        act_(r_[:, 0:tw], p_[:, 0:tw], AF.Relu, r=[pk_], w=[rk_])
                        tt_("dve", o_[:, c, 0:tw], r_[:, 0:tw], r_[:, 0:tw], ALU.mult, r=[rk_], w=[okey])
                    S.dma("sp", h1T[c8 * 1024:(c8 + 1) * 1024, t0:t0 + tw].rearrange("(c p) t -> p c t", p=128), o_[:, :, 0:tw], r=[okey],
                          w=[("h1T", t, c8) for t in tks])
        S.barrier()
        with ExitStack() as st:
            w2 = sb(st, "Iw2", [128, 32, D], BF16)
            for q in range(4):
                S.dma("pool", w2[:, q * 8:(q + 1) * 8, :], w2_in[l, q * 1024:(q + 1) * 1024, :].rearrange("(kc p) n -> p kc n", p=128), w=["Iw2"])
            lnw = sb(st, "Iln", [128, 2, D])
            S.dma("sp", lnw[:], ln_in[l, 2:4].rearrange("k p c -> p k c"), w=["lnw"])
            h1 = [sb(st, f"Ih1{i}", [128, 32, 128], BF16) for i in range(2)]
            xm_ = [sb(st, f"Ixm{i}", [128, D]) for i in range(2)]
            hh = sb(st, "Ihh", [128, D])
            hsq = sb(st, "Ihsq", [128, D])
            st4 = sb(st, "Ist4", [128, 4])
            xo = [sb(st, f"Ixo{i}", [128, D]) for i in range(2)]
            pJ = [ps(st, f"pJ{i}", [128, 512]) for i in range(4)]
            kp = 0
            for tt in range(NT):
                i = tt % 2
                g = 1 if tt < 2 else 0
                lo, hi = tt * 128, (tt + 1) * 128
                S.dma("sp", h1[i][:], h1T[:, lo:hi].rearrange("(c p) t -> p c t", p=128), r=[("h1T", tt, c8) for c8 in range(4)], w=[("Ih1", i)])
                S.dma("sp", xm_[i][:], xmid[lo:hi, :], r=[("xmid", tt)], w=[("Ixm", i)])
                for hf in range(2):
                    p_ = pJ[kp % 4]
                    pk_ = ("pJ", kp % 4)
                    kp += 1
                    for kc in range(32):
                        mm_(p_[:], h1[i][:, kc, :], w2[:, kc, hf * 512:(hf + 1) * 512], kc == 0, kc == 31, r=[("Ih1", i), "Iw2"], w=[pk_])
                    tt_("dve", hh[:, hf * 512:(hf + 1) * 512], p_[:], gtb[:, 1, g, hf * 512:(hf + 1) * 512], ALU.mult, r=[pk_, "gtb"], w=["Ihh"])
                S.op("dve", lambda e, i=i: e.scalar_tensor_tensor(out=hh[:], in0=xm_[i][:], scalar=ALPHA, in1=hh[:], op0=ALU.mult, op1=ALU.add),
                     r=[("Ixm", i), "Ihh"], w=["Ihh"])
                ln_tail((hsq, st4), hh, lnw[:, 0, :], lnw[:, 1, :], xo[i], "Ihh", ("Ixo", i))
                S.dma("act", xs[lo:hi, :], xo[i][:], r=[("Ixo", i)], w=[("xs", tt)])
                if l == n_layers - 1 and tt >= 2:
                    S.dma("act", out[lo - NCTX:hi - NCTX, :], xo[i][:], r=[("Ixo", i)], w=[("out", tt)])
        S.barrier()
        stHI.close()
        if stop_after == "I":
            break

    S.barrier()
    top.close()
    return nc, S


def _prep_inputs(inp, b):
    f = lambda a: np.ascontiguousarray(a, dtype=np.float32)
    m = {}
    m["x_in"] = f(np.concatenate([inp["ctx"][b], inp["x"][b]], axis=0))
    cT = np.stack([inp["c"][b].reshape(8, 128).T, inp["c_ctx"].reshape(8, 128).T], axis=-1)
    m["cT"] = f(cT)
    m["ident"] = np.eye(128, dtype=np.float32)
    m["mod_w"] = f(inp["mod_w"])
    m["mod_bT"] = f(inp["mod_b"].reshape(DEPTH, 48, 128).transpose(0, 2, 1))
    m["w_in"] = f(inp["w_in"])
    m["qkgain"] = f(np.broadcast_to(np.concatenate([np.tile(inp["attn_q_gain"], (1, 16)), np.tile(inp["attn_k_gain"], (1, 4))],
                                                   axis=1)[:, None, :], (DEPTH, 128, 1280)))
    rows = NLAT // 64
    row = np.repeat(np.arange(rows, dtype=np.float32), 64)
    col = np.tile(np.arange(64, dtype=np.float32), rows)
    inv = (np.float32(10000.0) ** (-np.arange(16, dtype=np.float32) / np.float32(16))).astype(np.float32)
    ang = np.stack([row, col], axis=-1)[:, :, None] * inv
    ang = np.broadcast_to(ang[:, :, None, :], (NLAT, 2, 2, 16)).reshape(NLAT, 64)
    sgn = np.broadcast_to(np.array([-1.0, 1.0], np.float32)[None, None, :, None], (NLAT, 2, 2, 16)).reshape(NLAT, 64)
    m["rope_cos"] = f(np.cos(ang))
    m["rope_sin"] = f(np.sin(ang) * sgn)
    m["ones"] = np.ones((128, 128), np.float32)
    ii = np.arange(128)
    mS0 = (ii[:, None] < ii[None, :]).astype(np.float32)
    mS1 = (ii[:, None] > ii[None, :]).astype(np.float32)
    eye = np.eye(128, dtype=np.float32)
    m["masks"] = f(np.stack([mS0, mS1, mS0 + eye, mS1 + eye]))
    bc = lambda a: np.broadcast_to(a[:, None, :], (DEPTH, 128, a.shape[-1]))
    m["mu_bc"] = f(bc(inp["rwkv_mu"]))
    m["rwp_bc"] = f(np.stack([bc(inp[k]) for k in ("rwkv_k_k", "rwkv_k_a", "rwkv_r_k", "rwkv_gn_w", "rwkv_gn_b")], axis=1))
    m["lrb"] = f(np.concatenate([inp["rwkv_w0"], inp["rwkv_a0"]], axis=1))
    m["lrw"] = f(np.concatenate([inp["rwkv_w_up"], inp["rwkv_a_up"]], axis=2).transpose(0, 2, 1, 3))
    m["g_up"] = f(inp["rwkv_g_up"])
    rep = lambda a: np.broadcast_to(a[:, :, None, :], (DEPTH, 2, 128, 2048))
    m["ss_tab"] = f(np.stack([rep(inp["ssm_a_re"].reshape(DEPTH, 2, 2048)), rep(inp["ssm_a_im"].reshape(DEPTH, 2, 2048)),
                              rep(np.repeat(inp["ssm_log_dt"], 64, axis=-1))], axis=2))
    p_ = np.arange(128, dtype=np.float32)
    m["mcol"] = f(np.stack([p_ + 1, 128 - p_, -(p_ + 1), -(128 - p_)], axis=1))
    bblk = np.zeros((DEPTH, 2, 4, 8, 16, 8, 64), np.float32)
    for k_, nm in enumerate(("ssm_b_re", "ssm_b_im")):
        b5 = inp[nm].reshape(DEPTH, 4, 8, 64, 16)
        for g_ in range(8):
            bblk[:, k_, :, g_, :, g_, :] = b5[:, :, g_].transpose(0, 1, 3, 2)
    m["ss_bblk"] = f(bblk.reshape(DEPTH, 2, 4, 128, 512).transpose(0, 1, 3, 2, 4))
    cblk = np.zeros((DEPTH, 2, 16, 2, 64, 8, 16), np.float32)
    for k_, nm in enumerate(("ssm_c_re", "ssm_c_im")):
        c5 = inp[nm].reshape(DEPTH, 16, 2, 16, 64)
        for gp in range(16):
            for g2 in range(2):
                gl = (2 * gp + g2) % 8
                cblk[:, k_, gp, g2, :, gl, :] = c5[:, gp, g2].transpose(0, 2, 1)
    m["ss_cblk"] = f(cblk.reshape(DEPTH, 32, 128, 128).transpose(0, 2, 1, 3))
    m["ss_vec"] = f(np.stack([inp["ssm_d"].reshape(DEPTH, 4, 128).transpose(0, 2, 1),
                              inp["ssm_glu_b"].reshape(DEPTH, 4, 128).transpose(0, 2, 1)], axis=2))
    m["glu_w"] = f(inp["ssm_glu_w"])
    for k_ in ("proj_a", "proj_b", "proj_c", "w_out", "mlp_w1", "mlp_w2"):
        m[k_] = f(inp[k_])
    m["ln_bc"] = f(np.stack([bc(inp[k_]) for k_ in ("ln1_g", "ln1_b", "ln2_g", "ln2_b")], axis=1))
    selm = np.zeros((4, 128, 128), np.float32)
    selm[0, 127, :] = 1.0
    selm[1, 0, :] = 1.0
    selm[2, 127, :] = -1.0
    selm[3, 0, :] = -1.0
    m["sel"] = selm
    return m


def kernel(**inp):
    inp = {k: np.asarray(v) for k, v in inp.items()}
    nc, _ = build()
    in_maps = [_prep_inputs(inp, b) for b in range(8)]
    res = run_bass_kernel_spmd(nc, in_maps, core_ids=list(range(8)))
    return np.stack([r["out"] for r in res.results], axis=0).astype(np.float32)
```

```python
import numpy as np
from contextlib import ExitStack
import concourse.bass as bass
import concourse.mybir as mybir
from concourse.bass_utils import run_bass_kernel_spmd

F32 = mybir.dt.float32
BF16 = mybir.dt.bfloat16
AF = mybir.ActivationFunctionType
ALU = mybir.AluOpType
AX = mybir.AxisListType

D = 1024
DEPTH = 4
NCTX = 256
NLAT = 4096
N = NCTX + NLAT
NT = N // 128
INC = 6912
C_RW, C_AT, C_SS, C_G = 0, 1792, 3328, 3840
ALPHA = (2 * DEPTH) ** 0.25
LN_EPS, RMS_EPS, GN_EPS = 1e-5, 1e-6, 64e-5
PI = float(np.pi)
DSTAGE = 9
DSUB = 9


class _Res:
    __slots__ = ("lw", "rd")

    def __init__(self):
        self.lw = None
        self.rd = {}


class _Eng:
    def __init__(self, name, h, sem, selfsync):
        self.name, self.h, self.sem, self.selfsync = name, h, sem, selfsync
        self.cnt = 0
        self.seen = {}


class Sched:
    def __init__(self, nc, stack, ndma=(("sp", 8), ("pool", 6), ("act", 8))):
        self.nc = nc
        hs = {"pe": nc.tensor, "act": nc.scalar, "dve": nc.vector, "pool": nc.gpsimd, "sp": nc.sync}
        self.engs = {}
        for n, h in hs.items():
            sem = stack.enter_context(nc.semaphore("s_" + n))
            self.engs[n] = _Eng(n, h, sem, selfsync=(n in ("dve", "act", "pool")))
        self.dsems, self.dnext = {}, {}
        for q, k in ndma:
            self.dsems[q] = [[stack.enter_context(nc.semaphore(f"d_{q}{i}")), 0] for i in range(k)]
            self.dnext[q] = 0
        self.res = {}
        self.nins = 0

    def R(self, k):
        r = self.res.get(k)
        if r is None:
            r = self.res[k] = _Res()
        return r

    def _deps(self, r, w):
        deps = {}

        def need(ev):
            if ev is not None and deps.get(id(ev[0]), (None, 0))[1] < ev[1]:
                deps[id(ev[0])] = ev

        for k in r:
            need(self.R(k).lw)
        for k in w:
            R = self.R(k)
            need(R.lw)
            for ev in R.rd.values():
                need(ev)
        return deps

    def _wait(self, E, deps):
        for s, v in deps.values():
            if s is E.sem and not E.selfsync:
                continue
            if E.seen.get(id(s), 0) < v:
                E.seen[id(s)] = v
                E.h.wait_ge(s, v)

    def _mark(self, r, w, ev):
        for k in r:
            self.R(k).rd[id(ev[0])] = ev
        for k in w:
            R = self.R(k)
            R.lw = ev
            R.rd = {}

    def op(self, en, fn, r=(), w=()):
        E = self.engs[en]
        self._wait(E, self._deps(r, w))
        E.cnt += 1
        fn(E.h).then_inc(E.sem, 1)
        self.nins += 1
        self._mark(r, w, (E.sem, E.cnt))

    def dma(self, q, out, in_, r=(), w=(), **kw):
        E = self.engs[q]
        lst = self.dsems[q]
        ds = lst[self.dnext[q] % len(lst)]
        self.dnext[q] += 1
        deps = self._deps(r, w)
        if ds[1] > 0:
            deps[id(ds[0])] = (ds[0], ds[1])
        self._wait(E, deps)
        ds[1] += 16
        E.h.dma_start(out=out, in_=in_, **kw).then_inc(ds[0], 16)
        self.nins += 1
        self._mark(r, w, (ds[0], ds[1]))

    def barrier(self):
        evs = [(E.sem, E.cnt) for E in self.engs.values() if E.cnt]
        for lst in self.dsems.values():
            evs += [(s, v) for s, v in lst if v]
        for E in self.engs.values():
            for s, v in evs:
                if s is E.sem and not E.selfsync:
                    continue
                if E.seen.get(id(s), 0) < v:
                    E.seen[id(s)] = v
                    E.h.wait_ge(s, v)
        self.res = {}


def build(n_layers=DEPTH, stop_after=None, dbg=()):
    nc = bass.Bass("TRN2", target_bir_lowering=False)
    top = ExitStack()
    S = Sched(nc, top)
    dram_in = {}

    def din(name, shape, dt=F32):
        dram_in[name] = nc.dram_tensor(name, list(shape), dt, kind="ExternalInput").ap()
        return dram_in[name]

    def dscr(name, shape, dt=F32):
        kind = "ExternalOutput" if name in dbg else "Internal"
        return nc.dram_tensor(name, list(shape), dt, kind=kind).ap()

    x_in = din("x_in", [N, D])
    cT_in = din("cT", [128, 8, 2])
    ident_in = din("ident", [128, 128])
    mod_w = din("mod_w", [DEPTH, D, 6 * D])
    mod_bT = din("mod_bT", [DEPTH, 128, 48])
    w_in = din("w_in", [DEPTH, D, INC])
    out = nc.dram_tensor("out", [NLAT, D], F32, kind="ExternalOutput").ap()
    qkgain_in = din("qkgain", [DEPTH, 128, 1280])
    rope_cos = din("rope_cos", [NLAT, 64])
    rope_sin = din("rope_sin", [NLAT, 64])
    ones_in = din("ones", [128, 128])
    masks_in = din("masks", [4, 128, 128])
    mu_in = din("mu_bc", [DEPTH, 128, 1792])
    rwp_in = din("rwp_bc", [DEPTH, 5, 128, 512])
    lrb_in = din("lrb", [DEPTH, 4, 512])
    lrw_in = din("lrw", [DEPTH, 128, 2, 512])
    gup_in = din("g_up", [DEPTH, 128, 512])
    sstab_in = din("ss_tab", [DEPTH, 2, 3, 128, 2048])
    mcol_in = din("mcol", [128, 4])
    bblk_in = din("ss_bblk", [DEPTH, 2, 128, 4, 512])
    cblk_in = din("ss_cblk", [DEPTH, 128, 32, 128])
    ssvec_in = din("ss_vec", [DEPTH, 128, 2, 4])
    gluw_in = din("glu_w", [DEPTH, 512, 512])
    sel_in = din("sel", [4, 128, 128])
    proja_in = din("proj_a", [DEPTH, 512, D])
    projb_in = din("proj_b", [DEPTH, D, D])
    projc_in = din("proj_c", [DEPTH, 512, D])
    wout_in = din("w_out", [DEPTH, D, D])
    ln_in = din("ln_bc", [DEPTH, 4, 128, D])
    w1_in = din("mlp_w1", [DEPTH, D, 4 * D])
    w2_in = din("mlp_w2", [DEPTH, 4 * D, D])

    xs = dscr("xs", [N, D])
    p_tm = dscr("p_tm", [N, C_SS])
    p_ssT = dscr("p_ssT", [512, N])
    gT = dscr("gT", [3072, N], BF16)
    ybT = dscr("ybT", [16, 64, N], BF16)
    rw_s = {nm: dscr("rw_" + nm, [N, 512]) for nm in ("r", "v", "kk", "lw0", "lw1", "b0", "b1", "kd0", "kd1", "g", "y0", "y1")}
    yaT = dscr("yaT", [512, N], BF16)
    ysT = [dscr(f"ysT{d}", [512, N]) for d in range(2)]
    ycT = dscr("ycT", [512, N], BF16)
    xmid = dscr("xmid", [N, D])
    xm2T = dscr("xm2T", [D, N], BF16)
    h1T = dscr("h1T", [4 * D, N], BF16)

    uid = [0]

    def sb(st, name, shape, dt=F32):
        uid[0] += 1
        return st.enter_context(nc.sbuf_tensor(f"sb{uid[0]}_{name}", list(shape), dt))

    def ps(st, name, shape, dt=F32):
        uid[0] += 1
        ne = 512 if dt == F32 else 1024
        t = st.enter_context(nc.psum_tensor(f"ps{uid[0]}_{name}", [128, ne], dt))
        n = int(np.prod(shape[1:]))
        v = t[0:shape[0], 0:n]
        if len(shape) == 3:
            v = v.rearrange("p (a b) -> p a b", b=shape[2])
        return v

    def run_interleaved(gens):
        gens = list(gens)
        while gens:
            for g_ in list(gens):
                try:
                    next(g_)
                except StopIteration:
                    gens.remove(g_)

    def tt_(en, o, a, b, op, r, w):
        S.op(en, lambda e: e.tensor_tensor(out=o, in0=a, in1=b, op=op), r=r, w=w)

    def ts_(en, o, a, s1, s2, op0, op1, r, w):
        if s2 is None:
            S.op(en, lambda e: e.tensor_scalar(out=o, in0=a, scalar1=s1, scalar2=None, op0=op0), r=r, w=w)
        else:
            S.op(en, lambda e: e.tensor_scalar(out=o, in0=a, scalar1=s1, scalar2=s2, op0=op0, op1=op1), r=r, w=w)

    def act_(o, a, func, r, w, **kw):
        S.op("act", lambda e: e.activation(out=o, in_=a, func=func, **kw), r=r, w=w)

    def mm_(o, lhsT, rhs, start, stop, r, w):
        S.op("pe", lambda e: e.matmul(o, lhsT=lhsT, rhs=rhs, start=start, stop=stop), r=r, w=w)

    def rstd_(o, a, scale, eps, r, w):
        ts_("dve", o, a, scale, eps, ALU.mult, ALU.add, r=r, w=w)
        act_(o, o, AF.Sqrt, r=w, w=w)
        S.op("dve", lambda e: e.reciprocal(out=o, in_=o), r=w, w=w)

    def tr_(o, a, idn, r, w):
        S.op("pe", lambda e: e.transpose(o, a, idn), r=r, w=w)

    def cp_(en, o, a, r, w):
        S.op(en, lambda e: e.tensor_copy(out=o, in_=a), r=r, w=w)

    def red_(en, o, a, r, w):
        S.op(en, lambda e: e.tensor_reduce(out=o, in_=a, axis=AX.X, op=ALU.add), r=r, w=w)

    ones = sb(top, "ones", [128, 128])
    S.dma("sp", ones[:], ones_in, w=["ones"])
    masks = sb(top, "masks", [128, 4, 128])
    S.dma("sp", masks[:], masks_in.rearrange("m p t -> p m t"), w=["masks"])
    bsum = sb(top, "bsum", [128, NT, 8])
    ident = sb(top, "ident", [128, 128])
    identb = sb(top, "identb", [128, 128], BF16)
    modT = sb(top, "modT", [128, DEPTH, 48, 2])
    csil = sb(top, "csil", [128, 8, 2])
    S.dma("sp", ident[:], ident_in, w=["ident"])
    S.op("act", lambda e: e.activation(out=identb[:], in_=ident[:], func=AF.Copy), r=["ident"], w=["identb"])
    S.dma("sp", csil[:], cT_in, w=["csil"])
    S.op("act", lambda e: e.activation(out=csil[:], in_=csil[:], func=AF.Silu), r=["csil"], w=["csil"])

    for i in range(0, NT, 2):
        S.dma("sp", xs[i * 128:(i + 2) * 128, :], x_in[i * 128:(i + 2) * 128, :], w=[("xs", i), ("xs", i + 1)])

    with ExitStack() as st:
        wm = [sb(st, f"modw{i}", [128, 8, 512]) for i in range(2)]
        mbt = sb(st, "mbt", [128, DEPTH, 48])
        pm = ps(st, "pm", [128, 48, 2])
        S.dma("sp", mbt[:], mod_bT.rearrange("l p c -> p l c"), w=["mbt"])
        it = 0
        for l in range(n_layers):
            for cb in range(12):
                wt = wm[it % 2]
                key = ("modw", it % 2)
                it += 1
                S.dma("sp" if it % 2 else "pool", wt[:],
                      mod_w[l, :, cb * 512:(cb + 1) * 512].rearrange("(kc p) n -> p kc n", p=128), w=[key])
                for j in range(4):
                    cc = cb * 4 + j
                    for kc in range(8):
                        S.op("pe", lambda e, wt=wt, kc=kc, j=j, cc=cc: e.matmul(
                            pm[:, cc, :], lhsT=wt[:, kc, j * 128:(j + 1) * 128], rhs=csil[:, kc, :],
                            start=(kc == 0), stop=(kc == 7)), r=[key, "csil"], w=["pm"])
            for g in range(2):
                S.op("dve", lambda e, l=l, g=g: e.tensor_tensor(out=modT[:, l, :, g], in0=pm[:, :, g], in1=mbt[:, l, :],
                                                                op=ALU.add), r=["pm", "mbt"], w=["modT"])
            for c0 in (8, 32):
                S.op("dve", lambda e, l=l, c0=c0: e.tensor_scalar(out=modT[:, l, c0:c0 + 8, :], in0=modT[:, l, c0:c0 + 8, :],
                                                                  scalar1=1.0, scalar2=None, op0=ALU.add),
                     r=["modT"], w=["modT"])
    S.barrier()
    if stop_after == "A":
        dbg_t = nc.dram_tensor("dbg_modT", [128, DEPTH * 96], F32, kind="ExternalOutput").ap()
        S.dma("sp", dbg_t, modT[:].rearrange("p l c g -> p (l c g)"), r=["modT"])

    for l in range(n_layers):
        if stop_after == "A":
            break
        with ExitStack() as st:
            xmT = sb(st, "xmT", [128, 8, N], BF16)
            xt = [sb(st, f"xt{i}", [128, D]) for i in range(2)]
            pT = [ps(st, f"pT{i}", [128, 4, 128]) for i in range(2)]
            for tt in range(NT):
                g = 1 if tt < 2 else 0
                x_t = xt[tt % 2]
                S.dma("sp", x_t[:], xs[tt * 128:(tt + 1) * 128, :], r=[("xs", tt)], w=[("xt", tt % 2)])
                for hf in range(2):
                    for j in range(4):
                        kc = hf * 4 + j
                        S.op("pe", lambda e, x_t=x_t, kc=kc, hf=hf, j=j: e.transpose(
                            pT[hf][:, j, :], x_t[:, kc * 128:(kc + 1) * 128], ident[:]),
                            r=[("xt", tt % 2), "ident"], w=[("pT", hf)])
                    for j in range(4):
                        kc = hf * 4 + j
                        S.op("act", lambda e, kc=kc, hf=hf, j=j, tt=tt, g=g: e.activation(
                            out=xmT[:, kc, tt * 128:(tt + 1) * 128], in_=pT[hf][:, j, :], func=AF.Identity,
                            scale=modT[:, l, 8 + kc, g:g + 1], bias=modT[:, l, kc, g:g + 1]),
                            r=[("pT", hf), "modT"], w=[("xmT", tt)])
            wb = [sb(st, f"wb{i}", [128, 8, 512], BF16) for i in range(3)]
            pp = [ps(st, f"pp{i}", [128, 512]) for i in range(4)]
            stg = [sb(st, f"stg{i}", [128, 512]) for i in range(4)]
            stgT = [sb(st, f"stgT{i}", [128, N]) for i in range(2)]
            stgG = [sb(st, f"stgG{i}", [128, N], BF16) for i in range(2)]
            blks = [(c0, min(512, C_SS - c0)) for c0 in range(0, C_SS, 512)] + \
                   [(c0, min(512, INC - c0)) for c0 in range(C_SS, INC, 512)]
            k = 0
            kf = 0
            for bi, (c0, cw) in enumerate(blks):
                w_t = wb[bi % 3]
                wkey = ("wb", bi % 3)
                S.dma("pool", w_t[:, :, 0:cw], w_in[l, :, c0:c0 + cw].rearrange("(kc p) n -> p kc n", p=128), w=[wkey])
                if c0 < C_SS:
                    for tt in range(NT):
                        i = k % 4
                        k += 1
                        for kc in range(8):
                            S.op("pe", lambda e, i=i, kc=kc, tt=tt, w_t=w_t, cw=cw: e.matmul(
                                pp[i][:, 0:cw], lhsT=xmT[:, kc, tt * 128:(tt + 1) * 128], rhs=w_t[:, kc, 0:cw],
                                start=(kc == 0), stop=(kc == 7)), r=[("xmT", tt), wkey], w=[("pp", i)])
                        if i % 2 == 0:
                            S.op("act", lambda e, i=i, cw=cw: e.activation(out=stg[i][:, 0:cw], in_=pp[i][:, 0:cw], func=AF.Copy),
                                 r=[("pp", i)], w=[("stg", i)])
                        else:
                            S.op("dve", lambda e, i=i, cw=cw: e.tensor_copy(out=stg[i][:, 0:cw], in_=pp[i][:, 0:cw]),
                                 r=[("pp", i)], w=[("stg", i)])
                        S.dma("sp", p_tm[tt * 128:(tt + 1) * 128, c0:c0 + cw], stg[i][:, 0:cw], r=[("stg", i)],
                              w=[("p_tm", tt)])
                else:
                    for j in range(cw // 128):
                        cc = c0 + j * 128
                        is_g = cc >= C_G
                        f = kf % 2
                        kf += 1
                        dstt = stgG[f] if is_g else stgT[f]
                        skey = ("stgG", f) if is_g else ("stgT", f)
                        for t0 in range(0, N, 512):
                            tw = min(512, N - t0)
                            i = k % 4
                            k += 1
                            for kc in range(8):
                                S.op("pe", lambda e, i=i, kc=kc, t0=t0, tw=tw, w_t=w_t, j=j: e.matmul(
                                    pp[i][:, 0:tw], lhsT=w_t[:, kc, j * 128:(j + 1) * 128], rhs=xmT[:, kc, t0:t0 + tw],
                                    start=(kc == 0), stop=(kc == 7)),
                                    r=[("xmT", t) for t in range(t0 // 128, (t0 + tw) // 128)] + [wkey], w=[("pp", i)])
                            if is_g:
                                S.op("act", lambda e, i=i, t0=t0, tw=tw, dstt=dstt: e.activation(
                                    out=dstt[:, t0:t0 + tw], in_=pp[i][:, 0:tw], func=AF.Sigmoid), r=[("pp", i)], w=[skey])
                            else:
                                S.op("dve", lambda e, i=i, t0=t0, tw=tw, dstt=dstt: e.tensor_copy(
                                    out=dstt[:, t0:t0 + tw], in_=pp[i][:, 0:tw]), r=[("pp", i)], w=[skey])
                        if is_g:
                            S.dma("sp", gT[cc - C_G:cc - C_G + 128, :], dstt[:], r=[skey], w=[("gT", (cc - C_G) // 128)])
                        else:
                            S.dma("sp", p_ssT[cc - C_SS:cc - C_SS + 128, :], dstt[:], r=[skey], w=[("p_ssT", (cc - C_SS) // 128)])
        S.barrier()
        if stop_after == "B":
            break


        with ExitStack() as st:
            mu = sb(st, "mu", [128, 1792])
            rwp = sb(st, "rwp", [128, 5, 512])
            lrb = sb(st, "lrb", [1, 4, 512])
            lrw = sb(st, "lrw", [128, 2, 512])
            gup = sb(st, "gup", [128, 512])
            S.dma("sp", mu[:], mu_in[l], w=["mu"])
            S.dma("sp", rwp[:], rwp_in[l].rearrange("k p c -> p k c"), w=["rwp"])
            S.dma("sp", lrb[:], lrb_in[l:l + 1], w=["lrb"])
            S.dma("sp", lrw[:], lrw_in[l], w=["lrw"])
            S.dma("sp", gup[:], gup_in[l], w=["gup"])
            P0 = [sb(st, f"P0{i}", [128, 1792]) for i in range(2)]
            Pm = [sb(st, f"Pm{i}", [128, 1792]) for i in range(2)]
            Pp = [sb(st, f"Pp{i}", [128, 1792]) for i in range(2)]
            tl2 = [sb(st, f"tl{i}", [128, 1792]) for i in range(2)]
            pl = [sb(st, f"pl{i}", [128, 1792]) for i in range(2)]
            lrT2 = [sb(st, f"lrT{i}", [128, 128]) for i in range(2)]
            gsT2 = [sb(st, f"gsT{i}", [128, 128]) for i in range(2)]
            ptr = [ps(st, f"ptr{i}", [128, 128]) for i in range(2)]
            pq = [ps(st, f"pq{i}", [128, 512]) for i in range(5)]
            sg2 = [[sb(st, f"sg{i}{q}", [128, 512]) for q in range(4)] for i in range(2)]
            o52 = [[sb(st, f"o5{i}{q}", [128, 512]) for q in range(10)] for i in range(2)]
            kx2 = [sb(st, f"kx{i}", [128, 512]) for i in range(2)]
            sq52 = [sb(st, f"sq5{i}", [128, 512]) for i in range(2)]
            ss82 = [sb(st, f"ss8{i}", [128, 8]) for i in range(2)]

            def c_tile(tt):
                i = tt % 2
                tl, lrT, gsT, sg, o5, kx, sq5, ss8 = tl2[i], lrT2[i], gsT2[i], sg2[i], o52[i], kx2[i], sq52[i], ss82[i]
                lo, hi = tt * 128, (tt + 1) * 128
                first = tt in (0, 2)
                last = tt in (1, NT - 1)
                S.dma("sp", P0[i][:], p_tm[lo:hi, 0:1792], r=[("p_tm", tt)], w=[("P0", i)])
                if first:
                    S.op("pool", lambda e, i=i: e.memset(Pm[i][:], 0.0), w=[("Pm", i)])
                    S.dma("sp", Pm[i][1:128, :], p_tm[lo:hi - 1, 0:1792], r=[("p_tm", tt)], w=[("Pm", i)])
                else:
                    S.dma("sp", Pm[i][:], p_tm[lo - 1:hi - 1, 0:1792], r=[("p_tm", tt), ("p_tm", tt - 1)], w=[("Pm", i)])
                if last:
                    S.op("pool", lambda e, i=i: e.memset(Pp[i][:], 0.0), w=[("Pp", i)])
                    S.dma("sp", Pp[i][0:127, :], p_tm[lo + 1:hi, 0:1792], r=[("p_tm", tt)], w=[("Pp", i)])
                else:
                    S.dma("sp", Pp[i][:], p_tm[lo + 1:hi + 1, 0:1792], r=[("p_tm", tt), ("p_tm", tt + 1)], w=[("Pp", i)])
                p_ = pl[i]
                pk = ("pl", i)
                tt_("dve", tl[:], Pm[i][:], Pp[i][:], ALU.add, r=[("Pm", i), ("Pp", i)], w=[("tl", i)])
                S.op("dve", lambda e, i=i: e.scalar_tensor_tensor(out=tl[:], in0=tl[:], scalar=0.5, in1=P0[i][:], op0=ALU.mult,
                                                                op1=ALU.subtract), r=[("tl", i), ("P0", i)], w=[("tl", i)])
                tt_("dve", tl[:], tl[:], mu[:], ALU.mult, r=[("tl", i), "mu"], w=[("tl", i)])
                tt_("dve", p_[:], tl[:], P0[i][:], ALU.add, r=[("tl", i), ("P0", i)], w=[pk])
                S.dma("act", rw_s["r"][lo:hi, :], p_[:, 0:512], r=[pk], w=[("rw_r", tt)])
                S.dma("act", rw_s["v"][lo:hi, :], p_[:, 1024:1536], r=[pk], w=[("rw_v", tt)])
                yield
                tr_(ptr[0][:], p_[:, 1536:1664], ident[:], r=[pk, "ident"], w=[("ptr", 0)])
                tr_(ptr[1][:], p_[:, 1664:1792], ident[:], r=[pk, "ident"], w=[("ptr", 1)])
                act_(lrT[0:64, :], ptr[0][0:64, :], AF.Tanh, r=[("ptr", 0)], w=[("lrT", i)])
                act_(lrT[64:128, :], ptr[0][64:128, :], AF.Copy, r=[("ptr", 0)], w=[("lrT", i)])
                act_(gsT[:], ptr[1][:], AF.Sigmoid, r=[("ptr", 1)], w=[("gsT", i)])
                yield
                for d in range(2):
                    mm_(pq[d][:], ones[0:1, :], lrb[0:1, d, :], True, False, r=["ones", "lrb"], w=[("pq", d)])
                    mm_(pq[d][:], lrT[0:64, :], lrw[0:64, d, :], False, True, r=[("lrT", i), "lrw"], w=[("pq", d)])
                    mm_(pq[2 + d][:], ones[0:1, :], lrb[0:1, 2 + d, :], True, False, r=["ones", "lrb"], w=[("pq", 2 + d)])
                    mm_(pq[2 + d][:], lrT[64:128, :], lrw[64:128, d, :], False, True, r=[("lrT", i), "lrw"], w=[("pq", 2 + d)])
                mm_(pq[4][:], gsT[:], gup[:], True, True, r=[("gsT", i), "gup"], w=[("pq", 4)])
                for q in range(4):
                    act_(sg[q][:], pq[q][:], AF.Sigmoid, r=[("pq", q)], w=[("sg", i, q)])
                o = o5
                ok = lambda q: ("o5", i, q)
                cp_("dve", o[9][:], pq[4][:], r=[("pq", 4)], w=[ok(9)])
                S.dma("act", rw_s["g"][lo:hi, :], o[9][:], r=[ok(9)], w=[("rw_g", tt)])
                yield
                k_ = p_[:, 512:1024]
                r_ = p_[:, 0:512]
                h3 = lambda a: a.rearrange("p (h d) -> p h d", d=64)
                tt_("dve", kx[:], k_, rwp[:, 0, :], ALU.mult, r=[pk, "rwp"], w=[("kx", i)])
                tt_("dve", sq5[:], kx[:], kx[:], ALU.mult, r=[("kx", i)], w=[("sq5", i)])
                red_("dve", ss8[:], h3(sq5[:]), r=[("sq5", i)], w=[("ss8", i)])
                ts_("dve", ss8[:], ss8[:], 1e-24, None, ALU.max, None, r=[("ss8", i)], w=[("ss8", i)])
                act_(ss8[:], ss8[:], AF.Sqrt, r=[("ss8", i)], w=[("ss8", i)])
                S.op("dve", lambda e: e.reciprocal(out=ss8[:], in_=ss8[:]), r=[("ss8", i)], w=[("ss8", i)])
                tt_("dve", h3(o[0][:]), h3(kx[:]), ss8[:].unsqueeze(2).broadcast_to([128, 8, 64]), ALU.mult, r=[("kx", i), ("ss8", i)], w=[ok(0)])
                S.dma("act", rw_s["kk"][lo:hi, :], o[0][:], r=[ok(0)], w=[("rw_kk", tt)])
                yield
                for d in range(2):
                    ts_("pool", o[1 + d][:], sg[d][:], -0.6065306597126334, None, ALU.mult, None, r=[("sg", i, d)], w=[ok(1 + d)])
                    S.dma("act", rw_s[f"lw{d}"][lo:hi, :], o[1 + d][:], r=[ok(1 + d)], w=[(f"rw_lw{d}", tt)])
                    tt_("dve", o[3 + d][:], o[0][:], sg[2 + d][:], ALU.mult, r=[ok(0), ("sg", i, 2 + d)], w=[ok(3 + d)])
                    S.dma("act", rw_s[f"b{d}"][lo:hi, :], o[3 + d][:], r=[ok(3 + d)], w=[(f"rw_b{d}", tt)])
                    S.op("dve", lambda e, d=d: e.scalar_tensor_tensor(out=o[5 + d][:], in0=sg[2 + d][:], scalar=-1.0, in1=rwp[:, 1, :],
                                                                     op0=ALU.add, op1=ALU.mult), r=[("sg", i, 2 + d), "rwp"], w=[ok(5 + d)])
                    S.op("dve", lambda e, d=d, k_=k_: e.scalar_tensor_tensor(out=o[5 + d][:], in0=o[5 + d][:], scalar=1.0, in1=k_,
                                                                            op0=ALU.add, op1=ALU.mult), r=[ok(5 + d), pk], w=[ok(5 + d)])
                    S.dma("act", rw_s[f"kd{d}"][lo:hi, :], o[5 + d][:], r=[ok(5 + d)], w=[(f"rw_kd{d}", tt)])
                tt_("pool", o[7][:], o[5][:], o[6][:], ALU.add, r=[ok(5), ok(6)], w=[ok(7)])
                tt_("dve", o[8][:], r_, rwp[:, 2, :], ALU.mult, r=[pk, "rwp"], w=[ok(8)])
                tt_("pool", o[8][:], o[8][:], o[7][:], ALU.mult, r=[ok(8), ok(7)], w=[ok(8)])
                red_("dve", bsum[:, tt, :], h3(o[8][:]), r=[ok(8)], w=[("bsum", tt)])

            for t2 in range(0, NT, 2):
                run_interleaved([c_tile(t2), c_tile(t2 + 1)])
        S.barrier()
        if stop_after == "C":
            break

        with ExitStack() as st:
            pf = [ps(st, f"pf{i}", [128, 512]) for i in range(8)]
            npf = [0]

            npfd = [0, 0]

            def PFd(d):
                i = 4 * d + npfd[d] % 4
                npfd[d] += 1
                return pf[i], ("pf", i)

            Bd = []
            for d in range(2):
                B = {}
                for nm in ("r", "v", "kk", "lw", "b", "kd", "cum", "e1", "e2", "e3", "e4", "abarf", "bbarf", "kbarf", "rbarf"):
                    B[nm] = sb(st, f"D{d}{nm}", [128, 512])
                for nm in ("abar", "Bt", "Kt", "Vb", "Z", "U"):
                    B[nm] = sb(st, f"D{d}{nm}", [128, 512], BF16)
                for nm in ("abarT", "bbarT", "kbarT"):
                    B[nm] = sb(st, f"D{d}{nm}", [128, 4, 128], BF16)
                for nm in ("AakT", "MrbT", "MrkT", "abarZ", "bbarZ", "rbarZ", "Rb"):
                    B[nm] = sb(st, f"D{d}{nm}", [128, 8, 128], BF16)
                for nm in ("AabT", "Aab", "P0", "P1", "Q0", "Q1", "R0", "R1"):
                    B[nm] = sb(st, f"D{d}{nm}", [128, 8, 128], F32)
                for nm in ("abarZ", "bbarZ", "rbarZ"):
                    S.op("pool", lambda e, B=B, nm=nm: e.memset(B[nm][:], 0.0), w=[(nm, d)])
                B["rbT1"] = sb(st, f"D{d}rbT1", [128, 8, 128], BF16)
                B["WT"] = sb(st, f"D{d}WT", [128, 8, 128], BF16)
                S.op("pool", lambda e, B=B: e.memset(B["rbT1"][:], 0.0), w=[("rbT1", d)])
                S.op("pool", lambda e, B=B: e.memset(B["WT"][:], 0.0), w=[("WT", d)])
                B["gl"] = sb(st, f"D{d}gl", [64, 8])
                B["Sf"] = sb(st, f"D{d}Sf", [64, 8, 64])
                B["Stmp"] = sb(st, f"D{d}Stmp", [64, 8, 64])
                B["Sbf"] = sb(st, f"D{d}Sbf", [128, 8, 64], BF16)
                S.op("pool", lambda e, B=B: e.memset(B["Sf"][:], 0.0), w=[("Sf", d)])
                S.op("pool", lambda e, B=B: e.memset(B["Sbf"][:], 0.0), w=[("Sbf", d)])
                Bd.append(B)

            def rw_chunk(c, d):
                B = Bd[d]
                K_ = lambda n: (n, d)
                lo, hi = c * 128, (c + 1) * 128
                hs = lambda h: slice(h * 64, (h + 1) * 64)
                for nm, src in (("r", "r"), ("v", "v"), ("kk", "kk"), ("lw", f"lw{d}"), ("b", f"b{d}"), ("kd", f"kd{d}")):
                    S.dma("sp", B[nm][:], rw_s[src][lo:hi, :], r=[("rw_" + src, c)], w=[K_(nm)])
                    yield
                pc, pck = PFd(d)
                mm_(pc[:], masks[:, 2 + d, :], B["lw"][:], True, True, r=["masks", K_("lw")], w=[pck])
                pt, ptk = PFd(d)
                mm_(pt[:], ones[:], B["lw"][:], True, True, r=["ones", K_("lw")], w=[ptk])
                act_(B["cum"][:], pc[:], AF.Copy, r=[pck], w=[K_("cum")])
                yield
                act_(B["e1"][:], B["cum"][:], AF.Exp, r=[K_("cum")], w=[K_("e1")])
                yield
                act_(B["e2"][:], B["cum"][:], AF.Exp, r=[K_("cum")], w=[K_("e2")], scale=-1.0)
                yield
                tt_("pool", B["e3"][:], B["cum"][:], B["lw"][:], ALU.subtract, r=[K_("cum"), K_("lw")], w=[K_("e3")])
                yield
                act_(B["e3"][:], B["e3"][:], AF.Exp, r=[K_("e3")], w=[K_("e3")])
                yield
                tt_("dve", B["e4"][:], pt[:], B["cum"][:], ALU.subtract, r=[ptk, K_("cum")], w=[K_("e4")])
                yield
                act_(B["e4"][:], B["e4"][:], AF.Exp, r=[K_("e4")], w=[K_("e4")])
                yield
                S.op("dve", lambda e: e.scalar_tensor_tensor(out=B["abarf"][:], in0=B["kk"][:], scalar=-1.0, in1=B["e3"][:],
                                                            op0=ALU.mult, op1=ALU.mult), r=[K_("kk"), K_("e3")], w=[K_("abarf")])
                yield
                tt_("pool", B["bbarf"][:], B["b"][:], B["e2"][:], ALU.mult, r=[K_("b"), K_("e2")], w=[K_("bbarf")])
                yield
                tt_("dve", B["kbarf"][:], B["kd"][:], B["e2"][:], ALU.mult, r=[K_("kd"), K_("e2")], w=[K_("kbarf")])
                yield
                tt_("pool", B["rbarf"][:], B["r"][:], B["e1"][:], ALU.mult, r=[K_("r"), K_("e1")], w=[K_("rbarf")])
                yield
                tt_("dve", B["Bt"][:], B["b"][:], B["e4"][:], ALU.mult, r=[K_("b"), K_("e4")], w=[K_("Bt")])
                yield
                tt_("pool", B["Kt"][:], B["kd"][:], B["e4"][:], ALU.mult, r=[K_("kd"), K_("e4")], w=[K_("Kt")])
                yield
                cp_("pool", B["Vb"][:], B["v"][:], r=[K_("v")], w=[K_("Vb")])
                yield
                pg, pgk = PFd(d)
                for h in range(8):
                    mm_(pg[0:64, 2 * h:2 * h + 2], B["lw"][:, hs(h)], ones[:, 0:2], True, True, r=[K_("lw"), "ones"], w=[pgk])
                act_(B["gl"][:], pg[0:64, 0:16].rearrange("p (h two) -> p h two", two=2)[:, :, 0], AF.Exp, r=[pgk], w=[K_("gl")])
                yield
                if DSTAGE < 1:
                    return
                for nm in ("abar", "bbar", "kbar", "rbar")[:DSUB]:
                    p2, p2k = PFd(d)
                    for j in range(4):
                        tr_(p2[:, j * 128:(j + 1) * 128], B[nm + "f"][:, j * 128:(j + 1) * 128], ident[:], r=[K_(nm + "f"), "ident"], w=[p2k])
                    p3 = p2[:].rearrange("p (j t) -> p j t", t=128)
                    if nm != "rbar":
                        act_(B[nm + "T"][:], p3, AF.Copy, r=[p2k], w=[K_(nm + "T")])
                        yield
                    if nm != "kbar":
                        z4 = lambda lo_: B[nm + "Z"][lo_:lo_ + 64].rearrange("p (j two) t -> p j two t", two=2)
                        act_(z4(0)[:, :, 0, :], p3[0:64], AF.Copy, r=[p2k], w=[K_(nm + "Z")])
                        yield
                        cp_("dve", z4(64)[:, :, 1, :], p3[64:128], r=[p2k], w=[K_(nm + "Z")])
                        yield
                for hh in range(2 if DSUB > 4 else 0):
                    p2, p2k = PFd(d)
                    for h4 in range(4):
                        h = hh * 4 + h4
                        tr_(p2[0:64, h4 * 128:(h4 + 1) * 128], B["rbarf"][:, hs(h)], ident[:], r=[K_("rbarf"), "ident"], w=[p2k])
                    act_(B["rbT1"][0:64, hh * 4:(hh + 1) * 4, :], p2[0:64, :].rearrange("p (j t) -> p j t", t=128), AF.Copy, r=[p2k], w=[K_("rbT1")])
                    yield
                cp_("dve", B["abar"][:], B["abarf"][:], r=[K_("abarf")], w=[K_("abar")])
                yield
                if DSTAGE < 2:
                    return
                for nm, lt, rt, mk in (("AabT", "bbarT", "abarZ", d), ("Aab", "abarT", "bbarZ", 1 - d), ("AakT", "kbarT", "abarZ", d),
                                       ("MrbT", "bbarT", "rbarZ", 2 + d), ("MrkT", "kbarT", "rbarZ", 2 + d)):
                    for hh in range(2):
                        p_, pk_ = PFd(d)
                        for h4 in range(4):
                            h = hh * 4 + h4
                            mm_(p_[:, h4 * 128:(h4 + 1) * 128], B[lt][:, h // 2, :], B[rt][:, h, :], True, True,
                                r=[K_(lt), K_(rt)], w=[pk_])
                        tt_("dve", B[nm][:, hh * 4:(hh + 1) * 4, :], p_[:].rearrange("p (h t) -> p h t", t=128),
                            masks[:, mk, :].unsqueeze(1).broadcast_to([128, 4, 128]), ALU.mult, r=[pk_, "masks"], w=[K_(nm)])
                        yield
                if DSTAGE < 3:
                    return
                P_, Q_ = "AabT", "Aab"
                tt_("dve", B["R0"][:], B["AabT"][:], ident[:].unsqueeze(1).broadcast_to([128, 8, 128]), ALU.add,
                    r=[K_("AabT"), "ident"], w=[K_("R0")])
                yield
                Rc = "R0"
                for lev in range(6):
                    Pn, Qn, Rn = f"P{lev % 2}", f"Q{lev % 2}", f"R{(lev + 1) % 2}"
                    for hh in range(2):
                        if lev < 5:
                            p_, pk_ = PFd(d)
                            for h4 in range(4):
                                h = hh * 4 + h4
                                mm_(p_[:, h4 * 128:(h4 + 1) * 128], B[Q_][:, h, :], B[P_][:, h, :], True, True, r=[K_(Q_), K_(P_)], w=[pk_])
                            act_(B[Pn][:, hh * 4:(hh + 1) * 4, :], p_[:].rearrange("p (h t) -> p h t", t=128), AF.Copy, r=[pk_], w=[K_(Pn)])
                            yield
                        p_, pk_ = PFd(d)
                        for h4 in range(4):
                            h = hh * 4 + h4
                            mm_(p_[:, h4 * 128:(h4 + 1) * 128], B[P_][:, h, :], B[Q_][:, h, :], True, True, r=[K_(Q_), K_(P_)], w=[pk_])
                        cp_("dve", B[Qn][:, hh * 4:(hh + 1) * 4, :], p_[:].rearrange("p (h t) -> p h t", t=128), r=[pk_], w=[K_(Qn)])
                        yield
                    for hh in range(2):
                        p_, pk_ = PFd(d)
                        for h4 in range(4):
                            h = hh * 4 + h4
                            mm_(p_[:, h4 * 128:(h4 + 1) * 128], B[Qn][:, h, :], B[Rc][:, h, :], True, True, r=[K_(Qn), K_(Rc)], w=[pk_])
                        tt_("dve", B[Rn][:, hh * 4:(hh + 1) * 4, :], p_[:].rearrange("p (h t) -> p h t", t=128),
                            B[Rc][:, hh * 4:(hh + 1) * 4, :], ALU.add, r=[pk_, K_(Rc)], w=[K_(Rn)])
                        yield
                    P_, Q_, Rc = Pn, Qn, Rn
                if DSTAGE < 4:
                    return
                act_(B["Rb"][:], B[Rc][:], AF.Copy, r=[K_(Rc)], w=[K_("Rb")])
                yield
                Rc = "Rb"
                p_, pk_ = PFd(d)
                for h in range(8):
                    mm_(p_[:, hs(h)], B["AakT"][:, h, :], B["Vb"][:, hs(h)], True, True, r=[K_("AakT"), K_("Vb")], w=[pk_])
                act_(B["Z"][:], p_[:], AF.Copy, r=[pk_], w=[K_("Z")])
                yield
                p_, pk_ = PFd(d)
                for h in range(8):
                    mm_(p_[:, hs(h)], B[Rc][:, h, :], B["Z"][:, hs(h)], True, True, r=[K_(Rc), K_("Z")], w=[pk_])
                cp_("dve", B["e1"][:], p_[:], r=[pk_], w=[K_("e1")])
                yield
                for hh in range(2):
                    p_, pk_ = PFd(d)
                    for h4 in range(4):
                        h = hh * 4 + h4
                        mm_(p_[0:64, h4 * 128:(h4 + 1) * 128], B["abar"][:, hs(h)], B[Rc][:, h, :], True, True, r=[K_("abar"), K_(Rc)], w=[pk_])
                    act_(B["WT"][0:64, hh * 4:(hh + 1) * 4, :], p_[0:64, :].rearrange("p (h t) -> p h t", t=128), AF.Copy, r=[pk_], w=[K_("WT")])
                    yield
                if DSTAGE < 5:
                    return
                p_, pk_ = PFd(d)
                for h in range(8):
                    mm_(p_[:, hs(h)], B["WT"][:, h, :], B["Sbf"][:, h, :], True, True, r=[K_("WT"), K_("Sbf")], w=[pk_])
                tt_("dve", B["U"][:], p_[:], B["e1"][:], ALU.add, r=[pk_, K_("e1")], w=[K_("U")])
                yield
                py, pyk = PFd(d)
                for h in range(8):
                    mm_(py[:, hs(h)], B["rbT1"][:, h, :], B["Sbf"][:, h, :], True, False, r=[K_("rbT1"), K_("Sbf")], w=[pyk])
                    mm_(py[:, hs(h)], B["MrbT"][:, h, :], B["U"][:, hs(h)], False, False, r=[K_("MrbT"), K_("U")], w=[pyk])
                    mm_(py[:, hs(h)], B["MrkT"][:, h, :], B["Vb"][:, hs(h)], False, True, r=[K_("MrkT"), K_("Vb")], w=[pyk])
                act_(B["e2"][:], py[:], AF.Copy, r=[pyk], w=[K_("e2")])
                yield
                S.dma("act", rw_s[f"y{d}"][lo:hi, :], B["e2"][:], r=[K_("e2")], w=[(f"rw_y{d}", c)])
                yield
                pS, pSk = PFd(d)
                for h in range(8):
                    mm_(pS[0:64, hs(h)], B["Kt"][:, hs(h)], B["Vb"][:, hs(h)], True, False, r=[K_("Kt"), K_("Vb")], w=[pSk])
                    mm_(pS[0:64, hs(h)], B["Bt"][:, hs(h)], B["U"][:, hs(h)], False, True, r=[K_("Bt"), K_("U")], w=[pSk])
                tt_("dve", B["Stmp"][:], B["Sf"][:], B["gl"][:].unsqueeze(2).broadcast_to([64, 8, 64]), ALU.mult,
                    r=[K_("Sf"), K_("gl")], w=[K_("Stmp")])
                yield
                tt_("dve", B["Sf"][:], pS[0:64, :].rearrange("p (h v) -> p h v", v=64), B["Stmp"][:], ALU.add, r=[pSk, K_("Stmp")], w=[K_("Sf")])
                yield
                act_(B["Sbf"][0:64], B["Sf"][:], AF.Copy, r=[K_("Sf")], w=[K_("Sbf")])
                yield

            ord0 = list(range(NT))
            ord1 = [1, 0] + list(range(NT - 1, 1, -1))
            for i in range(NT):
                run_interleaved([rw_chunk(ord0[i], 0), rw_chunk(ord1[i], 1)])
        S.barrier()
        if stop_after == "D":
            break

        with ExitStack() as st:
            rwp = sb(st, "rwpE", [128, 5, 512])
            S.dma("sp", rwp[:], rwp_in[l].rearrange("k p c -> p k c"), w=["rwpE"])
            Y0 = [sb(st, f"EY0{i}", [128, 512]) for i in range(2)]
            Y1 = [sb(st, f"EY1{i}", [128, 512]) for i in range(2)]
            Vv = [sb(st, f"EV{i}", [128, 512]) for i in range(2)]
            Gg = [sb(st, f"EG{i}", [128, 512]) for i in range(2)]
            y = sb(st, "Ey", [128, 512])
            ysq = sb(st, "Eysq", [128, 512])
            bon = sb(st, "Ebon", [128, 512])
            yab = sb(st, "Eyab", [128, 512], BF16)
            st8 = sb(st, "Est8", [128, 4, 8])
            yT = [sb(st, f"EyT{i}", [128, 4, 128], BF16) for i in range(2)]
            pbE = [ps(st, f"pbE{i}", [128, 4, 128], BF16) for i in range(2)]
            h3 = lambda a: a.rearrange("p (h d) -> p h d", d=64)
            b3 = lambda a: a.unsqueeze(2).broadcast_to([128, 8, 64])
            for tt in range(NT):
                i = tt % 2
                lo, hi = tt * 128, (tt + 1) * 128
                S.dma("sp", Y0[i][:], rw_s["y0"][lo:hi, :], r=[("rw_y0", tt)], w=[("EY0", i)])
                S.dma("sp", Y1[i][:], rw_s["y1"][lo:hi, :], r=[("rw_y1", tt)], w=[("EY1", i)])
                S.dma("sp", Vv[i][:], rw_s["v"][lo:hi, :], r=[("rw_v", tt)], w=[("EV", i)])
                S.dma("sp", Gg[i][:], rw_s["g"][lo:hi, :], r=[("rw_g", tt)], w=[("EG", i)])
                tt_("pool", y[:], Y0[i][:], Y1[i][:], ALU.add, r=[("EY0", i), ("EY1", i)], w=["Ey"])
                red_("dve", st8[:, 0, :], h3(y[:]), r=["Ey"], w=["Est8"])
                tt_("dve", ysq[:], y[:], y[:], ALU.mult, r=["Ey"], w=["Eysq"])
                red_("dve", st8[:, 1, :], h3(ysq[:]), r=["Eysq"], w=["Est8"])
                ts_("dve", st8[:, 0, :], st8[:, 0, :], 1.0 / 64, None, ALU.mult, None, r=["Est8"], w=["Est8"])
                tt_("dve", st8[:, 2, :], st8[:, 0, :], st8[:, 0, :], ALU.mult, r=["Est8"], w=["Est8"])
                S.op("dve", lambda e: e.scalar_tensor_tensor(out=st8[:, 3, :], in0=st8[:, 1, :], scalar=1.0 / 64, in1=st8[:, 2, :],
                                                           op0=ALU.mult, op1=ALU.subtract), r=["Est8"], w=["Est8"])
                rstd_(st8[:, 3, :], st8[:, 3, :], 1.0, GN_EPS, r=["Est8"], w=["Est8"])
                tt_("dve", h3(y[:]), h3(y[:]), b3(st8[:, 0, :]), ALU.subtract, r=["Ey", "Est8"], w=["Ey"])
                tt_("dve", h3(y[:]), h3(y[:]), b3(st8[:, 3, :]), ALU.mult, r=["Ey", "Est8"], w=["Ey"])
                tt_("dve", y[:], y[:], rwp[:, 3, :], ALU.mult, r=["Ey", "rwpE"], w=["Ey"])
                tt_("pool", y[:], y[:], rwp[:, 4, :], ALU.add, r=["Ey", "rwpE"], w=["Ey"])
                tt_("pool", h3(bon[:]), h3(Vv[i][:]), b3(bsum[:, tt, :]), ALU.mult, r=[("EV", i), ("bsum", tt)], w=["Ebon"])
                tt_("dve", y[:], y[:], bon[:], ALU.add, r=["Ey", "Ebon"], w=["Ey"])
                tt_("pool", yab[:], y[:], Gg[i][:], ALU.mult, r=["Ey", ("EG", i)], w=["Eyab"])
                for j in range(4):
                    tr_(pbE[i][:, j, :], yab[:, j * 128:(j + 1) * 128], identb[:], r=["Eyab", "identb"], w=[("pbE", i)])
                act_(yT[i][:], pbE[i][:], AF.Copy, r=[("pbE", i)], w=[("EyT", i)])
                S.dma("act", yaT[:, lo:hi].rearrange("(j p) t -> p j t", p=128), yT[i][:], r=[("EyT", i)], w=[("yaT", tt)])
        S.barrier()
        if stop_after == "E":
            break

        with ExitStack() as st:
            mcol = sb(st, "mcol", [128, 4])
            S.dma("sp", mcol[:], mcol_in, w=["mcol"])
            Ep = [[sb(st, f"Ep{d}{k}", [128, 2048]) for k in range(2)] for d in range(2)]
            En = [[sb(st, f"En{d}{k}", [128, 2048]) for k in range(2)] for d in range(2)]
            Bp = [sb(st, f"Bp{d}", [128, 4, 1024], BF16) for d in range(2)]
            Cb = sb(st, "Cb", [128, 32, 128], BF16)
            S.dma("pool", Cb[:], cblk_in[l], w=["Cb"])
            trib = sb(st, "trib", [128, 2, 128], BF16)
            selb = sb(st, "selb", [128, 4, 128], BF16)
            S.dma("pool", trib[:], masks_in[2:4].rearrange("m p t -> p m t"), w=["trib"])
            S.dma("pool", selb[:], sel_in.rearrange("m p t -> p m t"), w=["selb"])
            MAGIC = 12582912.0
            C1 = 6.28125
            C2 = float(2 * np.pi - 6.28125)
            with ExitStack() as st2:
                T_ = {nm: sb(st2, "G" + nm, [128, 2048]) for nm in ("are", "aim", "dt", "x1", "x2", "arg", "k", "sn", "cs", "mg", "t1", "t2")}
                bb = [sb(st2, f"Gbb{k}", [128, 4, 512]) for k in range(2)]

                G = lambda nm: "G" + nm

                def g_tt(o, x, y, op, eng="dve"):
                    tt_(eng, T_[o][:], T_[x][:], T_[y][:], op, r=[G(x), G(y)], w=[G(o)])

                def sincos(src, shift, dst):
                    a_, k_ = T_["arg"], T_["k"]
                    ts_("dve", a_[:], T_[src][:], shift, None, ALU.add, None, r=[G(src)], w=[G("arg")])
                    ts_("dve", k_[:], a_[:], float(1 / (2 * np.pi)), MAGIC, ALU.mult, ALU.add, r=[G("arg")], w=[G("k")])
                    ts_("dve", k_[:], k_[:], -MAGIC, None, ALU.add, None, r=[G("k")], w=[G("k")])
                    S.op("dve", lambda e: e.scalar_tensor_tensor(out=a_[:], in0=k_[:], scalar=-C1, in1=a_[:], op0=ALU.mult, op1=ALU.add),
                         r=[G("k"), G("arg")], w=[G("arg")])
                    S.op("dve", lambda e: e.scalar_tensor_tensor(out=a_[:], in0=k_[:], scalar=-C2, in1=a_[:], op0=ALU.mult, op1=ALU.add),
                         r=[G("k"), G("arg")], w=[G("arg")])
                    act_(T_[dst][:], a_[:], AF.Sin, r=[G("arg")], w=[G(dst)])

                c4 = lambda t: t[:].rearrange("p (q c) -> p q c", c=512)
                for d in range(2):
                    S.dma("sp", T_["are"][:], sstab_in[l, d, 0], w=[G("are")])
                    S.dma("sp", T_["aim"][:], sstab_in[l, d, 1], w=[G("aim")])
                    S.dma("sp", T_["dt"][:], sstab_in[l, d, 2], w=[G("dt")])
                    if d == 0:
                        S.dma("sp", bb[0][:], bblk_in[l, 0], w=["Gbb0"])
                        S.dma("sp", bb[1][:], bblk_in[l, 1], w=["Gbb1"])
                    ts_("dve", T_["are"][:], T_["are"][:], -1e-4, None, ALU.min, None, r=[G("are")], w=[G("are")])
                    act_(T_["dt"][:], T_["dt"][:], AF.Exp, r=[G("dt")], w=[G("dt")])
                    g_tt("x1", "are", "dt", ALU.mult)
                    g_tt("x2", "aim", "dt", ALU.mult)
                    sincos("x2", 0.0, "sn")
                    sincos("x2", PI / 2, "cs")
                    act_(T_["mg"][:], T_["x1"][:], AF.Exp, r=[G("x1")], w=[G("mg")])
                    g_tt("t1", "sn", "mg", ALU.mult)
                    g_tt("t2", "cs", "mg", ALU.mult)
                    ts_("dve", T_["t2"][:], T_["t2"][:], -1.0, None, ALU.add, None, r=[G("t2")], w=[G("t2")])
                    g_tt("sn", "are", "are", ALU.mult)
                    g_tt("cs", "aim", "aim", ALU.mult)
                    g_tt("sn", "sn", "cs", ALU.add)
                    S.op("dve", lambda e: e.reciprocal(out=T_["sn"][:], in_=T_["sn"][:]), r=[G("sn")], w=[G("sn")])
                    g_tt("mg", "t2", "are", ALU.mult)
                    g_tt("cs", "t1", "aim", ALU.mult)
                    g_tt("mg", "mg", "cs", ALU.add)
                    g_tt("mg", "mg", "sn", ALU.mult)
                    g_tt("cs", "t1", "are", ALU.mult)
                    g_tt("k", "t2", "aim", ALU.mult)
                    g_tt("cs", "cs", "k", ALU.subtract)
                    g_tt("cs", "cs", "sn", ALU.mult)
                    tt_("dve", c4(T_["t1"]), bb[0][:], c4(T_["mg"]), ALU.mult, r=["Gbb0", G("mg")], w=[G("t1")])
                    tt_("dve", c4(T_["t2"]), bb[1][:], c4(T_["cs"]), ALU.mult, r=["Gbb1", G("cs")], w=[G("t2")])
                    tt_("pool", Bp[d][:, :, 0:512], c4(T_["t1"]), c4(T_["t2"]), ALU.subtract, r=[G("t1"), G("t2")], w=[("Bp", d)])
                    tt_("dve", c4(T_["t1"]), bb[0][:], c4(T_["cs"]), ALU.mult, r=["Gbb0", G("cs")], w=[G("t1")])
                    tt_("dve", c4(T_["t2"]), bb[1][:], c4(T_["mg"]), ALU.mult, r=["Gbb1", G("mg")], w=[G("t2")])
                    tt_("pool", Bp[d][:, :, 512:1024], c4(T_["t1"]), c4(T_["t2"]), ALU.add, r=[G("t1"), G("t2")], w=[("Bp", d)])
                    ts_("dve", T_["t1"][:], T_["x2"][:], mcol[:, d:d + 1], None, ALU.mult, None, r=[G("x2"), "mcol"], w=[G("t1")])
                    sincos("t1", 0.0, "sn")
                    sincos("t1", PI / 2, "cs")
                    act_(T_["mg"][:], T_["x1"][:], AF.Exp, r=[G("x1")], w=[G("mg")], scale=mcol[:, d:d + 1])
                    tt_("dve", Ep[d][0][:], T_["mg"][:], T_["cs"][:], ALU.mult, r=[G("mg"), G("cs")], w=[("Ep", d)])
                    tt_("pool", Ep[d][1][:], T_["mg"][:], T_["sn"][:], ALU.mult, r=[G("mg"), G("sn")], w=[("Ep", d)])
                    act_(T_["mg"][:], T_["x1"][:], AF.Exp, r=[G("x1")], w=[G("mg")], scale=mcol[:, 2 + d:3 + d])
                    tt_("dve", En[d][0][:], T_["mg"][:], T_["cs"][:], ALU.mult, r=[G("mg"), G("cs")], w=[("En", d)])
                    tt_("pool", En[d][1][:], T_["mg"][:], T_["sn"][:], ALU.mult, r=[G("mg"), G("sn")], w=[("En", d)])
            S.barrier()
            uf = [sb(st, f"Guf{i}", [128, 4, 128]) for i in range(2)]
            ub = [sb(st, f"Gub{i}", [128, 4, 128], BF16) for i in range(2)]
            hb = [[sb(st, f"Ghb{d}{i}", [128, 4, 2, 512], BF16) for i in range(2)] for d in range(2)]
            zb = [sb(st, f"Gzb{i}", [128, 2, 512], BF16) for i in range(2)]
            tmd = [[sb(st, f"Gtm{d_}{i}", [128, 512]) for i in range(4)] for d_ in range(2)]
            hT = [sb(st, f"GhT{i}", [128, 32, 128], BF16) for i in range(2)]
            ysb = [sb(st, f"Gys{i}", [128, 4, 128]) for i in range(2)]
            pg_ = [ps(st, f"pgf{i}", [128, 512]) for i in range(6)]
            pgb = [ps(st, f"pgb{i}", [128, 8, 128], BF16) for i in range(2)]
            cnt = {"f": 0, "b": 0, "z": 0, "u": 0}

            def PGF():
                i = cnt["f"] % 6
                cnt["f"] += 1
                return pg_[i], ("pgf", i)

            for d in range(2):
                for i in range(2):
                    S.op("pool", lambda e, d=d, i=i: e.memset(hb[d][i][:], 0.0), w=[("hb", d, i)])
            nstep = [0, 0]

            def s5_chunk(c, d):
                lo, hi = c * 128, (c + 1) * 128
                iu = d
                tm = tmd[d]
                S.dma("sp", uf[iu][:], p_ssT[:, lo:hi].rearrange("(q p) t -> p q t", p=128), r=[("p_ssT", q) for q in range(4)], w=[("uf", iu)])
                yield
                cp_("pool", ub[iu][:], uf[iu][:], r=[("uf", iu)], w=[("ub", iu)])
                yield
                hi_ = nstep[d] % 2
                nstep[d] += 1
                hcur, hprev = hb[d][hi_], hb[d][1 - hi_]
                kcur, kprev = ("hb", d, hi_), ("hb", d, 1 - hi_)
                for q in range(4):
                    cs_ = slice(q * 512, (q + 1) * 512)
                    pbr, pbrk = pg_[3 * d], ("pgf", 3 * d)
                    pbi, pbik = pg_[3 * d + 1], ("pgf", 3 * d + 1)
                    mm_(pbr[:], ub[iu][:, q, :], Bp[d][:, q, 0:512], True, True, r=[("ub", iu), ("Bp", d)], w=[pbrk])
                    mm_(pbi[:], ub[iu][:, q, :], Bp[d][:, q, 512:1024], True, True, r=[("ub", iu), ("Bp", d)], w=[pbik])
                    iz = d
                    z = zb[iz]
                    zk = ("zb", iz)
                    tt_("dve", tm[0][:], pbr[:], En[d][0][:, cs_], ALU.mult, r=[pbrk, ("En", d)], w=[("tm", d, 0)])
                    yield
                    tt_("dve", tm[1][:], pbi[:], En[d][1][:, cs_], ALU.mult, r=[pbik, ("En", d)], w=[("tm", d, 1)])
                    yield
                    tt_("dve", z[:, 0, :], tm[0][:], tm[1][:], ALU.add, r=[("tm", d, 0), ("tm", d, 1)], w=[zk])
                    yield
                    tt_("dve", tm[2][:], pbi[:], En[d][0][:, cs_], ALU.mult, r=[pbik, ("En", d)], w=[("tm", d, 2)])
                    yield
                    tt_("dve", tm[3][:], pbr[:], En[d][1][:, cs_], ALU.mult, r=[pbrk, ("En", d)], w=[("tm", d, 3)])
                    yield
                    tt_("dve", z[:, 1, :], tm[2][:], tm[3][:], ALU.subtract, r=[("tm", d, 2), ("tm", d, 3)], w=[zk])
                    yield
                    pcr, pcrk = pg_[3 * d + 2], ("pgf", 3 * d + 2)
                    pci, pcik = pg_[3 * d], ("pgf", 3 * d)
                    mm_(pcr[:], trib[:, d, :], z[:, 0, :], True, False, r=["trib", zk], w=[pcrk])
                    mm_(pcr[:], selb[:, d, :], hprev[:, q, 0, :], False, True, r=["selb", kprev], w=[pcrk])
                    mm_(pci[:], trib[:, d, :], z[:, 1, :], True, False, r=["trib", zk], w=[pcik])
                    mm_(pci[:], selb[:, 2 + d, :], hprev[:, q, 1, :], False, True, r=["selb", kprev], w=[pcik])
                    tt_("dve", tm[0][:], pcr[:], Ep[d][0][:, cs_], ALU.mult, r=[pcrk, ("Ep", d)], w=[("tm", d, 0)])
                    yield
                    tt_("dve", tm[1][:], pci[:], Ep[d][1][:, cs_], ALU.mult, r=[pcik, ("Ep", d)], w=[("tm", d, 1)])
                    yield
                    tt_("dve", hcur[:, q, 0, :], tm[0][:], tm[1][:], ALU.subtract, r=[("tm", d, 0), ("tm", d, 1)], w=[kcur])
                    yield
                    tt_("dve", tm[2][:], pcr[:], Ep[d][1][:, cs_], ALU.mult, r=[pcrk, ("Ep", d)], w=[("tm", d, 2)])
                    yield
                    tt_("dve", tm[3][:], pci[:], Ep[d][0][:, cs_], ALU.mult, r=[pcik, ("Ep", d)], w=[("tm", d, 3)])
                    yield
                    S.op("dve", lambda e, q=q: e.scalar_tensor_tensor(out=hcur[:, q, 1, :], in0=tm[2][:], scalar=-1.0, in1=tm[3][:],
                                                                   op0=ALU.mult, op1=ALU.subtract), r=[("tm", d, 2), ("tm", d, 3)], w=[kcur])
                    yield
                it = d
                for part in range(2):
                    for q2 in range(2):
                        ib = cnt["b"] % 2
                        cnt["b"] += 1
                        for q1 in range(2):
                            q = q2 * 2 + q1
                            for sb_ in range(4):
                                tr_(pgb[ib][:, q1 * 4 + sb_, :], hcur[:, q, part, sb_ * 128:(sb_ + 1) * 128], identb[:],
                                    r=[kcur, "identb"], w=[("pgb", ib)])
                        k0 = part * 16 + q2 * 8
                        act_(hT[it][:, k0:k0 + 8, :], pgb[ib][:], AF.Copy, r=[("pgb", ib)], w=[("hT", it)])
                        yield
                py, pyk = pg_[3 * d + 1], ("pgf", 3 * d + 1)
                for ct in range(4):
                    kqs = [part * 16 + 4 * ct + sb_ for part in range(2) for sb_ in range(4)]
                    for n_, kq in enumerate(kqs):
                        mm_(py[:, ct * 128:(ct + 1) * 128], Cb[:, kq, :], hT[it][:, kq, :], n_ == 0, n_ == 7, r=["Cb", ("hT", it)], w=[pyk])
                act_(ysb[it][:], py[:].rearrange("p (c t) -> p c t", t=128), AF.Copy, r=[pyk], w=[("ysb", it)])
                yield
                S.dma("act", ysT[d][:, lo:hi].rearrange("(c p) t -> p c t", p=128), ysb[it][:], r=[("ysb", it)], w=[(f"ysT{d}", c)])
                yield

            ord0 = list(range(NT))
            ord1 = [1, 0] + list(range(NT - 1, 1, -1))
            for i in range(NT):
                run_interleaved([s5_chunk(ord0[i], 0), s5_chunk(ord1[i], 1)])
        S.barrier()
        with ExitStack() as st:
            vec = sb(st, "Gvec", [128, 2, 4])
            S.dma("sp", vec[:], ssvec_in[l], w=["Gvec"])
            gw = sb(st, "Ggw", [128, 4, 512], BF16)
            S.dma("pool", gw[:], gluw_in[l].rearrange("(kc p) n -> p kc n", p=128), w=["Ggw"])
            Y0 = [sb(st, f"GY0{i}", [128, 4, 512]) for i in range(2)]
            Y1 = [sb(st, f"GY1{i}", [128, 4, 512]) for i in range(2)]
            Uu = [sb(st, f"GU{i}", [128, 4, 512]) for i in range(2)]
            yy = sb(st, "Gyy", [128, 4, 512])
            y3 = sb(st, "Gy3", [128, 4, 512])
            y2b = sb(st, "Gy2b", [128, 4, 512], BF16)
            sg_ = sb(st, "Gsg", [128, 512])
            yo = [sb(st, f"Gyo{i}", [128, 4, 512], BF16) for i in range(2)]
            pz = [ps(st, f"pz{i}", [128, 512]) for i in range(2)]
            kz = 0
            for gi, t0 in enumerate(range(0, N, 512)):
                tw = min(512, N - t0)
                i = gi % 2
                tks = list(range(t0 // 128, (t0 + tw) // 128))
                S.dma("sp", Y0[i][:, :, 0:tw], ysT[0][:, t0:t0 + tw].rearrange("(c p) t -> p c t", p=128), r=[("ysT0", t) for t in tks], w=[("GY0", i)])
                S.dma("sp", Y1[i][:, :, 0:tw], ysT[1][:, t0:t0 + tw].rearrange("(c p) t -> p c t", p=128), r=[("ysT1", t) for t in tks], w=[("GY1", i)])
                S.dma("sp", Uu[i][:, :, 0:tw], p_ssT[:, t0:t0 + tw].rearrange("(c p) t -> p c t", p=128), r=[("p_ssT", q) for q in range(4)], w=[("GU", i)])
                tt_("pool", yy[:, :, 0:tw], Y0[i][:, :, 0:tw], Y1[i][:, :, 0:tw], ALU.add, r=[("GY0", i), ("GY1", i)], w=["Gyy"])
                for c in range(4):
                    S.op("dve", lambda e, c=c, i=i, tw=tw: e.scalar_tensor_tensor(out=yy[:, c, 0:tw], in0=Uu[i][:, c, 0:tw], scalar=vec[:, 0, c:c + 1],
                                                                            in1=yy[:, c, 0:tw], op0=ALU.mult, op1=ALU.add),
                         r=[("GU", i), "Gvec", "Gyy"], w=["Gyy"])
                tt_("pool", y3[:, :, 0:tw], yy[:, :, 0:tw], yy[:, :, 0:tw], ALU.mult, r=["Gyy"], w=["Gy3"])
                ts_("pool", y3[:, :, 0:tw], y3[:, :, 0:tw], 0.044715, 1.0, ALU.mult, ALU.add, r=["Gy3"], w=["Gy3"])
                tt_("pool", y3[:, :, 0:tw], y3[:, :, 0:tw], yy[:, :, 0:tw], ALU.mult, r=["Gy3", "Gyy"], w=["Gy3"])
                act_(y3[:, :, 0:tw], y3[:, :, 0:tw], AF.Tanh, r=["Gy3"], w=["Gy3"], scale=0.7978845608028654)
                ts_("pool", y3[:, :, 0:tw], y3[:, :, 0:tw], 1.0, 0.5, ALU.add, ALU.mult, r=["Gy3"], w=["Gy3"])
                tt_("pool", yy[:, :, 0:tw], yy[:, :, 0:tw], y3[:, :, 0:tw], ALU.mult, r=["Gyy", "Gy3"], w=["Gyy"])
                cp_("pool", y2b[:, :, 0:tw], yy[:, :, 0:tw], r=["Gyy"], w=["Gy2b"])
                for c in range(4):
                    p_ = pz[kz % 2]
                    pk_ = ("pz", kz % 2)
                    kz += 1
                    for kc in range(4):
                        mm_(p_[:, 0:tw], gw[:, kc, c * 128:(c + 1) * 128], y2b[:, kc, 0:tw], kc == 0, kc == 3, r=["Ggw", "Gy2b"], w=[pk_])
                    act_(sg_[:, 0:tw], p_[:, 0:tw], AF.Sigmoid, r=[pk_], w=["Gsg"], bias=vec[:, 1, c:c + 1])
                    tt_("dve", yo[i][:, c, 0:tw], yy[:, c, 0:tw], sg_[:, 0:tw], ALU.mult, r=["Gyy", "Gsg"], w=[("Gyo", i)])
                S.dma("sp", ycT[:, t0:t0 + tw].rearrange("(c p) t -> p c t", p=128), yo[i][:, :, 0:tw], r=[("Gyo", i)], w=[("ycT", t) for t in tks])
        S.barrier()
        if stop_after == "G":
            break

        with ExitStack() as st:
            KT = sb(st, "KT", [128, 4, N], BF16)
            S.op("pool", lambda e: e.memset(KT[64:128], 0.0), w=[("KT", t_) for t_ in range(NT)])
            Vt = sb(st, "Vt", [128, NT, 4, 65], BF16)
            gain = sb(st, "gain", [128, 1280])
            S.dma("sp", gain[:], qkgain_in[l], w=["gain"])
            S.op("pool", lambda e: e.memset(Vt[:], 1.0), w=["Vt"])
            xq = [sb(st, f"xq{i}", [128, 1280]) for i in range(2)]
            xv = [sb(st, f"xv{i}", [128, 256]) for i in range(2)]
            sq = sb(st, "sq", [128, 1280])
            ssq = sb(st, "ssq", [128, 20])
            xn = sb(st, "xn", [128, 1280])
            t1 = sb(st, "t1", [128, 1280])
            t2 = sb(st, "t2", [128, 1280])
            xb = sb(st, "xb", [128, 1280], BF16)
            cs = [sb(st, f"cs{i}", [128, 2, 64]) for i in range(2)]
            trp = [ps(st, f"trp{i}", [64, 8, 128], BF16) for i in range(1)]

            def qk_prep(tt, src, c0, nh, i):
                w_ = nh * 64
                v3 = lambda t: t[:, 0:w_].rearrange("p (h d) -> p h d", d=64)
                tt_("dve", sq[:, 0:w_], src[:, 0:w_], src[:, 0:w_], ALU.mult, r=[("xq", i)], w=["sq"])
                red_("dve", ssq[:, 0:nh], v3(sq), r=["sq"], w=["ssq"])
                rstd_(ssq[:, 0:nh], ssq[:, 0:nh], 1.0 / 64, RMS_EPS, r=["ssq"], w=["ssq"])
                tt_("dve", v3(xn), v3(src), ssq[:, 0:nh].unsqueeze(2).broadcast_to([128, nh, 64]), ALU.mult,
                    r=[("xq", i), "ssq"], w=["xn"])
                if tt < 2:
                    tt_("dve", xb[:, 0:w_], xn[:, 0:w_], gain[:, c0:c0 + w_], ALU.mult, r=["xn", "gain"], w=["xb"])
                    return
                tt_("dve", xn[:, 0:w_], xn[:, 0:w_], gain[:, c0:c0 + w_], ALU.mult, r=["xn", "gain"], w=["xn"])
                cst = cs[tt % 2]
                ck = ("cs", tt % 2)
                cosb = cst[:, 0, :].unsqueeze(1).broadcast_to([128, nh, 64])
                tt_("dve", v3(t1), v3(xn), cosb, ALU.mult, r=["xn", ck], w=["t1"])
                v4 = lambda t: t[:, 0:w_].rearrange("p (h a b i) -> p (h a) b i", a=2, b=2, i=16)
                sn4 = cst[:, 1, :].rearrange("p (a b i) -> p a b i", a=2, b=2)
                for hb in range(2):
                    snb = sn4[:, :, hb, :].unsqueeze(1).broadcast_to([128, nh, 2, 16])
                    o4 = t2[:, 0:w_].rearrange("p (h a b i) -> p h a b i", a=2, b=2, i=16)[:, :, :, hb, :]
                    i4 = xn[:, 0:w_].rearrange("p (h a b i) -> p h a b i", a=2, b=2, i=16)[:, :, :, 1 - hb, :]
                    tt_("dve", o4, i4, snb, ALU.mult, r=["xn", ck], w=["t2"])
                tt_("dve", xb[:, 0:w_], t1[:, 0:w_], t2[:, 0:w_], ALU.add, r=["t1", "t2"], w=["xb"])

            def load_cs(tt):
                if tt >= 2:
                    r0 = (tt - 2) * 128
                    S.dma("sp", cs[tt % 2][:, 0, :], rope_cos[r0:r0 + 128, :], w=[("cs", tt % 2)])
                    S.dma("sp", cs[tt % 2][:, 1, :], rope_sin[r0:r0 + 128, :], w=[("cs", tt % 2)])

            for tt in range(NT):
                i = tt % 2
                S.dma("sp", xq[i][:, 0:256], p_tm[tt * 128:(tt + 1) * 128, C_AT + 1024:C_AT + 1280], r=[("p_tm", tt)], w=[("xq", i)])
                S.dma("sp", xv[i][:], p_tm[tt * 128:(tt + 1) * 128, C_AT + 1280:C_AT + 1536], r=[("p_tm", tt)], w=[("xv", i)])
                load_cs(tt)
                qk_prep(tt, xq[i], 1024, 4, i)
                for g in range(4):
                    tr_(trp[0][:, g, :], xb[:, g * 64:(g + 1) * 64], identb[:], r=["xb", "identb"], w=[("trp", 0)])
                act_(KT[0:64, :, tt * 128:(tt + 1) * 128], trp[0][:, 0:4, :], AF.Copy, r=[("trp", 0)], w=[("KT", tt)])
                cp_("pool", Vt[:, tt, :, 0:64], xv[i][:].rearrange("p (g d) -> p g d", d=64), r=[("xv", i)], w=[("Vt", tt)])
            QT = [sb(st, f"QT{i}", [128, 16, 128], BF16) for i in range(2)]
            for i_ in range(2):
                S.op("pool", lambda e, i_=i_: e.memset(QT[i_][64:128], 0.0), w=[("QT", i_)])
            NB = 4
            Pt = [sb(st, f"Pt{i}", [128, 512], BF16) for i in range(NB)]
            sps = [ps(st, f"sps{i}", [128, 512]) for i in range(NB)]
            ops_ = [ps(st, f"ops{i}", [65, 512]) for i in range(2)]
            bcp = ps(st, "bcp", [64, 512])
            rd = sb(st, "rd", [65, 512])
            osb = sb(st, "osb", [64, 512])
            yb = [sb(st, f"yb{i}", [64, 512], BF16) for i in range(2)]
            kq = 0
            ko = 0
            pending = []
            for tt in range(NT):
                i = tt % 2
                S.dma("sp", xq[i][:, 0:1024], p_tm[tt * 128:(tt + 1) * 128, C_AT:C_AT + 1024], r=[("p_tm", tt)], w=[("xq", i)])
                load_cs(tt)
                qk_prep(tt, xq[i], 0, 16, i)
                for hh in range(2):
                    for h in range(8):
                        hd = hh * 8 + h
                        tr_(trp[0][:, h, :], xb[:, hd * 64:(hd + 1) * 64], identb[:], r=["xb", "identb"], w=[("trp", 0)])
                    act_(QT[i][0:64, hh * 8:(hh + 1) * 8, :], trp[0][:], AF.Copy, r=[("trp", 0)], w=[("QT", i)])
                kts = [0, 1] if tt < 2 else list(range(NT))
                for g in range(4):
                    o_ = ops_[ko % 2]
                    okey = ("ops", ko % 2)
                    ybt = yb[ko % 2]
                    ykey = ("yb", ko % 2)
                    ko += 1
                    PIPE = 3
                    nk = len(kts)
                    base = kq
                    kq += nk
                    for n_ in range(nk + PIPE):
                        if n_ == min(6, nk - 1) and pending:
                            pending.pop(0)()
                        if n_ < nk:
                            kt = kts[n_]
                            j = (base + n_) % NB
                            mm_(sps[j][:], KT[:, g, kt * 128:(kt + 1) * 128], QT[i][:, 4 * g:4 * g + 4, :].rearrange("p h t -> p (h t)"),
                                True, True, r=[("KT", kt), ("QT", i)], w=[("sps", j)])
                            act_(Pt[j][:], sps[j][:], AF.Exp, r=[("sps", j)], w=[("Pt", j)], scale=0.125)
                        m_ = n_ - PIPE
                        if m_ >= 0:
                            kt = kts[m_]
                            j = (base + m_) % NB
                            mm_(o_[:], Vt[:, kt, g, :], Pt[j][:], m_ == 0, m_ == nk - 1, r=[("Vt", kt), ("Pt", j)], w=[okey])
                    def epilogue(o_=o_, okey=okey, ybt=ybt, ykey=ykey, g=g, tt=tt):
                        S.op("dve", lambda e: e.reciprocal(out=rd[64:65, :], in_=o_[64:65, :]), r=[okey], w=["rd"])
                        mm_(bcp[:], ones[64:65, 0:64], rd[64:65, :], True, True, r=["ones", "rd"], w=["bcp"])
                        cp_("pool" if False else "dve", osb[:], o_[0:64, :], r=[okey], w=["osb"])
                        tt_("dve", ybt[:], osb[:], bcp[:], ALU.mult, r=["osb", "bcp"], w=[ykey])
                        S.dma("sp", ybT[4 * g:4 * g + 4, :, tt * 128:(tt + 1) * 128].rearrange("h d t -> d h t"),
                              ybt[:].rearrange("p (h t) -> p h t", t=128), r=[ykey], w=[("ybT", tt)])
                    pending.append(epilogue)
            while pending:
                pending.pop(0)()
        S.barrier()
        if stop_after == "F":
            break

        stHI = ExitStack()
        gtb = sb(stHI, "gtb", [128, 2, 2, D])
        with ExitStack() as st:
            dg = [sb(st, f"dg{i}", [128, 128]) for i in range(2)]
            pgt = [ps(st, f"pgt{i}", [128, 512]) for i in range(2)]
            kk_ = 0
            for wh, c0 in ((0, 16), (1, 40)):
                for g in range(2):
                    for hf in range(2):
                        p_ = pgt[kk_ % 2]
                        pk_ = ("pgt", kk_ % 2)
                        kk_ += 1
                        for j in range(4):
                            kc = hf * 4 + j
                            i = kc % 2
                            ts_("dve", dg[i][:], ident[:], modT[:, l, c0 + kc, g:g + 1], None, ALU.mult, None, r=["ident", "modT"], w=[("dg", i)])
                            mm_(p_[:, j * 128:(j + 1) * 128], ones[:], dg[i][:], True, True, r=["ones", ("dg", i)], w=[pk_])
                        act_(gtb[:, wh, g, hf * 512:(hf + 1) * 512], p_[:], AF.Copy, r=[pk_], w=["gtb"])
        S.barrier()

        def ln_tail(st_tiles, h, lng, lnb, outt, hk, ok):
            hsq, st4 = st_tiles
            red_("dve", st4[:, 0:1], h[:], r=[hk], w=["st4"])
            tt_("dve", hsq[:], h[:], h[:], ALU.mult, r=[hk], w=["hsq"])
            red_("dve", st4[:, 1:2], hsq[:], r=["hsq"], w=["st4"])
            ts_("dve", st4[:, 0:1], st4[:, 0:1], 1.0 / D, None, ALU.mult, None, r=["st4"], w=["st4"])
            tt_("dve", st4[:, 2:3], st4[:, 0:1], st4[:, 0:1], ALU.mult, r=["st4"], w=["st4"])
            S.op("dve", lambda e: e.scalar_tensor_tensor(out=st4[:, 3:4], in0=st4[:, 1:2], scalar=1.0 / D, in1=st4[:, 2:3],
                                                       op0=ALU.mult, op1=ALU.subtract), r=["st4"], w=["st4"])
            rstd_(st4[:, 3:4], st4[:, 3:4], 1.0, LN_EPS, r=["st4"], w=["st4"])
            ts_("dve", hsq[:], h[:], st4[:, 0:1], st4[:, 3:4], ALU.subtract, ALU.mult, r=[hk, "st4"], w=["hsq"])
            tt_("dve", hsq[:], hsq[:], lng, ALU.mult, r=["hsq", "lnw"], w=["hsq"])
            tt_("dve", outt[:], hsq[:], lnb, ALU.add, r=["hsq", "lnw"], w=[ok])

        with ExitStack() as st:
            wa = sb(st, "Hwa", [128, 4, D], BF16)
            wbb = sb(st, "Hwb", [128, 8, D], BF16)
            wc = sb(st, "Hwc", [128, 4, D], BF16)
            wo = sb(st, "Hwo", [128, 8, D], BF16)
            lnw = sb(st, "Hln", [128, 2, D])
            S.dma("pool", wa[:], proja_in[l].rearrange("(kc p) n -> p kc n", p=128), w=["Hwa"])
            S.dma("pool", wbb[:], projb_in[l].rearrange("(kc p) n -> p kc n", p=128), w=["Hwb"])
            S.dma("pool", wc[:], projc_in[l].rearrange("(kc p) n -> p kc n", p=128), w=["Hwc"])
            S.dma("pool", wo[:], wout_in[l].rearrange("(kc p) n -> p kc n", p=128), w=["Hwo"])
            S.dma("sp", lnw[:], ln_in[l, 0:2].rearrange("k p c -> p k c"), w=["lnw"])
            TW = 256
            ya_t = [sb(st, f"Hya{i}", [128, 4, TW], BF16) for i in range(2)]
            yb_t = [sb(st, f"Hyb{i}", [128, 8, TW], BF16) for i in range(2)]
            yc_t = [sb(st, f"Hyc{i}", [128, 4, TW], BF16) for i in range(2)]
            g_t = [sb(st, f"Hg{i}", [128, 24, TW], BF16) for i in range(2)]
            mg = sb(st, "Hmg", [128, TW])
            t_ = sb(st, "Ht", [128, TW])
            mT = sb(st, "HmT", [128, 8, TW], BF16)
            xt_ = [sb(st, f"Hx{i}", [128, D]) for i in range(2)]
            hh = sb(st, "Hh", [128, D])
            hsq = sb(st, "Hhsq", [128, D])
            st4 = sb(st, "Hst4", [128, 4])
            xo = [sb(st, f"Hxo{i}", [128, D]) for i in range(2)]
            x2T = [sb(st, f"Hx2T{i}", [128, 8, 128], BF16) for i in range(2)]
            pH = [ps(st, f"pH{i}", [128, 512]) for i in range(6)]
            pT2 = [ps(st, f"pT2{i}", [128, 4, 128]) for i in range(2)]
            kp = [0]

            def PH():
                i = kp[0] % 6
                kp[0] += 1
                return pH[i], ("pH", i)

            for gi, t0 in enumerate(range(0, N, TW)):
                i = gi % 2
                tks = [t0 // 128, t0 // 128 + 1]
                S.dma("sp", ya_t[i][:], yaT[:, t0:t0 + TW].rearrange("(kc p) t -> p kc t", p=128), r=[("yaT", t) for t in tks], w=[("Hya", i)])
                S.dma("sp", yb_t[i][:], ybT.rearrange("h d t -> (h d) t")[:, t0:t0 + TW].rearrange("(kc p) t -> p kc t", p=128),
                      r=[("ybT", t) for t in tks], w=[("Hyb", i)])
                S.dma("sp", yc_t[i][:], ycT[:, t0:t0 + TW].rearrange("(kc p) t -> p kc t", p=128), r=[("ycT", t) for t in tks], w=[("Hyc", i)])
                S.dma("sp", g_t[i][:], gT[:, t0:t0 + TW].rearrange("(kc p) t -> p kc t", p=128), r=[("gT", c) for c in range(24)], w=[("Hg", i)])
                for m_ in range(8):
                    ms = slice(m_ * 128, (m_ + 1) * 128)
                    pa, pak = PH()
                    for kc in range(4):
                        mm_(pa[:, 0:TW], wa[:, kc, ms], ya_t[i][:, kc, :], kc == 0, kc == 3, r=["Hwa", ("Hya", i)], w=[pak])
                    pb_, pbk = PH()
                    for kc in range(8):
                        mm_(pb_[:, 0:TW], wbb[:, kc, ms], yb_t[i][:, kc, :], kc == 0, kc == 7, r=["Hwb", ("Hyb", i)], w=[pbk])
                    pc_, pck = PH()
                    for kc in range(4):
                        mm_(pc_[:, 0:TW], wc[:, kc, ms], yc_t[i][:, kc, :], kc == 0, kc == 3, r=["Hwc", ("Hyc", i)], w=[pck])
                    tt_("dve", mg[:], pa[:, 0:TW], g_t[i][:, m_, :], ALU.mult, r=[pak, ("Hg", i)], w=["Hmg"])
                    tt_("dve", t_[:], pb_[:, 0:TW], g_t[i][:, 8 + m_, :], ALU.mult, r=[pbk, ("Hg", i)], w=["Ht"])
                    tt_("dve", mg[:], mg[:], t_[:], ALU.add, r=["Hmg", "Ht"], w=["Hmg"])
                    tt_("dve", t_[:], pc_[:, 0:TW], g_t[i][:, 16 + m_, :], ALU.mult, r=[pck, ("Hg", i)], w=["Ht"])
                    tt_("dve", mT[:, m_, :], mg[:], t_[:], ALU.add, r=["Hmg", "Ht"], w=["HmT"])
                for sub in range(2):
                    tt = t0 // 128 + sub
                    g = 1 if tt < 2 else 0
                    j = tt % 2
                    ts_l = slice(sub * 128, (sub + 1) * 128)
                    S.dma("sp", xt_[j][:], xs[tt * 128:(tt + 1) * 128, :], r=[("xs", tt)], w=[("Hx", j)])
                    for hf in range(2):
                        p_, pk_ = PH()
                        for kc in range(8):
                            mm_(p_[:], mT[:, kc, ts_l], wo[:, kc, hf * 512:(hf + 1) * 512], kc == 0, kc == 7, r=["HmT", "Hwo"], w=[pk_])
                        tt_("dve", hh[:, hf * 512:(hf + 1) * 512], p_[:], gtb[:, 0, g, hf * 512:(hf + 1) * 512], ALU.mult, r=[pk_, "gtb"], w=["Hh"])
                    S.op("dve", lambda e, j=j: e.scalar_tensor_tensor(out=hh[:], in0=xt_[j][:], scalar=ALPHA, in1=hh[:], op0=ALU.mult, op1=ALU.add),
                         r=[("Hx", j), "Hh"], w=["Hh"])
                    ln_tail((hsq, st4), hh, lnw[:, 0, :], lnw[:, 1, :], xo[j], "Hh", ("Hxo", j))
                    S.dma("act", xmid[tt * 128:(tt + 1) * 128, :], xo[j][:], r=[("Hxo", j)], w=[("xmid", tt)])
                    for hf in range(2):
                        for q in range(4):
                            kc = hf * 4 + q
                            tr_(pT2[hf][:, q, :], xo[j][:, kc * 128:(kc + 1) * 128], ident[:], r=[("Hxo", j), "ident"], w=[("pT2", hf)])
                        for q in range(4):
                            kc = hf * 4 + q
                            act_(x2T[j][:, kc, :], pT2[hf][:, q, :], AF.Identity, r=[("pT2", hf), "modT"], w=[("Hx2T", j)],
                                 scale=modT[:, l, 32 + kc, g:g + 1], bias=modT[:, l, 24 + kc, g:g + 1])
                    S.dma("act", xm2T[:, tt * 128:(tt + 1) * 128].rearrange("(kc p) t -> p kc t", p=128), x2T[j][:], r=[("Hx2T", j)], w=[("xm2T", tt)])
        S.barrier()
        if stop_after == "H":
            stHI.close()
            break

        with ExitStack() as st:
            w1 = sb(st, "Iw1", [128, 8, 4 * D], BF16)
            for q in range(4):
                S.dma("pool", w1[:, :, q * D:(q + 1) * D], w1_in[l, :, q * D:(q + 1) * D].rearrange("(kc p) n -> p kc n", p=128), w=["Iw1"])
            xi = [sb(st, f"Ixi{i}", [128, 8, 512], BF16) for i in range(2)]
            rl = [sb(st, f"Irl{i}", [128, 512]) for i in range(2)]
            ho = [sb(st, f"Iho{i}", [128, 8, 512], BF16) for i in range(2)]
            pI = [ps(st, f"pI{i}", [128, 512]) for i in range(4)]
            kp = 0
            ko = 0
            for gi, t0 in enumerate(range(0, N, 512)):
                tw = min(512, N - t0)
                i = gi % 2
                tks = list(range(t0 // 128, (t0 + tw) // 128))
                S.dma("sp", xi[i][:, :, 0:tw], xm2T[:, t0:t0 + tw].rearrange("(kc p) t -> p kc t", p=128), r=[("xm2T", t) for t in tks], w=[("Ixi", i)])
                for c8 in range(4):
                    o_ = ho[ko % 2]
                    okey = ("Iho", ko % 2)
                    ko += 1
                    for c in range(8):
                        cb = c8 * 8 + c
                        p_ = pI[kp % 4]
                        pk_ = ("pI", kp % 4)
                        r_ = rl[kp % 2]
                        rk_ = ("Irl", kp % 2)
                        kp += 1
                        for kc in range(8):
                            mm_(p_[:, 0:tw], w1[:, kc, cb * 128:(cb + 1) * 128], xi[i][:, kc, 0:tw], kc == 0, kc == 7, r=["Iw1", ("Ixi", i)], w=[pk_])
                        act_(r_[:, 0:tw], p_[:, 0:tw], AF.Relu, r=[pk_], w=[rk_])
                        tt_("dve", o_[:, c, 0:tw], r_[:, 0:tw], r_[:, 0:tw], ALU.mult, r=[rk_], w=[okey])
                    S.dma("sp", h1T[c8 * 1024:(c8 + 1) * 1024, t0:t0 + tw].rearrange("(c p) t -> p c t", p=128), o_[:, :, 0:tw], r=[okey],
                          w=[("h1T", t, c8) for t in tks])
        S.barrier()
        with ExitStack() as st:
            w2 = sb(st, "Iw2", [128, 32, D], BF16)
            for q in range(4):
                S.dma("pool", w2[:, q * 8:(q + 1) * 8, :], w2_in[l, q * 1024:(q + 1) * 1024, :].rearrange("(kc p) n -> p kc n", p=128), w=["Iw2"])
            lnw = sb(st, "Iln", [128, 2, D])
            S.dma("sp", lnw[:], ln_in[l, 2:4].rearrange("k p c -> p k c"), w=["lnw"])
            h1 = [sb(st, f"Ih1{i}", [128, 32, 128], BF16) for i in range(2)]
            xm_ = [sb(st, f"Ixm{i}", [128, D]) for i in range(2)]
            hh = sb(st, "Ihh", [128, D])
            hsq = sb(st, "Ihsq", [128, D])
            st4 = sb(st, "Ist4", [128, 4])
            xo = [sb(st, f"Ixo{i}", [128, D]) for i in range(2)]
            pJ = [ps(st, f"pJ{i}", [128, 512]) for i in range(4)]
            kp = 0
            for tt in range(NT):
                i = tt % 2
                g = 1 if tt < 2 else 0
                lo, hi = tt * 128, (tt + 1) * 128
                S.dma("sp", h1[i][:], h1T[:, lo:hi].rearrange("(c p) t -> p c t", p=128), r=[("h1T", tt, c8) for c8 in range(4)], w=[("Ih1", i)])
                S.dma("sp", xm_[i][:], xmid[lo:hi, :], r=[("xmid", tt)], w=[("Ixm", i)])
                for hf in range(2):
                    p_ = pJ[kp % 4]
                    pk_ = ("pJ", kp % 4)
                    kp += 1
                    for kc in range(32):
                        mm_(p_[:], h1[i][:, kc, :], w2[:, kc, hf * 512:(hf + 1) * 512], kc == 0, kc == 31, r=[("Ih1", i), "Iw2"], w=[pk_])
                    tt_("dve", hh[:, hf * 512:(hf + 1) * 512], p_[:], gtb[:, 1, g, hf * 512:(hf + 1) * 512], ALU.mult, r=[pk_, "gtb"], w=["Ihh"])
                S.op("dve", lambda e, i=i: e.scalar_tensor_tensor(out=hh[:], in0=xm_[i][:], scalar=ALPHA, in1=hh[:], op0=ALU.mult, op1=ALU.add),
                     r=[("Ixm", i), "Ihh"], w=["Ihh"])
                ln_tail((hsq, st4), hh, lnw[:, 0, :], lnw[:, 1, :], xo[i], "Ihh", ("Ixo", i))
                S.dma("act", xs[lo:hi, :], xo[i][:], r=[("Ixo", i)], w=[("xs", tt)])
                if l == n_layers - 1 and tt >= 2:
                    S.dma("act", out[lo - NCTX:hi - NCTX, :], xo[i][:], r=[("Ixo", i)], w=[("out", tt)])
        S.barrier()
        stHI.close()
        if stop_after == "I":
            break

    S.barrier()
    top.close()
    return nc, S


def _prep_inputs(inp, b):
    f = lambda a: np.ascontiguousarray(a, dtype=np.float32)
    m = {}
    m["x_in"] = f(np.concatenate([inp["ctx"][b], inp["x"][b]], axis=0))
    cT = np.stack([inp["c"][b].reshape(8, 128).T, inp["c_ctx"].reshape(8, 128).T], axis=-1)
    m["cT"] = f(cT)
    m["ident"] = np.eye(128, dtype=np.float32)
    m["mod_w"] = f(inp["mod_w"])
    m["mod_bT"] = f(inp["mod_b"].reshape(DEPTH, 48, 128).transpose(0, 2, 1))
    m["w_in"] = f(inp["w_in"])
    m["qkgain"] = f(np.broadcast_to(np.concatenate([np.tile(inp["attn_q_gain"], (1, 16)), np.tile(inp["attn_k_gain"], (1, 4))],
                                                   axis=1)[:, None, :], (DEPTH, 128, 1280)))
    rows = NLAT // 64
    row = np.repeat(np.arange(rows, dtype=np.float32), 64)
    col = np.tile(np.arange(64, dtype=np.float32), rows)
    inv = (np.float32(10000.0) ** (-np.arange(16, dtype=np.float32) / np.float32(16))).astype(np.float32)
    ang = np.stack([row, col], axis=-1)[:, :, None] * inv
    ang = np.broadcast_to(ang[:, :, None, :], (NLAT, 2, 2, 16)).reshape(NLAT, 64)
    sgn = np.broadcast_to(np.array([-1.0, 1.0], np.float32)[None, None, :, None], (NLAT, 2, 2, 16)).reshape(NLAT, 64)
    m["rope_cos"] = f(np.cos(ang))
    m["rope_sin"] = f(np.sin(ang) * sgn)
    m["ones"] = np.ones((128, 128), np.float32)
    ii = np.arange(128)
    mS0 = (ii[:, None] < ii[None, :]).astype(np.float32)
    mS1 = (ii[:, None] > ii[None, :]).astype(np.float32)
    eye = np.eye(128, dtype=np.float32)
    m["masks"] = f(np.stack([mS0, mS1, mS0 + eye, mS1 + eye]))
    bc = lambda a: np.broadcast_to(a[:, None, :], (DEPTH, 128, a.shape[-1]))
    m["mu_bc"] = f(bc(inp["rwkv_mu"]))
    m["rwp_bc"] = f(np.stack([bc(inp[k]) for k in ("rwkv_k_k", "rwkv_k_a", "rwkv_r_k", "rwkv_gn_w", "rwkv_gn_b")], axis=1))
    m["lrb"] = f(np.concatenate([inp["rwkv_w0"], inp["rwkv_a0"]], axis=1))
    m["lrw"] = f(np.concatenate([inp["rwkv_w_up"], inp["rwkv_a_up"]], axis=2).transpose(0, 2, 1, 3))
    m["g_up"] = f(inp["rwkv_g_up"])
    rep = lambda a: np.broadcast_to(a[:, :, None, :], (DEPTH, 2, 128, 2048))
    m["ss_tab"] = f(np.stack([rep(inp["ssm_a_re"].reshape(DEPTH, 2, 2048)), rep(inp["ssm_a_im"].reshape(DEPTH, 2, 2048)),
                              rep(np.repeat(inp["ssm_log_dt"], 64, axis=-1))], axis=2))
    p_ = np.arange(128, dtype=np.float32)
    m["mcol"] = f(np.stack([p_ + 1, 128 - p_, -(p_ + 1), -(128 - p_)], axis=1))
    bblk = np.zeros((DEPTH, 2, 4, 8, 16, 8, 64), np.float32)
    for k_, nm in enumerate(("ssm_b_re", "ssm_b_im")):
        b5 = inp[nm].reshape(DEPTH, 4, 8, 64, 16)
        for g_ in range(8):
            bblk[:, k_, :, g_, :, g_, :] = b5[:, :, g_].transpose(0, 1, 3, 2)
    m["ss_bblk"] = f(bblk.reshape(DEPTH, 2, 4, 128, 512).transpose(0, 1, 3, 2, 4))
    cblk = np.zeros((DEPTH, 2, 16, 2, 64, 8, 16), np.float32)
    for k_, nm in enumerate(("ssm_c_re", "ssm_c_im")):
        c5 = inp[nm].reshape(DEPTH, 16, 2, 16, 64)
        for gp in range(16):
            for g2 in range(2):
                gl = (2 * gp + g2) % 8
                cblk[:, k_, gp, g2, :, gl, :] = c5[:, gp, g2].transpose(0, 2, 1)
    m["ss_cblk"] = f(cblk.reshape(DEPTH, 32, 128, 128).transpose(0, 2, 1, 3))
    m["ss_vec"] = f(np.stack([inp["ssm_d"].reshape(DEPTH, 4, 128).transpose(0, 2, 1),
                              inp["ssm_glu_b"].reshape(DEPTH, 4, 128).transpose(0, 2, 1)], axis=2))
    m["glu_w"] = f(inp["ssm_glu_w"])
    for k_ in ("proj_a", "proj_b", "proj_c", "w_out", "mlp_w1", "mlp_w2"):
        m[k_] = f(inp[k_])
    m["ln_bc"] = f(np.stack([bc(inp[k_]) for k_ in ("ln1_g", "ln1_b", "ln2_g", "ln2_b")], axis=1))
    selm = np.zeros((4, 128, 128), np.float32)
    selm[0, 127, :] = 1.0
    selm[1, 0, :] = 1.0
    selm[2, 127, :] = -1.0
    selm[3, 0, :] = -1.0
    m["sel"] = selm
    return m


def kernel(**inp):
    inp = {k: np.asarray(v) for k, v in inp.items()}
    nc, _ = build()
    in_maps = [_prep_inputs(inp, b) for b in range(8)]
    res = run_bass_kernel_spmd(nc, in_maps, core_ids=list(range(8)))
    return np.stack([r["out"] for r in res.results], axis=0).astype(np.float32)
```
